# Optimizing a Trainium2 kernel written in Bass

```python
import math
import jax, jax.numpy as jnp
from jax import lax
import numpy as np

D_MODEL = 2048
BATCH = 4
SEQ = 4096
DEPTH = 2

GRID_W = 64
CTX_LEN = 256
EPS = 1e-6
N_MOD = 6

DN_HEADS = 8
DN_DK = 128
DN_DV = 128
DN_CONV = 5
DN_CHUNK = 64

MLA_HEADS = 8
MLA_Q_RANK = 512
MLA_KV_RANK = 256
MLA_NOPE = 128
MLA_ROPE = 64
MLA_V = 128
MLA_SCALE = (MLA_NOPE + MLA_ROPE) ** -0.5
Q_BLOCK = 128
ROPE_BASE = 10000.0

FN_GROUPS = 4
FN_GROUP_W = 256

N_BRANCH = 3
BRANCH_W = DN_HEADS * DN_DV
FF_HIDDEN = ((8 * D_MODEL + 3 * 256 - 1) // (3 * 256)) * 256

DN_QKV_W = DN_HEADS * (2 * DN_DK + DN_DV)
DN_Z_W = DN_HEADS * DN_DV
FN_W = FN_GROUPS * FN_GROUP_W
IN_SPLITS = (DN_QKV_W, DN_Z_W, 2 * DN_HEADS, 2 * DN_HEADS, MLA_Q_RANK, MLA_KV_RANK, MLA_ROPE, FN_W, N_BRANCH * D_MODEL)
N_IN = sum(IN_SPLITS)

kernel_name = "hybrid_dit_deltanet_fnet_mla"


def _split_cols(t, sizes):
    out, start = [], 0
    for s in sizes:
        out.append(t[..., start:start + s])
        start += s
    return out


def _rms_norm(x, g):
    xf = x.astype(jnp.float32)
    y = xf * lax.rsqrt(jnp.mean(xf * xf, axis=-1, keepdims=True) + EPS)
    return (y * g.astype(jnp.float32)).astype(x.dtype)


def _modulate(h, shift, scale):
    return h * (1.0 + scale) + shift


def _l2norm(x):
    return x * lax.rsqrt(jnp.sum(x * x, axis=-1, keepdims=True) + EPS)


def _axial_rope(n_tokens):
    rows = n_tokens // GRID_W
    row = jnp.repeat(jnp.arange(rows, dtype=jnp.float32), GRID_W)
    col = jnp.tile(jnp.arange(GRID_W, dtype=jnp.float32), rows)
    axis_dim = MLA_ROPE // 2
    inv_freq = ROPE_BASE ** (-jnp.arange(0, axis_dim, 2, dtype=jnp.float32) / axis_dim)
    ang_r = row[:, None] * inv_freq
    ang_c = col[:, None] * inv_freq
    ang = jnp.concatenate([ang_r, ang_r, ang_c, ang_c], axis=-1)
    return jnp.cos(ang), jnp.sin(ang)


def _apply_rope(x, cos, sin):
    xf = x.astype(jnp.float32)
    a1, a2, b1, b2 = jnp.split(xf, 4, axis=-1)
    rot = jnp.concatenate([-a2, a1, -b2, b1], axis=-1)
    return (xf * cos + rot * sin).astype(x.dtype)


def _short_conv(x, w):
    pad = DN_CONV // 2
    return lax.conv_general_dilated(x, w[:, None, :].astype(x.dtype), window_strides=(1,), padding=[(pad, pad)],
                                    dimension_numbers=("NWC", "WIO", "NWC"), feature_group_count=x.shape[-1])


def _dn_prep(qkv, b_raw, a_raw, lp):
    B, L, _ = qkv.shape
    qkv = jax.nn.silu(_short_conv(qkv, lp["dn_conv"])).astype(jnp.float32)
    q, k, v = _split_cols(qkv, (DN_HEADS * DN_DK, DN_HEADS * DN_DK, DN_HEADS * DN_DV))
    q = _l2norm(q.reshape(B, L, DN_HEADS, DN_DK)) * (DN_DK ** -0.5)
    k = _l2norm(k.reshape(B, L, DN_HEADS, DN_DK))
    v = v.reshape(B, L, DN_HEADS, DN_DV)
    beta = jax.nn.sigmoid(b_raw.astype(jnp.float32).reshape(B, L, 2, DN_HEADS))
    a_raw = a_raw.astype(jnp.float32).reshape(B, L, 2, DN_HEADS)
    g = -jnp.exp(lp["dn_a_log"].astype(jnp.float32)) * jax.nn.softplus(a_raw + lp["dn_dt_bias"].astype(jnp.float32))
    return q, k, v, beta, g


def _gated_delta_chunked(q, k, v, g, beta, s0):
    B, L, H, dk = q.shape
    dv = v.shape[-1]
    n = L // DN_CHUNK

    def chunks(t):
        t = t.reshape(B, n, DN_CHUNK, H, *t.shape[3:])
        return jnp.moveaxis(t, (1, 3), (0, 2))

    qc, kc, vc, bc = chunks(q), chunks(k), chunks(v), chunks(beta)
    gc = jnp.cumsum(chunks(g), axis=-1)
    idx = jnp.arange(DN_CHUNK)
    incl = idx[:, None] >= idx[None, :]
    strict = idx[:, None] > idx[None, :]
    decay = jnp.exp(jnp.where(incl, gc[..., :, None] - gc[..., None, :], -jnp.inf))
    kb = kc * bc[..., None]
    a_mat = jnp.where(strict, jnp.einsum("nbhid,nbhjd->nbhij", kb, kc) * decay, 0.0)
    rhs = jnp.concatenate([vc * bc[..., None], kb * jnp.exp(gc)[..., None]], axis=-1)
    sol = lax.linalg.triangular_solve(a_mat + jnp.eye(DN_CHUNK, dtype=a_mat.dtype), rhs,
                                      left_side=True, lower=True, unit_diagonal=True)
    u, w = sol[..., :dv], sol[..., dv:]
    qk = jnp.where(incl, jnp.einsum("nbhid,nbhjd->nbhij", qc, kc) * decay, 0.0)

    def step(s, xs):
        q_i, k_i, u_i, w_i, qk_i, g_i = xs
        v_new = u_i - jnp.einsum("bhcd,bhde->bhce", w_i, s)
        o_i = jnp.einsum("bhcd,bhde->bhce", q_i * jnp.exp(g_i)[..., None], s) + jnp.einsum("bhij,bhje->bhie", qk_i, v_new)
        g_last = g_i[..., -1:]
        s = s * jnp.exp(g_last)[..., None] + jnp.einsum("bhcd,bhce->bhde", k_i * jnp.exp(g_last - g_i)[..., None], v_new)
        return s, o_i

    s_fin, o = lax.scan(step, s0, (qc, kc, u, w, qk, gc))
    o = jnp.moveaxis(o, (0, 2), (1, 3)).reshape(B, L, H, dv)
    return o, s_fin


def _bidir_delta(q, k, v, beta, g, s0_fwd, s0_bwd):
    o_f, s_f = _gated_delta_chunked(q, k, v, g[:, :, 0], beta[:, :, 0], s0_fwd)
    rev = lambda t: jnp.flip(t, axis=1)
    o_b, s_b = _gated_delta_chunked(rev(q), rev(k), rev(v), rev(g[:, :, 1]), rev(beta[:, :, 1]), s0_bwd)
    return o_f + rev(o_b), s_f, s_b


def _gated_head_norm(o, z, g):
    B, L = o.shape[:2]
    zf = z.astype(jnp.float32).reshape(B, L, DN_HEADS, DN_DV)
    y = o * lax.rsqrt(jnp.mean(o * o, axis=-1, keepdims=True) + EPS) * g.astype(jnp.float32) * jax.nn.silu(zf)
    return y.reshape(B, L, DN_HEADS * DN_DV).astype(z.dtype)


def _mla_qkv(cq, ckv, kr, lp, rope):
    B, L, _ = cq.shape
    q = (_rms_norm(cq, lp["mla_q_norm_g"]) @ lp["w_uq"]).reshape(B, L, MLA_HEADS, MLA_NOPE + MLA_ROPE)
    q_nope, q_rope = q[..., :MLA_NOPE], q[..., MLA_NOPE:]
    kv = (_rms_norm(ckv, lp["mla_kv_norm_g"]) @ lp["w_ukv"]).reshape(B, L, MLA_HEADS, MLA_NOPE + MLA_V)
    k_nope, v = kv[..., :MLA_NOPE], kv[..., MLA_NOPE:]
    if rope is not None:
        cos, sin = rope
        q_rope = _apply_rope(q_rope, cos[:, None, :], sin[:, None, :])
        kr = _apply_rope(kr, cos, sin)
    return q_nope, q_rope, k_nope, kr, v


def _mla_attend(q_nope, q_rope, k_nope, k_rope, v):
    s = jnp.einsum("bqhd,bkhd->bhqk", q_nope, k_nope) + jnp.einsum("bqhd,bkd->bhqk", q_rope, k_rope)
    p = jax.nn.softmax(s.astype(jnp.float32) * MLA_SCALE, axis=-1).astype(v.dtype)
    return jnp.einsum("bhqk,bkhd->bqhd", p, v)


def _mla_blocked(q_nope, q_rope, k_nope, k_rope, v):
    B, L = q_nope.shape[:2]
    nb = L // Q_BLOCK
    blk = lambda t: jnp.moveaxis(t.reshape(B, nb, Q_BLOCK, *t.shape[2:]), 1, 0)
    o = lax.map(lambda qs: _mla_attend(qs[0], qs[1], k_nope, k_rope, v), (blk(q_nope), blk(q_rope)))
    return jnp.moveaxis(o, 0, 1).reshape(B, L, MLA_HEADS * MLA_V)


def _fourier(u):
    B, L, _ = u.shape
    ug = u.astype(jnp.float32).reshape(B, L, FN_GROUPS, FN_GROUP_W)
    y = jnp.fft.fft2(ug, axes=(1, 3), norm="ortho").real
    return y.reshape(B, L, FN_W).astype(u.dtype)


def _merge(branches, gate_logits, lp):
    B, L, _ = gate_logits.shape
    gates = jax.nn.sigmoid(gate_logits.reshape(B, L, N_BRANCH, D_MODEL))
    merged = gates[:, :, 0] * (branches[0] @ lp["w_branch"][0])
    for i in range(1, N_BRANCH):
        merged = merged + gates[:, :, i] * (branches[i] @ lp["w_branch"][i])
    return merged @ lp["w_out"]


def _mixer(h_lat, h_ctx, lp, rope, ctx_out):
    qkv_l, z_l, b_l, a_l, cq_l, ckv_l, kr_l, u_l, gate_l = _split_cols(h_lat @ lp["w_in"], IN_SPLITS)
    qkv_c, z_c, b_c, a_c, cq_c, ckv_c, kr_c, u_c, gate_c = _split_cols(h_ctx @ lp["w_in"], IN_SPLITS)
    B = h_lat.shape[0]
    s0 = jnp.zeros((B, DN_HEADS, DN_DK, DN_DV), jnp.float32)
    o_c, s_fwd, s_bwd = _bidir_delta(*_dn_prep(qkv_c, b_c, a_c, lp), s0, s0)
    o_l, _, _ = _bidir_delta(*_dn_prep(qkv_l, b_l, a_l, lp), s_fwd, s_bwd)
    y_dn_l = _gated_head_norm(o_l, z_l, lp["dn_norm_g"])
    qn_c, qr_c, kn_c, kr_c, v_c = _mla_qkv(cq_c, ckv_c, kr_c, lp, None)
    qn_l, qr_l, kn_l, kr_l2, v_l = _mla_qkv(cq_l, ckv_l, kr_l, lp, rope)
    y_mla_l = _mla_blocked(qn_l, qr_l, jnp.concatenate([kn_c, kn_l], axis=1),
                           jnp.concatenate([kr_c, kr_l2], axis=1), jnp.concatenate([v_c, v_l], axis=1))
    y_fn_l = _fourier(u_l)
    y_lat = _merge((y_dn_l, y_fn_l, y_mla_l), gate_l, lp)
    if not ctx_out:
        return y_lat, None
    y_dn_c = _gated_head_norm(o_c, z_c, lp["dn_norm_g"])
    y_mla_c = _mla_attend(qn_c, qr_c, kn_c, kr_c, v_c).reshape(B, -1, MLA_HEADS * MLA_V)
    y_fn_c = _fourier(u_c)
    y_ctx = _merge((y_dn_c, y_fn_c, y_mla_c), gate_c, lp)
    return y_lat, y_ctx


def _swiglu(h, w_in, w_out):
    gu = h @ w_in
    return (jax.nn.silu(gu[..., :FF_HIDDEN]) * gu[..., FF_HIDDEN:]) @ w_out


def setup_inputs(seed: int = 0) -> dict:
    key = jax.random.key(seed)
    ks = jax.random.split(key, 20)
    f32 = jnp.float32
    nrm = lambda k, shape, scale: jax.random.normal(k, shape, f32) * scale
    dt = jnp.exp(jax.random.uniform(ks[10], (DEPTH, 2, DN_HEADS), f32, minval=math.log(1e-3), maxval=math.log(1e-1)))
    return {
        "x": nrm(ks[0], (BATCH, SEQ, D_MODEL), 1.0),
        "c": nrm(ks[1], (BATCH, D_MODEL), 1.0),
        "ctx": nrm(ks[2], (BATCH, CTX_LEN, D_MODEL), 1.0),
        "c_ctx": nrm(ks[3], (D_MODEL,), 1.0),
        "w_ada": nrm(ks[4], (DEPTH, D_MODEL, N_MOD * D_MODEL), 0.5 * D_MODEL ** -0.5),
        "b_ada": nrm(ks[5], (DEPTH, N_MOD * D_MODEL), 0.02),
        "norm_g": 1.0 + nrm(ks[6], (DEPTH, 4, D_MODEL), 0.05),
        "w_in": nrm(ks[7], (DEPTH, D_MODEL, N_IN), D_MODEL ** -0.5),
        "dn_conv": nrm(ks[8], (DEPTH, DN_CONV, DN_QKV_W), DN_CONV ** -0.5),
        "dn_a_log": jnp.log(jax.random.uniform(ks[9], (DEPTH, 2, DN_HEADS), f32, minval=1.0, maxval=16.0)),
        "dn_dt_bias": dt + jnp.log(-jnp.expm1(-dt)),
        "dn_norm_g": 1.0 + nrm(ks[11], (DEPTH, DN_DV), 0.05),
        "mla_q_norm_g": 1.0 + nrm(ks[12], (DEPTH, MLA_Q_RANK), 0.05),
        "mla_kv_norm_g": 1.0 + nrm(ks[13], (DEPTH, MLA_KV_RANK), 0.05),
        "w_uq": nrm(ks[14], (DEPTH, MLA_Q_RANK, MLA_HEADS * (MLA_NOPE + MLA_ROPE)), MLA_Q_RANK ** -0.5),
        "w_ukv": nrm(ks[15], (DEPTH, MLA_KV_RANK, MLA_HEADS * (MLA_NOPE + MLA_V)), MLA_KV_RANK ** -0.5),
        "w_branch": nrm(ks[16], (DEPTH, N_BRANCH, BRANCH_W, D_MODEL), BRANCH_W ** -0.5),
        "w_out": nrm(ks[17], (DEPTH, D_MODEL, D_MODEL), D_MODEL ** -0.5),
        "w_ffn_in": nrm(ks[18], (DEPTH, D_MODEL, 2 * FF_HIDDEN), D_MODEL ** -0.5),
        "w_ffn_out": nrm(ks[19], (DEPTH, FF_HIDDEN, D_MODEL), FF_HIDDEN ** -0.5),
    }


def reference(x, c, ctx, c_ctx, w_ada, b_ada, norm_g, w_in, dn_conv, dn_a_log, dn_dt_bias, dn_norm_g,
              mla_q_norm_g, mla_kv_norm_g, w_uq, w_ukv, w_branch, w_out, w_ffn_in, w_ffn_out):
    B, L, _ = x.shape
    rope = _axial_rope(L)
    for l in range(DEPTH):
        last = l == DEPTH - 1
        lp = {"w_in": w_in[l], "dn_conv": dn_conv[l], "dn_a_log": dn_a_log[l], "dn_dt_bias": dn_dt_bias[l],
              "dn_norm_g": dn_norm_g[l], "mla_q_norm_g": mla_q_norm_g[l], "mla_kv_norm_g": mla_kv_norm_g[l],
              "w_uq": w_uq[l], "w_ukv": w_ukv[l], "w_branch": w_branch[l], "w_out": w_out[l]}
        mod_l = (jax.nn.silu(c) @ w_ada[l] + b_ada[l]).reshape(B, N_MOD, 1, D_MODEL)
        mod_c = (jax.nn.silu(c_ctx) @ w_ada[l] + b_ada[l]).reshape(N_MOD, D_MODEL)
        h_l = _modulate(_rms_norm(x, norm_g[l, 0]), mod_l[:, 0], mod_l[:, 1])
        h_c = _modulate(_rms_norm(ctx, norm_g[l, 0]), mod_c[0], mod_c[1])
        y_l, y_c = _mixer(h_l, h_c, lp, rope, not last)
        x = x + mod_l[:, 2] * _rms_norm(y_l, norm_g[l, 1])
        h_l = _modulate(_rms_norm(x, norm_g[l, 2]), mod_l[:, 3], mod_l[:, 4])
        x = x + mod_l[:, 5] * _rms_norm(_swiglu(h_l, w_ffn_in[l], w_ffn_out[l]), norm_g[l, 3])
        if not last:
            ctx = ctx + mod_c[2] * _rms_norm(y_c, norm_g[l, 1])
            h_c = _modulate(_rms_norm(ctx, norm_g[l, 2]), mod_c[3], mod_c[4])
            ctx = ctx + mod_c[5] * _rms_norm(_swiglu(h_c, w_ffn_in[l], w_ffn_out[l]), norm_g[l, 3])
    return x
```

```python
import numpy as np
from contextlib import ExitStack
import concourse.bass as bass
import concourse.mybir as mybir
F32 = mybir.dt.float32; BF16 = mybir.dt.bfloat16; I32 = mybir.dt.int32
AF = mybir.ActivationFunctionType
ALU = mybir.AluOpType
AX = mybir.AxisListType

class Buf:
    __slots__ = ("name", "w", "r", "excl")
    def __init__(self, name="", excl=False):
        self.name = name
        self.excl = excl
        self.w = None
        self.r = {}

EPOCH = 12000
class Prog:
    ENG = ("pe", "dve", "act", "pool", "sp")
    def __init__(self, nc, es, n_dma_sems=12):
        self.nc = nc; self.es = es; self.es_global = es
        self.engobj = {"pe": nc.tensor, "dve": nc.vector, "act": nc.scalar, "pool": nc.gpsimd, "sp": nc.sync}
        self.streams = {e: [] for e in self.ENG}
        self.sems = {}
        self.cnt = {}
        self.cur = {}
        self.ep = {e: 0 for e in self.ENG}
        for e in self.ENG:
            self._new_epoch(e)
        self.seen = {e: {} for e in self.ENG}
        self.dma_keys = []
        for i in range(n_dma_sems):
            k = ("dma", i)
            self.sems[k] = es.enter_context(nc.semaphore(f"dma{i}"))
            self.cnt[k] = 0
            self.dma_keys.append(k)
        self.dma_rr = 0
        self.n_ops = 0
    def _new_epoch(self, e):
        k = (e, self.ep[e]); self.ep[e] += 1
        self.sems[k] = self.es_global.enter_context(self.nc.semaphore(f"s_{e}_{k[1]}"))
        self.cnt[k] = 0; self.cur[e] = k
    def _deps(self, reads, writes):
        deps = {}
        def need(k, c):
            if deps.get(k, 0) < c: deps[k] = c
        for b in reads:
            if b.w is not None: need(*b.w)
        for b in writes:
            if b.w is not None: need(*b.w)
            for k, c in b.r.items(): need(k, c)
        return deps
    def _emit_waits(self, e, deps, skip_key=None):
        seen = self.seen[e]
        for k, c in deps.items():
            if k == skip_key: continue
            if seen.get(k, 0) >= c: continue
            seen[k] = c
            sem = self.sems[k]
            self.streams[e].append(lambda eng, sem=sem, c=c: eng.wait_ge(sem, c))
    def _mark(self, key, c, reads, writes):
        for b in reads:
            if b.r.get(key, 0) < c: b.r[key] = c
        for b in writes:
            b.w = (key, c); b.r = {}
    def op(self, e, meth, *args, reads=(), writes=(), same_engine_sync=True, **kw):
        writes = list(writes) + [b for b in reads if b.excl]
        reads = [b for b in reads if not b.excl]
        deps = self._deps(reads, writes)
        key = self.cur[e]
        if self.cnt[key] >= EPOCH:
            self._new_epoch(e); key = self.cur[e]
        skip = None
        if e == "pe" or not same_engine_sync:
            deps = {k: c for k, c in deps.items() if k[0] != e}
        self._emit_waits(e, deps, skip)
        self.cnt[key] += 1
        c = self.cnt[key]; sem = self.sems[key]
        self.streams[e].append(lambda eng, meth=meth, args=args, kw=kw, sem=sem: getattr(eng, meth)(*args, **kw).then_inc(sem, 1))
        self._mark(key, c, reads, writes)
        self.n_ops += 1
    def dma(self, q, out, in_, reads=(), writes=(), **kw):
        deps = self._deps(reads, writes)
        k = self.dma_keys[self.dma_rr]; self.dma_rr = (self.dma_rr + 1) % len(self.dma_keys)
        if self.cnt[k] > 0: deps[k] = max(deps.get(k, 0), self.cnt[k])
        self._emit_waits(q, deps)
        self.cnt[k] += 16
        c = self.cnt[k]; sem = self.sems[k]
        self.streams[q].append(lambda eng, out=out, in_=in_, sem=sem, kw=kw: eng.dma_start(out=out, in_=in_, **kw).then_inc(sem, 16))
        self._mark(k, c, reads, writes)
        self.n_ops += 1
    def coll(self, kind, out, in_, groups, reads=(), writes=()):
        deps = self._deps(reads, writes)
        k = ("cc", 0)
        if k not in self.sems:
            self.sems[k] = self.es_global.enter_context(self.nc.semaphore("cc0"))
            self.cnt[k] = 0
            self.dma_keys.append(k)
        self.cnt[k] += 1
        self._emit_waits("pool", deps)
        sem = self.sems[k]
        self.streams["pool"].append(lambda eng, out=out, in_=in_, sem=sem: eng.collective_compute(kind, mybir.AluOpType.bypass, replica_groups=groups, ins=[in_.opt()], outs=[out.opt()]).then_inc(sem, 1))
        self._mark(k, self.cnt[k], reads, writes)
        self.n_ops += 1
    def barrier(self):
        deps = {k: c for k, c in self.cnt.items() if c > 0}
        for e in self.ENG:
            self._emit_waits(e, {k: c for k, c in deps.items() if k != self.cur[e]})
    def flush(self):
        nc = self.nc
        streams = self.streams
        self.streams = {e: [] for e in self.ENG}
        with nc.Block() as block:
            @block.sync
            def _(eng):
                for f in streams["sp"]: f(eng)
            @block.tensor
            def _(eng):
                for f in streams["pe"]: f(eng)
            @block.vector
            def _(eng):
                for f in streams["dve"]: f(eng)
            @block.scalar
            def _(eng):
                for f in streams["act"]: f(eng)
            @block.gpsimd
            def _(eng):
                for f in streams["pool"]: f(eng)
    def finish(self):
        deps = {k: self.cnt[k] for k in self.dma_keys if self.cnt[k] > 0}
        self._emit_waits("sp", deps)
        self.flush()

class Pool:
    def __init__(self, P, name, shape, dtype, n, psum=False):
        self.tiles = []
        for i in range(n):
            if psum:
                t = P.es.enter_context(P.nc.psum_tensor(f"pp_{name}{i}", shape, dtype))
            else:
                t = P.es.enter_context(P.nc.sbuf_tensor(f"sp_{name}{i}", shape, dtype))
            self.tiles.append((t, Buf(f"{name}{i}", excl=psum)))
        self.i = 0
    def get(self):
        t = self.tiles[self.i]; self.i = (self.i + 1) % len(self.tiles)
        return t

def sb(P, name, shape, dtype=F32):
    return P.es.enter_context(P.nc.sbuf_tensor("sb_" + name, shape, dtype)), Buf(name)
def ps(P, name, shape, dtype=F32):
    return P.es.enter_context(P.nc.psum_tensor("ps_" + name, shape, dtype)), Buf(name, excl=True)

import math
D = 2048; KC = 16; FF = 5632; FKC = 44
NCTX = 256; NLAT = 4096; NTOT = NCTX + NLAT; NCH = NTOT // 64
NLH = 2048; NCH2 = 128; NTH = NLH + NCH2
MLA_SCALE = 192 ** -0.5
NAR = 52000
PAIRS = [[0, 1], [2, 3], [4, 5], [6, 7]]
F32R = mybir.dt.float32r
def RR_(ap): return ap.bitcast(F32R)
FM_CHUNKS = [(c0, 128) for c0 in range(0, 2304, 128)] + [(2304, 64)] + [(2368 + i * 128, 128) for i in range(4)]
FM_ROWS = 2880; TM_COL0 = 2880; TM_W = 528; WINR = 3408

class Arena:
    def __init__(self, P):
        self.P = P
        self.banks = [(P.es_global.enter_context(P.nc.psum_tensor(f"bank{i}", [128, 512], F32)), Buf(f"bank{i}", excl=True)) for i in range(8)]
        self.ph = None; self.nph = 0
    def begin(self, nN, nR):
        assert nN + nR <= NAR, (nN, nR)
        self.end()
        self.ph = ExitStack(); self.nph += 1
        self.tN = self.ph.enter_context(self.P.nc.sbuf_tensor(f"arN{self.nph}", [128, max(nN, 8)], F32))
        self.tR = self.ph.enter_context(self.P.nc.sbuf_tensor(f"arR{self.nph}", [128, max(nR, 8)], F32))
        self.cap = {False: nN, True: nR}; self.off = {False: 0, True: 0}; self.bi = 0
    def end(self):
        self.P.barrier()
        if self.ph is not None:
            self.P.flush(); self.ph.close(); self.ph = None
    def sb(self, name, shape, R=False):
        n = int(np.prod(shape[1:]))
        assert self.off[R] + n <= self.cap[R], (name, R, self.off[R], n, self.cap[R])
        t = self.tR if R else self.tN
        v = t[0:shape[0], self.off[R]:self.off[R] + n]
        self.off[R] += n
        if len(shape) == 3: v = v.rearrange("p (a b) -> p a b", a=shape[1])
        elif len(shape) == 4: v = v.rearrange("p (a b c) -> p a b c", a=shape[1], b=shape[2])
        return v, Buf(name)
    def pool(self, name, shape, n, R=False):
        return RR([self.sb(f"{name}{i}", shape, R) for i in range(n)])
    def pspool(self, n):
        b = self.banks[self.bi:self.bi + n]; assert len(b) == n; self.bi += n
        return RR(b)

class RR:
    def __init__(self, tiles): self.tiles = tiles; self.i = 0
    def get(self):
        t = self.tiles[self.i]; self.i = (self.i + 1) % len(self.tiles); return t

def common(P, AR):
    class E: pass
    E = E()
    E.ones, E.onesb = AR.sb("ones", [128, 128]); P.op("pool", "memset", E.ones[:], 1.0, writes=[E.onesb])
    E.eps = {}
    for dim, val in ((2048, 2048e-6), (512, 512e-6), (256, 256e-6), (1, 1e-6)):
        t, b = AR.sb(f"eps{dim}", [128, 1]); P.op("pool", "memset", t[:], val, writes=[b]); E.eps[dim] = (t, b)
    E.sqp = AR.pool("sq", [128, 512], 2)
    E.ssp, E.sspb = AR.pspool(1).get()
    E.rstd, E.rstdb = AR.sb("rstd", [128, 512])
    E.ev = 0
    return E

def rms_rstd(P, E, src, srcb, nch, tb, dim):
    for c in range(nch):
        sq, sqb = E.sqp.get()
        P.op("act", "activation", out=sq[:, 0:tb], in_=src[:, c, 0:tb], func=AF.Square, reads=[srcb], writes=[sqb])
        P.op("pe", "matmul", E.ssp[:, 0:tb], E.ones[:], sq[:, 0:tb], start=(c == 0), stop=(c == nch - 1), reads=[sqb, E.onesb], writes=[E.sspb])
    eb = E.eps[dim]
    P.op("act", "activation", out=E.rstd[:, 0:tb], in_=E.ssp[:, 0:tb], func=AF.Sqrt, bias=eb[0][:, 0:1], reads=[E.sspb, eb[1]], writes=[E.rstdb])
    P.op("dve", "reciprocal", E.rstd[:, 0:tb], E.rstd[:, 0:tb], reads=[E.rstdb], writes=[E.rstdb])

def evac(P, E, dst, src, reads, writes):
    if E.ev % 2 == 0: P.op("dve", "tensor_copy", dst, src, reads=reads, writes=writes)
    else: P.op("act", "activation", out=dst, in_=src, func=AF.Copy, reads=reads, writes=writes)
    E.ev += 1

def ld(P, AR, name, src, shape, q="sp"):
    t, b = AR.sb(name, shape); P.dma(q, t[:], src, writes=[b]); return t, b

def emit_mod(P, AR, E, wget, pmod, cT_d, wada_d, bada_d, nchunks, cpt=2):
    ct, cb = ld(P, AR, "cT", cT_d, [128, KC, 2]); bt, bb = ld(P, AR, "bada", bada_d, [128, 96])
    sc, scb = AR.sb("sc", [128, KC, 2]); mod_t, mod_b = AR.sb("mod", [128, 96, 2])
    P.op("act", "activation", out=sc[:], in_=ct[:], func=AF.Silu, reads=[cb], writes=[scb])
    for n0 in range(0, nchunks, cpt):
        wt, wb = wget()
        P.dma("sp", wt[:, :, 0:cpt * 128], wada_d[:, n0 * 128:(n0 + cpt) * 128].rearrange("(c p) n -> p c n", p=128), writes=[wb])
        for gi in range(cpt):
            n = n0 + gi
            pt, pb = pmod.get()
            for k in range(KC):
                P.op("pe", "matmul", pt[:, 0:2], wt[:, k, gi * 128:(gi + 1) * 128], sc[:, k, :], start=(k == 0), stop=(k == KC - 1), reads=[wb, scb], writes=[pb])
            P.op("dve", "tensor_scalar", mod_t[:, n, :], pt[:, 0:2], bt[:, n:n + 1], None, ALU.add, reads=[pb, bb], writes=[mod_b])
    return mod_t, mod_b

def yl_write(P, YL, row0, col0, width, src, reads):
    c = col0
    while c < col0 + width:
        blk, off = c // 256, c % 256
        w = min(256 - off, col0 + width - c)
        P.dma("act", YL[blk, row0:row0 + 128, off:off + w], src[:, c - col0:c - col0 + w], reads=reads)
        c += w

def emit_A(P, AR, dr, l):
    AR.begin(21500, 24576); E = common(P, AR)
    xsrc = dr["xT_in"] if l == 0 else dr["XG"]
    wpool = AR.pool("w", [128, KC, 512], 2, R=True)
    pmm = AR.pspool(5)
    wmod = AR.pool("wm", [128, KC, 256], 2)
    mod_t, mod_b = emit_mod(P, AR, E, lambda: wmod.get(), AR.pspool(2), dr["cT"], dr[f"w_ada{l}"], dr[f"b_ada{l}"], 32)
    g0, g0b = ld(P, AR, "g0", dr[f"g{l}"][:, 0, :], [128, KC])
    At, Ab = AR.sb("A", [128, KC, 2])
    for j in range(2):
        P.op("dve", "tensor_scalar", At[:, :, j], mod_t[:, KC:2 * KC, j], 1.0, math.sqrt(D), ALU.add, ALU.mult, reads=[mod_b], writes=[Ab])
        P.op("dve", "tensor_tensor", At[:, :, j], At[:, :, j], g0[:], ALU.mult, reads=[Ab, g0b], writes=[Ab])
    xs, xsb = AR.sb("xs", [128, KC, 512]); hs, hsb = AR.sb("hs", [128, KC, 512], R=True)
    opool = AR.pool("o", [128, 512], 3)
    win = dr[f"w_inr{l}"]; PF = dr["PF"]; PT = dr["PT"]
    blocks = []
    for r in range(2):
        for t0 in range(0, NLH, 512): blocks.append((r, t0, 512, 0, NCTX + r * NLH + t0))
        blocks.append((r, NLH, NCH2, 1, r * NCH2))
    for (r, t0, tb, j, dst0) in blocks:
        P.dma("sp", xs[:, :, 0:tb], xsrc.rearrange("c (r p) t -> r p c t", r=2)[r][:, :, t0:t0 + tb], writes=[xsb])
        rms_rstd(P, E, xs, xsb, KC, tb, 2048)
        for c in range(KC):
            P.op("dve", "scalar_tensor_tensor", RR_(hs[:, c, 0:tb]), xs[:, c, 0:tb], At[:, c, j:j + 1], E.rstd[:, 0:tb], ALU.mult, ALU.mult, reads=[xsb, Ab, E.rstdb], writes=[hsb])
            P.op("act", "activation", out=RR_(hs[:, c, 0:tb]), in_=hs[:, c, 0:tb], func=AF.Identity, bias=mod_t[:, c, j:j + 1], reads=[hsb, mod_b], writes=[hsb])
        row = 0; wt = None; wcol0 = None
        for (c0, wd) in FM_CHUNKS:
            if wt is None or not (wcol0 <= c0 and c0 + wd <= wcol0 + 512):
                wt, wb = wpool.get(); wcol0 = c0
                ncols = min(512, FM_ROWS - c0)
                P.dma("pool", RR_(wt[:, :, 0:ncols]), win[:, c0:c0 + ncols].rearrange("(c p) n -> p c n", p=128), writes=[wb])
            pt, pb = pmm.get(); off = c0 - wcol0
            for k in range(KC):
                P.op("pe", "matmul", pt[0:wd, 0:tb], RR_(wt[:, k, off:off + wd]), RR_(hs[:, k, 0:tb]), start=(k == 0), stop=(k == KC - 1), reads=[wb, hsb], writes=[pb])
            ot, ob = opool.get()
            evac(P, E, ot[0:wd, 0:tb], pt[0:wd, 0:tb], [pb], [ob])
            P.dma("act", PF[row:row + wd, dst0:dst0 + tb], ot[0:wd, 0:tb], reads=[ob])
            row += wd
        for n0 in range(0, TM_W, 512):
            nw = min(512, TM_W - n0)
            wt, wb = wpool.get()
            P.dma("pool", RR_(wt[:, :, 0:nw]), win[:, TM_COL0 + n0:TM_COL0 + n0 + nw].rearrange("(c p) n -> p c n", p=128), writes=[wb])
            for ts in range(tb // 128):
                pt, pb = pmm.get()
                for k in range(KC):
                    P.op("pe", "matmul", pt[:, 0:nw], RR_(hs[:, k, ts * 128:(ts + 1) * 128]), RR_(wt[:, k, 0:nw]), start=(k == 0), stop=(k == KC - 1), reads=[wb, hsb], writes=[pb])
                ot, ob = opool.get()
                evac(P, E, ot[:, 0:nw], pt[:, 0:nw], [pb], [ob])
                P.dma("act", PT[dst0 + ts * 128:dst0 + (ts + 1) * 128, n0:n0 + nw], ot[:, 0:nw], reads=[ob])

def emit_C(P, AR, dr, l, last):
    AR.begin(15000, 36864); E = common(P, AR)
    TB = 256
    xown = dr["xT_own"] if l == 0 else dr["XL2"]
    YG = dr["YG"]
    wpool = AR.pool("w", [128, 5632], 2, R=True)
    wmodc = AR.pool("wm", [128, KC, 128], 1)
    def wget():
        t, b = wpool.get(); return t[:, 0:KC * 256].rearrange("p (c n) -> p c n", c=KC), b
    pmm = AR.pspool(5)
    mod_t, mod_b = emit_mod(P, AR, E, lambda: wmodc.get(), AR.pspool(2), dr["cT"], dr[f"w_ada{l}"], dr[f"b_ada{l}"], 96, cpt=1)
    g, gb = ld(P, AR, "g", dr[f"g{l}"], [128, 4, KC])
    msk, mskb = ld(P, AR, "msk", dr["msk"], [128, 2])
    S, Sb = AR.sb("S", [128, 4, KC, 2])
    for j in range(2):
        for i, (mi, plus1) in enumerate([(1, True), (2, False), (4, True), (5, False)]):
            P.op("dve", "tensor_scalar", S[:, i, :, j], mod_t[:, mi * KC:(mi + 1) * KC, j], 1.0 if plus1 else 0.0, math.sqrt(D), ALU.add, ALU.mult, reads=[mod_b], writes=[Sb])
            P.op("dve", "tensor_tensor", S[:, i, :, j], S[:, i, :, j], g[:, i, :], ALU.mult, reads=[Sb, gb], writes=[Sb])
    xs, xsb = AR.sb("xs", [128, KC, TB]); hs, hsb = AR.sb("hs", [128, KC, TB], R=True)
    ys, ysb = AR.sb("ys", [128, 24, TB], R=True); mg, mgb = AR.sb("mg", [128, KC, TB], R=True); yl, ylb = AR.sb("yl", [128, KC, TB])
    act, actb = AR.sb("act", [128, FKC, TB], R=True)
    tmpp = AR.pool("tmp", [128, TB], 3); selp = AR.pool("sel", [128, TB], 4)
    wg = dr[f"w_gate{l}"]; wbr = dr[f"w_branch{l}"]; wout = dr[f"w_out{l}"]; wfi = dr[f"w_ffn_in{l}"]; wfo = dr[f"w_ffn_out{l}"]
    blocks = [(t0, min(TB, NLH - t0), 0) for t0 in range(0, NLH, TB)]
    if not last: blocks += [(NLH, NCH2, 1)]
    xTr = xown.rearrange("(c p) t -> p c t", p=128)
    outd = dr["xo"] if last else dr["XL2"]
    outr = outd.rearrange("(c p) t -> p c t", p=128)
    def normmod(src, srcb, dst, dstb, si, bi, j, tb):
        rms_rstd(P, E, src, srcb, KC, tb, 2048)
        for c in range(KC):
            P.op("dve", "scalar_tensor_tensor", RR_(dst[:, c, 0:tb]), src[:, c, 0:tb], S[:, si, c, j:j + 1], E.rstd[:, 0:tb], ALU.mult, ALU.mult, reads=[srcb, Sb, E.rstdb], writes=[dstb])
            P.op("act", "activation", out=RR_(dst[:, c, 0:tb]), in_=dst[:, c, 0:tb], func=AF.Identity, bias=mod_t[:, bi * KC + c, j:j + 1], reads=[dstb, mod_b], writes=[dstb])
    def resid(src, srcb, si, j, tb):
        rms_rstd(P, E, src, srcb, KC, tb, 2048)
        for c in range(KC):
            tt, ttb = tmpp.get()
            P.op("dve", "scalar_tensor_tensor", tt[:, 0:tb], src[:, c, 0:tb], S[:, si, c, j:j + 1], E.rstd[:, 0:tb], ALU.mult, ALU.mult, reads=[srcb, Sb, E.rstdb], writes=[ttb])
            P.op("dve", "tensor_tensor", xs[:, c, 0:tb], xs[:, c, 0:tb], tt[:, 0:tb], ALU.add, reads=[xsb, ttb], writes=[xsb])
    for (t0, tb, j) in blocks:
        P.dma("sp", xs[:, :, 0:tb], xTr[:, :, t0:t0 + tb], writes=[xsb])
        for i in range(3):
            for gg in range(2):
                for c in range(4):
                    ch = i * 8 + gg * 4 + c
                    r0 = gg * 1536 + i * 512 + c * 128
                    cols = [(NCTX + r * NLH + t0) if j == 0 else (r * NCH2) for r in range(2)]
                    s0, s0b = selp.get(); st, stb = selp.get()
                    P.dma("act", s0[:, 0:tb], YG[cols[0] // 256, r0:r0 + 128, cols[0] % 256:cols[0] % 256 + tb], writes=[s0b])
                    P.dma("act", st[:, 0:tb], YG[cols[1] // 256, r0:r0 + 128, cols[1] % 256:cols[1] % 256 + tb], writes=[stb])
                    P.op("dve", "tensor_scalar", s0[:, 0:tb], s0[:, 0:tb], msk[:, 0:1], None, ALU.mult, reads=[s0b, mskb], writes=[s0b])
                    P.op("dve", "scalar_tensor_tensor", RR_(ys[:, ch, 0:tb]), st[:, 0:tb], msk[:, 1:2], s0[:, 0:tb], ALU.mult, ALU.add, reads=[stb, mskb, s0b], writes=[ysb])
        normmod(xs, xsb, hs, hsb, 0, 0, j, tb)
        for n in range(KC):
            for i in range(3):
                wt, wb = wpool.get()
                wgv = wt[:, 0:KC * 128].rearrange("p (c n) -> p c n", c=KC)
                wbv = wt[:, KC * 128:KC * 128 + 8 * 128].rearrange("p (c n) -> p c n", c=8)
                P.dma("pool", RR_(wgv), wg[:, i * D + n * 128: i * D + (n + 1) * 128].rearrange("(c p) n -> p c n", p=128), writes=[wb])
                P.dma("pool", RR_(wbv), wbr[i, :, n * 128:(n + 1) * 128].rearrange("(c p) n -> p c n", p=128), writes=[wb])
                p1, p1b = pmm.get()
                for k in range(KC):
                    P.op("pe", "matmul", p1[:, 0:tb], RR_(wgv[:, k, :]), RR_(hs[:, k, 0:tb]), start=(k == 0), stop=(k == KC - 1), reads=[wb, hsb], writes=[p1b])
                p2, p2b = pmm.get()
                for k in range(8):
                    P.op("pe", "matmul", p2[:, 0:tb], RR_(wbv[:, k, :]), RR_(ys[:, i * 8 + k, 0:tb]), start=(k == 0), stop=(k == 7), reads=[wb, ysb], writes=[p2b])
                tt, ttb = tmpp.get()
                P.op("act", "activation", out=tt[:, 0:tb], in_=p1[:, 0:tb], func=AF.Sigmoid, reads=[p1b], writes=[ttb])
                if i == 0:
                    P.op("dve", "tensor_tensor", RR_(mg[:, n, 0:tb]), tt[:, 0:tb], p2[:, 0:tb], ALU.mult, reads=[ttb, p2b], writes=[mgb])
                else:
                    P.op("dve", "tensor_tensor", tt[:, 0:tb], tt[:, 0:tb], p2[:, 0:tb], ALU.mult, reads=[ttb, p2b], writes=[ttb])
                    P.op("dve", "tensor_tensor", RR_(mg[:, n, 0:tb]), mg[:, n, 0:tb], tt[:, 0:tb], ALU.add, reads=[ttb, mgb], writes=[mgb])
        for n0 in range(0, KC, 2):
            wv, wb = wget()
            P.dma("pool", RR_(wv), wout[:, n0 * 128:(n0 + 2) * 128].rearrange("(c p) n -> p c n", p=128), writes=[wb])
            for gi in range(2):
                pt, pb = pmm.get()
                for k in range(KC):
                    P.op("pe", "matmul", pt[:, 0:tb], RR_(wv[:, k, gi * 128:(gi + 1) * 128]), RR_(mg[:, k, 0:tb]), start=(k == 0), stop=(k == KC - 1), reads=[wb, mgb], writes=[pb])
                P.op("act", "activation", out=yl[:, n0 + gi, 0:tb], in_=pt[:, 0:tb], func=AF.Copy, reads=[pb], writes=[ylb])
        resid(yl, ylb, 1, j, tb)
        normmod(xs, xsb, hs, hsb, 2, 3, j, tb)
        for h0 in range(0, FKC, 2):
            wv, wb = wget()
            P.dma("pool", RR_(wv), wfi[:, h0 * 128:(h0 + 2) * 128].rearrange("(c p) n -> p c n", p=128), writes=[wb])
            wv2, wb2 = wget()
            P.dma("pool", RR_(wv2), wfi[:, FF + h0 * 128:FF + (h0 + 2) * 128].rearrange("(c p) n -> p c n", p=128), writes=[wb2])
            for gi in range(2):
                pg, pgb = pmm.get()
                for k in range(KC):
                    P.op("pe", "matmul", pg[:, 0:tb], RR_(wv[:, k, gi * 128:(gi + 1) * 128]), RR_(hs[:, k, 0:tb]), start=(k == 0), stop=(k == KC - 1), reads=[wb, hsb], writes=[pgb])
                pu, pub = pmm.get()
                for k in range(KC):
                    P.op("pe", "matmul", pu[:, 0:tb], RR_(wv2[:, k, gi * 128:(gi + 1) * 128]), RR_(hs[:, k, 0:tb]), start=(k == 0), stop=(k == KC - 1), reads=[wb2, hsb], writes=[pub])
                tt, ttb = tmpp.get()
                P.op("act", "activation", out=tt[:, 0:tb], in_=pg[:, 0:tb], func=AF.Silu, reads=[pgb], writes=[ttb])
                P.op("dve", "tensor_tensor", RR_(act[:, h0 + gi, 0:tb]), tt[:, 0:tb], pu[:, 0:tb], ALU.mult, reads=[ttb, pub], writes=[actb])
        for n in range(KC):
            wt, wb = wpool.get()
            wv = wt[:, 0:FKC * 128].rearrange("p (c n) -> p c n", c=FKC)
            P.dma("pool", RR_(wv), wfo[:, n * 128:(n + 1) * 128].rearrange("(c p) n -> p c n", p=128), writes=[wb])
            pt, pb = pmm.get()
            for k in range(FKC):
                P.op("pe", "matmul", pt[:, 0:tb], RR_(wv[:, k, :]), RR_(act[:, k, 0:tb]), start=(k == 0), stop=(k == FKC - 1), reads=[wb, actb], writes=[pb])
            P.op("act", "activation", out=yl[:, n, 0:tb], in_=pt[:, 0:tb], func=AF.Copy, reads=[pb], writes=[ylb])
        resid(yl, ylb, 3, j, tb)
        P.dma("sp", outr[:, :, t0:t0 + tb], xs[:, :, 0:tb], reads=[xsb])

def emit_FN(P, AR, dr, last):
    AR.begin(4000, 39424); E = common(P, AR)
    PF = dr["PF"]; YL = dr["YL"]; cld = dr["cl"]; sld = dr["sln"]
    cw2, cw2b = AR.sb("cw2s", [128, 2, 512], R=True); P.dma("pool", RR_(cw2[:]), dr["cw2"].rearrange("(c p) n -> p c n", p=128), writes=[cw2b])
    us, usb = AR.sb("us", [128, 2, NTOT], R=True); Zs, Zsb = AR.sb("Zs", [128, 32, 512], R=True); Zc, Zcb = AR.sb("Zc", [128, 2, 512], R=True)
    cp = AR.pool("ct", [128, 4, 512], 3, R=True); spn = AR.pool("st", [128, 4, 512], 3, R=True)
    pz = AR.pspool(2); py = AR.pspool(4); op = AR.pool("o", [128, 512], 3)
    uTr = PF[2368:2880, :].rearrange("(c p) t -> p c t", p=128)
    for g in range(2):
        P.dma("pool", RR_(us[:]), uTr[:, 2 * g:2 * g + 2, :], writes=[usb])
        for t in range(32):
            pt, pb = pz.get()
            for kc in range(2):
                P.op("pe", "matmul", pt[:], RR_(us[:, kc, NCTX + t * 128:NCTX + (t + 1) * 128]), RR_(cw2[:, kc, :]), start=(kc == 0), stop=(kc == 1), reads=[usb, cw2b], writes=[pb])
            evac(P, E, RR_(Zs[:, t, :]), pt[:], [pb], [Zsb])
        if not last:
            for t in range(2):
                pt, pb = pz.get()
                for kc in range(2):
                    P.op("pe", "matmul", pt[:], RR_(us[:, kc, t * 128:(t + 1) * 128]), RR_(cw2[:, kc, :]), start=(kc == 0), stop=(kc == 1), reads=[usb, cw2b], writes=[pb])
                P.op("dve", "tensor_copy", RR_(Zc[:, t, 0:256]), pt[:, 0:256], reads=[pb], writes=[Zcb])
                P.op("dve", "tensor_scalar", RR_(Zc[:, t, 256:512]), pt[:, 256:512], -1.0, None, ALU.mult, reads=[pb], writes=[Zcb])
            for ch in range(2):
                pt, pb = py.get(); i = 0
                for t in range(2):
                    for part in range(2):
                        P.op("pe", "matmul", pt[:, 0:256], RR_(Zc[:, t, part * 256 + ch * 128: part * 256 + (ch + 1) * 128]), RR_(cw2[:, t, part * 256:(part + 1) * 256]), start=(i == 0), stop=(i == 3), reads=[Zcb, cw2b], writes=[pb])
                        i += 1
                ot, ob = op.get()
                P.op("act", "activation", out=ot[:, 0:256], in_=pt[:, 0:256], func=AF.Copy, scale=1.0 / 256.0, reads=[pb], writes=[ob])
                yl_write(P, YL, 512 + g * 256 + ch * 128, 0, 256, ot, [ob])
        for o in range(8):
            pts = [py.get(), py.get()]
            for t0 in range(0, 32, 4):
                ct, cb = cp.get(); st, stb = spn.get()
                P.dma("pool", RR_(ct[:]), cld[t0 * 128:(t0 + 4) * 128, o * 512:(o + 1) * 512].rearrange("(t p) n -> p t n", p=128), writes=[cb])
                P.dma("pool", RR_(st[:]), sld[t0 * 128:(t0 + 4) * 128, o * 512:(o + 1) * 512].rearrange("(t p) n -> p t n", p=128), writes=[stb])
                for tt in range(4):
                    t = t0 + tt
                    for ch in range(2):
                        pt, pb = pts[ch]
                        P.op("pe", "matmul", pt[:], RR_(Zs[:, t, ch * 128:(ch + 1) * 128]), RR_(ct[:, tt, :]), start=(t == 0), stop=False, reads=[Zsb, cb], writes=[pb])
                        P.op("pe", "matmul", pt[:], RR_(Zs[:, t, 256 + ch * 128:256 + (ch + 1) * 128]), RR_(st[:, tt, :]), start=False, stop=(t == 31), reads=[Zsb, stb], writes=[pb])
            for ch in range(2):
                pt, pb = pts[ch]; ot, ob = op.get()
                if ch == 0: P.op("act", "activation", out=ot[:], in_=pt[:], func=AF.Copy, scale=1.0 / 1024.0, reads=[pb], writes=[ob])
                else: P.op("dve", "tensor_scalar", ot[:], pt[:], 1.0 / 1024.0, None, ALU.mult, reads=[pb], writes=[ob])
                yl_write(P, YL, 512 + g * 256 + ch * 128, NCTX + o * 512, 512, ot, [ob])

def emit_MLA(P, AR, dr, l, last, NH=4):
    AR.begin(28000, 17408); E = common(P, AR)
    PF = dr["PF"]; YL = dr["YL"]
    cin = PF[1536:2368, :]
    gq, gqb = ld(P, AR, "gq_s", dr[f"gq{l}"], [128, 4]); gkv, gkvb = ld(P, AR, "gkv_s", dr[f"gkv{l}"], [128, 2])
    P.op("dve", "tensor_scalar", gq[:], gq[:], math.sqrt(512.0), None, ALU.mult, reads=[gqb], writes=[gqb])
    P.op("dve", "tensor_scalar", gkv[:], gkv[:], math.sqrt(256.0), None, ALU.mult, reads=[gkvb], writes=[gkvb])
    wqn, wqnb = ld(P, AR, "wqn_s", dr[f"wqn{l}"].rearrange("(c p) n -> p c n", p=128), [128, 4, NH * 128])
    wqr, wqrb = ld(P, AR, "wqr_s", dr[f"wqr{l}"].rearrange("(c p) n -> p c n", p=128), [128, 4, NH * 64])
    wk, wkb = ld(P, AR, "wk_s", dr[f"wk{l}"].rearrange("(c p) n -> p c n", p=128), [128, 2, NH * 128])
    wv, wvb = ld(P, AR, "wv_s", dr[f"wv{l}"].rearrange("(c p) n -> p c n", p=128), [128, 2, NH * 128])
    Rm, Rmb = ld(P, AR, "R_s", dr["Rm"], [64, 64]); ident, identb = ld(P, AR, "id_s", dr["ident"], [128, 128])
    Qn, Qnb = AR.sb("Qn", [128, NTOT], R=True); Qr, Qrb = AR.sb("Qr", [64, NTOT], R=True)
    Kn, Knb = AR.sb("Kn", [128, NTOT], R=True); Kr, Krb = AR.sb("Kr", [64, NTOT], R=True)
    V, Vb = AR.sb("V", [128, NTOT // 128, 128]); Ss, Ssb = AR.sb("Ss", [128, NTOT])
    cb_t, cb_b = AR.sb("cblk", [128, 6, 512]); krb_t, krb_b = AR.sb("krblk", [64, 512])
    cqn, cqnb = AR.sb("cqn", [128, 4, 512]); ckvn, ckvnb = AR.sb("ckvn", [128, 2, 512])
    cs_t, cs_b = AR.sb("cosb", [64, 512]); sn_t, sn_b = AR.sb("sinb", [64, 512])
    tq, tqb = AR.sb("tq", [64, 512]); t2, t2b = AR.sb("t2", [64, 512])
    pmm = AR.pspool(5); po_p = AR.pspool(2)
    PTp = AR.pool("PT", [128, 512], 2); osb_p = AR.pool("osb", [128, 128], 2); oT_p = AR.pool("oT", [128, 128], 2); st_p = AR.pool("stat", [128, 4], 2)
    cinr = cin[0:768, :].rearrange("(c p) t -> p c t", p=128)
    cos_d = dr["cosT"]; sin_d = dr["sinT"]
    blocks = [(0, 256, False)] + [(NCTX + i * 512, 512, True) for i in range(8)]
    for h in range(NH):
        for (t0, tb, lat) in blocks:
            P.dma("sp", cb_t[:, :, 0:tb], cinr[:, :, t0:t0 + tb], writes=[cb_b])
            P.dma("sp", krb_t[:, 0:tb], cin[768:832, t0:t0 + tb], writes=[krb_b])
            if lat:
                P.dma("act", cs_t[:, 0:tb], cos_d[:, t0 - NCTX:t0 - NCTX + tb], writes=[cs_b])
                P.dma("act", sn_t[:, 0:tb], sin_d[:, t0 - NCTX:t0 - NCTX + tb], writes=[sn_b])
            rms_rstd(P, E, cb_t[:, 0:4, :], cb_b, 4, tb, 512)
            for c in range(4):
                P.op("dve", "scalar_tensor_tensor", cqn[:, c, 0:tb], cb_t[:, c, 0:tb], gq[:, c:c + 1], E.rstd[:, 0:tb], ALU.mult, ALU.mult, reads=[cb_b, gqb, E.rstdb], writes=[cqnb])
            rms_rstd(P, E, cb_t[:, 4:6, :], cb_b, 2, tb, 256)
            for c in range(2):
                P.op("dve", "scalar_tensor_tensor", ckvn[:, c, 0:tb], cb_t[:, 4 + c, 0:tb], gkv[:, c:c + 1], E.rstd[:, 0:tb], ALU.mult, ALU.mult, reads=[cb_b, gkvb, E.rstdb], writes=[ckvnb])
            pt, pb = pmm.get()
            for kc in range(4):
                P.op("pe", "matmul", pt[:, 0:tb], wqn[:, kc, h * 128:(h + 1) * 128], cqn[:, kc, 0:tb], start=(kc == 0), stop=(kc == 3), reads=[wqnb, cqnb], writes=[pb])
            evac(P, E, RR_(Qn[:, t0:t0 + tb]), pt[:, 0:tb], [pb], [Qnb])
            pt, pb = pmm.get()
            for kc in range(2):
                P.op("pe", "matmul", pt[:, 0:tb], wk[:, kc, h * 128:(h + 1) * 128], ckvn[:, kc, 0:tb], start=(kc == 0), stop=(kc == 1), reads=[wkb, ckvnb], writes=[pb])
            evac(P, E, RR_(Kn[:, t0:t0 + tb]), pt[:, 0:tb], [pb], [Knb])
            for ts in range(tb // 128):
                pt, pb = pmm.get()
                for kc in range(2):
                    P.op("pe", "matmul", pt[:, 0:128], ckvn[:, kc, ts * 128:(ts + 1) * 128], wv[:, kc, h * 128:(h + 1) * 128], start=(kc == 0), stop=(kc == 1), reads=[wvb, ckvnb], writes=[pb])
                evac(P, E, V[:, t0 // 128 + ts, :], pt[:, 0:128], [pb], [Vb])
            pt, pb = pmm.get()
            for kc in range(4):
                P.op("pe", "matmul", pt[0:64, 0:tb], wqr[:, kc, h * 64:(h + 1) * 64], cqn[:, kc, 0:tb], start=(kc == 0), stop=(kc == 3), reads=[wqrb, cqnb], writes=[pb])
            def rope(dst, dstb, src, srcb):
                p2, p2b = pmm.get()
                P.op("pe", "matmul", p2[0:64, 0:tb], Rm[:, :], src, start=True, stop=True, reads=[Rmb, srcb], writes=[p2b])
                P.op("dve", "tensor_tensor", t2[:, 0:tb], p2[0:64, 0:tb], sn_t[:, 0:tb], ALU.mult, reads=[p2b, sn_b], writes=[t2b])
                P.op("pool", "tensor_tensor", RR_(dst[:, t0:t0 + tb]), src, cs_t[:, 0:tb], ALU.mult, reads=[srcb, cs_b], writes=[dstb])
                P.op("dve", "tensor_tensor", RR_(dst[:, t0:t0 + tb]), dst[:, t0:t0 + tb], t2[:, 0:tb], ALU.add, reads=[dstb, t2b], writes=[dstb])
            if lat:
                evac(P, E, tq[:, 0:tb], pt[0:64, 0:tb], [pb], [tqb])
                rope(Qr, Qrb, tq[:, 0:tb], tqb)
                rope(Kr, Krb, krb_t[:, 0:tb], krb_b)
            else:
                evac(P, E, RR_(Qr[:, t0:t0 + tb]), pt[0:64, 0:tb], [pb], [Qrb])
                P.op("pool", "tensor_copy", RR_(Kr[:, t0:t0 + tb]), krb_t[:, 0:tb], reads=[krb_b], writes=[Krb])
        qtiles = [(NCTX + qt * 128, 0, NTOT) for qt in range(32)]
        if not last: qtiles = [(qt * 128, 0, NCTX) for qt in range(2)] + qtiles
        for (q0, k0, k1) in qtiles:
            nk = k1 - k0
            for kb0 in range(k0, k1, 512):
                kw = min(512, k1 - kb0)
                pt, pb = pmm.get()
                P.op("pe", "matmul", pt[:, 0:kw], RR_(Qn[:, q0:q0 + 128]), RR_(Kn[:, kb0:kb0 + kw]), start=True, stop=False, reads=[Qnb, Knb], writes=[pb])
                P.op("pe", "matmul", pt[:, 0:kw], RR_(Qr[:, q0:q0 + 128]), RR_(Kr[:, kb0:kb0 + kw]), start=False, stop=True, reads=[Qrb, Krb], writes=[pb])
                evac(P, E, Ss[:, kb0:kb0 + kw], pt[:, 0:kw], [pb], [Ssb])
            stt, stb = st_p.get()
            P.op("dve", "tensor_reduce", stt[:, 0:1], Ss[:, k0:k1], AX.X, ALU.max, reads=[Ssb], writes=[stb])
            P.op("dve", "tensor_scalar", stt[:, 1:2], stt[:, 0:1], -MLA_SCALE, None, ALU.mult, reads=[stb], writes=[stb])
            P.op("pool", "memset", stt[:, 2:3], 0.0, writes=[stb])
            P.op("act", "activation", out=Ss[:, k0:k1], in_=Ss[:, k0:k1], func=AF.Exp, scale=MLA_SCALE, bias=stt[:, 1:2], accum_out=stt[:, 2:3], reads=[Ssb, stb], writes=[Ssb, stb])
            P.op("dve", "reciprocal", stt[:, 3:4], stt[:, 2:3], reads=[stb], writes=[stb])
            po, pob = po_p.get()
            ntile = nk // 128
            for g0 in range(0, ntile, 4):
                gn = min(4, ntile - g0)
                ptp, ptpb = pmm.get()
                for i in range(gn):
                    kt = k0 // 128 + g0 + i
                    P.op("pe", "transpose", ptp[:, i * 128:(i + 1) * 128], Ss[:, kt * 128:(kt + 1) * 128], ident[:], reads=[Ssb, identb], writes=[ptpb])
                PT, PTb = PTp.get()
                evac(P, E, PT[:, 0:gn * 128], ptp[:, 0:gn * 128], [ptpb], [PTb])
                for i in range(gn):
                    kt = k0 // 128 + g0 + i
                    P.op("pe", "matmul", po[:, 0:128], PT[:, i * 128:(i + 1) * 128], V[:, kt, :], start=(g0 + i == 0), stop=(g0 + i == ntile - 1), reads=[PTb, Vb], writes=[pob])
            ot, ob = osb_p.get()
            P.op("dve", "tensor_scalar", ot[:], po[:, 0:128], stt[:, 3:4], None, ALU.mult, reads=[pob, stb], writes=[ob])
            pq, pqb = pmm.get()
            P.op("pe", "transpose", pq[:, 0:128], ot[:], ident[:], reads=[ob, identb], writes=[pqb])
            oT, oTb = oT_p.get()
            evac(P, E, oT[:], pq[:, 0:128], [pqb], [oTb])
            yl_write(P, YL, 1024 + h * 128, q0, 128, oT, [oTb])

def emit_DN(P, AR, dr, l, last, NH=4):
    AR.begin(51900, 8); E = common(P, AR)
    G = 2 * NH
    PF = dr["PF"]; PT = dr["PT"]; YL = dr["YL"]
    TRI2, TRI2b = ld(P, AR, "tri2s", dr["tri2"], [64, 2, 64]); MS2, MS2b = ld(P, AR, "ms2s", dr["ms2"], [64, 2, 64])
    I2, I2b = ld(P, AR, "i2s", dr["i2"], [64, 2, 64]); ident, identb = ld(P, AR, "ids", dr["ident"], [128, 128])
    cw, cwb = ld(P, AR, "cws", dr[f"convw{l}"], [128, 3 * NH, 5]); gn, gnb = ld(P, AR, "gns", dr[f"gnorm{l}"], [64, 128])
    alog, alogb = ld(P, AR, "alogs", dr[f"alog{l}"], [64, G]); dtb, dtbb = ld(P, AR, "dtbs", dr[f"dtb{l}"], [64, G])
    ones, onesb = E.ones, E.onesb
    one1, one1b = AR.sb("one1", [128, 1]); P.op("pool", "memset", one1[:], 1.0, writes=[one1b])
    eps6, eps6b = E.eps[1]
    psp = AR.pspool(7)
    bl, blb = ld(P, AR, "bls", PT[:, 512:512 + G].rearrange("(n c) x -> c n x", c=64), [64, NCH, G])
    al, alb = ld(P, AR, "als", PT[:, 512 + G:512 + 2 * G].rearrange("(n c) x -> c n x", c=64), [64, NCH, G], q="act")
    BETA, BETAb = AR.sb("BETA", [64, NCH, G]); NBETA, NBETAb = AR.sb("NBETA", [64, NCH, G])
    gt, gtb = AR.sb("gt", [64, NCH, G]); GC, GCb = AR.sb("GC", [64, NCH, G]); NGC, NGCb = AR.sb("NGC", [64, NCH, G])
    BEG, BEGb = AR.sb("BEG", [64, NCH, G]); EKD, EKDb = AR.sb("EKD", [64, NCH, G]); EGL, EGLb = AR.sb("EGL", [128, NCH, G])
    P.op("act", "activation", out=BETA[:], in_=bl[:], func=AF.Sigmoid, reads=[blb], writes=[BETAb])
    P.op("dve", "tensor_scalar", NBETA[:], BETA[:], -1.0, None, ALU.mult, reads=[BETAb], writes=[NBETAb])
    for c in range(G):
        P.op("act", "activation", out=gt[:, :, c], in_=al[:, :, c], func=AF.Exp, bias=dtb[:, c:c + 1], reads=[alb, dtbb], writes=[gtb])
    P.op("act", "activation", out=gt[:], in_=gt[:], func=AF.Ln, bias=one1[0:64, 0:1], reads=[gtb, one1b], writes=[gtb])
    P.op("act", "activation", out=alog[:], in_=alog[:], func=AF.Exp, reads=[alogb], writes=[alogb])
    P.op("dve", "tensor_scalar", alog[:], alog[:], -1.0, None, ALU.mult, reads=[alogb], writes=[alogb])
    for c in range(G):
        P.op("dve", "tensor_scalar", gt[:, :, c], gt[:, :, c], alog[:, c:c + 1], None, ALU.mult, reads=[gtb, alogb], writes=[gtb])
    gflat = gt[:].rearrange("p n g -> p (n g)")
    NF = NCH * G; H2 = NF // 2
    for d in range(2):
        pt, pb = psp.get(); pt2, pb2 = psp.get()
        for (pp, ppb, c0) in ((pt, pb, 0), (pt2, pb2, H2)):
            P.op("pe", "matmul", pp[0:64, 0:H2], TRI2[:, d, :], gflat[:, c0:c0 + H2], start=True, stop=True, reads=[TRI2b, gtb], writes=[ppb])
        for (pp, ppb, c0) in ((pt, pb, 0), (pt2, pb2, H2)):
            nn = H2 // G
            src = pp[0:64, 0:H2].rearrange("p (n g) -> p n g", g=G)
            P.op("dve", "tensor_copy", GC[:, c0 // G:c0 // G + nn, d * NH:(d + 1) * NH], src[:, :, d * NH:(d + 1) * NH], reads=[ppb], writes=[GCb])
    P.op("dve", "tensor_scalar", NGC[:], GC[:], -1.0, None, ALU.mult, reads=[GCb], writes=[NGCb])
    P.op("act", "activation", out=BEG[:], in_=GC[:], func=AF.Exp, reads=[GCb], writes=[BEGb])
    P.op("dve", "tensor_tensor", BEG[:], BEG[:], BETA[:], ALU.mult, reads=[BEGb, BETAb], writes=[BEGb])
    EGLf = EGL[:].rearrange("p n g -> p (n g)"); EKDf = EKD[:].rearrange("p n g -> p (n g)"); GCf = GC[:].rearrange("p n g -> p (n g)")
    for c0 in (0, H2):
        pt, pb = psp.get()
        P.op("pe", "matmul", pt[:, 0:H2], ones[0:64, :], gflat[:, c0:c0 + H2], start=True, stop=True, reads=[onesb, gtb], writes=[pb])
        P.op("dve", "tensor_tensor", EKDf[:, c0:c0 + H2], pt[0:64, 0:H2], GCf[:, c0:c0 + H2], ALU.subtract, reads=[pb, GCb], writes=[EKDb])
        P.op("act", "activation", out=EGLf[:, c0:c0 + H2], in_=pt[:, 0:H2], func=AF.Exp, reads=[pb], writes=[EGLb])
    P.op("act", "activation", out=EKD[:], in_=EKD[:], func=AF.Exp, reads=[EKDb], writes=[EKDb])
    QT, QTb = AR.sb("QT", [128, NTOT]); KT, KTb = AR.sb("KT", [128, NTOT]); VT, VTb = AR.sb("VT", [128, NTOT])
    Xr, Xrb = AR.sb("Xr", [128, NTOT]); O, Ob = AR.sb("O", [64, NCH, 128])
    rs, rsb = E.rstd, E.rstdb
    w128 = AR.pool("w128", [64, 2, 64], 10); xxp = AR.pool("xx", [64, 2, 128], 4); zp = AR.pool("zz", [64, 2, 64], 14); qkp = AR.pool("qk", [64, 2, 64], 3)
    egp = AR.pool("egr", [128, 2, 64], 2); qgp = AR.pool("qg", [128, 2, 64], 2); nwp = AR.pool("nw", [128, 2, 64], 2)
    tmp = AR.pool("tm", [64, 2, 128], 8)
    Sp = [AR.pool(f"S{d}", [128, 128], 2) for d in range(2)]
    zt, ztb = AR.sb("zt", [64, 17, 128]); yt, ytb = AR.sb("yt", [64, 17, 128])
    st17, st17b = AR.sb("st17", [64, 17]); yT_p = AR.pool("yT", [128, 512], 2)
    segs = [(0, NCTX), (NCTX, NTOT)]
    for h in range(NH):
        for qi, (dst, dstb) in enumerate(((QT, QTb), (KT, KTb), (VT, VTb))):
            ci = qi * NH + h
            P.dma("sp", Xr[:], PF[qi * 512 + h * 128: qi * 512 + (h + 1) * 128, :], writes=[Xrb])
            for (a, b) in segs:
                P.op("act", "activation", out=dst[:, a:b], in_=Xr[:, a:b], func=AF.Copy, scale=cw[:, ci, 2:3], reads=[Xrb, cwb], writes=[dstb])
                for tap in (0, 1, 3, 4):
                    off = tap - 2
                    if off < 0: o0, o1, i0, i1 = a - off, b, a, b + off
                    else: o0, o1, i0, i1 = a, b - off, a + off, b
                    P.op("dve", "scalar_tensor_tensor", dst[:, o0:o1], Xr[:, i0:i1], cw[:, ci, tap:tap + 1], dst[:, o0:o1], ALU.mult, ALU.add, reads=[Xrb, cwb, dstb], writes=[dstb])
            P.op("act", "activation", out=dst[:], in_=dst[:], func=AF.Silu, reads=[dstb], writes=[dstb])
            if qi < 2:
                for t0 in range(0, NTOT, 512):
                    tb = min(512, NTOT - t0)
                    sq, sqb = E.sqp.get()
                    P.op("act", "activation", out=sq[:, 0:tb], in_=dst[:, t0:t0 + tb], func=AF.Square, reads=[dstb], writes=[sqb])
                    pt, pb = psp.get()
                    P.op("pe", "matmul", pt[:, 0:tb], ones[:], sq[:, 0:tb], start=True, stop=True, reads=[onesb, sqb], writes=[pb])
                    P.op("act", "activation", out=rs[:, 0:tb], in_=pt[:, 0:tb], func=AF.Sqrt, bias=eps6[:, 0:1], reads=[pb, eps6b], writes=[rsb])
                    P.op("dve", "reciprocal", rs[:, 0:tb], rs[:, 0:tb], reads=[rsb], writes=[rsb])
                    if qi == 0:
                        P.op("dve", "scalar_tensor_tensor", dst[:, t0:t0 + tb], dst[:, t0:t0 + tb], 128 ** -0.5, rs[:, 0:tb], ALU.mult, ALU.mult, reads=[dstb, rsb], writes=[dstb])
                    else:
                        P.op("dve", "tensor_tensor", dst[:, t0:t0 + tb], dst[:, t0:t0 + tb], rs[:, 0:tb], ALU.mult, reads=[dstb, rsb], writes=[dstb])
        S = []
        for d in range(2):
            st, stb = Sp[d].get()
            P.op("pool", "memset", st[:], 0.0, writes=[stb])
            S.append((st, stb))
        visited = set()
        def chunk_of(s, d):
            if d == 0: return s
            return 3 - s if s < 4 else 71 - s
        def pre(s):
            ns = [chunk_of(s, d) for d in range(2)]; cols = [d * NH + h for d in range(2)]
            toks = [slice(n * 64, (n + 1) * 64) for n in ns]
            Gd, Gdb = w128.get()
            for d in range(2):
                P.op("dve", "tensor_scalar", Gd[:, d, :], TRI2[:, d, :], gt[:, ns[d], cols[d]:cols[d] + 1], None, ALU.mult, reads=[TRI2b, gtb], writes=[Gdb])
            pa, pab = psp.get()
            P.op("pe", "matmul", pa[:, 0:128], ones[0:64, :], Gd[:].rearrange("p d j -> p (d j)"), start=True, stop=True, reads=[onesb, Gdb], writes=[pab])
            E1, E1b = w128.get(); E2, E2b = w128.get(); EGr, EGrb = egp.get()
            for d in range(2):
                P.op("act", "activation", out=E1[:, d, :], in_=pa[0:64, d * 64:(d + 1) * 64], func=AF.Exp, scale=-1.0, bias=GC[:, ns[d], cols[d]:cols[d] + 1], reads=[pab, GCb], writes=[E1b])
                P.op("act", "activation", out=E2[:, d, :], in_=pa[0:64, d * 64:(d + 1) * 64], func=AF.Exp, bias=NGC[:, ns[d], cols[d]:cols[d] + 1], reads=[pab, NGCb], writes=[E2b])
            P.op("act", "activation", out=EGr[:].rearrange("p d j -> p (d j)"), in_=pa[:, 0:128], func=AF.Exp, reads=[pab], writes=[EGrb])
            D1, D1b = w128.get(); D2, D2b = w128.get()
            P.op("dve", "scalar_tensor_tensor", D1[:], E1[:], 1.0, MS2[:], ALU.min, ALU.mult, reads=[E1b, MS2b], writes=[D1b])
            P.op("dve", "scalar_tensor_tensor", D2[:], E2[:], 1.0, TRI2[:], ALU.min, ALU.mult, reads=[E2b, TRI2b], writes=[D2b])
            for d in range(2):
                P.op("pool", "tensor_scalar", D1[:, d, :], D1[:, d, :], NBETA[:, ns[d], cols[d]:cols[d] + 1], None, ALU.mult, reads=[D1b, NBETAb], writes=[D1b])
            pk, pkb = psp.get()
            for d in range(2):
                P.op("pe", "matmul", pk[0:64, d * 64:(d + 1) * 64], KT[:, toks[d]], KT[:, toks[d]], start=True, stop=True, reads=[KTb], writes=[pkb])
                P.op("pe", "matmul", pk[0:64, 128 + d * 64:128 + (d + 1) * 64], KT[:, toks[d]], QT[:, toks[d]], start=True, stop=True, reads=[KTb, QTb], writes=[pkb])
            XX, XXb = xxp.get(); QK, QKb = qkp.get()
            P.op("dve", "tensor_tensor", XX[:, :, 0:64], pk[0:64, 0:128].rearrange("p (d j) -> p d j", d=2), D1[:], ALU.mult, reads=[pkb, D1b], writes=[XXb])
            P.op("dve", "tensor_tensor", QK[:], pk[0:64, 128:256].rearrange("p (d j) -> p d j", d=2), D2[:], ALU.mult, reads=[pkb, D2b], writes=[QKb])
            pc, pcb = psp.get()
            for d in range(2):
                P.op("pe", "transpose", pc[0:64, d * 64:(d + 1) * 64], XX[:, d, 0:64], ident[0:64, 0:64], reads=[XXb, identb], writes=[pcb])
            pcv = pc[0:64, 0:128].rearrange("p (d j) -> p d j", d=2)
            P.op("act", "activation", out=XX[:, :, 64:128], in_=pcv, func=AF.Copy, reads=[pcb], writes=[XXb])
            Z, Zb = zp.get()
            P.op("dve", "tensor_tensor", Z[:], pcv, I2[:], ALU.add, reads=[pcb, I2b], writes=[Zb])
            for lvl in range(5):
                pd, pdb = psp.get()
                for d in range(2):
                    P.op("pe", "matmul", pd[0:64, d * 128:d * 128 + 64], XX[:, d, 64:128], XX[:, d, 0:64], start=True, stop=True, reads=[XXb], writes=[pdb])
                    P.op("pe", "matmul", pd[0:64, d * 128 + 64:d * 128 + 128], XX[:, d, 0:64], XX[:, d, 64:128], start=True, stop=True, reads=[XXb], writes=[pdb])
                XXn, XXnb = xxp.get()
                P.op("act", "activation", out=XXn[:].rearrange("p d j -> p (d j)"), in_=pd[0:64, 0:256], func=AF.Copy, reads=[pdb], writes=[XXnb])
                XX, XXb = XXn, XXnb
                pe_, peb = psp.get()
                for d in range(2):
                    P.op("pe", "matmul", pe_[0:64, d * 64:(d + 1) * 64], XX[:, d, 0:64], Z[:, d, :], start=True, stop=True, reads=[XXb, Zb], writes=[peb])
                Zn, Znb = zp.get()
                P.op("dve", "tensor_tensor", Zn[:], Z[:], pe_[0:64, 0:128].rearrange("p (d j) -> p d j", d=2), ALU.add, reads=[Zb, peb], writes=[Znb])
                Z, Zb = Zn, Znb
            ptk, ptkb = psp.get()
            for d in range(2):
                P.op("pe", "transpose", ptk[0:64, d * 128:(d + 1) * 128], KT[:, toks[d]], ident[:], reads=[KTb, identb], writes=[ptkb])
                P.op("pe", "transpose", ptk[0:64, 256 + d * 128:256 + (d + 1) * 128], VT[:, toks[d]], ident[:], reads=[VTb, identb], writes=[ptkb])
            VB, VBb = tmp.get(); KBG, KBGb = tmp.get(); KD, KDb = tmp.get()
            for d in range(2):
                n, c = ns[d], cols[d]
                P.op("act", "activation", out=KBG[:, d, :], in_=ptk[0:64, d * 128:(d + 1) * 128], func=AF.Copy, scale=BEG[:, n, c:c + 1], reads=[ptkb, BEGb], writes=[KBGb])
                P.op("dve", "tensor_scalar", KD[:, d, :], ptk[0:64, d * 128:(d + 1) * 128], EKD[:, n, c:c + 1], None, ALU.mult, reads=[ptkb, EKDb], writes=[KDb])
                P.op("dve", "tensor_scalar", VB[:, d, :], ptk[0:64, 256 + d * 128:256 + (d + 1) * 128], BETA[:, n, c:c + 1], None, ALU.mult, reads=[ptkb, BETAb], writes=[VBb])
            pw, pwb = psp.get()
            for d in range(2):
                P.op("pe", "matmul", pw[:, d * 64:(d + 1) * 64], KBG[:, d, :], Z[:, d, :], start=True, stop=True, reads=[KBGb, Zb], writes=[pwb])
            NW, NWb = nwp.get()
            P.op("act", "activation", out=NW[:].rearrange("p d j -> p (d j)"), in_=pw[:, 0:128], func=AF.Copy, scale=-1.0, reads=[pwb], writes=[NWb])
            QG, QGb = qgp.get()
            for d in range(2):
                P.op("pool", "tensor_tensor", QG[:, d, :], QT[:, toks[d]], EGr[:, d, :], ALU.mult, reads=[QTb, EGrb], writes=[QGb])
            return dict(ns=ns, cols=cols, Z=(Z, Zb), VB=(VB, VBb), KD=(KD, KDb), NW=(NW, NWb), QG=(QG, QGb), QK=(QK, QKb))
        def seq(R):
            ns, cols = R["ns"], R["cols"]
            Z, Zb = R["Z"]; VB, VBb = R["VB"]; KD, KDb = R["KD"]; NW, NWb = R["NW"]; QG, QGb = R["QG"]; QK, QKb = R["QK"]
            pv, pvb = psp.get()
            for d in range(2):
                P.op("pe", "matmul", pv[0:64, d * 128:(d + 1) * 128], Z[:, d, :], VB[:, d, :], start=True, stop=False, reads=[Zb, VBb], writes=[pvb])
                P.op("pe", "matmul", pv[0:64, d * 128:(d + 1) * 128], NW[:, d, :], S[d][0][:], start=False, stop=True, reads=[NWb, S[d][1]], writes=[pvb])
            VN, VNb = tmp.get()
            P.op("act", "activation", out=VN[:].rearrange("p d e -> p (d e)"), in_=pv[0:64, 0:256], func=AF.Copy, reads=[pvb], writes=[VNb])
            po, pob = psp.get()
            for d in range(2):
                P.op("pe", "matmul", po[0:64, d * 128:(d + 1) * 128], QG[:, d, :], S[d][0][:], start=True, stop=False, reads=[QGb, S[d][1]], writes=[pob])
                P.op("pe", "matmul", po[0:64, d * 128:(d + 1) * 128], QK[:, d, :], VN[:, d, :], start=False, stop=True, reads=[QKb, VNb], writes=[pob])
            for d in range(2):
                n = ns[d]
                if n in visited:
                    P.op("dve", "tensor_tensor", O[:, n, :], O[:, n, :], po[0:64, d * 128:(d + 1) * 128], ALU.add, reads=[Ob, pob], writes=[Ob])
                else:
                    visited.add(n)
                    P.op("dve", "tensor_copy", O[:, n, :], po[0:64, d * 128:(d + 1) * 128], reads=[pob], writes=[Ob])
            pS, pSb = psp.get()
            for d in range(2):
                P.op("pe", "matmul", pS[:, d * 128:(d + 1) * 128], KD[:, d, :], VN[:, d, :], start=True, stop=True, reads=[KDb, VNb], writes=[pSb])
            for d in range(2):
                sn, snb = Sp[d].get()
                P.op("dve", "scalar_tensor_tensor", sn[:], S[d][0][:], EGL[:, ns[d], cols[d]:cols[d] + 1], pS[:, d * 128:(d + 1) * 128], ALU.mult, ALU.add, reads=[S[d][1], EGLb, pSb], writes=[snb])
                S[d] = (sn, snb)
        Rn = pre(0)
        for s in range(NCH):
            Rc = Rn
            if s + 1 < NCH: Rn = pre(s + 1)
            seq(Rc)
        zr = PT[:, h * 128:(h + 1) * 128].rearrange("(n c) e -> c n e", c=64)
        for n0 in range(0, NCH, 17):
            if last and n0 + 17 <= 4: continue
            P.dma("sp", zt[:], zr[:, n0:n0 + 17, :], writes=[ztb])
            P.op("dve", "tensor_tensor", yt[:], O[:, n0:n0 + 17, :], O[:, n0:n0 + 17, :], ALU.mult, reads=[Ob], writes=[ytb])
            P.op("dve", "tensor_reduce", st17[:], yt[:], AX.X, ALU.add, reads=[ytb], writes=[st17b])
            P.op("act", "activation", out=st17[:], in_=st17[:], func=AF.Sqrt, scale=1.0 / 128.0, bias=eps6[0:64, 0:1], reads=[st17b, eps6b], writes=[st17b])
            P.op("dve", "reciprocal", st17[:], st17[:], reads=[st17b], writes=[st17b])
            for i in range(17):
                P.op("dve", "scalar_tensor_tensor", yt[:, i, :], O[:, n0 + i, :], st17[:, i:i + 1], gn[:], ALU.mult, ALU.mult, reads=[Ob, st17b, gnb], writes=[ytb])
            P.op("act", "activation", out=zt[:], in_=zt[:], func=AF.Silu, reads=[ztb], writes=[ztb])
            P.op("dve", "tensor_tensor", yt[:], yt[:], zt[:], ALU.mult, reads=[ytb, ztb], writes=[ytb])
            for i0 in range(0, 17, 8):
                ni = min(8, 17 - i0)
                pq, pqb = psp.get()
                for i in range(ni):
                    P.op("pe", "transpose", pq[:, i * 64:(i + 1) * 64], yt[:, i0 + i, :], ident[0:64, 0:64], reads=[ytb, identb], writes=[pqb])
                yT, yTb = yT_p.get()
                evac(P, E, yT[:, 0:ni * 64], pq[:, 0:ni * 64], [pqb], [yTb])
                tok0 = (n0 + i0) * 64
                yl_write(P, YL, h * 128, tok0, ni * 64, yT, [yTb])

EXT_SHAPES = None
class LazyDr(dict):
    def __init__(self, nc):
        super().__init__(); self.nc = nc; self.specs = {}; self.used_ext = []
    def __missing__(self, name):
        kind, shape = self.specs[name]
        ap = self.nc.dram_tensor(name, list(shape), F32, kind=kind).ap()
        if kind == "ExternalInput": self.used_ext.append(name)
        self[name] = ap
        return ap

def build_fused(nc, stop=None):
    dr = LazyDr(nc)
    def ext(name, shape): dr.specs[name] = ("ExternalInput", shape)
    def internal(name, shape): dr.specs[name] = ("Internal", shape)
    ext("xT_in", [KC, 256, NTH]); ext("xT_own", [D, NTH]); ext("cT", [128, KC, 2]); ext("msk", [128, 2])
    ext("cw2", [256, 512]); ext("cl", [NLAT, NLAT]); ext("sln", [NLAT, NLAT])
    ext("cosT", [64, NLAT]); ext("sinT", [64, NLAT]); ext("Rm", [64, 64]); ext("ident", [128, 128])
    ext("tri2", [64, 2, 64]); ext("ms2", [64, 2, 64]); ext("i2", [64, 2, 64])
    for l in range(2):
        ext(f"w_ada{l}", [D, 12288]); ext(f"b_ada{l}", [128, 96]); ext(f"g{l}", [128, 4, KC]); ext(f"w_inr{l}", [D, WINR])
        ext(f"w_gate{l}", [D, 6144]); ext(f"w_branch{l}", [3, 1024, D]); ext(f"w_out{l}", [D, D]); ext(f"w_ffn_in{l}", [D, 2 * FF]); ext(f"w_ffn_out{l}", [FF, D])
        ext(f"gq{l}", [128, 4]); ext(f"gkv{l}", [128, 2]); ext(f"wqn{l}", [512, 512]); ext(f"wqr{l}", [512, 256]); ext(f"wk{l}", [256, 512]); ext(f"wv{l}", [256, 512])
        ext(f"convw{l}", [128, 12, 5]); ext(f"alog{l}", [64, 8]); ext(f"dtb{l}", [64, 8]); ext(f"gnorm{l}", [64, 128])
    dr.specs["xo"] = ("ExternalOutput", [D, NLH])
    internal("PF", [FM_ROWS, NTOT]); internal("PT", [NTOT, TM_W]); internal("YL", [17, 1536, 256]); internal("YG", [17, 2 * 1536, 256])
    internal("XL2", [D, NTH]); internal("XG", [KC, 256, NTH])
    dr["xo"]
    with ExitStack() as es:
        P = Prog(nc, es)
        AR = Arena(P)
        def steps():
            for l in range(2):
                last = (l == 1)
                yield f"A{l}", lambda: emit_A(P, AR, dr, l)
                yield f"DN{l}", lambda: emit_DN(P, AR, dr, l, last)
                yield f"FN{l}", lambda: emit_FN(P, AR, dr, last)
                yield f"MLA{l}", lambda: emit_MLA(P, AR, dr, l, last)
                def g1():
                    AR.end()
                    for blk in range(17): P.coll("AllGather", dr["YG"][blk], dr["YL"][blk], PAIRS)
                yield f"G1{l}", g1
                yield f"C{l}", lambda: emit_C(P, AR, dr, l, last)
                if not last:
                    def g2():
                        AR.end()
                        for c in range(KC): P.coll("AllGather", dr["XG"][c], dr["XL2"][c * 128:(c + 1) * 128, :], PAIRS)
                    yield f"G2{l}", g2
        for name, fn in steps():
            if stop is not None and name not in stop: continue
            fn()
        AR.end()
        P.finish()
        print("fused ops", P.n_ops, dict(P.ep))
    nc._used_ext = list(dr.used_ext)
    return nc

from concourse.bass_utils import run_bass_kernel_spmd

def dft_tables():
    n = np.arange(256, dtype=np.float64)
    ang = 2 * np.pi * np.outer(n, n) / 256.0
    cw2 = np.concatenate([np.cos(ang), np.sin(ang)], 1).astype(np.float32)
    n = np.arange(NLAT, dtype=np.int64)
    ang = 2 * np.pi * (np.outer(n, n) % NLAT).astype(np.float64) / NLAT
    return cw2, np.cos(ang).astype(np.float32), (-np.sin(ang)).astype(np.float32)

def rope_tables():
    rows = NLAT // 64
    row = np.repeat(np.arange(rows, dtype=np.float32), 64)
    col = np.tile(np.arange(64, dtype=np.float32), rows)
    inv = (10000.0 ** (-np.arange(0, 32, 2, dtype=np.float32) / 32)).astype(np.float32)
    ar = row[:, None] * inv; ac = col[:, None] * inv
    ang = np.concatenate([ar, ar, ac, ac], -1)
    cosT = np.ascontiguousarray(np.cos(ang).T.astype(np.float32)); sinT = np.ascontiguousarray(np.sin(ang).T.astype(np.float32))
    R = np.zeros((64, 64), np.float32)
    for i in range(16):
        R[16 + i, i] = -1; R[i, 16 + i] = 1; R[48 + i, 32 + i] = -1; R[32 + i, 48 + i] = 1
    return cosT, sinT, R

def dn_consts():
    p = np.arange(64)[:, None]; f = np.arange(64)[None, :]
    ple = (p <= f).astype(np.float32); pge = (p >= f).astype(np.float32)
    pgt = (p > f).astype(np.float32); plt = (p < f).astype(np.float32)
    return {"tri2": np.ascontiguousarray(np.stack([ple, pge], 1)), "ms2": np.ascontiguousarray(np.stack([pgt, plt], 1)),
            "i2": np.ascontiguousarray(np.stack([np.eye(64, dtype=np.float32)] * 2, 1)), "ident": np.eye(128, dtype=np.float32)}

_NC = []
STOP = None
def kernel(**inputs):
    inp = {k: np.asarray(v) for k, v in inputs.items()}
    B = inp["x"].shape[0]
    if not _NC:
        nc = bass.Bass("TRN2", target_bir_lowering=False, num_devices=8)
        build_fused(nc, STOP); _NC.append(nc)
    nc = _NC[0]
    cw2, cl, sln = dft_tables(); cosT, sinT, R = rope_tables(); dnc = dn_consts()
    shared = {"cw2": cw2, "cl": cl, "sln": sln, "cosT": cosT, "sinT": sinT, "Rm": R}
    shared.update(dnc)
    for l in range(2):
        shared[f"w_ada{l}"] = np.ascontiguousarray(inp["w_ada"][l])
        shared[f"b_ada{l}"] = np.ascontiguousarray(inp["b_ada"][l].reshape(96, 128).T)
        shared[f"g{l}"] = np.ascontiguousarray(inp["norm_g"][l].reshape(4, KC, 128).transpose(2, 0, 1))
        shared[f"w_gate{l}"] = np.ascontiguousarray(inp["w_in"][l][:, 5984:])
        shared[f"w_branch{l}"] = np.ascontiguousarray(inp["w_branch"][l]); shared[f"w_out{l}"] = np.ascontiguousarray(inp["w_out"][l])
        shared[f"w_ffn_in{l}"] = np.ascontiguousarray(inp["w_ffn_in"][l]); shared[f"w_ffn_out{l}"] = np.ascontiguousarray(inp["w_ffn_out"][l])
        shared[f"gq{l}"] = np.ascontiguousarray(inp["mla_q_norm_g"][l].reshape(4, 128).T)
        shared[f"gkv{l}"] = np.ascontiguousarray(inp["mla_kv_norm_g"][l].reshape(2, 128).T)
        shared[f"gnorm{l}"] = np.ascontiguousarray(np.tile(inp["dn_norm_g"][l][None], (64, 1)).astype(np.float32))
    percore = {}
    for r in range(2):
        heads = np.arange(r * 4, (r + 1) * 4)
        colsel = np.concatenate([heads, 8 + heads])
        pc = {}
        for l in range(2):
            w = inp["w_in"][l]
            cols = np.concatenate([np.arange(r * 512, (r + 1) * 512), 1024 + np.arange(r * 512, (r + 1) * 512), 2048 + np.arange(r * 512, (r + 1) * 512),
                                   np.arange(4128, 4960), 4960 + np.arange(r * 512, (r + 1) * 512), 3072 + np.arange(r * 512, (r + 1) * 512), 4096 + colsel, 4112 + colsel])
            assert cols.size == WINR
            pc[f"w_inr{l}"] = np.ascontiguousarray(w[:, cols])
            wuq = inp["w_uq"][l]; wukv = inp["w_ukv"][l]
            pc[f"wqn{l}"] = np.ascontiguousarray(np.concatenate([wuq[:, h * 192: h * 192 + 128] for h in heads], 1))
            pc[f"wqr{l}"] = np.ascontiguousarray(np.concatenate([wuq[:, h * 192 + 128: h * 192 + 192] for h in heads], 1))
            pc[f"wk{l}"] = np.ascontiguousarray(np.concatenate([wukv[:, h * 256: h * 256 + 128] for h in heads], 1))
            pc[f"wv{l}"] = np.ascontiguousarray(np.concatenate([wukv[:, h * 256 + 128: h * 256 + 256] for h in heads], 1))
            conv = inp["dn_conv"][l]
            cwl = [conv[:, qi * 1024 + h * 128: qi * 1024 + (h + 1) * 128].T for qi in range(3) for h in heads]
            pc[f"convw{l}"] = np.ascontiguousarray(np.stack(cwl, 1).astype(np.float32))
            pc[f"alog{l}"] = np.ascontiguousarray(np.tile(inp["dn_a_log"][l].reshape(16)[colsel][None], (64, 1)).astype(np.float32))
            pc[f"dtb{l}"] = np.ascontiguousarray(np.tile(inp["dn_dt_bias"][l].reshape(16)[colsel][None], (64, 1)).astype(np.float32))
        m = np.zeros((128, 2), np.float32); m[:, r] = 1.0
        pc["msk"] = m
        percore[r] = pc
    in_maps = []
    for i in range(8):
        b, r = i // 2, i % 2
        halves = [np.concatenate([inp["x"][b, q * NLH:(q + 1) * NLH], inp["ctx"][b, q * NCH2:(q + 1) * NCH2]], 0).T for q in range(2)]
        d = dict(shared); d.update(percore[r])
        d["xT_in"] = np.ascontiguousarray(np.stack([h_.reshape(KC, 128, NTH) for h_ in halves], 1).reshape(KC, 256, NTH))
        d["xT_own"] = np.ascontiguousarray(halves[r])
        cvec = np.stack([inp["c"][b], inp["c_ctx"]], -1)
        d["cT"] = np.ascontiguousarray(cvec.reshape(KC, 128, 2).transpose(1, 0, 2))
        in_maps.append(d)
    in_maps = [{k: d[k] for k in nc._used_ext} for d in in_maps]
    res = run_bass_kernel_spmd(nc, in_maps, core_ids=list(range(8))).results
    out = np.empty((B, NLAT, D), np.float32)
    for i in range(8):
        b, r = i // 2, i % 2
        out[b, r * NLH:(r + 1) * NLH] = res[i]["xo"].T
    return out
```

```python
import numpy as np
from contextlib import ExitStack
import concourse.bass as bass
import concourse.mybir as mybir
F32 = mybir.dt.float32; BF16 = mybir.dt.bfloat16; I32 = mybir.dt.int32
AF = mybir.ActivationFunctionType
ALU = mybir.AluOpType
AX = mybir.AxisListType

class Buf:
    __slots__ = ("name", "w", "r", "excl")
    def __init__(self, name="", excl=False):
        self.name = name
        self.excl = excl
        self.w = None
        self.r = {}

EPOCH = 12000
class Prog:
    ENG = ("pe", "dve", "act", "pool", "sp")
    def __init__(self, nc, es, n_dma_sems=12):
        self.nc = nc; self.es = es; self.es_global = es
        self.engobj = {"pe": nc.tensor, "dve": nc.vector, "act": nc.scalar, "pool": nc.gpsimd, "sp": nc.sync}
        self.streams = {e: [] for e in self.ENG}
        self.sems = {}
        self.cnt = {}
        self.cur = {}
        self.ep = {e: 0 for e in self.ENG}
        for e in self.ENG:
            self._new_epoch(e)
        self.seen = {e: {} for e in self.ENG}
        self.dma_keys = []
        for i in range(n_dma_sems):
            k = ("dma", i)
            self.sems[k] = es.enter_context(nc.semaphore(f"dma{i}"))
            self.cnt[k] = 0
            self.dma_keys.append(k)
        self.dma_rr = 0
        self.n_ops = 0
    def _new_epoch(self, e):
        k = (e, self.ep[e]); self.ep[e] += 1
        self.sems[k] = self.es_global.enter_context(self.nc.semaphore(f"s_{e}_{k[1]}"))
        self.cnt[k] = 0; self.cur[e] = k
    def _deps(self, reads, writes):
        deps = {}
        def need(k, c):
            if deps.get(k, 0) < c: deps[k] = c
        for b in reads:
            if b.w is not None: need(*b.w)
        for b in writes:
            if b.w is not None: need(*b.w)
            for k, c in b.r.items(): need(k, c)
        return deps
    def _emit_waits(self, e, deps, skip_key=None):
        seen = self.seen[e]
        for k, c in deps.items():
            if k == skip_key: continue
            if seen.get(k, 0) >= c: continue
            seen[k] = c
            sem = self.sems[k]
            self.streams[e].append(lambda eng, sem=sem, c=c: eng.wait_ge(sem, c))
    def _mark(self, key, c, reads, writes):
        for b in reads:
            if b.r.get(key, 0) < c: b.r[key] = c
        for b in writes:
            b.w = (key, c); b.r = {}
    def op(self, e, meth, *args, reads=(), writes=(), same_engine_sync=True, **kw):
        writes = list(writes) + [b for b in reads if b.excl]
        reads = [b for b in reads if not b.excl]
        deps = self._deps(reads, writes)
        key = self.cur[e]
        if self.cnt[key] >= EPOCH:
            self._new_epoch(e); key = self.cur[e]
        skip = None
        if e == "pe" or not same_engine_sync:
            deps = {k: c for k, c in deps.items() if k[0] != e}
        self._emit_waits(e, deps, skip)
        self.cnt[key] += 1
        c = self.cnt[key]; sem = self.sems[key]
        self.streams[e].append(lambda eng, meth=meth, args=args, kw=kw, sem=sem: getattr(eng, meth)(*args, **kw).then_inc(sem, 1))
        self._mark(key, c, reads, writes)
        self.n_ops += 1
    def dma(self, q, out, in_, reads=(), writes=(), **kw):
        deps = self._deps(reads, writes)
        k = self.dma_keys[self.dma_rr]; self.dma_rr = (self.dma_rr + 1) % len(self.dma_keys)
        if self.cnt[k] > 0: deps[k] = max(deps.get(k, 0), self.cnt[k])
        self._emit_waits(q, deps)
        self.cnt[k] += 16
        c = self.cnt[k]; sem = self.sems[k]
        self.streams[q].append(lambda eng, out=out, in_=in_, sem=sem, kw=kw: eng.dma_start(out=out, in_=in_, **kw).then_inc(sem, 16))
        self._mark(k, c, reads, writes)
        self.n_ops += 1
    def coll(self, kind, out, in_, groups, reads=(), writes=()):
        deps = self._deps(reads, writes)
        k = ("cc", 0)
        if k not in self.sems:
            self.sems[k] = self.es_global.enter_context(self.nc.semaphore("cc0"))
            self.cnt[k] = 0
            self.dma_keys.append(k)
        self.cnt[k] += 1
        self._emit_waits("pool", deps)
        sem = self.sems[k]
        self.streams["pool"].append(lambda eng, out=out, in_=in_, sem=sem: eng.collective_compute(kind, mybir.AluOpType.bypass, replica_groups=groups, ins=[in_.opt()], outs=[out.opt()]).then_inc(sem, 1))
        self._mark(k, self.cnt[k], reads, writes)
        self.n_ops += 1
    def barrier(self):
        deps = {k: c for k, c in self.cnt.items() if c > 0}
        for e in self.ENG:
            self._emit_waits(e, {k: c for k, c in deps.items() if k != self.cur[e]})
    def flush(self):
        nc = self.nc
        streams = self.streams
        self.streams = {e: [] for e in self.ENG}
        with nc.Block() as block:
            @block.sync
            def _(eng):
                for f in streams["sp"]: f(eng)
            @block.tensor
            def _(eng):
                for f in streams["pe"]: f(eng)
            @block.vector
            def _(eng):
                for f in streams["dve"]: f(eng)
            @block.scalar
            def _(eng):
                for f in streams["act"]: f(eng)
            @block.gpsimd
            def _(eng):
                for f in streams["pool"]: f(eng)
    def finish(self):
        deps = {k: self.cnt[k] for k in self.dma_keys if self.cnt[k] > 0}
        self._emit_waits("sp", deps)
        self.flush()

class Pool:
    def __init__(self, P, name, shape, dtype, n, psum=False):
        self.tiles = []
        for i in range(n):
            if psum:
                t = P.es.enter_context(P.nc.psum_tensor(f"pp_{name}{i}", shape, dtype))
            else:
                t = P.es.enter_context(P.nc.sbuf_tensor(f"sp_{name}{i}", shape, dtype))
            self.tiles.append((t, Buf(f"{name}{i}", excl=psum)))
        self.i = 0
    def get(self):
        t = self.tiles[self.i]; self.i = (self.i + 1) % len(self.tiles)
        return t

def sb(P, name, shape, dtype=F32):
    return P.es.enter_context(P.nc.sbuf_tensor("sb_" + name, shape, dtype)), Buf(name)
def ps(P, name, shape, dtype=F32):
    return P.es.enter_context(P.nc.psum_tensor("ps_" + name, shape, dtype)), Buf(name, excl=True)

import math
D = 2048; KC = 16; FF = 5632; FKC = 44
NCTX = 256; NLAT = 4096; NTOT = NCTX + NLAT; NCH = NTOT // 64
NLH = 2048; NCH2 = 128; NTH = NLH + NCH2
MLA_SCALE = 192 ** -0.5
NAR = 52000
PAIRS = [[0, 1], [2, 3], [4, 5], [6, 7]]
F32R = mybir.dt.float32r
def RR_(ap): return ap.bitcast(F32R)
FM_CHUNKS = [(c0, 128) for c0 in range(0, 2304, 128)] + [(2304, 64)] + [(2368 + i * 128, 128) for i in range(4)]
FM_ROWS = 2880; TM_COL0 = 2880; TM_W = 528; WINR = 3408

class Arena:
    def __init__(self, P):
        self.P = P
        self.banks = [(P.es_global.enter_context(P.nc.psum_tensor(f"bank{i}", [128, 512], F32)), Buf(f"bank{i}", excl=True)) for i in range(8)]
        self.ph = None; self.nph = 0
    def begin(self, nN, nR):
        assert nN + nR <= NAR, (nN, nR)
        self.end()
        self.ph = ExitStack(); self.nph += 1
        self.tN = self.ph.enter_context(self.P.nc.sbuf_tensor(f"arN{self.nph}", [128, max(nN, 8)], F32))
        self.tR = self.ph.enter_context(self.P.nc.sbuf_tensor(f"arR{self.nph}", [128, max(nR, 8)], F32))
        self.cap = {False: nN, True: nR}; self.off = {False: 0, True: 0}; self.bi = 0
    def end(self):
        self.P.barrier()
        if self.ph is not None:
            self.P.flush(); self.ph.close(); self.ph = None
    def sb(self, name, shape, R=False):
        n = int(np.prod(shape[1:]))
        assert self.off[R] + n <= self.cap[R], (name, R, self.off[R], n, self.cap[R])
        t = self.tR if R else self.tN
        v = t[0:shape[0], self.off[R]:self.off[R] + n]
        self.off[R] += n
        if len(shape) == 3: v = v.rearrange("p (a b) -> p a b", a=shape[1])
        elif len(shape) == 4: v = v.rearrange("p (a b c) -> p a b c", a=shape[1], b=shape[2])
        return v, Buf(name)
    def pool(self, name, shape, n, R=False):
        return RR([self.sb(f"{name}{i}", shape, R) for i in range(n)])
    def pspool(self, n):
        b = self.banks[self.bi:self.bi + n]; assert len(b) == n; self.bi += n
        return RR(b)

class RR:
    def __init__(self, tiles): self.tiles = tiles; self.i = 0
    def get(self):
        t = self.tiles[self.i]; self.i = (self.i + 1) % len(self.tiles); return t

def common(P, AR):
    class E: pass
    E = E()
    E.ones, E.onesb = AR.sb("ones", [128, 128]); P.op("pool", "memset", E.ones[:], 1.0, writes=[E.onesb])
    E.eps = {}
    for dim, val in ((2048, 2048e-6), (512, 512e-6), (256, 256e-6), (1, 1e-6)):
        t, b = AR.sb(f"eps{dim}", [128, 1]); P.op("pool", "memset", t[:], val, writes=[b]); E.eps[dim] = (t, b)
    E.sqp = AR.pool("sq", [128, 512], 2)
    E.ssp, E.sspb = AR.pspool(1).get()
    E.rstd, E.rstdb = AR.sb("rstd", [128, 512])
    E.ev = 0
    return E

def rms_rstd(P, E, src, srcb, nch, tb, dim):
    for c in range(nch):
        sq, sqb = E.sqp.get()
        P.op("act", "activation", out=sq[:, 0:tb], in_=src[:, c, 0:tb], func=AF.Square, reads=[srcb], writes=[sqb])
        P.op("pe", "matmul", E.ssp[:, 0:tb], E.ones[:], sq[:, 0:tb], start=(c == 0), stop=(c == nch - 1), reads=[sqb, E.onesb], writes=[E.sspb])
    eb = E.eps[dim]
    P.op("act", "activation", out=E.rstd[:, 0:tb], in_=E.ssp[:, 0:tb], func=AF.Sqrt, bias=eb[0][:, 0:1], reads=[E.sspb, eb[1]], writes=[E.rstdb])
    P.op("dve", "reciprocal", E.rstd[:, 0:tb], E.rstd[:, 0:tb], reads=[E.rstdb], writes=[E.rstdb])

def evac(P, E, dst, src, reads, writes):
    if E.ev % 2 == 0: P.op("dve", "tensor_copy", dst, src, reads=reads, writes=writes)
    else: P.op("act", "activation", out=dst, in_=src, func=AF.Copy, reads=reads, writes=writes)
    E.ev += 1

def ld(P, AR, name, src, shape, q="sp"):
    t, b = AR.sb(name, shape); P.dma(q, t[:], src, writes=[b]); return t, b

def emit_mod(P, AR, E, wget, pmod, cT_d, wada_d, bada_d, nchunks, cpt=2):
    ct, cb = ld(P, AR, "cT", cT_d, [128, KC, 2]); bt, bb = ld(P, AR, "bada", bada_d, [128, 96])
    sc, scb = AR.sb("sc", [128, KC, 2]); mod_t, mod_b = AR.sb("mod", [128, 96, 2])
    P.op("act", "activation", out=sc[:], in_=ct[:], func=AF.Silu, reads=[cb], writes=[scb])
    for n0 in range(0, nchunks, cpt):
        wt, wb = wget()
        P.dma("sp", wt[:, :, 0:cpt * 128], wada_d[:, n0 * 128:(n0 + cpt) * 128].rearrange("(c p) n -> p c n", p=128), writes=[wb])
        for gi in range(cpt):
            n = n0 + gi
            pt, pb = pmod.get()
            for k in range(KC):
                P.op("pe", "matmul", pt[:, 0:2], wt[:, k, gi * 128:(gi + 1) * 128], sc[:, k, :], start=(k == 0), stop=(k == KC - 1), reads=[wb, scb], writes=[pb])
            P.op("dve", "tensor_scalar", mod_t[:, n, :], pt[:, 0:2], bt[:, n:n + 1], None, ALU.add, reads=[pb, bb], writes=[mod_b])
    return mod_t, mod_b

def yl_write(P, YL, row0, col0, width, src, reads):
    c = col0
    while c < col0 + width:
        blk, off = c // 256, c % 256
        w = min(256 - off, col0 + width - c)
        P.dma("act", YL[blk, row0:row0 + 128, off:off + w], src[:, c - col0:c - col0 + w], reads=reads)
        c += w

def emit_A(P, AR, dr, l):
    AR.begin(21500, 24576); E = common(P, AR)
    xsrc = dr["xT_in"] if l == 0 else dr["XG"]
    wpool = AR.pool("w", [128, KC, 512], 2, R=True)
    pmm = AR.pspool(5)
    wmod = AR.pool("wm", [128, KC, 256], 2)
    mod_t, mod_b = emit_mod(P, AR, E, lambda: wmod.get(), AR.pspool(2), dr["cT"], dr[f"w_ada{l}"], dr[f"b_ada{l}"], 32)
    g0, g0b = ld(P, AR, "g0", dr[f"g{l}"][:, 0, :], [128, KC])
    At, Ab = AR.sb("A", [128, KC, 2])
    for j in range(2):
        P.op("dve", "tensor_scalar", At[:, :, j], mod_t[:, KC:2 * KC, j], 1.0, math.sqrt(D), ALU.add, ALU.mult, reads=[mod_b], writes=[Ab])
        P.op("dve", "tensor_tensor", At[:, :, j], At[:, :, j], g0[:], ALU.mult, reads=[Ab, g0b], writes=[Ab])
    xs, xsb = AR.sb("xs", [128, KC, 512]); hs, hsb = AR.sb("hs", [128, KC, 512], R=True)
    opool = AR.pool("o", [128, 512], 3)
    win = dr[f"w_inr{l}"]; PF = dr["PF"]; PT = dr["PT"]
    blocks = []
    for r in range(2):
        for t0 in range(0, NLH, 512): blocks.append((r, t0, 512, 0, NCTX + r * NLH + t0))
        blocks.append((r, NLH, NCH2, 1, r * NCH2))
    for (r, t0, tb, j, dst0) in blocks:
        P.dma("sp", xs[:, :, 0:tb], xsrc.rearrange("c (r p) t -> r p c t", r=2)[r][:, :, t0:t0 + tb], writes=[xsb])
        rms_rstd(P, E, xs, xsb, KC, tb, 2048)
        for c in range(KC):
            P.op("dve", "scalar_tensor_tensor", RR_(hs[:, c, 0:tb]), xs[:, c, 0:tb], At[:, c, j:j + 1], E.rstd[:, 0:tb], ALU.mult, ALU.mult, reads=[xsb, Ab, E.rstdb], writes=[hsb])
            P.op("act", "activation", out=RR_(hs[:, c, 0:tb]), in_=hs[:, c, 0:tb], func=AF.Identity, bias=mod_t[:, c, j:j + 1], reads=[hsb, mod_b], writes=[hsb])
        row = 0; wt = None; wcol0 = None
        for (c0, wd) in FM_CHUNKS:
            if wt is None or not (wcol0 <= c0 and c0 + wd <= wcol0 + 512):
                wt, wb = wpool.get(); wcol0 = c0
                ncols = min(512, FM_ROWS - c0)
                P.dma("pool", RR_(wt[:, :, 0:ncols]), win[:, c0:c0 + ncols].rearrange("(c p) n -> p c n", p=128), writes=[wb])
            pt, pb = pmm.get(); off = c0 - wcol0
            for k in range(KC):
                P.op("pe", "matmul", pt[0:wd, 0:tb], RR_(wt[:, k, off:off + wd]), RR_(hs[:, k, 0:tb]), start=(k == 0), stop=(k == KC - 1), reads=[wb, hsb], writes=[pb])
            ot, ob = opool.get()
            evac(P, E, ot[0:wd, 0:tb], pt[0:wd, 0:tb], [pb], [ob])
            P.dma("act", PF[row:row + wd, dst0:dst0 + tb], ot[0:wd, 0:tb], reads=[ob])
            row += wd
        for n0 in range(0, TM_W, 512):
            nw = min(512, TM_W - n0)
            wt, wb = wpool.get()
            P.dma("pool", RR_(wt[:, :, 0:nw]), win[:, TM_COL0 + n0:TM_COL0 + n0 + nw].rearrange("(c p) n -> p c n", p=128), writes=[wb])
            for ts in range(tb // 128):
                pt, pb = pmm.get()
                for k in range(KC):
                    P.op("pe", "matmul", pt[:, 0:nw], RR_(hs[:, k, ts * 128:(ts + 1) * 128]), RR_(wt[:, k, 0:nw]), start=(k == 0), stop=(k == KC - 1), reads=[wb, hsb], writes=[pb])
                ot, ob = opool.get()
                evac(P, E, ot[:, 0:nw], pt[:, 0:nw], [pb], [ob])
                P.dma("act", PT[dst0 + ts * 128:dst0 + (ts + 1) * 128, n0:n0 + nw], ot[:, 0:nw], reads=[ob])

def emit_C(P, AR, dr, l, last):
    AR.begin(15000, 36864); E = common(P, AR)
    TB = 256
    xown = dr["xT_own"] if l == 0 else dr["XL2"]
    YG = dr["YG"]
    wpool = AR.pool("w", [128, 5632], 2, R=True)
    wmodc = AR.pool("wm", [128, KC, 128], 1)
    def wget():
        t, b = wpool.get(); return t[:, 0:KC * 256].rearrange("p (c n) -> p c n", c=KC), b
    pmm = AR.pspool(5)
    mod_t, mod_b = emit_mod(P, AR, E, lambda: wmodc.get(), AR.pspool(2), dr["cT"], dr[f"w_ada{l}"], dr[f"b_ada{l}"], 96, cpt=1)
    g, gb = ld(P, AR, "g", dr[f"g{l}"], [128, 4, KC])
    msk, mskb = ld(P, AR, "msk", dr["msk"], [128, 2])
    S, Sb = AR.sb("S", [128, 4, KC, 2])
    for j in range(2):
        for i, (mi, plus1) in enumerate([(1, True), (2, False), (4, True), (5, False)]):
            P.op("dve", "tensor_scalar", S[:, i, :, j], mod_t[:, mi * KC:(mi + 1) * KC, j], 1.0 if plus1 else 0.0, math.sqrt(D), ALU.add, ALU.mult, reads=[mod_b], writes=[Sb])
            P.op("dve", "tensor_tensor", S[:, i, :, j], S[:, i, :, j], g[:, i, :], ALU.mult, reads=[Sb, gb], writes=[Sb])
    xs, xsb = AR.sb("xs", [128, KC, TB]); hs, hsb = AR.sb("hs", [128, KC, TB], R=True)
    ys, ysb = AR.sb("ys", [128, 24, TB], R=True); mg, mgb = AR.sb("mg", [128, KC, TB], R=True); yl, ylb = AR.sb("yl", [128, KC, TB])
    act, actb = AR.sb("act", [128, FKC, TB], R=True)
    tmpp = AR.pool("tmp", [128, TB], 3); selp = AR.pool("sel", [128, TB], 4)
    wg = dr[f"w_gate{l}"]; wbr = dr[f"w_branch{l}"]; wout = dr[f"w_out{l}"]; wfi = dr[f"w_ffn_in{l}"]; wfo = dr[f"w_ffn_out{l}"]
    blocks = [(t0, min(TB, NLH - t0), 0) for t0 in range(0, NLH, TB)]
    if not last: blocks += [(NLH, NCH2, 1)]
    xTr = xown.rearrange("(c p) t -> p c t", p=128)
    outd = dr["xo"] if last else dr["XL2"]
    outr = outd.rearrange("(c p) t -> p c t", p=128)
    def normmod(src, srcb, dst, dstb, si, bi, j, tb):
        rms_rstd(P, E, src, srcb, KC, tb, 2048)
        for c in range(KC):
            P.op("dve", "scalar_tensor_tensor", RR_(dst[:, c, 0:tb]), src[:, c, 0:tb], S[:, si, c, j:j + 1], E.rstd[:, 0:tb], ALU.mult, ALU.mult, reads=[srcb, Sb, E.rstdb], writes=[dstb])
            P.op("act", "activation", out=RR_(dst[:, c, 0:tb]), in_=dst[:, c, 0:tb], func=AF.Identity, bias=mod_t[:, bi * KC + c, j:j + 1], reads=[dstb, mod_b], writes=[dstb])
    def resid(src, srcb, si, j, tb):
        rms_rstd(P, E, src, srcb, KC, tb, 2048)
        for c in range(KC):
            tt, ttb = tmpp.get()
            P.op("dve", "scalar_tensor_tensor", tt[:, 0:tb], src[:, c, 0:tb], S[:, si, c, j:j + 1], E.rstd[:, 0:tb], ALU.mult, ALU.mult, reads=[srcb, Sb, E.rstdb], writes=[ttb])
            P.op("dve", "tensor_tensor", xs[:, c, 0:tb], xs[:, c, 0:tb], tt[:, 0:tb], ALU.add, reads=[xsb, ttb], writes=[xsb])
    for (t0, tb, j) in blocks:
        P.dma("sp", xs[:, :, 0:tb], xTr[:, :, t0:t0 + tb], writes=[xsb])
        for i in range(3):
            for gg in range(2):
                for c in range(4):
                    ch = i * 8 + gg * 4 + c
                    r0 = gg * 1536 + i * 512 + c * 128
                    cols = [(NCTX + r * NLH + t0) if j == 0 else (r * NCH2) for r in range(2)]
                    s0, s0b = selp.get(); st, stb = selp.get()
                    P.dma("act", s0[:, 0:tb], YG[cols[0] // 256, r0:r0 + 128, cols[0] % 256:cols[0] % 256 + tb], writes=[s0b])
                    P.dma("act", st[:, 0:tb], YG[cols[1] // 256, r0:r0 + 128, cols[1] % 256:cols[1] % 256 + tb], writes=[stb])
                    P.op("dve", "tensor_scalar", s0[:, 0:tb], s0[:, 0:tb], msk[:, 0:1], None, ALU.mult, reads=[s0b, mskb], writes=[s0b])
                    P.op("dve", "scalar_tensor_tensor", RR_(ys[:, ch, 0:tb]), st[:, 0:tb], msk[:, 1:2], s0[:, 0:tb], ALU.mult, ALU.add, reads=[stb, mskb, s0b], writes=[ysb])
        normmod(xs, xsb, hs, hsb, 0, 0, j, tb)
        for n in range(KC):
            for i in range(3):
                wt, wb = wpool.get()
                wgv = wt[:, 0:KC * 128].rearrange("p (c n) -> p c n", c=KC)
                wbv = wt[:, KC * 128:KC * 128 + 8 * 128].rearrange("p (c n) -> p c n", c=8)
                P.dma("pool", RR_(wgv), wg[:, i * D + n * 128: i * D + (n + 1) * 128].rearrange("(c p) n -> p c n", p=128), writes=[wb])
                P.dma("pool", RR_(wbv), wbr[i, :, n * 128:(n + 1) * 128].rearrange("(c p) n -> p c n", p=128), writes=[wb])
                p1, p1b = pmm.get()
                for k in range(KC):
                    P.op("pe", "matmul", p1[:, 0:tb], RR_(wgv[:, k, :]), RR_(hs[:, k, 0:tb]), start=(k == 0), stop=(k == KC - 1), reads=[wb, hsb], writes=[p1b])
                p2, p2b = pmm.get()
                for k in range(8):
                    P.op("pe", "matmul", p2[:, 0:tb], RR_(wbv[:, k, :]), RR_(ys[:, i * 8 + k, 0:tb]), start=(k == 0), stop=(k == 7), reads=[wb, ysb], writes=[p2b])
                tt, ttb = tmpp.get()
                P.op("act", "activation", out=tt[:, 0:tb], in_=p1[:, 0:tb], func=AF.Sigmoid, reads=[p1b], writes=[ttb])
                if i == 0:
                    P.op("dve", "tensor_tensor", RR_(mg[:, n, 0:tb]), tt[:, 0:tb], p2[:, 0:tb], ALU.mult, reads=[ttb, p2b], writes=[mgb])
                else:
                    P.op("dve", "tensor_tensor", tt[:, 0:tb], tt[:, 0:tb], p2[:, 0:tb], ALU.mult, reads=[ttb, p2b], writes=[ttb])
                    P.op("dve", "tensor_tensor", RR_(mg[:, n, 0:tb]), mg[:, n, 0:tb], tt[:, 0:tb], ALU.add, reads=[ttb, mgb], writes=[mgb])
        for n0 in range(0, KC, 2):
            wv, wb = wget()
            P.dma("pool", RR_(wv), wout[:, n0 * 128:(n0 + 2) * 128].rearrange("(c p) n -> p c n", p=128), writes=[wb])
            for gi in range(2):
                pt, pb = pmm.get()
                for k in range(KC):
                    P.op("pe", "matmul", pt[:, 0:tb], RR_(wv[:, k, gi * 128:(gi + 1) * 128]), RR_(mg[:, k, 0:tb]), start=(k == 0), stop=(k == KC - 1), reads=[wb, mgb], writes=[pb])
                P.op("act", "activation", out=yl[:, n0 + gi, 0:tb], in_=pt[:, 0:tb], func=AF.Copy, reads=[pb], writes=[ylb])
        resid(yl, ylb, 1, j, tb)
        normmod(xs, xsb, hs, hsb, 2, 3, j, tb)
        for h0 in range(0, FKC, 2):
            wv, wb = wget()
            P.dma("pool", RR_(wv), wfi[:, h0 * 128:(h0 + 2) * 128].rearrange("(c p) n -> p c n", p=128), writes=[wb])
            wv2, wb2 = wget()
            P.dma("pool", RR_(wv2), wfi[:, FF + h0 * 128:FF + (h0 + 2) * 128].rearrange("(c p) n -> p c n", p=128), writes=[wb2])
            for gi in range(2):
                pg, pgb = pmm.get()
                for k in range(KC):
                    P.op("pe", "matmul", pg[:, 0:tb], RR_(wv[:, k, gi * 128:(gi + 1) * 128]), RR_(hs[:, k, 0:tb]), start=(k == 0), stop=(k == KC - 1), reads=[wb, hsb], writes=[pgb])
                pu, pub = pmm.get()
                for k in range(KC):
                    P.op("pe", "matmul", pu[:, 0:tb], RR_(wv2[:, k, gi * 128:(gi + 1) * 128]), RR_(hs[:, k, 0:tb]), start=(k == 0), stop=(k == KC - 1), reads=[wb2, hsb], writes=[pub])
                tt, ttb = tmpp.get()
                P.op("act", "activation", out=tt[:, 0:tb], in_=pg[:, 0:tb], func=AF.Silu, reads=[pgb], writes=[ttb])
                P.op("dve", "tensor_tensor", RR_(act[:, h0 + gi, 0:tb]), tt[:, 0:tb], pu[:, 0:tb], ALU.mult, reads=[ttb, pub], writes=[actb])
        for n in range(KC):
            wt, wb = wpool.get()
            wv = wt[:, 0:FKC * 128].rearrange("p (c n) -> p c n", c=FKC)
            P.dma("pool", RR_(wv), wfo[:, n * 128:(n + 1) * 128].rearrange("(c p) n -> p c n", p=128), writes=[wb])
            pt, pb = pmm.get()
            for k in range(FKC):
                P.op("pe", "matmul", pt[:, 0:tb], RR_(wv[:, k, :]), RR_(act[:, k, 0:tb]), start=(k == 0), stop=(k == FKC - 1), reads=[wb, actb], writes=[pb])
            P.op("act", "activation", out=yl[:, n, 0:tb], in_=pt[:, 0:tb], func=AF.Copy, reads=[pb], writes=[ylb])
        resid(yl, ylb, 3, j, tb)
        P.dma("sp", outr[:, :, t0:t0 + tb], xs[:, :, 0:tb], reads=[xsb])

def emit_FN(P, AR, dr, last):
    AR.begin(4000, 39424); E = common(P, AR)
    PF = dr["PF"]; YL = dr["YL"]; cld = dr["cl"]; sld = dr["sln"]
    cw2, cw2b = AR.sb("cw2s", [128, 2, 512], R=True); P.dma("pool", RR_(cw2[:]), dr["cw2"].rearrange("(c p) n -> p c n", p=128), writes=[cw2b])
    us, usb = AR.sb("us", [128, 2, NTOT], R=True); Zs, Zsb = AR.sb("Zs", [128, 32, 512], R=True); Zc, Zcb = AR.sb("Zc", [128, 2, 512], R=True)
    cp = AR.pool("ct", [128, 4, 512], 3, R=True); spn = AR.pool("st", [128, 4, 512], 3, R=True)
    pz = AR.pspool(2); py = AR.pspool(4); op = AR.pool("o", [128, 512], 3)
    uTr = PF[2368:2880, :].rearrange("(c p) t -> p c t", p=128)
    for g in range(2):
        P.dma("pool", RR_(us[:]), uTr[:, 2 * g:2 * g + 2, :], writes=[usb])
        for t in range(32):
            pt, pb = pz.get()
            for kc in range(2):
                P.op("pe", "matmul", pt[:], RR_(us[:, kc, NCTX + t * 128:NCTX + (t + 1) * 128]), RR_(cw2[:, kc, :]), start=(kc == 0), stop=(kc == 1), reads=[usb, cw2b], writes=[pb])
            evac(P, E, RR_(Zs[:, t, :]), pt[:], [pb], [Zsb])
        if not last:
            for t in range(2):
                pt, pb = pz.get()
                for kc in range(2):
                    P.op("pe", "matmul", pt[:], RR_(us[:, kc, t * 128:(t + 1) * 128]), RR_(cw2[:, kc, :]), start=(kc == 0), stop=(kc == 1), reads=[usb, cw2b], writes=[pb])
                P.op("dve", "tensor_copy", RR_(Zc[:, t, 0:256]), pt[:, 0:256], reads=[pb], writes=[Zcb])
                P.op("dve", "tensor_scalar", RR_(Zc[:, t, 256:512]), pt[:, 256:512], -1.0, None, ALU.mult, reads=[pb], writes=[Zcb])
            for ch in range(2):
                pt, pb = py.get(); i = 0
                for t in range(2):
                    for part in range(2):
                        P.op("pe", "matmul", pt[:, 0:256], RR_(Zc[:, t, part * 256 + ch * 128: part * 256 + (ch + 1) * 128]), RR_(cw2[:, t, part * 256:(part + 1) * 256]), start=(i == 0), stop=(i == 3), reads=[Zcb, cw2b], writes=[pb])
                        i += 1
                ot, ob = op.get()
                P.op("act", "activation", out=ot[:, 0:256], in_=pt[:, 0:256], func=AF.Copy, scale=1.0 / 256.0, reads=[pb], writes=[ob])
                yl_write(P, YL, 512 + g * 256 + ch * 128, 0, 256, ot, [ob])
        for o in range(8):
            pts = [py.get(), py.get()]
            for t0 in range(0, 32, 4):
                ct, cb = cp.get(); st, stb = spn.get()
                P.dma("pool", RR_(ct[:]), cld[t0 * 128:(t0 + 4) * 128, o * 512:(o + 1) * 512].rearrange("(t p) n -> p t n", p=128), writes=[cb])
                P.dma("pool", RR_(st[:]), sld[t0 * 128:(t0 + 4) * 128, o * 512:(o + 1) * 512].rearrange("(t p) n -> p t n", p=128), writes=[stb])
                for tt in range(4):
                    t = t0 + tt
                    for ch in range(2):
                        pt, pb = pts[ch]
                        P.op("pe", "matmul", pt[:], RR_(Zs[:, t, ch * 128:(ch + 1) * 128]), RR_(ct[:, tt, :]), start=(t == 0), stop=False, reads=[Zsb, cb], writes=[pb])
                        P.op("pe", "matmul", pt[:], RR_(Zs[:, t, 256 + ch * 128:256 + (ch + 1) * 128]), RR_(st[:, tt, :]), start=False, stop=(t == 31), reads=[Zsb, stb], writes=[pb])
            for ch in range(2):
                pt, pb = pts[ch]; ot, ob = op.get()
                if ch == 0: P.op("act", "activation", out=ot[:], in_=pt[:], func=AF.Copy, scale=1.0 / 1024.0, reads=[pb], writes=[ob])
                else: P.op("dve", "tensor_scalar", ot[:], pt[:], 1.0 / 1024.0, None, ALU.mult, reads=[pb], writes=[ob])
                yl_write(P, YL, 512 + g * 256 + ch * 128, NCTX + o * 512, 512, ot, [ob])

def emit_MLA(P, AR, dr, l, last, NH=4):
    AR.begin(28000, 17408); E = common(P, AR)
    PF = dr["PF"]; YL = dr["YL"]
    cin = PF[1536:2368, :]
    gq, gqb = ld(P, AR, "gq_s", dr[f"gq{l}"], [128, 4]); gkv, gkvb = ld(P, AR, "gkv_s", dr[f"gkv{l}"], [128, 2])
    P.op("dve", "tensor_scalar", gq[:], gq[:], math.sqrt(512.0), None, ALU.mult, reads=[gqb], writes=[gqb])
    P.op("dve", "tensor_scalar", gkv[:], gkv[:], math.sqrt(256.0), None, ALU.mult, reads=[gkvb], writes=[gkvb])
    wqn, wqnb = ld(P, AR, "wqn_s", dr[f"wqn{l}"].rearrange("(c p) n -> p c n", p=128), [128, 4, NH * 128])
    wqr, wqrb = ld(P, AR, "wqr_s", dr[f"wqr{l}"].rearrange("(c p) n -> p c n", p=128), [128, 4, NH * 64])
    wk, wkb = ld(P, AR, "wk_s", dr[f"wk{l}"].rearrange("(c p) n -> p c n", p=128), [128, 2, NH * 128])
    wv, wvb = ld(P, AR, "wv_s", dr[f"wv{l}"].rearrange("(c p) n -> p c n", p=128), [128, 2, NH * 128])
    Rm, Rmb = ld(P, AR, "R_s", dr["Rm"], [64, 64]); ident, identb = ld(P, AR, "id_s", dr["ident"], [128, 128])
    Qn, Qnb = AR.sb("Qn", [128, NTOT], R=True); Qr, Qrb = AR.sb("Qr", [64, NTOT], R=True)
    Kn, Knb = AR.sb("Kn", [128, NTOT], R=True); Kr, Krb = AR.sb("Kr", [64, NTOT], R=True)
    V, Vb = AR.sb("V", [128, NTOT // 128, 128]); Ss, Ssb = AR.sb("Ss", [128, NTOT])
    cb_t, cb_b = AR.sb("cblk", [128, 6, 512]); krb_t, krb_b = AR.sb("krblk", [64, 512])
    cqn, cqnb = AR.sb("cqn", [128, 4, 512]); ckvn, ckvnb = AR.sb("ckvn", [128, 2, 512])
    cs_t, cs_b = AR.sb("cosb", [64, 512]); sn_t, sn_b = AR.sb("sinb", [64, 512])
    tq, tqb = AR.sb("tq", [64, 512]); t2, t2b = AR.sb("t2", [64, 512])
    pmm = AR.pspool(5); po_p = AR.pspool(2)
    PTp = AR.pool("PT", [128, 512], 2); osb_p = AR.pool("osb", [128, 128], 2); oT_p = AR.pool("oT", [128, 128], 2); st_p = AR.pool("stat", [128, 4], 2)
    cinr = cin[0:768, :].rearrange("(c p) t -> p c t", p=128)
    cos_d = dr["cosT"]; sin_d = dr["sinT"]
    blocks = [(0, 256, False)] + [(NCTX + i * 512, 512, True) for i in range(8)]
    for h in range(NH):
        for (t0, tb, lat) in blocks:
            P.dma("sp", cb_t[:, :, 0:tb], cinr[:, :, t0:t0 + tb], writes=[cb_b])
            P.dma("sp", krb_t[:, 0:tb], cin[768:832, t0:t0 + tb], writes=[krb_b])
            if lat:
                P.dma("act", cs_t[:, 0:tb], cos_d[:, t0 - NCTX:t0 - NCTX + tb], writes=[cs_b])
                P.dma("act", sn_t[:, 0:tb], sin_d[:, t0 - NCTX:t0 - NCTX + tb], writes=[sn_b])
            rms_rstd(P, E, cb_t[:, 0:4, :], cb_b, 4, tb, 512)
            for c in range(4):
                P.op("dve", "scalar_tensor_tensor", cqn[:, c, 0:tb], cb_t[:, c, 0:tb], gq[:, c:c + 1], E.rstd[:, 0:tb], ALU.mult, ALU.mult, reads=[cb_b, gqb, E.rstdb], writes=[cqnb])
            rms_rstd(P, E, cb_t[:, 4:6, :], cb_b, 2, tb, 256)
            for c in range(2):
                P.op("dve", "scalar_tensor_tensor", ckvn[:, c, 0:tb], cb_t[:, 4 + c, 0:tb], gkv[:, c:c + 1], E.rstd[:, 0:tb], ALU.mult, ALU.mult, reads=[cb_b, gkvb, E.rstdb], writes=[ckvnb])
            pt, pb = pmm.get()
            for kc in range(4):
                P.op("pe", "matmul", pt[:, 0:tb], wqn[:, kc, h * 128:(h + 1) * 128], cqn[:, kc, 0:tb], start=(kc == 0), stop=(kc == 3), reads=[wqnb, cqnb], writes=[pb])
            evac(P, E, RR_(Qn[:, t0:t0 + tb]), pt[:, 0:tb], [pb], [Qnb])
            pt, pb = pmm.get()
            for kc in range(2):
                P.op("pe", "matmul", pt[:, 0:tb], wk[:, kc, h * 128:(h + 1) * 128], ckvn[:, kc, 0:tb], start=(kc == 0), stop=(kc == 1), reads=[wkb, ckvnb], writes=[pb])
            evac(P, E, RR_(Kn[:, t0:t0 + tb]), pt[:, 0:tb], [pb], [Knb])
            for ts in range(tb // 128):
                pt, pb = pmm.get()
                for kc in range(2):
                    P.op("pe", "matmul", pt[:, 0:128], ckvn[:, kc, ts * 128:(ts + 1) * 128], wv[:, kc, h * 128:(h + 1) * 128], start=(kc == 0), stop=(kc == 1), reads=[wvb, ckvnb], writes=[pb])
                evac(P, E, V[:, t0 // 128 + ts, :], pt[:, 0:128], [pb], [Vb])
            pt, pb = pmm.get()
            for kc in range(4):
                P.op("pe", "matmul", pt[0:64, 0:tb], wqr[:, kc, h * 64:(h + 1) * 64], cqn[:, kc, 0:tb], start=(kc == 0), stop=(kc == 3), reads=[wqrb, cqnb], writes=[pb])
            def rope(dst, dstb, src, srcb):
                p2, p2b = pmm.get()
                P.op("pe", "matmul", p2[0:64, 0:tb], Rm[:, :], src, start=True, stop=True, reads=[Rmb, srcb], writes=[p2b])
                P.op("dve", "tensor_tensor", t2[:, 0:tb], p2[0:64, 0:tb], sn_t[:, 0:tb], ALU.mult, reads=[p2b, sn_b], writes=[t2b])
                P.op("pool", "tensor_tensor", RR_(dst[:, t0:t0 + tb]), src, cs_t[:, 0:tb], ALU.mult, reads=[srcb, cs_b], writes=[dstb])
                P.op("dve", "tensor_tensor", RR_(dst[:, t0:t0 + tb]), dst[:, t0:t0 + tb], t2[:, 0:tb], ALU.add, reads=[dstb, t2b], writes=[dstb])
            if lat:
                evac(P, E, tq[:, 0:tb], pt[0:64, 0:tb], [pb], [tqb])
                rope(Qr, Qrb, tq[:, 0:tb], tqb)
                rope(Kr, Krb, krb_t[:, 0:tb], krb_b)
            else:
                evac(P, E, RR_(Qr[:, t0:t0 + tb]), pt[0:64, 0:tb], [pb], [Qrb])
                P.op("pool", "tensor_copy", RR_(Kr[:, t0:t0 + tb]), krb_t[:, 0:tb], reads=[krb_b], writes=[Krb])
        qtiles = [(NCTX + qt * 128, 0, NTOT) for qt in range(32)]
        if not last: qtiles = [(qt * 128, 0, NCTX) for qt in range(2)] + qtiles
        for (q0, k0, k1) in qtiles:
            nk = k1 - k0
            for kb0 in range(k0, k1, 512):
                kw = min(512, k1 - kb0)
                pt, pb = pmm.get()
                P.op("pe", "matmul", pt[:, 0:kw], RR_(Qn[:, q0:q0 + 128]), RR_(Kn[:, kb0:kb0 + kw]), start=True, stop=False, reads=[Qnb, Knb], writes=[pb])
                P.op("pe", "matmul", pt[:, 0:kw], RR_(Qr[:, q0:q0 + 128]), RR_(Kr[:, kb0:kb0 + kw]), start=False, stop=True, reads=[Qrb, Krb], writes=[pb])
                evac(P, E, Ss[:, kb0:kb0 + kw], pt[:, 0:kw], [pb], [Ssb])
            stt, stb = st_p.get()
            P.op("dve", "tensor_reduce", stt[:, 0:1], Ss[:, k0:k1], AX.X, ALU.max, reads=[Ssb], writes=[stb])
            P.op("dve", "tensor_scalar", stt[:, 1:2], stt[:, 0:1], -MLA_SCALE, None, ALU.mult, reads=[stb], writes=[stb])
            P.op("pool", "memset", stt[:, 2:3], 0.0, writes=[stb])
            P.op("act", "activation", out=Ss[:, k0:k1], in_=Ss[:, k0:k1], func=AF.Exp, scale=MLA_SCALE, bias=stt[:, 1:2], accum_out=stt[:, 2:3], reads=[Ssb, stb], writes=[Ssb, stb])
            P.op("dve", "reciprocal", stt[:, 3:4], stt[:, 2:3], reads=[stb], writes=[stb])
            po, pob = po_p.get()
            ntile = nk // 128
            for g0 in range(0, ntile, 4):
                gn = min(4, ntile - g0)
                ptp, ptpb = pmm.get()
                for i in range(gn):
                    kt = k0 // 128 + g0 + i
                    P.op("pe", "transpose", ptp[:, i * 128:(i + 1) * 128], Ss[:, kt * 128:(kt + 1) * 128], ident[:], reads=[Ssb, identb], writes=[ptpb])
                PT, PTb = PTp.get()
                evac(P, E, PT[:, 0:gn * 128], ptp[:, 0:gn * 128], [ptpb], [PTb])
                for i in range(gn):
                    kt = k0 // 128 + g0 + i
                    P.op("pe", "matmul", po[:, 0:128], PT[:, i * 128:(i + 1) * 128], V[:, kt, :], start=(g0 + i == 0), stop=(g0 + i == ntile - 1), reads=[PTb, Vb], writes=[pob])
            ot, ob = osb_p.get()
            P.op("dve", "tensor_scalar", ot[:], po[:, 0:128], stt[:, 3:4], None, ALU.mult, reads=[pob, stb], writes=[ob])
            pq, pqb = pmm.get()
            P.op("pe", "transpose", pq[:, 0:128], ot[:], ident[:], reads=[ob, identb], writes=[pqb])
            oT, oTb = oT_p.get()
            evac(P, E, oT[:], pq[:, 0:128], [pqb], [oTb])
            yl_write(P, YL, 1024 + h * 128, q0, 128, oT, [oTb])

def emit_DN(P, AR, dr, l, last, NH=4):
    AR.begin(51900, 8); E = common(P, AR)
    G = 2 * NH
    PF = dr["PF"]; PT = dr["PT"]; YL = dr["YL"]
    TRI2, TRI2b = ld(P, AR, "tri2s", dr["tri2"], [64, 2, 64]); MS2, MS2b = ld(P, AR, "ms2s", dr["ms2"], [64, 2, 64])
    I2, I2b = ld(P, AR, "i2s", dr["i2"], [64, 2, 64]); ident, identb = ld(P, AR, "ids", dr["ident"], [128, 128])
    cw, cwb = ld(P, AR, "cws", dr[f"convw{l}"], [128, 3 * NH, 5]); gn, gnb = ld(P, AR, "gns", dr[f"gnorm{l}"], [64, 128])
    alog, alogb = ld(P, AR, "alogs", dr[f"alog{l}"], [64, G]); dtb, dtbb = ld(P, AR, "dtbs", dr[f"dtb{l}"], [64, G])
    ones, onesb = E.ones, E.onesb
    one1, one1b = AR.sb("one1", [128, 1]); P.op("pool", "memset", one1[:], 1.0, writes=[one1b])
    eps6, eps6b = E.eps[1]
    psp = AR.pspool(7)
    bl, blb = ld(P, AR, "bls", PT[:, 512:512 + G].rearrange("(n c) x -> c n x", c=64), [64, NCH, G])
    al, alb = ld(P, AR, "als", PT[:, 512 + G:512 + 2 * G].rearrange("(n c) x -> c n x", c=64), [64, NCH, G], q="act")
    BETA, BETAb = AR.sb("BETA", [64, NCH, G]); NBETA, NBETAb = AR.sb("NBETA", [64, NCH, G])
    gt, gtb = AR.sb("gt", [64, NCH, G]); GC, GCb = AR.sb("GC", [64, NCH, G]); NGC, NGCb = AR.sb("NGC", [64, NCH, G])
    BEG, BEGb = AR.sb("BEG", [64, NCH, G]); EKD, EKDb = AR.sb("EKD", [64, NCH, G]); EGL, EGLb = AR.sb("EGL", [128, NCH, G])
    P.op("act", "activation", out=BETA[:], in_=bl[:], func=AF.Sigmoid, reads=[blb], writes=[BETAb])
    P.op("dve", "tensor_scalar", NBETA[:], BETA[:], -1.0, None, ALU.mult, reads=[BETAb], writes=[NBETAb])
    for c in range(G):
        P.op("act", "activation", out=gt[:, :, c], in_=al[:, :, c], func=AF.Exp, bias=dtb[:, c:c + 1], reads=[alb, dtbb], writes=[gtb])
    P.op("act", "activation", out=gt[:], in_=gt[:], func=AF.Ln, bias=one1[0:64, 0:1], reads=[gtb, one1b], writes=[gtb])
    P.op("act", "activation", out=alog[:], in_=alog[:], func=AF.Exp, reads=[alogb], writes=[alogb])
    P.op("dve", "tensor_scalar", alog[:], alog[:], -1.0, None, ALU.mult, reads=[alogb], writes=[alogb])
    for c in range(G):
        P.op("dve", "tensor_scalar", gt[:, :, c], gt[:, :, c], alog[:, c:c + 1], None, ALU.mult, reads=[gtb, alogb], writes=[gtb])
    gflat = gt[:].rearrange("p n g -> p (n g)")
    NF = NCH * G; H2 = NF // 2
    for d in range(2):
        pt, pb = psp.get(); pt2, pb2 = psp.get()
        for (pp, ppb, c0) in ((pt, pb, 0), (pt2, pb2, H2)):
            P.op("pe", "matmul", pp[0:64, 0:H2], TRI2[:, d, :], gflat[:, c0:c0 + H2], start=True, stop=True, reads=[TRI2b, gtb], writes=[ppb])
        for (pp, ppb, c0) in ((pt, pb, 0), (pt2, pb2, H2)):
            nn = H2 // G
            src = pp[0:64, 0:H2].rearrange("p (n g) -> p n g", g=G)
            P.op("dve", "tensor_copy", GC[:, c0 // G:c0 // G + nn, d * NH:(d + 1) * NH], src[:, :, d * NH:(d + 1) * NH], reads=[ppb], writes=[GCb])
    P.op("dve", "tensor_scalar", NGC[:], GC[:], -1.0, None, ALU.mult, reads=[GCb], writes=[NGCb])
    P.op("act", "activation", out=BEG[:], in_=GC[:], func=AF.Exp, reads=[GCb], writes=[BEGb])
    P.op("dve", "tensor_tensor", BEG[:], BEG[:], BETA[:], ALU.mult, reads=[BEGb, BETAb], writes=[BEGb])
    EGLf = EGL[:].rearrange("p n g -> p (n g)"); EKDf = EKD[:].rearrange("p n g -> p (n g)"); GCf = GC[:].rearrange("p n g -> p (n g)")
    for c0 in (0, H2):
        pt, pb = psp.get()
        P.op("pe", "matmul", pt[:, 0:H2], ones[0:64, :], gflat[:, c0:c0 + H2], start=True, stop=True, reads=[onesb, gtb], writes=[pb])
        P.op("dve", "tensor_tensor", EKDf[:, c0:c0 + H2], pt[0:64, 0:H2], GCf[:, c0:c0 + H2], ALU.subtract, reads=[pb, GCb], writes=[EKDb])
        P.op("act", "activation", out=EGLf[:, c0:c0 + H2], in_=pt[:, 0:H2], func=AF.Exp, reads=[pb], writes=[EGLb])
    P.op("act", "activation", out=EKD[:], in_=EKD[:], func=AF.Exp, reads=[EKDb], writes=[EKDb])
    QT, QTb = AR.sb("QT", [128, NTOT]); KT, KTb = AR.sb("KT", [128, NTOT]); VT, VTb = AR.sb("VT", [128, NTOT])
    Xr, Xrb = AR.sb("Xr", [128, NTOT]); O, Ob = AR.sb("O", [64, NCH, 128])
    rs, rsb = E.rstd, E.rstdb
    w128 = AR.pool("w128", [64, 2, 64], 12); xxp = AR.pool("xx", [64, 2, 128], 6); zp = AR.pool("zz", [64, 2, 64], 26); qkp = AR.pool("qk", [64, 2, 64], 6)
    egp = AR.pool("egr", [128, 2, 64], 4); qgp = AR.pool("qg", [128, 2, 64], 6); nwp = AR.pool("nw", [128, 2, 64], 6)
    tmp = AR.pool("tm", [64, 2, 128], 16)
    Sp = [AR.pool(f"S{d}", [128, 128], 2) for d in range(2)]
    GRP = 4
    zt, ztb = AR.sb("zt", [64, GRP, 128]); yt, ytb = AR.sb("yt", [64, GRP, 128])
    st17, st17b = AR.sb("st17", [64, GRP]); yT_p = AR.pool("yT", [128, 512], 2)
    segs = [(0, NCTX), (NCTX, NTOT)]
    for h in range(NH):
        for qi, (dst, dstb) in enumerate(((QT, QTb), (KT, KTb), (VT, VTb))):
            ci = qi * NH + h
            P.dma("sp", Xr[:], PF[qi * 512 + h * 128: qi * 512 + (h + 1) * 128, :], writes=[Xrb])
            for (a, b) in segs:
                P.op("act", "activation", out=dst[:, a:b], in_=Xr[:, a:b], func=AF.Copy, scale=cw[:, ci, 2:3], reads=[Xrb, cwb], writes=[dstb])
                for tap in (0, 1, 3, 4):
                    off = tap - 2
                    if off < 0: o0, o1, i0, i1 = a - off, b, a, b + off
                    else: o0, o1, i0, i1 = a, b - off, a + off, b
                    P.op("dve", "scalar_tensor_tensor", dst[:, o0:o1], Xr[:, i0:i1], cw[:, ci, tap:tap + 1], dst[:, o0:o1], ALU.mult, ALU.add, reads=[Xrb, cwb, dstb], writes=[dstb])
            P.op("act", "activation", out=dst[:], in_=dst[:], func=AF.Silu, reads=[dstb], writes=[dstb])
            if qi < 2:
                for t0 in range(0, NTOT, 512):
                    tb = min(512, NTOT - t0)
                    sq, sqb = E.sqp.get()
                    P.op("act", "activation", out=sq[:, 0:tb], in_=dst[:, t0:t0 + tb], func=AF.Square, reads=[dstb], writes=[sqb])
                    pt, pb = psp.get()
                    P.op("pe", "matmul", pt[:, 0:tb], ones[:], sq[:, 0:tb], start=True, stop=True, reads=[onesb, sqb], writes=[pb])
                    P.op("act", "activation", out=rs[:, 0:tb], in_=pt[:, 0:tb], func=AF.Sqrt, bias=eps6[:, 0:1], reads=[pb, eps6b], writes=[rsb])
                    P.op("dve", "reciprocal", rs[:, 0:tb], rs[:, 0:tb], reads=[rsb], writes=[rsb])
                    if qi == 0:
                        P.op("dve", "scalar_tensor_tensor", dst[:, t0:t0 + tb], dst[:, t0:t0 + tb], 128 ** -0.5, rs[:, 0:tb], ALU.mult, ALU.mult, reads=[dstb, rsb], writes=[dstb])
                    else:
                        P.op("dve", "tensor_tensor", dst[:, t0:t0 + tb], dst[:, t0:t0 + tb], rs[:, 0:tb], ALU.mult, reads=[dstb, rsb], writes=[dstb])
        S = []
        for d in range(2):
            st, stb = Sp[d].get()
            P.op("pool", "memset", st[:], 0.0, writes=[stb])
            S.append((st, stb))
        visited = set()
        def chunk_of(s, d):
            if d == 0: return s
            return 3 - s if s < 4 else 71 - s
        def pre(s):
            ns = [chunk_of(s, d) for d in range(2)]; cols = [d * NH + h for d in range(2)]
            toks = [slice(n * 64, (n + 1) * 64) for n in ns]
            Gd, Gdb = w128.get()
            for d in range(2):
                P.op("dve", "tensor_scalar", Gd[:, d, :], TRI2[:, d, :], gt[:, ns[d], cols[d]:cols[d] + 1], None, ALU.mult, reads=[TRI2b, gtb], writes=[Gdb])
            yield
            pa, pab = psp.get()
            P.op("pe", "matmul", pa[:, 0:128], ones[0:64, :], Gd[:].rearrange("p d j -> p (d j)"), start=True, stop=True, reads=[onesb, Gdb], writes=[pab])
            E1, E1b = w128.get(); E2, E2b = w128.get(); EGr, EGrb = egp.get()
            for d in range(2):
                P.op("act", "activation", out=E1[:, d, :], in_=pa[0:64, d * 64:(d + 1) * 64], func=AF.Exp, scale=-1.0, bias=GC[:, ns[d], cols[d]:cols[d] + 1], reads=[pab, GCb], writes=[E1b])
                P.op("act", "activation", out=E2[:, d, :], in_=pa[0:64, d * 64:(d + 1) * 64], func=AF.Exp, bias=NGC[:, ns[d], cols[d]:cols[d] + 1], reads=[pab, NGCb], writes=[E2b])
            P.op("act", "activation", out=EGr[:].rearrange("p d j -> p (d j)"), in_=pa[:, 0:128], func=AF.Exp, reads=[pab], writes=[EGrb])
            yield
            D1, D1b = w128.get(); D2, D2b = w128.get()
            P.op("dve", "scalar_tensor_tensor", D1[:], E1[:], 1.0, MS2[:], ALU.min, ALU.mult, reads=[E1b, MS2b], writes=[D1b])
            P.op("dve", "scalar_tensor_tensor", D2[:], E2[:], 1.0, TRI2[:], ALU.min, ALU.mult, reads=[E2b, TRI2b], writes=[D2b])
            for d in range(2):
                P.op("pool", "tensor_scalar", D1[:, d, :], D1[:, d, :], NBETA[:, ns[d], cols[d]:cols[d] + 1], None, ALU.mult, reads=[D1b, NBETAb], writes=[D1b])
            yield
            pk, pkb = psp.get()
            for d in range(2):
                P.op("pe", "matmul", pk[0:64, d * 64:(d + 1) * 64], KT[:, toks[d]], KT[:, toks[d]], start=True, stop=True, reads=[KTb], writes=[pkb])
                P.op("pe", "matmul", pk[0:64, 128 + d * 64:128 + (d + 1) * 64], KT[:, toks[d]], QT[:, toks[d]], start=True, stop=True, reads=[KTb, QTb], writes=[pkb])
            XX, XXb = xxp.get(); QK, QKb = qkp.get()
            P.op("dve", "tensor_tensor", XX[:, :, 0:64], pk[0:64, 0:128].rearrange("p (d j) -> p d j", d=2), D1[:], ALU.mult, reads=[pkb, D1b], writes=[XXb])
            P.op("dve", "tensor_tensor", QK[:], pk[0:64, 128:256].rearrange("p (d j) -> p d j", d=2), D2[:], ALU.mult, reads=[pkb, D2b], writes=[QKb])
            yield
            pc, pcb = psp.get()
            for d in range(2):
                P.op("pe", "transpose", pc[0:64, d * 64:(d + 1) * 64], XX[:, d, 0:64], ident[0:64, 0:64], reads=[XXb, identb], writes=[pcb])
            pcv = pc[0:64, 0:128].rearrange("p (d j) -> p d j", d=2)
            P.op("act", "activation", out=XX[:, :, 64:128], in_=pcv, func=AF.Copy, reads=[pcb], writes=[XXb])
            Z, Zb = zp.get()
            P.op("dve", "tensor_tensor", Z[:], pcv, I2[:], ALU.add, reads=[pcb, I2b], writes=[Zb])
            for lvl in range(5):
                yield
                pd, pdb = psp.get()
                for d in range(2):
                    P.op("pe", "matmul", pd[0:64, d * 128:d * 128 + 64], XX[:, d, 64:128], XX[:, d, 0:64], start=True, stop=True, reads=[XXb], writes=[pdb])
                    P.op("pe", "matmul", pd[0:64, d * 128 + 64:d * 128 + 128], XX[:, d, 0:64], XX[:, d, 64:128], start=True, stop=True, reads=[XXb], writes=[pdb])
                XXn, XXnb = xxp.get()
                P.op("act", "activation", out=XXn[:].rearrange("p d j -> p (d j)"), in_=pd[0:64, 0:256], func=AF.Copy, reads=[pdb], writes=[XXnb])
                XX, XXb = XXn, XXnb
                yield
                pe_, peb = psp.get()
                for d in range(2):
                    P.op("pe", "matmul", pe_[0:64, d * 64:(d + 1) * 64], XX[:, d, 0:64], Z[:, d, :], start=True, stop=True, reads=[XXb, Zb], writes=[peb])
                Zn, Znb = zp.get()
                P.op("dve", "tensor_tensor", Zn[:], Z[:], pe_[0:64, 0:128].rearrange("p (d j) -> p d j", d=2), ALU.add, reads=[Zb, peb], writes=[Znb])
                Z, Zb = Zn, Znb
            yield
            ptk, ptkb = psp.get()
            for d in range(2):
                P.op("pe", "transpose", ptk[0:64, d * 128:(d + 1) * 128], KT[:, toks[d]], ident[:], reads=[KTb, identb], writes=[ptkb])
                P.op("pe", "transpose", ptk[0:64, 256 + d * 128:256 + (d + 1) * 128], VT[:, toks[d]], ident[:], reads=[VTb, identb], writes=[ptkb])
            VB, VBb = tmp.get(); KBG, KBGb = tmp.get(); KD, KDb = tmp.get()
            for d in range(2):
                n, c = ns[d], cols[d]
                P.op("act", "activation", out=KBG[:, d, :], in_=ptk[0:64, d * 128:(d + 1) * 128], func=AF.Copy, scale=BEG[:, n, c:c + 1], reads=[ptkb, BEGb], writes=[KBGb])
                P.op("dve", "tensor_scalar", KD[:, d, :], ptk[0:64, d * 128:(d + 1) * 128], EKD[:, n, c:c + 1], None, ALU.mult, reads=[ptkb, EKDb], writes=[KDb])
                P.op("dve", "tensor_scalar", VB[:, d, :], ptk[0:64, 256 + d * 128:256 + (d + 1) * 128], BETA[:, n, c:c + 1], None, ALU.mult, reads=[ptkb, BETAb], writes=[VBb])
            yield
            pw, pwb = psp.get()
            for d in range(2):
                P.op("pe", "matmul", pw[:, d * 64:(d + 1) * 64], KBG[:, d, :], Z[:, d, :], start=True, stop=True, reads=[KBGb, Zb], writes=[pwb])
            NW, NWb = nwp.get()
            P.op("act", "activation", out=NW[:].rearrange("p d j -> p (d j)"), in_=pw[:, 0:128], func=AF.Copy, scale=-1.0, reads=[pwb], writes=[NWb])
            QG, QGb = qgp.get()
            for d in range(2):
                P.op("pool", "tensor_tensor", QG[:, d, :], QT[:, toks[d]], EGr[:, d, :], ALU.mult, reads=[QTb, EGrb], writes=[QGb])
            return dict(ns=ns, cols=cols, Z=(Z, Zb), VB=(VB, VBb), KD=(KD, KDb), NW=(NW, NWb), QG=(QG, QGb), QK=(QK, QKb))
        def seq(R):
            ns, cols = R["ns"], R["cols"]
            Z, Zb = R["Z"]; VB, VBb = R["VB"]; KD, KDb = R["KD"]; NW, NWb = R["NW"]; QG, QGb = R["QG"]; QK, QKb = R["QK"]
            pv, pvb = psp.get()
            for d in range(2):
                P.op("pe", "matmul", pv[0:64, d * 128:(d + 1) * 128], Z[:, d, :], VB[:, d, :], start=True, stop=False, reads=[Zb, VBb], writes=[pvb])
                P.op("pe", "matmul", pv[0:64, d * 128:(d + 1) * 128], NW[:, d, :], S[d][0][:], start=False, stop=True, reads=[NWb, S[d][1]], writes=[pvb])
            VN, VNb = tmp.get()
            P.op("act", "activation", out=VN[:].rearrange("p d e -> p (d e)"), in_=pv[0:64, 0:256], func=AF.Copy, reads=[pvb], writes=[VNb])
            po, pob = psp.get()
            for d in range(2):
                P.op("pe", "matmul", po[0:64, d * 128:(d + 1) * 128], QG[:, d, :], S[d][0][:], start=True, stop=False, reads=[QGb, S[d][1]], writes=[pob])
                P.op("pe", "matmul", po[0:64, d * 128:(d + 1) * 128], QK[:, d, :], VN[:, d, :], start=False, stop=True, reads=[QKb, VNb], writes=[pob])
            for d in range(2):
                n = ns[d]
                if n in visited:
                    P.op("dve", "tensor_tensor", O[:, n, :], O[:, n, :], po[0:64, d * 128:(d + 1) * 128], ALU.add, reads=[Ob, pob], writes=[Ob])
                else:
                    visited.add(n)
                    P.op("dve", "tensor_copy", O[:, n, :], po[0:64, d * 128:(d + 1) * 128], reads=[pob], writes=[Ob])
            pS, pSb = psp.get()
            for d in range(2):
                P.op("pe", "matmul", pS[:, d * 128:(d + 1) * 128], KD[:, d, :], VN[:, d, :], start=True, stop=True, reads=[KDb, VNb], writes=[pSb])
            for d in range(2):
                sn, snb = Sp[d].get()
                P.op("dve", "scalar_tensor_tensor", sn[:], S[d][0][:], EGL[:, ns[d], cols[d]:cols[d] + 1], pS[:, d * 128:(d + 1) * 128], ALU.mult, ALU.add, reads=[S[d][1], EGLb, pSb], writes=[snb])
                S[d] = (sn, snb)
        def drive(gens, res, nstages=None):
            k = 0
            while any(g is not None for g in gens):
                for i, g in enumerate(gens):
                    if g is None: continue
                    try: next(g)
                    except StopIteration as e:
                        res[i] = e.value; gens[i] = None
                k += 1
                if nstages is not None and k >= nstages: break
            return all(g is None for g in gens)
        cur = [None, None]
        drive([pre(0), pre(1)], cur)
        for p in range(0, NCH, 2):
            nxt = [None, None]; gens = [pre(p + 2), pre(p + 3)] if p + 2 < NCH else [None, None]
            drive(gens, nxt, 3)
            seq(cur[0])
            drive(gens, nxt, 6)
            seq(cur[1])
            drive(gens, nxt)
            cur = nxt
        zr = PT[:, h * 128:(h + 1) * 128].rearrange("(n c) e -> c n e", c=64)
        for n0 in range(0, NCH, GRP):
            P.dma("sp", zt[:], zr[:, n0:n0 + GRP, :], writes=[ztb])
            P.op("dve", "tensor_tensor", yt[:], O[:, n0:n0 + GRP, :], O[:, n0:n0 + GRP, :], ALU.mult, reads=[Ob], writes=[ytb])
            P.op("dve", "tensor_reduce", st17[:], yt[:], AX.X, ALU.add, reads=[ytb], writes=[st17b])
            P.op("act", "activation", out=st17[:], in_=st17[:], func=AF.Sqrt, scale=1.0 / 128.0, bias=eps6[0:64, 0:1], reads=[st17b, eps6b], writes=[st17b])
            P.op("dve", "reciprocal", st17[:], st17[:], reads=[st17b], writes=[st17b])
            for i in range(GRP):
                P.op("dve", "scalar_tensor_tensor", yt[:, i, :], O[:, n0 + i, :], st17[:, i:i + 1], gn[:], ALU.mult, ALU.mult, reads=[Ob, st17b, gnb], writes=[ytb])
            P.op("act", "activation", out=zt[:], in_=zt[:], func=AF.Silu, reads=[ztb], writes=[ztb])
            P.op("dve", "tensor_tensor", yt[:], yt[:], zt[:], ALU.mult, reads=[ytb, ztb], writes=[ytb])
            pq, pqb = psp.get()
            for i in range(GRP):
                P.op("pe", "transpose", pq[:, i * 64:(i + 1) * 64], yt[:, i, :], ident[0:64, 0:64], reads=[ytb, identb], writes=[pqb])
            yT, yTb = yT_p.get()
            evac(P, E, yT[:, 0:GRP * 64], pq[:, 0:GRP * 64], [pqb], [yTb])
            yl_write(P, YL, h * 128, n0 * 64, GRP * 64, yT, [yTb])

class LazyDr(dict):
    def __init__(self, nc):
        super().__init__(); self.nc = nc; self.specs = {}; self.used_ext = []
    def __missing__(self, name):
        kind, shape = self.specs[name]
        ap = self.nc.dram_tensor(name, list(shape), F32, kind=kind).ap()
        if kind == "ExternalInput": self.used_ext.append(name)
        self[name] = ap
        return ap

def build_fused(nc, stop=None):
    dr = LazyDr(nc)
    def ext(name, shape): dr.specs[name] = ("ExternalInput", shape)
    def internal(name, shape): dr.specs[name] = ("Internal", shape)
    ext("xT_in", [KC, 256, NTH]); ext("xT_own", [D, NTH]); ext("cT", [128, KC, 2]); ext("msk", [128, 2])
    ext("cw2", [256, 512]); ext("cl", [NLAT, NLAT]); ext("sln", [NLAT, NLAT])
    ext("cosT", [64, NLAT]); ext("sinT", [64, NLAT]); ext("Rm", [64, 64]); ext("ident", [128, 128])
    ext("tri2", [64, 2, 64]); ext("ms2", [64, 2, 64]); ext("i2", [64, 2, 64])
    for l in range(2):
        ext(f"w_ada{l}", [D, 12288]); ext(f"b_ada{l}", [128, 96]); ext(f"g{l}", [128, 4, KC]); ext(f"w_inr{l}", [D, WINR])
        ext(f"w_gate{l}", [D, 6144]); ext(f"w_branch{l}", [3, 1024, D]); ext(f"w_out{l}", [D, D]); ext(f"w_ffn_in{l}", [D, 2 * FF]); ext(f"w_ffn_out{l}", [FF, D])
        ext(f"gq{l}", [128, 4]); ext(f"gkv{l}", [128, 2]); ext(f"wqn{l}", [512, 512]); ext(f"wqr{l}", [512, 256]); ext(f"wk{l}", [256, 512]); ext(f"wv{l}", [256, 512])
        ext(f"convw{l}", [128, 12, 5]); ext(f"alog{l}", [64, 8]); ext(f"dtb{l}", [64, 8]); ext(f"gnorm{l}", [64, 128])
    dr.specs["xo"] = ("ExternalOutput", [D, NLH])
    internal("PF", [FM_ROWS, NTOT]); internal("PT", [NTOT, TM_W]); internal("YL", [17, 1536, 256]); internal("YG", [17, 2 * 1536, 256])
    internal("XL2", [D, NTH]); internal("XG", [KC, 256, NTH])
    dr["xo"]
    with ExitStack() as es:
        P = Prog(nc, es)
        AR = Arena(P)
        def steps():
            for l in range(2):
                last = (l == 1)
                yield f"A{l}", lambda: emit_A(P, AR, dr, l)
                yield f"DN{l}", lambda: emit_DN(P, AR, dr, l, last)
                yield f"FN{l}", lambda: emit_FN(P, AR, dr, last)
                yield f"MLA{l}", lambda: emit_MLA(P, AR, dr, l, last)
                def g1():
                    AR.end()
                    for blk in range(17): P.coll("AllGather", dr["YG"][blk], dr["YL"][blk], PAIRS)
                yield f"G1{l}", g1
                yield f"C{l}", lambda: emit_C(P, AR, dr, l, last)
                if not last:
                    def g2():
                        AR.end()
                        for c in range(KC): P.coll("AllGather", dr["XG"][c], dr["XL2"][c * 128:(c + 1) * 128, :], PAIRS)
                    yield f"G2{l}", g2
        for name, fn in steps():
            if stop is not None and name not in stop: continue
            fn()
        AR.end()
        P.finish()
        print("fused ops", P.n_ops, dict(P.ep))
    nc._used_ext = list(dr.used_ext)
    return nc

from concourse.bass_utils import run_bass_kernel_spmd

def dft_tables():
    n = np.arange(256, dtype=np.float64)
    ang = 2 * np.pi * np.outer(n, n) / 256.0
    cw2 = np.concatenate([np.cos(ang), np.sin(ang)], 1).astype(np.float32)
    n = np.arange(NLAT, dtype=np.int64)
    ang = 2 * np.pi * (np.outer(n, n) % NLAT).astype(np.float64) / NLAT
    return cw2, np.cos(ang).astype(np.float32), (-np.sin(ang)).astype(np.float32)

def rope_tables():
    rows = NLAT // 64
    row = np.repeat(np.arange(rows, dtype=np.float32), 64)
    col = np.tile(np.arange(64, dtype=np.float32), rows)
    inv = (10000.0 ** (-np.arange(0, 32, 2, dtype=np.float32) / 32)).astype(np.float32)
    ar = row[:, None] * inv; ac = col[:, None] * inv
    ang = np.concatenate([ar, ar, ac, ac], -1)
    cosT = np.ascontiguousarray(np.cos(ang).T.astype(np.float32)); sinT = np.ascontiguousarray(np.sin(ang).T.astype(np.float32))
    R = np.zeros((64, 64), np.float32)
    for i in range(16):
        R[16 + i, i] = -1; R[i, 16 + i] = 1; R[48 + i, 32 + i] = -1; R[32 + i, 48 + i] = 1
    return cosT, sinT, R

def dn_consts():
    p = np.arange(64)[:, None]; f = np.arange(64)[None, :]
    ple = (p <= f).astype(np.float32); pge = (p >= f).astype(np.float32)
    pgt = (p > f).astype(np.float32); plt = (p < f).astype(np.float32)
    return {"tri2": np.ascontiguousarray(np.stack([ple, pge], 1)), "ms2": np.ascontiguousarray(np.stack([pgt, plt], 1)),
            "i2": np.ascontiguousarray(np.stack([np.eye(64, dtype=np.float32)] * 2, 1)), "ident": np.eye(128, dtype=np.float32)}

_NC = []
STOP = None
def kernel(**inputs):
    inp = {k: np.asarray(v) for k, v in inputs.items()}
    B = inp["x"].shape[0]
    if not _NC:
        nc = bass.Bass("TRN2", target_bir_lowering=False, num_devices=8)
        build_fused(nc, STOP); _NC.append(nc)
    nc = _NC[0]
    cw2, cl, sln = dft_tables(); cosT, sinT, R = rope_tables(); dnc = dn_consts()
    shared = {"cw2": cw2, "cl": cl, "sln": sln, "cosT": cosT, "sinT": sinT, "Rm": R}
    shared.update(dnc)
    for l in range(2):
        shared[f"w_ada{l}"] = np.ascontiguousarray(inp["w_ada"][l])
        shared[f"b_ada{l}"] = np.ascontiguousarray(inp["b_ada"][l].reshape(96, 128).T)
        shared[f"g{l}"] = np.ascontiguousarray(inp["norm_g"][l].reshape(4, KC, 128).transpose(2, 0, 1))
        shared[f"w_gate{l}"] = np.ascontiguousarray(inp["w_in"][l][:, 5984:])
        shared[f"w_branch{l}"] = np.ascontiguousarray(inp["w_branch"][l]); shared[f"w_out{l}"] = np.ascontiguousarray(inp["w_out"][l])
        shared[f"w_ffn_in{l}"] = np.ascontiguousarray(inp["w_ffn_in"][l]); shared[f"w_ffn_out{l}"] = np.ascontiguousarray(inp["w_ffn_out"][l])
        shared[f"gq{l}"] = np.ascontiguousarray(inp["mla_q_norm_g"][l].reshape(4, 128).T)
        shared[f"gkv{l}"] = np.ascontiguousarray(inp["mla_kv_norm_g"][l].reshape(2, 128).T)
        shared[f"gnorm{l}"] = np.ascontiguousarray(np.tile(inp["dn_norm_g"][l][None], (64, 1)).astype(np.float32))
    percore = {}
    for r in range(2):
        heads = np.arange(r * 4, (r + 1) * 4)
        colsel = np.concatenate([heads, 8 + heads])
        pc = {}
        for l in range(2):
            w = inp["w_in"][l]
            cols = np.concatenate([np.arange(r * 512, (r + 1) * 512), 1024 + np.arange(r * 512, (r + 1) * 512), 2048 + np.arange(r * 512, (r + 1) * 512),
                                   np.arange(4128, 4960), 4960 + np.arange(r * 512, (r + 1) * 512), 3072 + np.arange(r * 512, (r + 1) * 512), 4096 + colsel, 4112 + colsel])
            assert cols.size == WINR
            pc[f"w_inr{l}"] = np.ascontiguousarray(w[:, cols])
            wuq = inp["w_uq"][l]; wukv = inp["w_ukv"][l]
            pc[f"wqn{l}"] = np.ascontiguousarray(np.concatenate([wuq[:, h * 192: h * 192 + 128] for h in heads], 1))
            pc[f"wqr{l}"] = np.ascontiguousarray(np.concatenate([wuq[:, h * 192 + 128: h * 192 + 192] for h in heads], 1))
            pc[f"wk{l}"] = np.ascontiguousarray(np.concatenate([wukv[:, h * 256: h * 256 + 128] for h in heads], 1))
            pc[f"wv{l}"] = np.ascontiguousarray(np.concatenate([wukv[:, h * 256 + 128: h * 256 + 256] for h in heads], 1))
            conv = inp["dn_conv"][l]
            cwl = [conv[:, qi * 1024 + h * 128: qi * 1024 + (h + 1) * 128].T for qi in range(3) for h in heads]
            pc[f"convw{l}"] = np.ascontiguousarray(np.stack(cwl, 1).astype(np.float32))
            pc[f"alog{l}"] = np.ascontiguousarray(np.tile(inp["dn_a_log"][l].reshape(16)[colsel][None], (64, 1)).astype(np.float32))
            pc[f"dtb{l}"] = np.ascontiguousarray(np.tile(inp["dn_dt_bias"][l].reshape(16)[colsel][None], (64, 1)).astype(np.float32))
        m = np.zeros((128, 2), np.float32); m[:, r] = 1.0
        pc["msk"] = m
        percore[r] = pc
    in_maps = []
    for i in range(8):
        b, r = i // 2, i % 2
        halves = [np.concatenate([inp["x"][b, q * NLH:(q + 1) * NLH], inp["ctx"][b, q * NCH2:(q + 1) * NCH2]], 0).T for q in range(2)]
        d = dict(shared); d.update(percore[r])
        d["xT_in"] = np.ascontiguousarray(np.stack([h_.reshape(KC, 128, NTH) for h_ in halves], 1).reshape(KC, 256, NTH))
        d["xT_own"] = np.ascontiguousarray(halves[r])
        cvec = np.stack([inp["c"][b], inp["c_ctx"]], -1)
        d["cT"] = np.ascontiguousarray(cvec.reshape(KC, 128, 2).transpose(1, 0, 2))
        in_maps.append(d)
    in_maps = [{k: d[k] for k in nc._used_ext} for d in in_maps]
    res = run_bass_kernel_spmd(nc, in_maps, core_ids=list(range(8))).results
    out = np.empty((B, NLAT, D), np.float32)
    for i in range(8):
        b, r = i // 2, i % 2
        out[b, r * NLH:(r + 1) * NLH] = res[i]["xo"].T
    return out
```

```python
import numpy as np
from contextlib import ExitStack
import concourse.bass as bass
import concourse.mybir as mybir
F32 = mybir.dt.float32; BF16 = mybir.dt.bfloat16; I32 = mybir.dt.int32
AF = mybir.ActivationFunctionType
ALU = mybir.AluOpType
AX = mybir.AxisListType

class Buf:
    __slots__ = ("name", "w", "r", "excl")
    def __init__(self, name="", excl=False):
        self.name = name
        self.excl = excl
        self.w = None
        self.r = {}

EPOCH = 12000
class Prog:
    ENG = ("pe", "dve", "act", "pool", "sp")
    def __init__(self, nc, es, n_dma_sems=12):
        self.nc = nc; self.es = es; self.es_global = es
        self.engobj = {"pe": nc.tensor, "dve": nc.vector, "act": nc.scalar, "pool": nc.gpsimd, "sp": nc.sync}
        self.streams = {e: [] for e in self.ENG}
        self.sems = {}
        self.cnt = {}
        self.cur = {}
        self.ep = {e: 0 for e in self.ENG}
        for e in self.ENG:
            self._new_epoch(e)
        self.seen = {e: {} for e in self.ENG}
        self.dma_keys = []
        for i in range(n_dma_sems):
            k = ("dma", i)
            self.sems[k] = es.enter_context(nc.semaphore(f"dma{i}"))
            self.cnt[k] = 0
            self.dma_keys.append(k)
        self.dma_rr = 0
        self.n_ops = 0
    def _new_epoch(self, e):
        k = (e, self.ep[e]); self.ep[e] += 1
        self.sems[k] = self.es_global.enter_context(self.nc.semaphore(f"s_{e}_{k[1]}"))
        self.cnt[k] = 0; self.cur[e] = k
    def _deps(self, reads, writes):
        deps = {}
        def need(k, c):
            if deps.get(k, 0) < c: deps[k] = c
        for b in reads:
            if b.w is not None: need(*b.w)
        for b in writes:
            if b.w is not None: need(*b.w)
            for k, c in b.r.items(): need(k, c)
        return deps
    def _emit_waits(self, e, deps, skip_key=None):
        seen = self.seen[e]
        for k, c in deps.items():
            if k == skip_key: continue
            if seen.get(k, 0) >= c: continue
            seen[k] = c
            sem = self.sems[k]
            self.streams[e].append(lambda eng, sem=sem, c=c: eng.wait_ge(sem, c))
    def _mark(self, key, c, reads, writes):
        for b in reads:
            if b.r.get(key, 0) < c: b.r[key] = c
        for b in writes:
            b.w = (key, c); b.r = {}
    def op(self, e, meth, *args, reads=(), writes=(), same_engine_sync=True, **kw):
        writes = list(writes) + [b for b in reads if b.excl]
        reads = [b for b in reads if not b.excl]
        deps = self._deps(reads, writes)
        key = self.cur[e]
        if self.cnt[key] >= EPOCH:
            self._new_epoch(e); key = self.cur[e]
        skip = None
        if e == "pe" or not same_engine_sync:
            deps = {k: c for k, c in deps.items() if k[0] != e}
        self._emit_waits(e, deps, skip)
        self.cnt[key] += 1
        c = self.cnt[key]; sem = self.sems[key]
        self.streams[e].append(lambda eng, meth=meth, args=args, kw=kw, sem=sem: getattr(eng, meth)(*args, **kw).then_inc(sem, 1))
        self._mark(key, c, reads, writes)
        self.n_ops += 1
    def dma(self, q, out, in_, reads=(), writes=(), **kw):
        deps = self._deps(reads, writes)
        k = self.dma_keys[self.dma_rr]; self.dma_rr = (self.dma_rr + 1) % len(self.dma_keys)
        if self.cnt[k] > 0: deps[k] = max(deps.get(k, 0), self.cnt[k])
        self._emit_waits(q, deps)
        self.cnt[k] += 16
        c = self.cnt[k]; sem = self.sems[k]
        self.streams[q].append(lambda eng, out=out, in_=in_, sem=sem, kw=kw: eng.dma_start(out=out, in_=in_, **kw).then_inc(sem, 16))
        self._mark(k, c, reads, writes)
        self.n_ops += 1
    def coll(self, kind, out, in_, groups, reads=(), writes=()):
        deps = self._deps(reads, writes)
        k = ("cc", 0)
        if k not in self.sems:
            self.sems[k] = self.es_global.enter_context(self.nc.semaphore("cc0"))
            self.cnt[k] = 0
            self.dma_keys.append(k)
        self.cnt[k] += 1
        self._emit_waits("pool", deps)
        sem = self.sems[k]
        self.streams["pool"].append(lambda eng, out=out, in_=in_, sem=sem: eng.collective_compute(kind, mybir.AluOpType.bypass, replica_groups=groups, ins=[in_.opt()], outs=[out.opt()]).then_inc(sem, 1))
        self._mark(k, self.cnt[k], reads, writes)
        self.n_ops += 1
    def barrier(self):
        deps = {k: c for k, c in self.cnt.items() if c > 0}
        for e in self.ENG:
            self._emit_waits(e, {k: c for k, c in deps.items() if k != self.cur[e]})
    def flush(self):
        nc = self.nc
        streams = self.streams
        self.streams = {e: [] for e in self.ENG}
        with nc.Block() as block:
            @block.sync
            def _(eng):
                for f in streams["sp"]: f(eng)
            @block.tensor
            def _(eng):
                for f in streams["pe"]: f(eng)
            @block.vector
            def _(eng):
                for f in streams["dve"]: f(eng)
            @block.scalar
            def _(eng):
                for f in streams["act"]: f(eng)
            @block.gpsimd
            def _(eng):
                for f in streams["pool"]: f(eng)
    def finish(self):
        deps = {k: self.cnt[k] for k in self.dma_keys if self.cnt[k] > 0}
        self._emit_waits("sp", deps)
        self.flush()

class Pool:
    def __init__(self, P, name, shape, dtype, n, psum=False):
        self.tiles = []
        for i in range(n):
            if psum:
                t = P.es.enter_context(P.nc.psum_tensor(f"pp_{name}{i}", shape, dtype))
            else:
                t = P.es.enter_context(P.nc.sbuf_tensor(f"sp_{name}{i}", shape, dtype))
            self.tiles.append((t, Buf(f"{name}{i}", excl=psum)))
        self.i = 0
    def get(self):
        t = self.tiles[self.i]; self.i = (self.i + 1) % len(self.tiles)
        return t

def sb(P, name, shape, dtype=F32):
    return P.es.enter_context(P.nc.sbuf_tensor("sb_" + name, shape, dtype)), Buf(name)
def ps(P, name, shape, dtype=F32):
    return P.es.enter_context(P.nc.psum_tensor("ps_" + name, shape, dtype)), Buf(name, excl=True)

import math
D = 2048; KC = 16; FF = 5632; FKC = 44
NCTX = 256; NLAT = 4096; NTOT = NCTX + NLAT; NCH = NTOT // 64
NLH = 2048; NCH2 = 128; NTH = NLH + NCH2
MLA_SCALE = 192 ** -0.5
NAR = 52000
PAIRS = [[0, 1], [2, 3], [4, 5], [6, 7]]
F32R = mybir.dt.float32r
def RR_(ap): return ap.bitcast(F32R)
FM_CHUNKS = [(c0, 128) for c0 in range(0, 2304, 128)] + [(2304, 64)] + [(2368 + i * 128, 128) for i in range(4)]
FM_ROWS = 2880; TM_COL0 = 2880; TM_W = 528; WINR = 3408

class Arena:
    def __init__(self, P):
        self.P = P
        self.banks = [(P.es_global.enter_context(P.nc.psum_tensor(f"bank{i}", [128, 512], F32)), Buf(f"bank{i}", excl=True)) for i in range(8)]
        self.ph = None; self.nph = 0
    def begin(self, nN, nR):
        assert nN + nR <= NAR, (nN, nR)
        self.end()
        self.ph = ExitStack(); self.nph += 1
        self.tN = self.ph.enter_context(self.P.nc.sbuf_tensor(f"arN{self.nph}", [128, max(nN, 8)], F32))
        self.tR = self.ph.enter_context(self.P.nc.sbuf_tensor(f"arR{self.nph}", [128, max(nR, 8)], F32))
        self.cap = {False: nN, True: nR}; self.off = {False: 0, True: 0}; self.bi = 0
    def end(self):
        self.P.barrier()
        if self.ph is not None:
            self.P.flush(); self.ph.close(); self.ph = None
    def sb(self, name, shape, R=False):
        n = int(np.prod(shape[1:]))
        assert self.off[R] + n <= self.cap[R], (name, R, self.off[R], n, self.cap[R])
        t = self.tR if R else self.tN
        v = t[0:shape[0], self.off[R]:self.off[R] + n]
        self.off[R] += n
        if len(shape) == 3: v = v.rearrange("p (a b) -> p a b", a=shape[1])
        elif len(shape) == 4: v = v.rearrange("p (a b c) -> p a b c", a=shape[1], b=shape[2])
        return v, Buf(name)
    def pool(self, name, shape, n, R=False):
        return RR([self.sb(f"{name}{i}", shape, R) for i in range(n)])
    def pspool(self, n):
        b = self.banks[self.bi:self.bi + n]; assert len(b) == n; self.bi += n
        return RR(b)

class RR:
    def __init__(self, tiles): self.tiles = tiles; self.i = 0
    def get(self):
        t = self.tiles[self.i]; self.i = (self.i + 1) % len(self.tiles); return t

def common(P, AR):
    class E: pass
    E = E()
    E.ones, E.onesb = AR.sb("ones", [128, 128]); P.op("pool", "memset", E.ones[:], 1.0, writes=[E.onesb])
    E.eps = {}
    for dim, val in ((2048, 2048e-6), (512, 512e-6), (256, 256e-6), (1, 1e-6)):
        t, b = AR.sb(f"eps{dim}", [128, 1]); P.op("pool", "memset", t[:], val, writes=[b]); E.eps[dim] = (t, b)
    E.sqp = AR.pool("sq", [128, 512], 2)
    E.ssp, E.sspb = AR.pspool(1).get()
    E.rstd, E.rstdb = AR.sb("rstd", [128, 512])
    E.ev = 0
    return E

def rms_rstd(P, E, src, srcb, nch, tb, dim):
    for c in range(nch):
        sq, sqb = E.sqp.get()
        P.op("act", "activation", out=sq[:, 0:tb], in_=src[:, c, 0:tb], func=AF.Square, reads=[srcb], writes=[sqb])
        P.op("pe", "matmul", E.ssp[:, 0:tb], E.ones[:], sq[:, 0:tb], start=(c == 0), stop=(c == nch - 1), reads=[sqb, E.onesb], writes=[E.sspb])
    eb = E.eps[dim]
    P.op("act", "activation", out=E.rstd[:, 0:tb], in_=E.ssp[:, 0:tb], func=AF.Sqrt, bias=eb[0][:, 0:1], reads=[E.sspb, eb[1]], writes=[E.rstdb])
    P.op("dve", "reciprocal", E.rstd[:, 0:tb], E.rstd[:, 0:tb], reads=[E.rstdb], writes=[E.rstdb])

def evac(P, E, dst, src, reads, writes):
    if E.ev % 2 == 0: P.op("dve", "tensor_copy", dst, src, reads=reads, writes=writes)
    else: P.op("act", "activation", out=dst, in_=src, func=AF.Copy, reads=reads, writes=writes)
    E.ev += 1

def ld(P, AR, name, src, shape, q="sp"):
    t, b = AR.sb(name, shape); P.dma(q, t[:], src, writes=[b]); return t, b

def emit_mod(P, AR, E, wget, pmod, cT_d, wada_d, bada_d, nchunks, cpt=2):
    ct, cb = ld(P, AR, "cT", cT_d, [128, KC, 2]); bt, bb = ld(P, AR, "bada", bada_d, [128, 96])
    sc, scb = AR.sb("sc", [128, KC, 2]); mod_t, mod_b = AR.sb("mod", [128, 96, 2])
    P.op("act", "activation", out=sc[:], in_=ct[:], func=AF.Silu, reads=[cb], writes=[scb])
    for n0 in range(0, nchunks, cpt):
        wt, wb = wget()
        P.dma("sp", wt[:, :, 0:cpt * 128], wada_d[:, n0 * 128:(n0 + cpt) * 128].rearrange("(c p) n -> p c n", p=128), writes=[wb])
        for gi in range(cpt):
            n = n0 + gi
            pt, pb = pmod.get()
            for k in range(KC):
                P.op("pe", "matmul", pt[:, 0:2], wt[:, k, gi * 128:(gi + 1) * 128], sc[:, k, :], start=(k == 0), stop=(k == KC - 1), reads=[wb, scb], writes=[pb])
            P.op("dve", "tensor_scalar", mod_t[:, n, :], pt[:, 0:2], bt[:, n:n + 1], None, ALU.add, reads=[pb, bb], writes=[mod_b])
    return mod_t, mod_b

def yl_write(P, YL, row0, col0, width, src, reads):
    c = col0
    while c < col0 + width:
        blk, off = c // 256, c % 256
        w = min(256 - off, col0 + width - c)
        P.dma("act", YL[blk, row0:row0 + 128, off:off + w], src[:, c - col0:c - col0 + w], reads=reads)
        c += w

def emit_A(P, AR, dr, l):
    AR.begin(21500, 24576); E = common(P, AR)
    xsrc = dr["xT_in"] if l == 0 else dr["XG"]
    wpool = AR.pool("w", [128, KC, 512], 2, R=True)
    pmm = AR.pspool(5)
    wmod = AR.pool("wm", [128, KC, 256], 2)
    mod_t, mod_b = emit_mod(P, AR, E, lambda: wmod.get(), AR.pspool(2), dr["cT"], dr[f"w_ada{l}"], dr[f"b_ada{l}"], 32)
    g0, g0b = ld(P, AR, "g0", dr[f"g{l}"][:, 0, :], [128, KC])
    At, Ab = AR.sb("A", [128, KC, 2])
    for j in range(2):
        P.op("dve", "tensor_scalar", At[:, :, j], mod_t[:, KC:2 * KC, j], 1.0, math.sqrt(D), ALU.add, ALU.mult, reads=[mod_b], writes=[Ab])
        P.op("dve", "tensor_tensor", At[:, :, j], At[:, :, j], g0[:], ALU.mult, reads=[Ab, g0b], writes=[Ab])
    xs, xsb = AR.sb("xs", [128, KC, 512]); hs, hsb = AR.sb("hs", [128, KC, 512], R=True)
    opool = AR.pool("o", [128, 512], 3)
    win = dr[f"w_inr{l}"]; PF = dr["PF"]; PT = dr["PT"]
    blocks = []
    for r in range(2):
        for t0 in range(0, NLH, 512): blocks.append((r, t0, 512, 0, NCTX + r * NLH + t0))
        blocks.append((r, NLH, NCH2, 1, r * NCH2))
    for (r, t0, tb, j, dst0) in blocks:
        P.dma("sp", xs[:, :, 0:tb], xsrc.rearrange("c (r p) t -> r p c t", r=2)[r][:, :, t0:t0 + tb], writes=[xsb])
        rms_rstd(P, E, xs, xsb, KC, tb, 2048)
        for c in range(KC):
            P.op("dve", "scalar_tensor_tensor", RR_(hs[:, c, 0:tb]), xs[:, c, 0:tb], At[:, c, j:j + 1], E.rstd[:, 0:tb], ALU.mult, ALU.mult, reads=[xsb, Ab, E.rstdb], writes=[hsb])
            P.op("act", "activation", out=RR_(hs[:, c, 0:tb]), in_=hs[:, c, 0:tb], func=AF.Identity, bias=mod_t[:, c, j:j + 1], reads=[hsb, mod_b], writes=[hsb])
        row = 0; wt = None; wcol0 = None
        for (c0, wd) in FM_CHUNKS:
            if wt is None or not (wcol0 <= c0 and c0 + wd <= wcol0 + 512):
                wt, wb = wpool.get(); wcol0 = c0
                ncols = min(512, FM_ROWS - c0)
                P.dma("pool", RR_(wt[:, :, 0:ncols]), win[:, c0:c0 + ncols].rearrange("(c p) n -> p c n", p=128), writes=[wb])
            pt, pb = pmm.get(); off = c0 - wcol0
            for k in range(KC):
                P.op("pe", "matmul", pt[0:wd, 0:tb], RR_(wt[:, k, off:off + wd]), RR_(hs[:, k, 0:tb]), start=(k == 0), stop=(k == KC - 1), reads=[wb, hsb], writes=[pb])
            ot, ob = opool.get()
            evac(P, E, ot[0:wd, 0:tb], pt[0:wd, 0:tb], [pb], [ob])
            P.dma("act", PF[row:row + wd, dst0:dst0 + tb], ot[0:wd, 0:tb], reads=[ob])
            row += wd
        for n0 in range(0, TM_W, 512):
            nw = min(512, TM_W - n0)
            wt, wb = wpool.get()
            P.dma("pool", RR_(wt[:, :, 0:nw]), win[:, TM_COL0 + n0:TM_COL0 + n0 + nw].rearrange("(c p) n -> p c n", p=128), writes=[wb])
            for ts in range(tb // 128):
                pt, pb = pmm.get()
                for k in range(KC):
                    P.op("pe", "matmul", pt[:, 0:nw], RR_(hs[:, k, ts * 128:(ts + 1) * 128]), RR_(wt[:, k, 0:nw]), start=(k == 0), stop=(k == KC - 1), reads=[wb, hsb], writes=[pb])
                ot, ob = opool.get()
                evac(P, E, ot[:, 0:nw], pt[:, 0:nw], [pb], [ob])
                P.dma("act", PT[dst0 + ts * 128:dst0 + (ts + 1) * 128, n0:n0 + nw], ot[:, 0:nw], reads=[ob])

def emit_C(P, AR, dr, l, last):
    AR.begin(15000, 36864); E = common(P, AR)
    TB = 256
    xown = dr["xT_own"] if l == 0 else dr["XL2"]
    YG = dr["YG"]
    wpool = AR.pool("w", [128, 5632], 2, R=True)
    wmodc = AR.pool("wm", [128, KC, 128], 1)
    def wget():
        t, b = wpool.get(); return t[:, 0:KC * 256].rearrange("p (c n) -> p c n", c=KC), b
    pmm = AR.pspool(5)
    mod_t, mod_b = emit_mod(P, AR, E, lambda: wmodc.get(), AR.pspool(2), dr["cT"], dr[f"w_ada{l}"], dr[f"b_ada{l}"], 96, cpt=1)
    g, gb = ld(P, AR, "g", dr[f"g{l}"], [128, 4, KC])
    msk, mskb = ld(P, AR, "msk", dr["msk"], [128, 2])
    S, Sb = AR.sb("S", [128, 4, KC, 2])
    for j in range(2):
        for i, (mi, plus1) in enumerate([(1, True), (2, False), (4, True), (5, False)]):
            P.op("dve", "tensor_scalar", S[:, i, :, j], mod_t[:, mi * KC:(mi + 1) * KC, j], 1.0 if plus1 else 0.0, math.sqrt(D), ALU.add, ALU.mult, reads=[mod_b], writes=[Sb])
            P.op("dve", "tensor_tensor", S[:, i, :, j], S[:, i, :, j], g[:, i, :], ALU.mult, reads=[Sb, gb], writes=[Sb])
    xs, xsb = AR.sb("xs", [128, KC, TB]); hs, hsb = AR.sb("hs", [128, KC, TB], R=True)
    ys, ysb = AR.sb("ys", [128, 24, TB], R=True); mg, mgb = AR.sb("mg", [128, KC, TB], R=True); yl, ylb = AR.sb("yl", [128, KC, TB])
    act, actb = AR.sb("act", [128, FKC, TB], R=True)
    tmpp = AR.pool("tmp", [128, TB], 3); selp = AR.pool("sel", [128, TB], 4)
    wg = dr[f"w_gate{l}"]; wbr = dr[f"w_branch{l}"]; wout = dr[f"w_out{l}"]; wfi = dr[f"w_ffn_in{l}"]; wfo = dr[f"w_ffn_out{l}"]
    blocks = [(t0, min(TB, NLH - t0), 0) for t0 in range(0, NLH, TB)]
    if not last: blocks += [(NLH, NCH2, 1)]
    xTr = xown.rearrange("(c p) t -> p c t", p=128)
    outd = dr["xo"] if last else dr["XL2"]
    outr = outd.rearrange("(c p) t -> p c t", p=128)
    def normmod(src, srcb, dst, dstb, si, bi, j, tb):
        rms_rstd(P, E, src, srcb, KC, tb, 2048)
        for c in range(KC):
            P.op("dve", "scalar_tensor_tensor", RR_(dst[:, c, 0:tb]), src[:, c, 0:tb], S[:, si, c, j:j + 1], E.rstd[:, 0:tb], ALU.mult, ALU.mult, reads=[srcb, Sb, E.rstdb], writes=[dstb])
            P.op("act", "activation", out=RR_(dst[:, c, 0:tb]), in_=dst[:, c, 0:tb], func=AF.Identity, bias=mod_t[:, bi * KC + c, j:j + 1], reads=[dstb, mod_b], writes=[dstb])
    def resid(src, srcb, si, j, tb):
        rms_rstd(P, E, src, srcb, KC, tb, 2048)
        for c in range(KC):
            tt, ttb = tmpp.get()
            P.op("dve", "scalar_tensor_tensor", tt[:, 0:tb], src[:, c, 0:tb], S[:, si, c, j:j + 1], E.rstd[:, 0:tb], ALU.mult, ALU.mult, reads=[srcb, Sb, E.rstdb], writes=[ttb])
            P.op("dve", "tensor_tensor", xs[:, c, 0:tb], xs[:, c, 0:tb], tt[:, 0:tb], ALU.add, reads=[xsb, ttb], writes=[xsb])
    for (t0, tb, j) in blocks:
        P.dma("sp", xs[:, :, 0:tb], xTr[:, :, t0:t0 + tb], writes=[xsb])
        for i in range(3):
            for gg in range(2):
                for c in range(4):
                    ch = i * 8 + gg * 4 + c
                    r0 = gg * 1536 + i * 512 + c * 128
                    cols = [(NCTX + r * NLH + t0) if j == 0 else (r * NCH2) for r in range(2)]
                    s0, s0b = selp.get(); st, stb = selp.get()
                    P.dma("act", s0[:, 0:tb], YG[cols[0] // 256, r0:r0 + 128, cols[0] % 256:cols[0] % 256 + tb], writes=[s0b])
                    P.dma("act", st[:, 0:tb], YG[cols[1] // 256, r0:r0 + 128, cols[1] % 256:cols[1] % 256 + tb], writes=[stb])
                    P.op("dve", "tensor_scalar", s0[:, 0:tb], s0[:, 0:tb], msk[:, 0:1], None, ALU.mult, reads=[s0b, mskb], writes=[s0b])
                    P.op("dve", "scalar_tensor_tensor", RR_(ys[:, ch, 0:tb]), st[:, 0:tb], msk[:, 1:2], s0[:, 0:tb], ALU.mult, ALU.add, reads=[stb, mskb, s0b], writes=[ysb])
        normmod(xs, xsb, hs, hsb, 0, 0, j, tb)
        for n in range(KC):
            for i in range(3):
                wt, wb = wpool.get()
                wgv = wt[:, 0:KC * 128].rearrange("p (c n) -> p c n", c=KC)
                wbv = wt[:, KC * 128:KC * 128 + 8 * 128].rearrange("p (c n) -> p c n", c=8)
                P.dma("pool", RR_(wgv), wg[:, i * D + n * 128: i * D + (n + 1) * 128].rearrange("(c p) n -> p c n", p=128), writes=[wb])
                P.dma("pool", RR_(wbv), wbr[i, :, n * 128:(n + 1) * 128].rearrange("(c p) n -> p c n", p=128), writes=[wb])
                p1, p1b = pmm.get()
                for k in range(KC):
                    P.op("pe", "matmul", p1[:, 0:tb], RR_(wgv[:, k, :]), RR_(hs[:, k, 0:tb]), start=(k == 0), stop=(k == KC - 1), reads=[wb, hsb], writes=[p1b])
                p2, p2b = pmm.get()
                for k in range(8):
                    P.op("pe", "matmul", p2[:, 0:tb], RR_(wbv[:, k, :]), RR_(ys[:, i * 8 + k, 0:tb]), start=(k == 0), stop=(k == 7), reads=[wb, ysb], writes=[p2b])
                tt, ttb = tmpp.get()
                P.op("act", "activation", out=tt[:, 0:tb], in_=p1[:, 0:tb], func=AF.Sigmoid, reads=[p1b], writes=[ttb])
                if i == 0:
                    P.op("dve", "tensor_tensor", RR_(mg[:, n, 0:tb]), tt[:, 0:tb], p2[:, 0:tb], ALU.mult, reads=[ttb, p2b], writes=[mgb])
                else:
                    P.op("dve", "tensor_tensor", tt[:, 0:tb], tt[:, 0:tb], p2[:, 0:tb], ALU.mult, reads=[ttb, p2b], writes=[ttb])
                    P.op("dve", "tensor_tensor", RR_(mg[:, n, 0:tb]), mg[:, n, 0:tb], tt[:, 0:tb], ALU.add, reads=[ttb, mgb], writes=[mgb])
        for n0 in range(0, KC, 2):
            wv, wb = wget()
            P.dma("pool", RR_(wv), wout[:, n0 * 128:(n0 + 2) * 128].rearrange("(c p) n -> p c n", p=128), writes=[wb])
            for gi in range(2):
                pt, pb = pmm.get()
                for k in range(KC):
                    P.op("pe", "matmul", pt[:, 0:tb], RR_(wv[:, k, gi * 128:(gi + 1) * 128]), RR_(mg[:, k, 0:tb]), start=(k == 0), stop=(k == KC - 1), reads=[wb, mgb], writes=[pb])
                P.op("act", "activation", out=yl[:, n0 + gi, 0:tb], in_=pt[:, 0:tb], func=AF.Copy, reads=[pb], writes=[ylb])
        resid(yl, ylb, 1, j, tb)
        normmod(xs, xsb, hs, hsb, 2, 3, j, tb)
        for h0 in range(0, FKC, 2):
            wv, wb = wget()
            P.dma("pool", RR_(wv), wfi[:, h0 * 128:(h0 + 2) * 128].rearrange("(c p) n -> p c n", p=128), writes=[wb])
            wv2, wb2 = wget()
            P.dma("pool", RR_(wv2), wfi[:, FF + h0 * 128:FF + (h0 + 2) * 128].rearrange("(c p) n -> p c n", p=128), writes=[wb2])
            for gi in range(2):
                pg, pgb = pmm.get()
                for k in range(KC):
                    P.op("pe", "matmul", pg[:, 0:tb], RR_(wv[:, k, gi * 128:(gi + 1) * 128]), RR_(hs[:, k, 0:tb]), start=(k == 0), stop=(k == KC - 1), reads=[wb, hsb], writes=[pgb])
                pu, pub = pmm.get()
                for k in range(KC):
                    P.op("pe", "matmul", pu[:, 0:tb], RR_(wv2[:, k, gi * 128:(gi + 1) * 128]), RR_(hs[:, k, 0:tb]), start=(k == 0), stop=(k == KC - 1), reads=[wb2, hsb], writes=[pub])
                tt, ttb = tmpp.get()
                P.op("act", "activation", out=tt[:, 0:tb], in_=pg[:, 0:tb], func=AF.Silu, reads=[pgb], writes=[ttb])
                P.op("dve", "tensor_tensor", RR_(act[:, h0 + gi, 0:tb]), tt[:, 0:tb], pu[:, 0:tb], ALU.mult, reads=[ttb, pub], writes=[actb])
        for n in range(KC):
            wt, wb = wpool.get()
            wv = wt[:, 0:FKC * 128].rearrange("p (c n) -> p c n", c=FKC)
            P.dma("pool", RR_(wv), wfo[:, n * 128:(n + 1) * 128].rearrange("(c p) n -> p c n", p=128), writes=[wb])
            pt, pb = pmm.get()
            for k in range(FKC):
                P.op("pe", "matmul", pt[:, 0:tb], RR_(wv[:, k, :]), RR_(act[:, k, 0:tb]), start=(k == 0), stop=(k == FKC - 1), reads=[wb, actb], writes=[pb])
            P.op("act", "activation", out=yl[:, n, 0:tb], in_=pt[:, 0:tb], func=AF.Copy, reads=[pb], writes=[ylb])
        resid(yl, ylb, 3, j, tb)
        P.dma("sp", outr[:, :, t0:t0 + tb], xs[:, :, 0:tb], reads=[xsb])

def emit_FN(P, AR, dr, last):
    AR.begin(4000, 39424); E = common(P, AR)
    PF = dr["PF"]; YL = dr["YL"]; cld = dr["cl"]; sld = dr["sln"]
    cw2, cw2b = AR.sb("cw2s", [128, 2, 512], R=True); P.dma("pool", RR_(cw2[:]), dr["cw2"].rearrange("(c p) n -> p c n", p=128), writes=[cw2b])
    us, usb = AR.sb("us", [128, 2, NTOT], R=True); Zs, Zsb = AR.sb("Zs", [128, 32, 512], R=True); Zc, Zcb = AR.sb("Zc", [128, 2, 512], R=True)
    cp = AR.pool("ct", [128, 4, 512], 3, R=True); spn = AR.pool("st", [128, 4, 512], 3, R=True)
    pz = AR.pspool(2); py = AR.pspool(4); op = AR.pool("o", [128, 512], 3)
    uTr = PF[2368:2880, :].rearrange("(c p) t -> p c t", p=128)
    for g in range(2):
        P.dma("pool", RR_(us[:]), uTr[:, 2 * g:2 * g + 2, :], writes=[usb])
        for t in range(32):
            pt, pb = pz.get()
            for kc in range(2):
                P.op("pe", "matmul", pt[:], RR_(us[:, kc, NCTX + t * 128:NCTX + (t + 1) * 128]), RR_(cw2[:, kc, :]), start=(kc == 0), stop=(kc == 1), reads=[usb, cw2b], writes=[pb])
            evac(P, E, RR_(Zs[:, t, :]), pt[:], [pb], [Zsb])
        if not last:
            for t in range(2):
                pt, pb = pz.get()
                for kc in range(2):
                    P.op("pe", "matmul", pt[:], RR_(us[:, kc, t * 128:(t + 1) * 128]), RR_(cw2[:, kc, :]), start=(kc == 0), stop=(kc == 1), reads=[usb, cw2b], writes=[pb])
                P.op("dve", "tensor_copy", RR_(Zc[:, t, 0:256]), pt[:, 0:256], reads=[pb], writes=[Zcb])
                P.op("dve", "tensor_scalar", RR_(Zc[:, t, 256:512]), pt[:, 256:512], -1.0, None, ALU.mult, reads=[pb], writes=[Zcb])
            for ch in range(2):
                pt, pb = py.get(); i = 0
                for t in range(2):
                    for part in range(2):
                        P.op("pe", "matmul", pt[:, 0:256], RR_(Zc[:, t, part * 256 + ch * 128: part * 256 + (ch + 1) * 128]), RR_(cw2[:, t, part * 256:(part + 1) * 256]), start=(i == 0), stop=(i == 3), reads=[Zcb, cw2b], writes=[pb])
                        i += 1
                ot, ob = op.get()
                P.op("act", "activation", out=ot[:, 0:256], in_=pt[:, 0:256], func=AF.Copy, scale=1.0 / 256.0, reads=[pb], writes=[ob])
                yl_write(P, YL, 512 + g * 256 + ch * 128, 0, 256, ot, [ob])
        for o in range(8):
            pts = [py.get(), py.get()]
            for t0 in range(0, 32, 4):
                ct, cb = cp.get(); st, stb = spn.get()
                P.dma("pool", RR_(ct[:]), cld[t0 * 128:(t0 + 4) * 128, o * 512:(o + 1) * 512].rearrange("(t p) n -> p t n", p=128), writes=[cb])
                P.dma("pool", RR_(st[:]), sld[t0 * 128:(t0 + 4) * 128, o * 512:(o + 1) * 512].rearrange("(t p) n -> p t n", p=128), writes=[stb])
                for tt in range(4):
                    t = t0 + tt
                    for ch in range(2):
                        pt, pb = pts[ch]
                        P.op("pe", "matmul", pt[:], RR_(Zs[:, t, ch * 128:(ch + 1) * 128]), RR_(ct[:, tt, :]), start=(t == 0), stop=False, reads=[Zsb, cb], writes=[pb])
                        P.op("pe", "matmul", pt[:], RR_(Zs[:, t, 256 + ch * 128:256 + (ch + 1) * 128]), RR_(st[:, tt, :]), start=False, stop=(t == 31), reads=[Zsb, stb], writes=[pb])
            for ch in range(2):
                pt, pb = pts[ch]; ot, ob = op.get()
                if ch == 0: P.op("act", "activation", out=ot[:], in_=pt[:], func=AF.Copy, scale=1.0 / 1024.0, reads=[pb], writes=[ob])
                else: P.op("dve", "tensor_scalar", ot[:], pt[:], 1.0 / 1024.0, None, ALU.mult, reads=[pb], writes=[ob])
                yl_write(P, YL, 512 + g * 256 + ch * 128, NCTX + o * 512, 512, ot, [ob])

def emit_MLA(P, AR, dr, l, last, NH=4):
    AR.begin(28000, 17408); E = common(P, AR)
    PF = dr["PF"]; YL = dr["YL"]
    cin = PF[1536:2368, :]
    gq, gqb = ld(P, AR, "gq_s", dr[f"gq{l}"], [128, 4]); gkv, gkvb = ld(P, AR, "gkv_s", dr[f"gkv{l}"], [128, 2])
    P.op("dve", "tensor_scalar", gq[:], gq[:], math.sqrt(512.0), None, ALU.mult, reads=[gqb], writes=[gqb])
    P.op("dve", "tensor_scalar", gkv[:], gkv[:], math.sqrt(256.0), None, ALU.mult, reads=[gkvb], writes=[gkvb])
    wqn, wqnb = ld(P, AR, "wqn_s", dr[f"wqn{l}"].rearrange("(c p) n -> p c n", p=128), [128, 4, NH * 128])
    wqr, wqrb = ld(P, AR, "wqr_s", dr[f"wqr{l}"].rearrange("(c p) n -> p c n", p=128), [128, 4, NH * 64])
    wk, wkb = ld(P, AR, "wk_s", dr[f"wk{l}"].rearrange("(c p) n -> p c n", p=128), [128, 2, NH * 128])
    wv, wvb = ld(P, AR, "wv_s", dr[f"wv{l}"].rearrange("(c p) n -> p c n", p=128), [128, 2, NH * 128])
    Rm, Rmb = ld(P, AR, "R_s", dr["Rm"], [64, 64]); ident, identb = ld(P, AR, "id_s", dr["ident"], [128, 128])
    Qn, Qnb = AR.sb("Qn", [128, NTOT], R=True); Qr, Qrb = AR.sb("Qr", [64, NTOT], R=True)
    Kn, Knb = AR.sb("Kn", [128, NTOT], R=True); Kr, Krb = AR.sb("Kr", [64, NTOT], R=True)
    V, Vb = AR.sb("V", [128, NTOT // 128, 128]); Ss, Ssb = AR.sb("Ss", [128, NTOT])
    cb_t, cb_b = AR.sb("cblk", [128, 6, 512]); krb_t, krb_b = AR.sb("krblk", [64, 512])
    cqn, cqnb = AR.sb("cqn", [128, 4, 512]); ckvn, ckvnb = AR.sb("ckvn", [128, 2, 512])
    cs_t, cs_b = AR.sb("cosb", [64, 512]); sn_t, sn_b = AR.sb("sinb", [64, 512])
    tq, tqb = AR.sb("tq", [64, 512]); t2, t2b = AR.sb("t2", [64, 512])
    pmm = AR.pspool(5); po_p = AR.pspool(2)
    PTp = AR.pool("PT", [128, 512], 2); osb_p = AR.pool("osb", [128, 128], 2); oT_p = AR.pool("oT", [128, 128], 2); st_p = AR.pool("stat", [128, 4], 2)
    cinr = cin[0:768, :].rearrange("(c p) t -> p c t", p=128)
    cos_d = dr["cosT"]; sin_d = dr["sinT"]
    blocks = [(0, 256, False)] + [(NCTX + i * 512, 512, True) for i in range(8)]
    for h in range(NH):
        for (t0, tb, lat) in blocks:
            P.dma("sp", cb_t[:, :, 0:tb], cinr[:, :, t0:t0 + tb], writes=[cb_b])
            P.dma("sp", krb_t[:, 0:tb], cin[768:832, t0:t0 + tb], writes=[krb_b])
            if lat:
                P.dma("act", cs_t[:, 0:tb], cos_d[:, t0 - NCTX:t0 - NCTX + tb], writes=[cs_b])
                P.dma("act", sn_t[:, 0:tb], sin_d[:, t0 - NCTX:t0 - NCTX + tb], writes=[sn_b])
            rms_rstd(P, E, cb_t[:, 0:4, :], cb_b, 4, tb, 512)
            for c in range(4):
                P.op("dve", "scalar_tensor_tensor", cqn[:, c, 0:tb], cb_t[:, c, 0:tb], gq[:, c:c + 1], E.rstd[:, 0:tb], ALU.mult, ALU.mult, reads=[cb_b, gqb, E.rstdb], writes=[cqnb])
            rms_rstd(P, E, cb_t[:, 4:6, :], cb_b, 2, tb, 256)
            for c in range(2):
                P.op("dve", "scalar_tensor_tensor", ckvn[:, c, 0:tb], cb_t[:, 4 + c, 0:tb], gkv[:, c:c + 1], E.rstd[:, 0:tb], ALU.mult, ALU.mult, reads=[cb_b, gkvb, E.rstdb], writes=[ckvnb])
            pt, pb = pmm.get()
            for kc in range(4):
                P.op("pe", "matmul", pt[:, 0:tb], wqn[:, kc, h * 128:(h + 1) * 128], cqn[:, kc, 0:tb], start=(kc == 0), stop=(kc == 3), reads=[wqnb, cqnb], writes=[pb])
            evac(P, E, RR_(Qn[:, t0:t0 + tb]), pt[:, 0:tb], [pb], [Qnb])
            pt, pb = pmm.get()
            for kc in range(2):
                P.op("pe", "matmul", pt[:, 0:tb], wk[:, kc, h * 128:(h + 1) * 128], ckvn[:, kc, 0:tb], start=(kc == 0), stop=(kc == 1), reads=[wkb, ckvnb], writes=[pb])
            evac(P, E, RR_(Kn[:, t0:t0 + tb]), pt[:, 0:tb], [pb], [Knb])
            for ts in range(tb // 128):
                pt, pb = pmm.get()
                for kc in range(2):
                    P.op("pe", "matmul", pt[:, 0:128], ckvn[:, kc, ts * 128:(ts + 1) * 128], wv[:, kc, h * 128:(h + 1) * 128], start=(kc == 0), stop=(kc == 1), reads=[wvb, ckvnb], writes=[pb])
                evac(P, E, V[:, t0 // 128 + ts, :], pt[:, 0:128], [pb], [Vb])
            pt, pb = pmm.get()
            for kc in range(4):
                P.op("pe", "matmul", pt[0:64, 0:tb], wqr[:, kc, h * 64:(h + 1) * 64], cqn[:, kc, 0:tb], start=(kc == 0), stop=(kc == 3), reads=[wqrb, cqnb], writes=[pb])
            def rope(dst, dstb, src, srcb):
                p2, p2b = pmm.get()
                P.op("pe", "matmul", p2[0:64, 0:tb], Rm[:, :], src, start=True, stop=True, reads=[Rmb, srcb], writes=[p2b])
                P.op("dve", "tensor_tensor", t2[:, 0:tb], p2[0:64, 0:tb], sn_t[:, 0:tb], ALU.mult, reads=[p2b, sn_b], writes=[t2b])
                P.op("pool", "tensor_tensor", RR_(dst[:, t0:t0 + tb]), src, cs_t[:, 0:tb], ALU.mult, reads=[srcb, cs_b], writes=[dstb])
                P.op("dve", "tensor_tensor", RR_(dst[:, t0:t0 + tb]), dst[:, t0:t0 + tb], t2[:, 0:tb], ALU.add, reads=[dstb, t2b], writes=[dstb])
            if lat:
                evac(P, E, tq[:, 0:tb], pt[0:64, 0:tb], [pb], [tqb])
                rope(Qr, Qrb, tq[:, 0:tb], tqb)
                rope(Kr, Krb, krb_t[:, 0:tb], krb_b)
            else:
                evac(P, E, RR_(Qr[:, t0:t0 + tb]), pt[0:64, 0:tb], [pb], [Qrb])
                P.op("pool", "tensor_copy", RR_(Kr[:, t0:t0 + tb]), krb_t[:, 0:tb], reads=[krb_b], writes=[Krb])
        qtiles = [(NCTX + qt * 128, 0, NTOT) for qt in range(32)]
        if not last: qtiles = [(qt * 128, 0, NCTX) for qt in range(2)] + qtiles
        for (q0, k0, k1) in qtiles:
            nk = k1 - k0
            for kb0 in range(k0, k1, 512):
                kw = min(512, k1 - kb0)
                pt, pb = pmm.get()
                P.op("pe", "matmul", pt[:, 0:kw], RR_(Qn[:, q0:q0 + 128]), RR_(Kn[:, kb0:kb0 + kw]), start=True, stop=False, reads=[Qnb, Knb], writes=[pb])
                P.op("pe", "matmul", pt[:, 0:kw], RR_(Qr[:, q0:q0 + 128]), RR_(Kr[:, kb0:kb0 + kw]), start=False, stop=True, reads=[Qrb, Krb], writes=[pb])
                evac(P, E, Ss[:, kb0:kb0 + kw], pt[:, 0:kw], [pb], [Ssb])
            stt, stb = st_p.get()
            P.op("dve", "tensor_reduce", stt[:, 0:1], Ss[:, k0:k1], AX.X, ALU.max, reads=[Ssb], writes=[stb])
            P.op("dve", "tensor_scalar", stt[:, 1:2], stt[:, 0:1], -MLA_SCALE, None, ALU.mult, reads=[stb], writes=[stb])
            P.op("pool", "memset", stt[:, 2:3], 0.0, writes=[stb])
            P.op("act", "activation", out=Ss[:, k0:k1], in_=Ss[:, k0:k1], func=AF.Exp, scale=MLA_SCALE, bias=stt[:, 1:2], accum_out=stt[:, 2:3], reads=[Ssb, stb], writes=[Ssb, stb])
            P.op("dve", "reciprocal", stt[:, 3:4], stt[:, 2:3], reads=[stb], writes=[stb])
            po, pob = po_p.get()
            ntile = nk // 128
            for g0 in range(0, ntile, 4):
                gn = min(4, ntile - g0)
                ptp, ptpb = pmm.get()
                for i in range(gn):
                    kt = k0 // 128 + g0 + i
                    P.op("pe", "transpose", ptp[:, i * 128:(i + 1) * 128], Ss[:, kt * 128:(kt + 1) * 128], ident[:], reads=[Ssb, identb], writes=[ptpb])
                PT, PTb = PTp.get()
                evac(P, E, PT[:, 0:gn * 128], ptp[:, 0:gn * 128], [ptpb], [PTb])
                for i in range(gn):
                    kt = k0 // 128 + g0 + i
                    P.op("pe", "matmul", po[:, 0:128], PT[:, i * 128:(i + 1) * 128], V[:, kt, :], start=(g0 + i == 0), stop=(g0 + i == ntile - 1), reads=[PTb, Vb], writes=[pob])
            ot, ob = osb_p.get()
            P.op("dve", "tensor_scalar", ot[:], po[:, 0:128], stt[:, 3:4], None, ALU.mult, reads=[pob, stb], writes=[ob])
            pq, pqb = pmm.get()
            P.op("pe", "transpose", pq[:, 0:128], ot[:], ident[:], reads=[ob, identb], writes=[pqb])
            oT, oTb = oT_p.get()
            evac(P, E, oT[:], pq[:, 0:128], [pqb], [oTb])
            yl_write(P, YL, 1024 + h * 128, q0, 128, oT, [oTb])

def emit_DN(P, AR, dr, l, last, NH=4):
    AR.begin(51900, 8); E = common(P, AR)
    G = 2 * NH
    PF = dr["PF"]; PT = dr["PT"]; YL = dr["YL"]
    TRI2, TRI2b = ld(P, AR, "tri2s", dr["tri2"], [64, 2, 64]); MS2, MS2b = ld(P, AR, "ms2s", dr["ms2"], [64, 2, 64])
    I2, I2b = ld(P, AR, "i2s", dr["i2"], [64, 2, 64]); ident, identb = ld(P, AR, "ids", dr["ident"], [128, 128])
    cw, cwb = ld(P, AR, "cws", dr[f"convw{l}"], [128, 3 * NH, 5]); gn, gnb = ld(P, AR, "gns", dr[f"gnorm{l}"], [64, 128])
    alog, alogb = ld(P, AR, "alogs", dr[f"alog{l}"], [64, G]); dtb, dtbb = ld(P, AR, "dtbs", dr[f"dtb{l}"], [64, G])
    ones, onesb = E.ones, E.onesb
    one1, one1b = AR.sb("one1", [128, 1]); P.op("pool", "memset", one1[:], 1.0, writes=[one1b])
    eps6, eps6b = E.eps[1]
    psp = AR.pspool(7)
    bl, blb = ld(P, AR, "bls", PT[:, 512:512 + G].rearrange("(n c) x -> c n x", c=64), [64, NCH, G])
    al, alb = ld(P, AR, "als", PT[:, 512 + G:512 + 2 * G].rearrange("(n c) x -> c n x", c=64), [64, NCH, G], q="act")
    BETA, BETAb = AR.sb("BETA", [64, NCH, G]); NBETA, NBETAb = AR.sb("NBETA", [64, NCH, G])
    gt, gtb = AR.sb("gt", [64, NCH, G]); GC, GCb = AR.sb("GC", [64, NCH, G]); NGC, NGCb = AR.sb("NGC", [64, NCH, G])
    BEG, BEGb = AR.sb("BEG", [64, NCH, G]); EKD, EKDb = AR.sb("EKD", [64, NCH, G]); EGL, EGLb = AR.sb("EGL", [128, NCH, G])
    P.op("act", "activation", out=BETA[:], in_=bl[:], func=AF.Sigmoid, reads=[blb], writes=[BETAb])
    P.op("dve", "tensor_scalar", NBETA[:], BETA[:], -1.0, None, ALU.mult, reads=[BETAb], writes=[NBETAb])
    for c in range(G):
        P.op("act", "activation", out=gt[:, :, c], in_=al[:, :, c], func=AF.Exp, bias=dtb[:, c:c + 1], reads=[alb, dtbb], writes=[gtb])
    P.op("act", "activation", out=gt[:], in_=gt[:], func=AF.Ln, bias=one1[0:64, 0:1], reads=[gtb, one1b], writes=[gtb])
    P.op("act", "activation", out=alog[:], in_=alog[:], func=AF.Exp, reads=[alogb], writes=[alogb])
    P.op("dve", "tensor_scalar", alog[:], alog[:], -1.0, None, ALU.mult, reads=[alogb], writes=[alogb])
    for c in range(G):
        P.op("dve", "tensor_scalar", gt[:, :, c], gt[:, :, c], alog[:, c:c + 1], None, ALU.mult, reads=[gtb, alogb], writes=[gtb])
    gflat = gt[:].rearrange("p n g -> p (n g)")
    NF = NCH * G; H2 = NF // 2
    for d in range(2):
        pt, pb = psp.get(); pt2, pb2 = psp.get()
        for (pp, ppb, c0) in ((pt, pb, 0), (pt2, pb2, H2)):
            P.op("pe", "matmul", pp[0:64, 0:H2], TRI2[:, d, :], gflat[:, c0:c0 + H2], start=True, stop=True, reads=[TRI2b, gtb], writes=[ppb])
        for (pp, ppb, c0) in ((pt, pb, 0), (pt2, pb2, H2)):
            nn = H2 // G
            src = pp[0:64, 0:H2].rearrange("p (n g) -> p n g", g=G)
            P.op("dve", "tensor_copy", GC[:, c0 // G:c0 // G + nn, d * NH:(d + 1) * NH], src[:, :, d * NH:(d + 1) * NH], reads=[ppb], writes=[GCb])
    P.op("dve", "tensor_scalar", NGC[:], GC[:], -1.0, None, ALU.mult, reads=[GCb], writes=[NGCb])
    P.op("act", "activation", out=BEG[:], in_=GC[:], func=AF.Exp, reads=[GCb], writes=[BEGb])
    P.op("dve", "tensor_tensor", BEG[:], BEG[:], BETA[:], ALU.mult, reads=[BEGb, BETAb], writes=[BEGb])
    EGLf = EGL[:].rearrange("p n g -> p (n g)"); EKDf = EKD[:].rearrange("p n g -> p (n g)"); GCf = GC[:].rearrange("p n g -> p (n g)")
    for c0 in (0, H2):
        pt, pb = psp.get()
        P.op("pe", "matmul", pt[:, 0:H2], ones[0:64, :], gflat[:, c0:c0 + H2], start=True, stop=True, reads=[onesb, gtb], writes=[pb])
        P.op("dve", "tensor_tensor", EKDf[:, c0:c0 + H2], pt[0:64, 0:H2], GCf[:, c0:c0 + H2], ALU.subtract, reads=[pb, GCb], writes=[EKDb])
        P.op("act", "activation", out=EGLf[:, c0:c0 + H2], in_=pt[:, 0:H2], func=AF.Exp, reads=[pb], writes=[EGLb])
    P.op("act", "activation", out=EKD[:], in_=EKD[:], func=AF.Exp, reads=[EKDb], writes=[EKDb])
    QT, QTb = AR.sb("QT", [128, NTOT]); KT, KTb = AR.sb("KT", [128, NTOT]); VT, VTb = AR.sb("VT", [128, NTOT])
    Xr, Xrb = AR.sb("Xr", [128, NTOT]); O, Ob = AR.sb("O", [64, NCH, 128])
    rs, rsb = E.rstd, E.rstdb
    w128 = AR.pool("w128", [64, 2, 64], 12); xxp = AR.pool("xx", [64, 2, 128], 6); zp = AR.pool("zz", [64, 2, 64], 26); qkp = AR.pool("qk", [64, 2, 64], 6)
    egp = AR.pool("egr", [128, 2, 64], 4); qgp = AR.pool("qg", [128, 2, 64], 6); nwp = AR.pool("nw", [128, 2, 64], 6)
    tmp = AR.pool("tm", [64, 2, 128], 16)
    Sp = [AR.pool(f"S{d}", [128, 128], 2) for d in range(2)]
    GRP = 4
    zt, ztb = AR.sb("zt", [64, GRP, 128]); yt, ytb = AR.sb("yt", [64, GRP, 128])
    st17, st17b = AR.sb("st17", [64, GRP]); yT_p = AR.pool("yT", [128, 512], 2)
    segs = [(0, NCTX), (NCTX, NTOT)]
    for h in range(NH):
        for qi, (dst, dstb) in enumerate(((QT, QTb), (KT, KTb), (VT, VTb))):
            ci = qi * NH + h
            P.dma("sp", Xr[:], PF[qi * 512 + h * 128: qi * 512 + (h + 1) * 128, :], writes=[Xrb])
            for (a, b) in segs:
                P.op("act", "activation", out=dst[:, a:b], in_=Xr[:, a:b], func=AF.Copy, scale=cw[:, ci, 2:3], reads=[Xrb, cwb], writes=[dstb])
                for tap in (0, 1, 3, 4):
                    off = tap - 2
                    if off < 0: o0, o1, i0, i1 = a - off, b, a, b + off
                    else: o0, o1, i0, i1 = a, b - off, a + off, b
                    P.op("dve", "scalar_tensor_tensor", dst[:, o0:o1], Xr[:, i0:i1], cw[:, ci, tap:tap + 1], dst[:, o0:o1], ALU.mult, ALU.add, reads=[Xrb, cwb, dstb], writes=[dstb])
            P.op("act", "activation", out=dst[:], in_=dst[:], func=AF.Silu, reads=[dstb], writes=[dstb])
            if qi < 2:
                for t0 in range(0, NTOT, 512):
                    tb = min(512, NTOT - t0)
                    sq, sqb = E.sqp.get()
                    P.op("act", "activation", out=sq[:, 0:tb], in_=dst[:, t0:t0 + tb], func=AF.Square, reads=[dstb], writes=[sqb])
                    pt, pb = psp.get()
                    P.op("pe", "matmul", pt[:, 0:tb], ones[:], sq[:, 0:tb], start=True, stop=True, reads=[onesb, sqb], writes=[pb])
                    P.op("act", "activation", out=rs[:, 0:tb], in_=pt[:, 0:tb], func=AF.Sqrt, bias=eps6[:, 0:1], reads=[pb, eps6b], writes=[rsb])
                    P.op("dve", "reciprocal", rs[:, 0:tb], rs[:, 0:tb], reads=[rsb], writes=[rsb])
                    if qi == 0:
                        P.op("dve", "scalar_tensor_tensor", dst[:, t0:t0 + tb], dst[:, t0:t0 + tb], 128 ** -0.5, rs[:, 0:tb], ALU.mult, ALU.mult, reads=[dstb, rsb], writes=[dstb])
                    else:
                        P.op("dve", "tensor_tensor", dst[:, t0:t0 + tb], dst[:, t0:t0 + tb], rs[:, 0:tb], ALU.mult, reads=[dstb, rsb], writes=[dstb])
        S = []
        for d in range(2):
            st, stb = Sp[d].get()
            P.op("pool", "memset", st[:], 0.0, writes=[stb])
            S.append((st, stb))
        visited = set()
        def chunk_of(s, d):
            if d == 0: return s
            return 3 - s if s < 4 else 71 - s
        def pre(s):
            ns = [chunk_of(s, d) for d in range(2)]; cols = [d * NH + h for d in range(2)]
            toks = [slice(n * 64, (n + 1) * 64) for n in ns]
            Gd, Gdb = w128.get()
            for d in range(2):
                P.op("dve", "tensor_scalar", Gd[:, d, :], TRI2[:, d, :], gt[:, ns[d], cols[d]:cols[d] + 1], None, ALU.mult, reads=[TRI2b, gtb], writes=[Gdb])
            yield
            pa, pab = psp.get()
            P.op("pe", "matmul", pa[:, 0:128], ones[0:64, :], Gd[:].rearrange("p d j -> p (d j)"), start=True, stop=True, reads=[onesb, Gdb], writes=[pab])
            E1, E1b = w128.get(); E2, E2b = w128.get(); EGr, EGrb = egp.get()
            for d in range(2):
                P.op("act", "activation", out=E1[:, d, :], in_=pa[0:64, d * 64:(d + 1) * 64], func=AF.Exp, scale=-1.0, bias=GC[:, ns[d], cols[d]:cols[d] + 1], reads=[pab, GCb], writes=[E1b])
                P.op("act", "activation", out=E2[:, d, :], in_=pa[0:64, d * 64:(d + 1) * 64], func=AF.Exp, bias=NGC[:, ns[d], cols[d]:cols[d] + 1], reads=[pab, NGCb], writes=[E2b])
            P.op("act", "activation", out=EGr[:].rearrange("p d j -> p (d j)"), in_=pa[:, 0:128], func=AF.Exp, reads=[pab], writes=[EGrb])
            yield
            D1, D1b = w128.get(); D2, D2b = w128.get()
            P.op("dve", "scalar_tensor_tensor", D1[:], E1[:], 1.0, MS2[:], ALU.min, ALU.mult, reads=[E1b, MS2b], writes=[D1b])
            P.op("dve", "scalar_tensor_tensor", D2[:], E2[:], 1.0, TRI2[:], ALU.min, ALU.mult, reads=[E2b, TRI2b], writes=[D2b])
            for d in range(2):
                P.op("pool", "tensor_scalar", D1[:, d, :], D1[:, d, :], NBETA[:, ns[d], cols[d]:cols[d] + 1], None, ALU.mult, reads=[D1b, NBETAb], writes=[D1b])
            yield
            pk, pkb = psp.get()
            for d in range(2):
                P.op("pe", "matmul", pk[0:64, d * 64:(d + 1) * 64], KT[:, toks[d]], KT[:, toks[d]], start=True, stop=True, reads=[KTb], writes=[pkb])
                P.op("pe", "matmul", pk[0:64, 128 + d * 64:128 + (d + 1) * 64], KT[:, toks[d]], QT[:, toks[d]], start=True, stop=True, reads=[KTb, QTb], writes=[pkb])
            XX, XXb = xxp.get(); QK, QKb = qkp.get()
            P.op("dve", "tensor_tensor", XX[:, :, 0:64], pk[0:64, 0:128].rearrange("p (d j) -> p d j", d=2), D1[:], ALU.mult, reads=[pkb, D1b], writes=[XXb])
            P.op("dve", "tensor_tensor", QK[:], pk[0:64, 128:256].rearrange("p (d j) -> p d j", d=2), D2[:], ALU.mult, reads=[pkb, D2b], writes=[QKb])
            yield
            pc, pcb = psp.get()
            for d in range(2):
                P.op("pe", "transpose", pc[0:64, d * 64:(d + 1) * 64], XX[:, d, 0:64], ident[0:64, 0:64], reads=[XXb, identb], writes=[pcb])
            pcv = pc[0:64, 0:128].rearrange("p (d j) -> p d j", d=2)
            P.op("act", "activation", out=XX[:, :, 64:128], in_=pcv, func=AF.Copy, reads=[pcb], writes=[XXb])
            Z, Zb = zp.get()
            P.op("dve", "tensor_tensor", Z[:], pcv, I2[:], ALU.add, reads=[pcb, I2b], writes=[Zb])
            for k in range(1, 6):
                yield
                pd, pdb = psp.get()
                for d in range(2):
                    P.op("pe", "matmul", pd[0:64, d * 128:d * 128 + 64], XX[:, d, 64:128], XX[:, d, 0:64], start=True, stop=True, reads=[XXb], writes=[pdb])
                    P.op("pe", "matmul", pd[0:64, d * 128 + 64:d * 128 + 128], XX[:, d, 0:64], XX[:, d, 64:128], start=True, stop=True, reads=[XXb], writes=[pdb])
                if k > 1:
                    pe_, peb = psp.get()
                    for d in range(2):
                        P.op("pe", "matmul", pe_[0:64, d * 64:(d + 1) * 64], XX[:, d, 0:64], Z[:, d, :], start=True, stop=True, reads=[XXb, Zb], writes=[peb])
                XXn, XXnb = xxp.get()
                P.op("act", "activation", out=XXn[:].rearrange("p d j -> p (d j)"), in_=pd[0:64, 0:256], func=AF.Copy, reads=[pdb], writes=[XXnb])
                if k > 1:
                    Zn, Znb = zp.get()
                    P.op("dve", "tensor_tensor", Zn[:], Z[:], pe_[0:64, 0:128].rearrange("p (d j) -> p d j", d=2), ALU.add, reads=[Zb, peb], writes=[Znb])
                    Z, Zb = Zn, Znb
                XX, XXb = XXn, XXnb
            yield
            pe_, peb = psp.get()
            for d in range(2):
                P.op("pe", "matmul", pe_[0:64, d * 64:(d + 1) * 64], XX[:, d, 0:64], Z[:, d, :], start=True, stop=True, reads=[XXb, Zb], writes=[peb])
            Zn, Znb = zp.get()
            P.op("dve", "tensor_tensor", Zn[:], Z[:], pe_[0:64, 0:128].rearrange("p (d j) -> p d j", d=2), ALU.add, reads=[Zb, peb], writes=[Znb])
            Z, Zb = Zn, Znb
            yield
            ptk, ptkb = psp.get()
            for d in range(2):
                P.op("pe", "transpose", ptk[0:64, d * 128:(d + 1) * 128], KT[:, toks[d]], ident[:], reads=[KTb, identb], writes=[ptkb])
                P.op("pe", "transpose", ptk[0:64, 256 + d * 128:256 + (d + 1) * 128], VT[:, toks[d]], ident[:], reads=[VTb, identb], writes=[ptkb])
            VB, VBb = tmp.get(); KBG, KBGb = tmp.get(); KD, KDb = tmp.get()
            for d in range(2):
                n, c = ns[d], cols[d]
                P.op("act", "activation", out=KBG[:, d, :], in_=ptk[0:64, d * 128:(d + 1) * 128], func=AF.Copy, scale=BEG[:, n, c:c + 1], reads=[ptkb, BEGb], writes=[KBGb])
                P.op("dve", "tensor_scalar", KD[:, d, :], ptk[0:64, d * 128:(d + 1) * 128], EKD[:, n, c:c + 1], None, ALU.mult, reads=[ptkb, EKDb], writes=[KDb])
                P.op("dve", "tensor_scalar", VB[:, d, :], ptk[0:64, 256 + d * 128:256 + (d + 1) * 128], BETA[:, n, c:c + 1], None, ALU.mult, reads=[ptkb, BETAb], writes=[VBb])
            yield
            pw, pwb = psp.get()
            for d in range(2):
                P.op("pe", "matmul", pw[:, d * 64:(d + 1) * 64], KBG[:, d, :], Z[:, d, :], start=True, stop=True, reads=[KBGb, Zb], writes=[pwb])
            NW, NWb = nwp.get()
            P.op("act", "activation", out=NW[:].rearrange("p d j -> p (d j)"), in_=pw[:, 0:128], func=AF.Copy, scale=-1.0, reads=[pwb], writes=[NWb])
            QG, QGb = qgp.get()
            for d in range(2):
                P.op("pool", "tensor_tensor", QG[:, d, :], QT[:, toks[d]], EGr[:, d, :], ALU.mult, reads=[QTb, EGrb], writes=[QGb])
            return dict(ns=ns, cols=cols, Z=(Z, Zb), VB=(VB, VBb), KD=(KD, KDb), NW=(NW, NWb), QG=(QG, QGb), QK=(QK, QKb))
        def seq(R):
            ns, cols = R["ns"], R["cols"]
            Z, Zb = R["Z"]; VB, VBb = R["VB"]; KD, KDb = R["KD"]; NW, NWb = R["NW"]; QG, QGb = R["QG"]; QK, QKb = R["QK"]
            pv, pvb = psp.get()
            for d in range(2):
                P.op("pe", "matmul", pv[0:64, d * 128:(d + 1) * 128], Z[:, d, :], VB[:, d, :], start=True, stop=False, reads=[Zb, VBb], writes=[pvb])
                P.op("pe", "matmul", pv[0:64, d * 128:(d + 1) * 128], NW[:, d, :], S[d][0][:], start=False, stop=True, reads=[NWb, S[d][1]], writes=[pvb])
            VN, VNb = tmp.get()
            P.op("act", "activation", out=VN[:].rearrange("p d e -> p (d e)"), in_=pv[0:64, 0:256], func=AF.Copy, reads=[pvb], writes=[VNb])
            po, pob = psp.get()
            for d in range(2):
                P.op("pe", "matmul", po[0:64, d * 128:(d + 1) * 128], QG[:, d, :], S[d][0][:], start=True, stop=False, reads=[QGb, S[d][1]], writes=[pob])
                P.op("pe", "matmul", po[0:64, d * 128:(d + 1) * 128], QK[:, d, :], VN[:, d, :], start=False, stop=True, reads=[QKb, VNb], writes=[pob])
            for d in range(2):
                n = ns[d]
                if n in visited:
                    P.op("dve", "tensor_tensor", O[:, n, :], O[:, n, :], po[0:64, d * 128:(d + 1) * 128], ALU.add, reads=[Ob, pob], writes=[Ob])
                else:
                    visited.add(n)
                    P.op("dve", "tensor_copy", O[:, n, :], po[0:64, d * 128:(d + 1) * 128], reads=[pob], writes=[Ob])
            pS, pSb = psp.get()
            for d in range(2):
                P.op("pe", "matmul", pS[:, d * 128:(d + 1) * 128], KD[:, d, :], VN[:, d, :], start=True, stop=True, reads=[KDb, VNb], writes=[pSb])
            for d in range(2):
                sn, snb = Sp[d].get()
                P.op("dve", "scalar_tensor_tensor", sn[:], S[d][0][:], EGL[:, ns[d], cols[d]:cols[d] + 1], pS[:, d * 128:(d + 1) * 128], ALU.mult, ALU.add, reads=[S[d][1], EGLb, pSb], writes=[snb])
                S[d] = (sn, snb)
        def drive(gens, res, nstages=None):
            k = 0
            while any(g is not None for g in gens):
                for i, g in enumerate(gens):
                    if g is None: continue
                    try: next(g)
                    except StopIteration as e:
                        res[i] = e.value; gens[i] = None
                k += 1
                if nstages is not None and k >= nstages: break
            return all(g is None for g in gens)
        cur = [None, None]
        drive([pre(0), pre(1)], cur)
        for p in range(0, NCH, 2):
            nxt = [None, None]; gens = [pre(p + 2), pre(p + 3)] if p + 2 < NCH else [None, None]
            drive(gens, nxt, 3)
            seq(cur[0])
            drive(gens, nxt, 6)
            seq(cur[1])
            drive(gens, nxt)
            cur = nxt
        zr = PT[:, h * 128:(h + 1) * 128].rearrange("(n c) e -> c n e", c=64)
        for n0 in range(0, NCH, GRP):
            P.dma("sp", zt[:], zr[:, n0:n0 + GRP, :], writes=[ztb])
            P.op("dve", "tensor_tensor", yt[:], O[:, n0:n0 + GRP, :], O[:, n0:n0 + GRP, :], ALU.mult, reads=[Ob], writes=[ytb])
            P.op("dve", "tensor_reduce", st17[:], yt[:], AX.X, ALU.add, reads=[ytb], writes=[st17b])
            P.op("act", "activation", out=st17[:], in_=st17[:], func=AF.Sqrt, scale=1.0 / 128.0, bias=eps6[0:64, 0:1], reads=[st17b, eps6b], writes=[st17b])
            P.op("dve", "reciprocal", st17[:], st17[:], reads=[st17b], writes=[st17b])
            for i in range(GRP):
                P.op("dve", "scalar_tensor_tensor", yt[:, i, :], O[:, n0 + i, :], st17[:, i:i + 1], gn[:], ALU.mult, ALU.mult, reads=[Ob, st17b, gnb], writes=[ytb])
            P.op("act", "activation", out=zt[:], in_=zt[:], func=AF.Silu, reads=[ztb], writes=[ztb])
            P.op("dve", "tensor_tensor", yt[:], yt[:], zt[:], ALU.mult, reads=[ytb, ztb], writes=[ytb])
            pq, pqb = psp.get()
            for i in range(GRP):
                P.op("pe", "transpose", pq[:, i * 64:(i + 1) * 64], yt[:, i, :], ident[0:64, 0:64], reads=[ytb, identb], writes=[pqb])
            yT, yTb = yT_p.get()
            evac(P, E, yT[:, 0:GRP * 64], pq[:, 0:GRP * 64], [pqb], [yTb])
            yl_write(P, YL, h * 128, n0 * 64, GRP * 64, yT, [yTb])

class LazyDr(dict):
    def __init__(self, nc):
        super().__init__(); self.nc = nc; self.specs = {}; self.used_ext = []
    def __missing__(self, name):
        kind, shape = self.specs[name]
        ap = self.nc.dram_tensor(name, list(shape), F32, kind=kind).ap()
        if kind == "ExternalInput": self.used_ext.append(name)
        self[name] = ap
        return ap

def build_fused(nc, stop=None):
    dr = LazyDr(nc)
    def ext(name, shape): dr.specs[name] = ("ExternalInput", shape)
    def internal(name, shape): dr.specs[name] = ("Internal", shape)
    ext("xT_in", [KC, 256, NTH]); ext("xT_own", [D, NTH]); ext("cT", [128, KC, 2]); ext("msk", [128, 2])
    ext("cw2", [256, 512]); ext("cl", [NLAT, NLAT]); ext("sln", [NLAT, NLAT])
    ext("cosT", [64, NLAT]); ext("sinT", [64, NLAT]); ext("Rm", [64, 64]); ext("ident", [128, 128])
    ext("tri2", [64, 2, 64]); ext("ms2", [64, 2, 64]); ext("i2", [64, 2, 64])
    for l in range(2):
        ext(f"w_ada{l}", [D, 12288]); ext(f"b_ada{l}", [128, 96]); ext(f"g{l}", [128, 4, KC]); ext(f"w_inr{l}", [D, WINR])
        ext(f"w_gate{l}", [D, 6144]); ext(f"w_branch{l}", [3, 1024, D]); ext(f"w_out{l}", [D, D]); ext(f"w_ffn_in{l}", [D, 2 * FF]); ext(f"w_ffn_out{l}", [FF, D])
        ext(f"gq{l}", [128, 4]); ext(f"gkv{l}", [128, 2]); ext(f"wqn{l}", [512, 512]); ext(f"wqr{l}", [512, 256]); ext(f"wk{l}", [256, 512]); ext(f"wv{l}", [256, 512])
        ext(f"convw{l}", [128, 12, 5]); ext(f"alog{l}", [64, 8]); ext(f"dtb{l}", [64, 8]); ext(f"gnorm{l}", [64, 128])
    dr.specs["xo"] = ("ExternalOutput", [D, NLH])
    internal("PF", [FM_ROWS, NTOT]); internal("PT", [NTOT, TM_W]); internal("YL", [17, 1536, 256]); internal("YG", [17, 2 * 1536, 256])
    internal("XL2", [D, NTH]); internal("XG", [KC, 256, NTH])
    dr["xo"]
    with ExitStack() as es:
        P = Prog(nc, es)
        AR = Arena(P)
        def steps():
            for l in range(2):
                last = (l == 1)
                yield f"A{l}", lambda: emit_A(P, AR, dr, l)
                yield f"DN{l}", lambda: emit_DN(P, AR, dr, l, last)
                yield f"FN{l}", lambda: emit_FN(P, AR, dr, last)
                yield f"MLA{l}", lambda: emit_MLA(P, AR, dr, l, last)
                def g1():
                    AR.end()
                    for blk in range(17): P.coll("AllGather", dr["YG"][blk], dr["YL"][blk], PAIRS)
                yield f"G1{l}", g1
                yield f"C{l}", lambda: emit_C(P, AR, dr, l, last)
                if not last:
                    def g2():
                        AR.end()
                        for c in range(KC): P.coll("AllGather", dr["XG"][c], dr["XL2"][c * 128:(c + 1) * 128, :], PAIRS)
                    yield f"G2{l}", g2
        for name, fn in steps():
            if stop is not None and name not in stop: continue
            fn()
        AR.end()
        P.finish()
        print("fused ops", P.n_ops, dict(P.ep))
    nc._used_ext = list(dr.used_ext)
    return nc

from concourse.bass_utils import run_bass_kernel_spmd

def dft_tables():
    n = np.arange(256, dtype=np.float64)
    ang = 2 * np.pi * np.outer(n, n) / 256.0
    cw2 = np.concatenate([np.cos(ang), np.sin(ang)], 1).astype(np.float32)
    n = np.arange(NLAT, dtype=np.int64)
    ang = 2 * np.pi * (np.outer(n, n) % NLAT).astype(np.float64) / NLAT
    return cw2, np.cos(ang).astype(np.float32), (-np.sin(ang)).astype(np.float32)

def rope_tables():
    rows = NLAT // 64
    row = np.repeat(np.arange(rows, dtype=np.float32), 64)
    col = np.tile(np.arange(64, dtype=np.float32), rows)
    inv = (10000.0 ** (-np.arange(0, 32, 2, dtype=np.float32) / 32)).astype(np.float32)
    ar = row[:, None] * inv; ac = col[:, None] * inv
    ang = np.concatenate([ar, ar, ac, ac], -1)
    cosT = np.ascontiguousarray(np.cos(ang).T.astype(np.float32)); sinT = np.ascontiguousarray(np.sin(ang).T.astype(np.float32))
    R = np.zeros((64, 64), np.float32)
    for i in range(16):
        R[16 + i, i] = -1; R[i, 16 + i] = 1; R[48 + i, 32 + i] = -1; R[32 + i, 48 + i] = 1
    return cosT, sinT, R

def dn_consts():
    p = np.arange(64)[:, None]; f = np.arange(64)[None, :]
    ple = (p <= f).astype(np.float32); pge = (p >= f).astype(np.float32)
    pgt = (p > f).astype(np.float32); plt = (p < f).astype(np.float32)
    return {"tri2": np.ascontiguousarray(np.stack([ple, pge], 1)), "ms2": np.ascontiguousarray(np.stack([pgt, plt], 1)),
            "i2": np.ascontiguousarray(np.stack([np.eye(64, dtype=np.float32)] * 2, 1)), "ident": np.eye(128, dtype=np.float32)}

_NC = []
STOP = None
def kernel(**inputs):
    inp = {k: np.asarray(v) for k, v in inputs.items()}
    B = inp["x"].shape[0]
    if not _NC:
        nc = bass.Bass("TRN2", target_bir_lowering=False, num_devices=8)
        build_fused(nc, STOP); _NC.append(nc)
    nc = _NC[0]
    cw2, cl, sln = dft_tables(); cosT, sinT, R = rope_tables(); dnc = dn_consts()
    shared = {"cw2": cw2, "cl": cl, "sln": sln, "cosT": cosT, "sinT": sinT, "Rm": R}
    shared.update(dnc)
    for l in range(2):
        shared[f"w_ada{l}"] = np.ascontiguousarray(inp["w_ada"][l])
        shared[f"b_ada{l}"] = np.ascontiguousarray(inp["b_ada"][l].reshape(96, 128).T)
        shared[f"g{l}"] = np.ascontiguousarray(inp["norm_g"][l].reshape(4, KC, 128).transpose(2, 0, 1))
        shared[f"w_gate{l}"] = np.ascontiguousarray(inp["w_in"][l][:, 5984:])
        shared[f"w_branch{l}"] = np.ascontiguousarray(inp["w_branch"][l]); shared[f"w_out{l}"] = np.ascontiguousarray(inp["w_out"][l])
        shared[f"w_ffn_in{l}"] = np.ascontiguousarray(inp["w_ffn_in"][l]); shared[f"w_ffn_out{l}"] = np.ascontiguousarray(inp["w_ffn_out"][l])
        shared[f"gq{l}"] = np.ascontiguousarray(inp["mla_q_norm_g"][l].reshape(4, 128).T)
        shared[f"gkv{l}"] = np.ascontiguousarray(inp["mla_kv_norm_g"][l].reshape(2, 128).T)
        shared[f"gnorm{l}"] = np.ascontiguousarray(np.tile(inp["dn_norm_g"][l][None], (64, 1)).astype(np.float32))
    percore = {}
    for r in range(2):
        heads = np.arange(r * 4, (r + 1) * 4)
        colsel = np.concatenate([heads, 8 + heads])
        pc = {}
        for l in range(2):
            w = inp["w_in"][l]
            cols = np.concatenate([np.arange(r * 512, (r + 1) * 512), 1024 + np.arange(r * 512, (r + 1) * 512), 2048 + np.arange(r * 512, (r + 1) * 512),
                                   np.arange(4128, 4960), 4960 + np.arange(r * 512, (r + 1) * 512), 3072 + np.arange(r * 512, (r + 1) * 512), 4096 + colsel, 4112 + colsel])
            assert cols.size == WINR
            pc[f"w_inr{l}"] = np.ascontiguousarray(w[:, cols])
            wuq = inp["w_uq"][l]; wukv = inp["w_ukv"][l]
            pc[f"wqn{l}"] = np.ascontiguousarray(np.concatenate([wuq[:, h * 192: h * 192 + 128] for h in heads], 1))
            pc[f"wqr{l}"] = np.ascontiguousarray(np.concatenate([wuq[:, h * 192 + 128: h * 192 + 192] for h in heads], 1))
            pc[f"wk{l}"] = np.ascontiguousarray(np.concatenate([wukv[:, h * 256: h * 256 + 128] for h in heads], 1))
            pc[f"wv{l}"] = np.ascontiguousarray(np.concatenate([wukv[:, h * 256 + 128: h * 256 + 256] for h in heads], 1))
            conv = inp["dn_conv"][l]
            cwl = [conv[:, qi * 1024 + h * 128: qi * 1024 + (h + 1) * 128].T for qi in range(3) for h in heads]
            pc[f"convw{l}"] = np.ascontiguousarray(np.stack(cwl, 1).astype(np.float32))
            pc[f"alog{l}"] = np.ascontiguousarray(np.tile(inp["dn_a_log"][l].reshape(16)[colsel][None], (64, 1)).astype(np.float32))
            pc[f"dtb{l}"] = np.ascontiguousarray(np.tile(inp["dn_dt_bias"][l].reshape(16)[colsel][None], (64, 1)).astype(np.float32))
        m = np.zeros((128, 2), np.float32); m[:, r] = 1.0
        pc["msk"] = m
        percore[r] = pc
    in_maps = []
    for i in range(8):
        b, r = i // 2, i % 2
        halves = [np.concatenate([inp["x"][b, q * NLH:(q + 1) * NLH], inp["ctx"][b, q * NCH2:(q + 1) * NCH2]], 0).T for q in range(2)]
        d = dict(shared); d.update(percore[r])
        d["xT_in"] = np.ascontiguousarray(np.stack([h_.reshape(KC, 128, NTH) for h_ in halves], 1).reshape(KC, 256, NTH))
        d["xT_own"] = np.ascontiguousarray(halves[r])
        cvec = np.stack([inp["c"][b], inp["c_ctx"]], -1)
        d["cT"] = np.ascontiguousarray(cvec.reshape(KC, 128, 2).transpose(1, 0, 2))
        in_maps.append(d)
    in_maps = [{k: d[k] for k in nc._used_ext} for d in in_maps]
    res = run_bass_kernel_spmd(nc, in_maps, core_ids=list(range(8))).results
    out = np.empty((B, NLAT, D), np.float32)
    for i in range(8):
        b, r = i // 2, i % 2
        out[b, r * NLH:(r + 1) * NLH] = res[i]["xo"].T
    return out
```

```python
import numpy as np
from contextlib import ExitStack
import concourse.bass as bass
import concourse.mybir as mybir
F32 = mybir.dt.float32; BF16 = mybir.dt.bfloat16; I32 = mybir.dt.int32
AF = mybir.ActivationFunctionType
ALU = mybir.AluOpType
AX = mybir.AxisListType

class Buf:
    __slots__ = ("name", "w", "r", "excl")
    def __init__(self, name="", excl=False):
        self.name = name
        self.excl = excl
        self.w = None
        self.r = {}

EPOCH = 12000
class Prog:
    ENG = ("pe", "dve", "act", "pool", "sp")
    def __init__(self, nc, es, n_dma_sems=12):
        self.nc = nc; self.es = es; self.es_global = es
        self.engobj = {"pe": nc.tensor, "dve": nc.vector, "act": nc.scalar, "pool": nc.gpsimd, "sp": nc.sync}
        self.streams = {e: [] for e in self.ENG}
        self.sems = {}
        self.cnt = {}
        self.cur = {}
        self.ep = {e: 0 for e in self.ENG}
        for e in self.ENG:
            self._new_epoch(e)
        self.seen = {e: {} for e in self.ENG}
        self.dma_keys = []
        for i in range(n_dma_sems):
            k = ("dma", i)
            self.sems[k] = es.enter_context(nc.semaphore(f"dma{i}"))
            self.cnt[k] = 0
            self.dma_keys.append(k)
        self.dma_rr = 0
        self.n_ops = 0
    def _new_epoch(self, e):
        k = (e, self.ep[e]); self.ep[e] += 1
        self.sems[k] = self.es_global.enter_context(self.nc.semaphore(f"s_{e}_{k[1]}"))
        self.cnt[k] = 0; self.cur[e] = k
    def _deps(self, reads, writes):
        deps = {}
        def need(k, c):
            if deps.get(k, 0) < c: deps[k] = c
        for b in reads:
            if b.w is not None: need(*b.w)
        for b in writes:
            if b.w is not None: need(*b.w)
            for k, c in b.r.items(): need(k, c)
        return deps
    def _emit_waits(self, e, deps, skip_key=None):
        seen = self.seen[e]
        for k, c in deps.items():
            if k == skip_key: continue
            if seen.get(k, 0) >= c: continue
            seen[k] = c
            sem = self.sems[k]
            self.streams[e].append(lambda eng, sem=sem, c=c: eng.wait_ge(sem, c))
    def _mark(self, key, c, reads, writes):
        for b in reads:
            if b.r.get(key, 0) < c: b.r[key] = c
        for b in writes:
            b.w = (key, c); b.r = {}
    def op(self, e, meth, *args, reads=(), writes=(), same_engine_sync=True, **kw):
        writes = list(writes) + [b for b in reads if b.excl]
        reads = [b for b in reads if not b.excl]
        deps = self._deps(reads, writes)
        key = self.cur[e]
        if self.cnt[key] >= EPOCH:
            self._new_epoch(e); key = self.cur[e]
        skip = None
        if e == "pe" or not same_engine_sync:
            deps = {k: c for k, c in deps.items() if k[0] != e}
        self._emit_waits(e, deps, skip)
        self.cnt[key] += 1
        c = self.cnt[key]; sem = self.sems[key]
        self.streams[e].append(lambda eng, meth=meth, args=args, kw=kw, sem=sem: getattr(eng, meth)(*args, **kw).then_inc(sem, 1))
        self._mark(key, c, reads, writes)
        self.n_ops += 1
    def dma(self, q, out, in_, reads=(), writes=(), **kw):
        deps = self._deps(reads, writes)
        k = self.dma_keys[self.dma_rr]; self.dma_rr = (self.dma_rr + 1) % len(self.dma_keys)
        if self.cnt[k] > 0: deps[k] = max(deps.get(k, 0), self.cnt[k])
        self._emit_waits(q, deps)
        self.cnt[k] += 16
        c = self.cnt[k]; sem = self.sems[k]
        self.streams[q].append(lambda eng, out=out, in_=in_, sem=sem, kw=kw: eng.dma_start(out=out, in_=in_, **kw).then_inc(sem, 16))
        self._mark(k, c, reads, writes)
        self.n_ops += 1
    def coll(self, kind, out, in_, groups, reads=(), writes=()):
        deps = self._deps(reads, writes)
        k = ("cc", 0)
        if k not in self.sems:
            self.sems[k] = self.es_global.enter_context(self.nc.semaphore("cc0"))
            self.cnt[k] = 0
            self.dma_keys.append(k)
        self.cnt[k] += 1
        self._emit_waits("pool", deps)
        sem = self.sems[k]
        self.streams["pool"].append(lambda eng, out=out, in_=in_, sem=sem: eng.collective_compute(kind, mybir.AluOpType.bypass, replica_groups=groups, ins=[in_.opt()], outs=[out.opt()]).then_inc(sem, 1))
        self._mark(k, self.cnt[k], reads, writes)
        self.n_ops += 1
    def barrier(self):
        deps = {k: c for k, c in self.cnt.items() if c > 0}
        for e in self.ENG:
            self._emit_waits(e, {k: c for k, c in deps.items() if k != self.cur[e]})
    def flush(self):
        nc = self.nc
        streams = self.streams
        self.streams = {e: [] for e in self.ENG}
        with nc.Block() as block:
            @block.sync
            def _(eng):
                for f in streams["sp"]: f(eng)
            @block.tensor
            def _(eng):
                for f in streams["pe"]: f(eng)
            @block.vector
            def _(eng):
                for f in streams["dve"]: f(eng)
            @block.scalar
            def _(eng):
                for f in streams["act"]: f(eng)
            @block.gpsimd
            def _(eng):
                for f in streams["pool"]: f(eng)
    def finish(self):
        deps = {k: self.cnt[k] for k in self.dma_keys if self.cnt[k] > 0}
        self._emit_waits("sp", deps)
        self.flush()

class Pool:
    def __init__(self, P, name, shape, dtype, n, psum=False):
        self.tiles = []
        for i in range(n):
            if psum:
                t = P.es.enter_context(P.nc.psum_tensor(f"pp_{name}{i}", shape, dtype))
            else:
                t = P.es.enter_context(P.nc.sbuf_tensor(f"sp_{name}{i}", shape, dtype))
            self.tiles.append((t, Buf(f"{name}{i}", excl=psum)))
        self.i = 0
    def get(self):
        t = self.tiles[self.i]; self.i = (self.i + 1) % len(self.tiles)
        return t

def sb(P, name, shape, dtype=F32):
    return P.es.enter_context(P.nc.sbuf_tensor("sb_" + name, shape, dtype)), Buf(name)
def ps(P, name, shape, dtype=F32):
    return P.es.enter_context(P.nc.psum_tensor("ps_" + name, shape, dtype)), Buf(name, excl=True)

import math
D = 2048; KC = 16; FF = 5632; FKC = 44
NCTX = 256; NLAT = 4096; NTOT = NCTX + NLAT; NCH = NTOT // 64
NLH = 2048; NCH2 = 128; NTH = NLH + NCH2
MLA_SCALE = 192 ** -0.5
NAR = 52000
PAIRS = [[0, 1], [2, 3], [4, 5], [6, 7]]
F32R = mybir.dt.float32r
def RR_(ap): return ap.bitcast(F32R)
FM_CHUNKS = [(c0, 128) for c0 in range(0, 2304, 128)] + [(2304, 64)] + [(2368 + i * 128, 128) for i in range(4)]
FM_ROWS = 2880; TM_COL0 = 2880; TM_W = 528; WINR = 3408

class Arena:
    def __init__(self, P):
        self.P = P
        self.banks = [(P.es_global.enter_context(P.nc.psum_tensor(f"bank{i}", [128, 512], F32)), Buf(f"bank{i}", excl=True)) for i in range(8)]
        self.ph = None; self.nph = 0
    def begin(self, nN, nR):
        assert nN + nR <= NAR, (nN, nR)
        self.end()
        self.ph = ExitStack(); self.nph += 1
        self.tN = self.ph.enter_context(self.P.nc.sbuf_tensor(f"arN{self.nph}", [128, max(nN, 8)], F32))
        self.tR = self.ph.enter_context(self.P.nc.sbuf_tensor(f"arR{self.nph}", [128, max(nR, 8)], F32))
        self.cap = {False: nN, True: nR}; self.off = {False: 0, True: 0}; self.bi = 0
    def end(self):
        self.P.barrier()
        if self.ph is not None:
            self.P.flush(); self.ph.close(); self.ph = None
    def sb(self, name, shape, R=False):
        n = int(np.prod(shape[1:]))
        assert self.off[R] + n <= self.cap[R], (name, R, self.off[R], n, self.cap[R])
        t = self.tR if R else self.tN
        v = t[0:shape[0], self.off[R]:self.off[R] + n]
        self.off[R] += n
        if len(shape) == 3: v = v.rearrange("p (a b) -> p a b", a=shape[1])
        elif len(shape) == 4: v = v.rearrange("p (a b c) -> p a b c", a=shape[1], b=shape[2])
        return v, Buf(name)
    def pool(self, name, shape, n, R=False):
        return RR([self.sb(f"{name}{i}", shape, R) for i in range(n)])
    def pspool(self, n):
        b = self.banks[self.bi:self.bi + n]; assert len(b) == n; self.bi += n
        return RR(b)

class RR:
    def __init__(self, tiles): self.tiles = tiles; self.i = 0
    def get(self):
        t = self.tiles[self.i]; self.i = (self.i + 1) % len(self.tiles); return t

def common(P, AR):
    class E: pass
    E = E()
    E.ones, E.onesb = AR.sb("ones", [128, 128]); P.op("pool", "memset", E.ones[:], 1.0, writes=[E.onesb])
    E.eps = {}
    for dim, val in ((2048, 2048e-6), (512, 512e-6), (256, 256e-6), (1, 1e-6)):
        t, b = AR.sb(f"eps{dim}", [128, 1]); P.op("pool", "memset", t[:], val, writes=[b]); E.eps[dim] = (t, b)
    E.sqp = AR.pool("sq", [128, 512], 2)
    E.ssp, E.sspb = AR.pspool(1).get()
    E.rstd, E.rstdb = AR.sb("rstd", [128, 512])
    E.ev = 0
    return E

def rms_rstd(P, E, src, srcb, nch, tb, dim):
    for c in range(nch):
        sq, sqb = E.sqp.get()
        P.op("act", "activation", out=sq[:, 0:tb], in_=src[:, c, 0:tb], func=AF.Square, reads=[srcb], writes=[sqb])
        P.op("pe", "matmul", E.ssp[:, 0:tb], E.ones[:], sq[:, 0:tb], start=(c == 0), stop=(c == nch - 1), reads=[sqb, E.onesb], writes=[E.sspb])
    eb = E.eps[dim]
    P.op("act", "activation", out=E.rstd[:, 0:tb], in_=E.ssp[:, 0:tb], func=AF.Sqrt, bias=eb[0][:, 0:1], reads=[E.sspb, eb[1]], writes=[E.rstdb])
    P.op("dve", "reciprocal", E.rstd[:, 0:tb], E.rstd[:, 0:tb], reads=[E.rstdb], writes=[E.rstdb])

def evac(P, E, dst, src, reads, writes):
    if E.ev % 2 == 0: P.op("dve", "tensor_copy", dst, src, reads=reads, writes=writes)
    else: P.op("act", "activation", out=dst, in_=src, func=AF.Copy, reads=reads, writes=writes)
    E.ev += 1

def ld(P, AR, name, src, shape, q="sp"):
    t, b = AR.sb(name, shape); P.dma(q, t[:], src, writes=[b]); return t, b

def emit_mod(P, AR, E, wget, pmod, cT_d, wada_d, bada_d, nchunks, cpt=2):
    ct, cb = ld(P, AR, "cT", cT_d, [128, KC, 2]); bt, bb = ld(P, AR, "bada", bada_d, [128, 96])
    sc, scb = AR.sb("sc", [128, KC, 2]); mod_t, mod_b = AR.sb("mod", [128, 96, 2])
    P.op("act", "activation", out=sc[:], in_=ct[:], func=AF.Silu, reads=[cb], writes=[scb])
    for n0 in range(0, nchunks, cpt):
        wt, wb = wget()
        P.dma("sp", wt[:, :, 0:cpt * 128], wada_d[:, n0 * 128:(n0 + cpt) * 128].rearrange("(c p) n -> p c n", p=128), writes=[wb])
        for gi in range(cpt):
            n = n0 + gi
            pt, pb = pmod.get()
            for k in range(KC):
                P.op("pe", "matmul", pt[:, 0:2], wt[:, k, gi * 128:(gi + 1) * 128], sc[:, k, :], start=(k == 0), stop=(k == KC - 1), reads=[wb, scb], writes=[pb])
            P.op("dve", "tensor_scalar", mod_t[:, n, :], pt[:, 0:2], bt[:, n:n + 1], None, ALU.add, reads=[pb, bb], writes=[mod_b])
    return mod_t, mod_b

def yl_write(P, YL, row0, col0, width, src, reads):
    c = col0
    while c < col0 + width:
        blk, off = c // 256, c % 256
        w = min(256 - off, col0 + width - c)
        P.dma("act", YL[blk, row0:row0 + 128, off:off + w], src[:, c - col0:c - col0 + w], reads=reads)
        c += w

def emit_A(P, AR, dr, l):
    AR.begin(21500, 24576); E = common(P, AR)
    xsrc = dr["xT_in"] if l == 0 else dr["XG"]
    wpool = AR.pool("w", [128, KC, 512], 2, R=True)
    pmm = AR.pspool(5)
    wmod = AR.pool("wm", [128, KC, 256], 2)
    mod_t, mod_b = emit_mod(P, AR, E, lambda: wmod.get(), AR.pspool(2), dr["cT"], dr[f"w_ada{l}"], dr[f"b_ada{l}"], 32)
    g0, g0b = ld(P, AR, "g0", dr[f"g{l}"][:, 0, :], [128, KC])
    At, Ab = AR.sb("A", [128, KC, 2])
    for j in range(2):
        P.op("dve", "tensor_scalar", At[:, :, j], mod_t[:, KC:2 * KC, j], 1.0, math.sqrt(D), ALU.add, ALU.mult, reads=[mod_b], writes=[Ab])
        P.op("dve", "tensor_tensor", At[:, :, j], At[:, :, j], g0[:], ALU.mult, reads=[Ab, g0b], writes=[Ab])
    xs, xsb = AR.sb("xs", [128, KC, 512]); hs, hsb = AR.sb("hs", [128, KC, 512], R=True)
    opool = AR.pool("o", [128, 512], 3)
    win = dr[f"w_inr{l}"]; PF = dr["PF"]; PT = dr["PT"]
    blocks = []
    for r in range(2):
        for t0 in range(0, NLH, 512): blocks.append((r, t0, 512, 0, NCTX + r * NLH + t0))
        blocks.append((r, NLH, NCH2, 1, r * NCH2))
    for (r, t0, tb, j, dst0) in blocks:
        P.dma("sp", xs[:, :, 0:tb], xsrc.rearrange("c (r p) t -> r p c t", r=2)[r][:, :, t0:t0 + tb], writes=[xsb])
        rms_rstd(P, E, xs, xsb, KC, tb, 2048)
        for c in range(KC):
            P.op("dve", "scalar_tensor_tensor", RR_(hs[:, c, 0:tb]), xs[:, c, 0:tb], At[:, c, j:j + 1], E.rstd[:, 0:tb], ALU.mult, ALU.mult, reads=[xsb, Ab, E.rstdb], writes=[hsb])
            P.op("act", "activation", out=RR_(hs[:, c, 0:tb]), in_=hs[:, c, 0:tb], func=AF.Identity, bias=mod_t[:, c, j:j + 1], reads=[hsb, mod_b], writes=[hsb])
        row = 0; wt = None; wcol0 = None
        for (c0, wd) in FM_CHUNKS:
            if wt is None or not (wcol0 <= c0 and c0 + wd <= wcol0 + 512):
                wt, wb = wpool.get(); wcol0 = c0
                ncols = min(512, FM_ROWS - c0)
                P.dma("pool", RR_(wt[:, :, 0:ncols]), win[:, c0:c0 + ncols].rearrange("(c p) n -> p c n", p=128), writes=[wb])
            pt, pb = pmm.get(); off = c0 - wcol0
            for k in range(KC):
                P.op("pe", "matmul", pt[0:wd, 0:tb], RR_(wt[:, k, off:off + wd]), RR_(hs[:, k, 0:tb]), start=(k == 0), stop=(k == KC - 1), reads=[wb, hsb], writes=[pb])
            ot, ob = opool.get()
            evac(P, E, ot[0:wd, 0:tb], pt[0:wd, 0:tb], [pb], [ob])
            P.dma("act", PF[row:row + wd, dst0:dst0 + tb], ot[0:wd, 0:tb], reads=[ob])
            row += wd
        for n0 in range(0, TM_W, 512):
            nw = min(512, TM_W - n0)
            wt, wb = wpool.get()
            P.dma("pool", RR_(wt[:, :, 0:nw]), win[:, TM_COL0 + n0:TM_COL0 + n0 + nw].rearrange("(c p) n -> p c n", p=128), writes=[wb])
            for ts in range(tb // 128):
                pt, pb = pmm.get()
                for k in range(KC):
                    P.op("pe", "matmul", pt[:, 0:nw], RR_(hs[:, k, ts * 128:(ts + 1) * 128]), RR_(wt[:, k, 0:nw]), start=(k == 0), stop=(k == KC - 1), reads=[wb, hsb], writes=[pb])
                ot, ob = opool.get()
                evac(P, E, ot[:, 0:nw], pt[:, 0:nw], [pb], [ob])
                P.dma("act", PT[dst0 + ts * 128:dst0 + (ts + 1) * 128, n0:n0 + nw], ot[:, 0:nw], reads=[ob])

def emit_C(P, AR, dr, l, last):
    AR.begin(15100, 36864); E = common(P, AR)
    TB = 256
    xown = dr["xT_own"] if l == 0 else dr["XL2"]
    YG = dr["YG"]
    wpool = AR.pool("w", [128, 5632], 2, R=True)
    wmodc = AR.pool("wm", [128, KC, 128], 1)
    def wget():
        t, b = wpool.get(); return t[:, 0:KC * 256].rearrange("p (c n) -> p c n", c=KC), b
    pmm = AR.pspool(5)
    mod_t, mod_b = emit_mod(P, AR, E, lambda: wmodc.get(), AR.pspool(2), dr["cT"], dr[f"w_ada{l}"], dr[f"b_ada{l}"], 96, cpt=1)
    g, gb = ld(P, AR, "g", dr[f"g{l}"], [128, 4, KC])
    msk, mskb = ld(P, AR, "msk", dr["msk"], [128, 2])
    S, Sb = AR.sb("S", [128, 4, KC, 2])
    for j in range(2):
        for i, (mi, plus1) in enumerate([(1, True), (2, False), (4, True), (5, False)]):
            P.op("dve", "tensor_scalar", S[:, i, :, j], mod_t[:, mi * KC:(mi + 1) * KC, j], 1.0 if plus1 else 0.0, math.sqrt(D), ALU.add, ALU.mult, reads=[mod_b], writes=[Sb])
            P.op("dve", "tensor_tensor", S[:, i, :, j], S[:, i, :, j], g[:, i, :], ALU.mult, reads=[Sb, gb], writes=[Sb])
    xs, xsb = AR.sb("xs", [128, KC, TB]); hs, hsb = AR.sb("hs", [128, KC, TB], R=True)
    ys, ysb = AR.sb("ys", [128, 24, TB], R=True); mg, mgb = AR.sb("mg", [128, KC, TB], R=True); yl, ylb = AR.sb("yl", [128, KC, TB])
    act, actb = AR.sb("act", [128, FKC, TB], R=True)
    tmpp = AR.pool("tmp", [128, TB], 6); selp = AR.pool("sel", [128, TB], 4)
    wg = dr[f"w_gate{l}"]; wbr = dr[f"w_branch{l}"]; wout = dr[f"w_out{l}"]; wfi = dr[f"w_ffn_in{l}"]; wfo = dr[f"w_ffn_out{l}"]
    blocks = [(t0, min(TB, NLH - t0), 0) for t0 in range(0, NLH, TB)]
    if not last: blocks += [(NLH, NCH2, 1)]
    xTr = xown.rearrange("(c p) t -> p c t", p=128)
    outd = dr["xo"] if last else dr["XL2"]
    outr = outd.rearrange("(c p) t -> p c t", p=128)
    def normmod(src, srcb, dst, dstb, si, bi, j, tb):
        rms_rstd(P, E, src, srcb, KC, tb, 2048)
        for c in range(KC):
            P.op("dve", "scalar_tensor_tensor", RR_(dst[:, c, 0:tb]), src[:, c, 0:tb], S[:, si, c, j:j + 1], E.rstd[:, 0:tb], ALU.mult, ALU.mult, reads=[srcb, Sb, E.rstdb], writes=[dstb])
            P.op("act", "activation", out=RR_(dst[:, c, 0:tb]), in_=dst[:, c, 0:tb], func=AF.Identity, bias=mod_t[:, bi * KC + c, j:j + 1], reads=[dstb, mod_b], writes=[dstb])
    def resid(src, srcb, si, j, tb):
        rms_rstd(P, E, src, srcb, KC, tb, 2048)
        for c in range(KC):
            tt, ttb = tmpp.get()
            P.op("dve", "scalar_tensor_tensor", tt[:, 0:tb], src[:, c, 0:tb], S[:, si, c, j:j + 1], E.rstd[:, 0:tb], ALU.mult, ALU.mult, reads=[srcb, Sb, E.rstdb], writes=[ttb])
            P.op("dve", "tensor_tensor", xs[:, c, 0:tb], xs[:, c, 0:tb], tt[:, 0:tb], ALU.add, reads=[xsb, ttb], writes=[xsb])
    for (t0, tb, j) in blocks:
        P.dma("sp", xs[:, :, 0:tb], xTr[:, :, t0:t0 + tb], writes=[xsb])
        for i in range(3):
            for gg in range(2):
                for c in range(4):
                    ch = i * 8 + gg * 4 + c
                    r0 = gg * 1536 + i * 512 + c * 128
                    cols = [(NCTX + r * NLH + t0) if j == 0 else (r * NCH2) for r in range(2)]
                    s0, s0b = selp.get(); st, stb = selp.get()
                    P.dma("act", s0[:, 0:tb], YG[cols[0] // 256, r0:r0 + 128, cols[0] % 256:cols[0] % 256 + tb], writes=[s0b])
                    P.dma("act", st[:, 0:tb], YG[cols[1] // 256, r0:r0 + 128, cols[1] % 256:cols[1] % 256 + tb], writes=[stb])
                    P.op("dve", "tensor_scalar", s0[:, 0:tb], s0[:, 0:tb], msk[:, 0:1], None, ALU.mult, reads=[s0b, mskb], writes=[s0b])
                    P.op("dve", "scalar_tensor_tensor", RR_(ys[:, ch, 0:tb]), st[:, 0:tb], msk[:, 1:2], s0[:, 0:tb], ALU.mult, ALU.add, reads=[stb, mskb, s0b], writes=[ysb])
        normmod(xs, xsb, hs, hsb, 0, 0, j, tb)
        for n in range(KC):
            for i in range(3):
                wt, wb = wpool.get()
                wgv = wt[:, 0:KC * 128].rearrange("p (c n) -> p c n", c=KC)
                wbv = wt[:, KC * 128:KC * 128 + 8 * 128].rearrange("p (c n) -> p c n", c=8)
                P.dma("pool", RR_(wgv), wg[:, i * D + n * 128: i * D + (n + 1) * 128].rearrange("(c p) n -> p c n", p=128), writes=[wb])
                P.dma("pool", RR_(wbv), wbr[i, :, n * 128:(n + 1) * 128].rearrange("(c p) n -> p c n", p=128), writes=[wb])
                p1, p1b = pmm.get()
                for k in range(KC):
                    P.op("pe", "matmul", p1[:, 0:tb], RR_(wgv[:, k, :]), RR_(hs[:, k, 0:tb]), start=(k == 0), stop=(k == KC - 1), reads=[wb, hsb], writes=[p1b])
                p2, p2b = pmm.get()
                for k in range(8):
                    P.op("pe", "matmul", p2[:, 0:tb], RR_(wbv[:, k, :]), RR_(ys[:, i * 8 + k, 0:tb]), start=(k == 0), stop=(k == 7), reads=[wb, ysb], writes=[p2b])
                tt, ttb = tmpp.get()
                P.op("act", "activation", out=tt[:, 0:tb], in_=p1[:, 0:tb], func=AF.Sigmoid, reads=[p1b], writes=[ttb])
                if i == 0:
                    P.op("dve", "tensor_tensor", RR_(mg[:, n, 0:tb]), tt[:, 0:tb], p2[:, 0:tb], ALU.mult, reads=[ttb, p2b], writes=[mgb])
                else:
                    P.op("dve", "tensor_tensor", tt[:, 0:tb], tt[:, 0:tb], p2[:, 0:tb], ALU.mult, reads=[ttb, p2b], writes=[ttb])
                    P.op("dve", "tensor_tensor", RR_(mg[:, n, 0:tb]), mg[:, n, 0:tb], tt[:, 0:tb], ALU.add, reads=[ttb, mgb], writes=[mgb])
        for n0 in range(0, KC, 2):
            wv, wb = wget()
            P.dma("pool", RR_(wv), wout[:, n0 * 128:(n0 + 2) * 128].rearrange("(c p) n -> p c n", p=128), writes=[wb])
            for gi in range(2):
                pt, pb = pmm.get()
                for k in range(KC):
                    P.op("pe", "matmul", pt[:, 0:tb], RR_(wv[:, k, gi * 128:(gi + 1) * 128]), RR_(mg[:, k, 0:tb]), start=(k == 0), stop=(k == KC - 1), reads=[wb, mgb], writes=[pb])
                P.op("act", "activation", out=yl[:, n0 + gi, 0:tb], in_=pt[:, 0:tb], func=AF.Copy, reads=[pb], writes=[ylb])
        resid(yl, ylb, 1, j, tb)
        normmod(xs, xsb, hs, hsb, 2, 3, j, tb)
        for h0 in range(0, FKC, 4):
            tts = []
            for part in range(2):
                pbs = [pmm.get() for _ in range(4)]
                for kh in range(2):
                    wt, wb = wpool.get()
                    wv = wt[:, 0:8 * 512].rearrange("p (c n) -> p c n", c=8)
                    P.dma("pool", RR_(wv), wfi[kh * 1024:(kh + 1) * 1024, part * FF + h0 * 128:part * FF + (h0 + 4) * 128].rearrange("(c p) n -> p c n", p=128), writes=[wb])
                    for gi in range(4):
                        for k in range(8):
                            P.op("pe", "matmul", pbs[gi][0][:, 0:tb], RR_(wv[:, k, gi * 128:(gi + 1) * 128]), RR_(hs[:, kh * 8 + k, 0:tb]), start=(kh == 0 and k == 0), stop=(kh == 1 and k == 7), reads=[wb, hsb], writes=[pbs[gi][1]])
                for gi in range(4):
                    if part == 0:
                        tt, ttb = tmpp.get()
                        P.op("act", "activation", out=tt[:, 0:tb], in_=pbs[gi][0][:, 0:tb], func=AF.Silu, reads=[pbs[gi][1]], writes=[ttb])
                        tts.append((tt, ttb))
                    else:
                        tt, ttb = tts[gi]
                        P.op("dve", "tensor_tensor", RR_(act[:, h0 + gi, 0:tb]), tt[:, 0:tb], pbs[gi][0][:, 0:tb], ALU.mult, reads=[ttb, pbs[gi][1]], writes=[actb])
        for ng in range(0, KC, 4):
            pbs = [pmm.get() for _ in range(4)]
            for kq in range(4):
                wt, wb = wpool.get()
                wv = wt[:, 0:11 * 512].rearrange("p (c n) -> p c n", c=11)
                P.dma("pool", RR_(wv), wfo[kq * 1408:(kq + 1) * 1408, ng * 128:(ng + 4) * 128].rearrange("(c p) n -> p c n", p=128), writes=[wb])
                for gi in range(4):
                    for k in range(11):
                        P.op("pe", "matmul", pbs[gi][0][:, 0:tb], RR_(wv[:, k, gi * 128:(gi + 1) * 128]), RR_(act[:, kq * 11 + k, 0:tb]), start=(kq == 0 and k == 0), stop=(kq == 3 and k == 10), reads=[wb, actb], writes=[pbs[gi][1]])
            for gi in range(4):
                P.op("act", "activation", out=yl[:, ng + gi, 0:tb], in_=pbs[gi][0][:, 0:tb], func=AF.Copy, reads=[pbs[gi][1]], writes=[ylb])
        resid(yl, ylb, 3, j, tb)
        P.dma("sp", outr[:, :, t0:t0 + tb], xs[:, :, 0:tb], reads=[xsb])

def emit_FN(P, AR, dr, last):
    AR.begin(4000, 39424); E = common(P, AR)
    PF = dr["PF"]; YL = dr["YL"]; cld = dr["cl"]; sld = dr["sln"]
    cw2, cw2b = AR.sb("cw2s", [128, 2, 512], R=True); P.dma("pool", RR_(cw2[:]), dr["cw2"].rearrange("(c p) n -> p c n", p=128), writes=[cw2b])
    us, usb = AR.sb("us", [128, 2, NTOT], R=True); Zs, Zsb = AR.sb("Zs", [128, 32, 512], R=True); Zc, Zcb = AR.sb("Zc", [128, 2, 512], R=True)
    cp = AR.pool("ct", [128, 4, 512], 3, R=True); spn = AR.pool("st", [128, 4, 512], 3, R=True)
    pz = AR.pspool(2); py = AR.pspool(4); op = AR.pool("o", [128, 512], 3)
    uTr = PF[2368:2880, :].rearrange("(c p) t -> p c t", p=128)
    for g in range(2):
        P.dma("pool", RR_(us[:]), uTr[:, 2 * g:2 * g + 2, :], writes=[usb])
        for t in range(32):
            pt, pb = pz.get()
            for kc in range(2):
                P.op("pe", "matmul", pt[:], RR_(us[:, kc, NCTX + t * 128:NCTX + (t + 1) * 128]), RR_(cw2[:, kc, :]), start=(kc == 0), stop=(kc == 1), reads=[usb, cw2b], writes=[pb])
            evac(P, E, RR_(Zs[:, t, :]), pt[:], [pb], [Zsb])
        if not last:
            for t in range(2):
                pt, pb = pz.get()
                for kc in range(2):
                    P.op("pe", "matmul", pt[:], RR_(us[:, kc, t * 128:(t + 1) * 128]), RR_(cw2[:, kc, :]), start=(kc == 0), stop=(kc == 1), reads=[usb, cw2b], writes=[pb])
                P.op("dve", "tensor_copy", RR_(Zc[:, t, 0:256]), pt[:, 0:256], reads=[pb], writes=[Zcb])
                P.op("dve", "tensor_scalar", RR_(Zc[:, t, 256:512]), pt[:, 256:512], -1.0, None, ALU.mult, reads=[pb], writes=[Zcb])
            for ch in range(2):
                pt, pb = py.get(); i = 0
                for t in range(2):
                    for part in range(2):
                        P.op("pe", "matmul", pt[:, 0:256], RR_(Zc[:, t, part * 256 + ch * 128: part * 256 + (ch + 1) * 128]), RR_(cw2[:, t, part * 256:(part + 1) * 256]), start=(i == 0), stop=(i == 3), reads=[Zcb, cw2b], writes=[pb])
                        i += 1
                ot, ob = op.get()
                P.op("act", "activation", out=ot[:, 0:256], in_=pt[:, 0:256], func=AF.Copy, scale=1.0 / 256.0, reads=[pb], writes=[ob])
                yl_write(P, YL, 512 + g * 256 + ch * 128, 0, 256, ot, [ob])
        for o in range(8):
            pts = [py.get(), py.get()]
            for t0 in range(0, 32, 4):
                ct, cb = cp.get(); st, stb = spn.get()
                P.dma("pool", RR_(ct[:]), cld[t0 * 128:(t0 + 4) * 128, o * 512:(o + 1) * 512].rearrange("(t p) n -> p t n", p=128), writes=[cb])
                P.dma("pool", RR_(st[:]), sld[t0 * 128:(t0 + 4) * 128, o * 512:(o + 1) * 512].rearrange("(t p) n -> p t n", p=128), writes=[stb])
                for tt in range(4):
                    t = t0 + tt
                    for ch in range(2):
                        pt, pb = pts[ch]
                        P.op("pe", "matmul", pt[:], RR_(Zs[:, t, ch * 128:(ch + 1) * 128]), RR_(ct[:, tt, :]), start=(t == 0), stop=False, reads=[Zsb, cb], writes=[pb])
                        P.op("pe", "matmul", pt[:], RR_(Zs[:, t, 256 + ch * 128:256 + (ch + 1) * 128]), RR_(st[:, tt, :]), start=False, stop=(t == 31), reads=[Zsb, stb], writes=[pb])
            for ch in range(2):
                pt, pb = pts[ch]; ot, ob = op.get()
                if ch == 0: P.op("act", "activation", out=ot[:], in_=pt[:], func=AF.Copy, scale=1.0 / 1024.0, reads=[pb], writes=[ob])
                else: P.op("dve", "tensor_scalar", ot[:], pt[:], 1.0 / 1024.0, None, ALU.mult, reads=[pb], writes=[ob])
                yl_write(P, YL, 512 + g * 256 + ch * 128, NCTX + o * 512, 512, ot, [ob])

def emit_MLA(P, AR, dr, l, last, NH=4):
    AR.begin(28000, 17408); E = common(P, AR)
    PF = dr["PF"]; YL = dr["YL"]
    cin = PF[1536:2368, :]
    gq, gqb = ld(P, AR, "gq_s", dr[f"gq{l}"], [128, 4]); gkv, gkvb = ld(P, AR, "gkv_s", dr[f"gkv{l}"], [128, 2])
    P.op("dve", "tensor_scalar", gq[:], gq[:], math.sqrt(512.0), None, ALU.mult, reads=[gqb], writes=[gqb])
    P.op("dve", "tensor_scalar", gkv[:], gkv[:], math.sqrt(256.0), None, ALU.mult, reads=[gkvb], writes=[gkvb])
    wqn, wqnb = ld(P, AR, "wqn_s", dr[f"wqn{l}"].rearrange("(c p) n -> p c n", p=128), [128, 4, NH * 128])
    wqr, wqrb = ld(P, AR, "wqr_s", dr[f"wqr{l}"].rearrange("(c p) n -> p c n", p=128), [128, 4, NH * 64])
    wk, wkb = ld(P, AR, "wk_s", dr[f"wk{l}"].rearrange("(c p) n -> p c n", p=128), [128, 2, NH * 128])
    wv, wvb = ld(P, AR, "wv_s", dr[f"wv{l}"].rearrange("(c p) n -> p c n", p=128), [128, 2, NH * 128])
    Rm, Rmb = ld(P, AR, "R_s", dr["Rm"], [64, 64]); ident, identb = ld(P, AR, "id_s", dr["ident"], [128, 128])
    Qn, Qnb = AR.sb("Qn", [128, NTOT], R=True); Qr, Qrb = AR.sb("Qr", [64, NTOT], R=True)
    Kn, Knb = AR.sb("Kn", [128, NTOT], R=True); Kr, Krb = AR.sb("Kr", [64, NTOT], R=True)
    V, Vb = AR.sb("V", [128, NTOT // 128, 128]); Ss, Ssb = AR.sb("Ss", [128, NTOT])
    cb_t, cb_b = AR.sb("cblk", [128, 6, 512]); krb_t, krb_b = AR.sb("krblk", [64, 512])
    cqn, cqnb = AR.sb("cqn", [128, 4, 512]); ckvn, ckvnb = AR.sb("ckvn", [128, 2, 512])
    cs_t, cs_b = AR.sb("cosb", [64, 512]); sn_t, sn_b = AR.sb("sinb", [64, 512])
    tq, tqb = AR.sb("tq", [64, 512]); t2, t2b = AR.sb("t2", [64, 512])
    pmm = AR.pspool(5); po_p = AR.pspool(2)
    PTp = AR.pool("PT", [128, 512], 2); osb_p = AR.pool("osb", [128, 128], 2); oT_p = AR.pool("oT", [128, 128], 2); st_p = AR.pool("stat", [128, 4], 2)
    cinr = cin[0:768, :].rearrange("(c p) t -> p c t", p=128)
    cos_d = dr["cosT"]; sin_d = dr["sinT"]
    blocks = [(0, 256, False)] + [(NCTX + i * 512, 512, True) for i in range(8)]
    for h in range(NH):
        for (t0, tb, lat) in blocks:
            P.dma("sp", cb_t[:, :, 0:tb], cinr[:, :, t0:t0 + tb], writes=[cb_b])
            P.dma("sp", krb_t[:, 0:tb], cin[768:832, t0:t0 + tb], writes=[krb_b])
            if lat:
                P.dma("act", cs_t[:, 0:tb], cos_d[:, t0 - NCTX:t0 - NCTX + tb], writes=[cs_b])
                P.dma("act", sn_t[:, 0:tb], sin_d[:, t0 - NCTX:t0 - NCTX + tb], writes=[sn_b])
            rms_rstd(P, E, cb_t[:, 0:4, :], cb_b, 4, tb, 512)
            for c in range(4):
                P.op("dve", "scalar_tensor_tensor", cqn[:, c, 0:tb], cb_t[:, c, 0:tb], gq[:, c:c + 1], E.rstd[:, 0:tb], ALU.mult, ALU.mult, reads=[cb_b, gqb, E.rstdb], writes=[cqnb])
            rms_rstd(P, E, cb_t[:, 4:6, :], cb_b, 2, tb, 256)
            for c in range(2):
                P.op("dve", "scalar_tensor_tensor", ckvn[:, c, 0:tb], cb_t[:, 4 + c, 0:tb], gkv[:, c:c + 1], E.rstd[:, 0:tb], ALU.mult, ALU.mult, reads=[cb_b, gkvb, E.rstdb], writes=[ckvnb])
            pt, pb = pmm.get()
            for kc in range(4):
                P.op("pe", "matmul", pt[:, 0:tb], wqn[:, kc, h * 128:(h + 1) * 128], cqn[:, kc, 0:tb], start=(kc == 0), stop=(kc == 3), reads=[wqnb, cqnb], writes=[pb])
            evac(P, E, RR_(Qn[:, t0:t0 + tb]), pt[:, 0:tb], [pb], [Qnb])
            pt, pb = pmm.get()
            for kc in range(2):
                P.op("pe", "matmul", pt[:, 0:tb], wk[:, kc, h * 128:(h + 1) * 128], ckvn[:, kc, 0:tb], start=(kc == 0), stop=(kc == 1), reads=[wkb, ckvnb], writes=[pb])
            evac(P, E, RR_(Kn[:, t0:t0 + tb]), pt[:, 0:tb], [pb], [Knb])
            for ts in range(tb // 128):
                pt, pb = pmm.get()
                for kc in range(2):
                    P.op("pe", "matmul", pt[:, 0:128], ckvn[:, kc, ts * 128:(ts + 1) * 128], wv[:, kc, h * 128:(h + 1) * 128], start=(kc == 0), stop=(kc == 1), reads=[wvb, ckvnb], writes=[pb])
                evac(P, E, V[:, t0 // 128 + ts, :], pt[:, 0:128], [pb], [Vb])
            pt, pb = pmm.get()
            for kc in range(4):
                P.op("pe", "matmul", pt[0:64, 0:tb], wqr[:, kc, h * 64:(h + 1) * 64], cqn[:, kc, 0:tb], start=(kc == 0), stop=(kc == 3), reads=[wqrb, cqnb], writes=[pb])
            def rope(dst, dstb, src, srcb):
                p2, p2b = pmm.get()
                P.op("pe", "matmul", p2[0:64, 0:tb], Rm[:, :], src, start=True, stop=True, reads=[Rmb, srcb], writes=[p2b])
                P.op("dve", "tensor_tensor", t2[:, 0:tb], p2[0:64, 0:tb], sn_t[:, 0:tb], ALU.mult, reads=[p2b, sn_b], writes=[t2b])
                P.op("pool", "tensor_tensor", RR_(dst[:, t0:t0 + tb]), src, cs_t[:, 0:tb], ALU.mult, reads=[srcb, cs_b], writes=[dstb])
                P.op("dve", "tensor_tensor", RR_(dst[:, t0:t0 + tb]), dst[:, t0:t0 + tb], t2[:, 0:tb], ALU.add, reads=[dstb, t2b], writes=[dstb])
            if lat:
                evac(P, E, tq[:, 0:tb], pt[0:64, 0:tb], [pb], [tqb])
                rope(Qr, Qrb, tq[:, 0:tb], tqb)
                rope(Kr, Krb, krb_t[:, 0:tb], krb_b)
            else:
                evac(P, E, RR_(Qr[:, t0:t0 + tb]), pt[0:64, 0:tb], [pb], [Qrb])
                P.op("pool", "tensor_copy", RR_(Kr[:, t0:t0 + tb]), krb_t[:, 0:tb], reads=[krb_b], writes=[Krb])
        qtiles = [(NCTX + qt * 128, 0, NTOT) for qt in range(32)]
        if not last: qtiles = [(qt * 128, 0, NCTX) for qt in range(2)] + qtiles
        for (q0, k0, k1) in qtiles:
            nk = k1 - k0
            for kb0 in range(k0, k1, 512):
                kw = min(512, k1 - kb0)
                pt, pb = pmm.get()
                P.op("pe", "matmul", pt[:, 0:kw], RR_(Qn[:, q0:q0 + 128]), RR_(Kn[:, kb0:kb0 + kw]), start=True, stop=False, reads=[Qnb, Knb], writes=[pb])
                P.op("pe", "matmul", pt[:, 0:kw], RR_(Qr[:, q0:q0 + 128]), RR_(Kr[:, kb0:kb0 + kw]), start=False, stop=True, reads=[Qrb, Krb], writes=[pb])
                evac(P, E, Ss[:, kb0:kb0 + kw], pt[:, 0:kw], [pb], [Ssb])
            stt, stb = st_p.get()
            P.op("dve", "tensor_reduce", stt[:, 0:1], Ss[:, k0:k1], AX.X, ALU.max, reads=[Ssb], writes=[stb])
            P.op("dve", "tensor_scalar", stt[:, 1:2], stt[:, 0:1], -MLA_SCALE, None, ALU.mult, reads=[stb], writes=[stb])
            P.op("pool", "memset", stt[:, 2:3], 0.0, writes=[stb])
            P.op("act", "activation", out=Ss[:, k0:k1], in_=Ss[:, k0:k1], func=AF.Exp, scale=MLA_SCALE, bias=stt[:, 1:2], accum_out=stt[:, 2:3], reads=[Ssb, stb], writes=[Ssb, stb])
            P.op("dve", "reciprocal", stt[:, 3:4], stt[:, 2:3], reads=[stb], writes=[stb])
            po, pob = po_p.get()
            ntile = nk // 128
            for g0 in range(0, ntile, 4):
                gn = min(4, ntile - g0)
                ptp, ptpb = pmm.get()
                for i in range(gn):
                    kt = k0 // 128 + g0 + i
                    P.op("pe", "transpose", ptp[:, i * 128:(i + 1) * 128], Ss[:, kt * 128:(kt + 1) * 128], ident[:], reads=[Ssb, identb], writes=[ptpb])
                PT, PTb = PTp.get()
                evac(P, E, PT[:, 0:gn * 128], ptp[:, 0:gn * 128], [ptpb], [PTb])
                for i in range(gn):
                    kt = k0 // 128 + g0 + i
                    P.op("pe", "matmul", po[:, 0:128], PT[:, i * 128:(i + 1) * 128], V[:, kt, :], start=(g0 + i == 0), stop=(g0 + i == ntile - 1), reads=[PTb, Vb], writes=[pob])
            ot, ob = osb_p.get()
            P.op("dve", "tensor_scalar", ot[:], po[:, 0:128], stt[:, 3:4], None, ALU.mult, reads=[pob, stb], writes=[ob])
            pq, pqb = pmm.get()
            P.op("pe", "transpose", pq[:, 0:128], ot[:], ident[:], reads=[ob, identb], writes=[pqb])
            oT, oTb = oT_p.get()
            evac(P, E, oT[:], pq[:, 0:128], [pqb], [oTb])
            yl_write(P, YL, 1024 + h * 128, q0, 128, oT, [oTb])

def emit_DN(P, AR, dr, l, last, NH=4):
    AR.begin(51900, 8); E = common(P, AR)
    G = 2 * NH
    PF = dr["PF"]; PT = dr["PT"]; YL = dr["YL"]
    TRI2, TRI2b = ld(P, AR, "tri2s", dr["tri2"], [64, 2, 64]); MS2, MS2b = ld(P, AR, "ms2s", dr["ms2"], [64, 2, 64])
    I2, I2b = ld(P, AR, "i2s", dr["i2"], [64, 2, 64]); ident, identb = ld(P, AR, "ids", dr["ident"], [128, 128])
    cw, cwb = ld(P, AR, "cws", dr[f"convw{l}"], [128, 3 * NH, 5]); gn, gnb = ld(P, AR, "gns", dr[f"gnorm{l}"], [64, 128])
    alog, alogb = ld(P, AR, "alogs", dr[f"alog{l}"], [64, G]); dtb, dtbb = ld(P, AR, "dtbs", dr[f"dtb{l}"], [64, G])
    ones, onesb = E.ones, E.onesb
    one1, one1b = AR.sb("one1", [128, 1]); P.op("pool", "memset", one1[:], 1.0, writes=[one1b])
    eps6, eps6b = E.eps[1]
    psp = AR.pspool(7)
    bl, blb = ld(P, AR, "bls", PT[:, 512:512 + G].rearrange("(n c) x -> c n x", c=64), [64, NCH, G])
    al, alb = ld(P, AR, "als", PT[:, 512 + G:512 + 2 * G].rearrange("(n c) x -> c n x", c=64), [64, NCH, G], q="act")
    BETA, BETAb = AR.sb("BETA", [64, NCH, G]); NBETA, NBETAb = AR.sb("NBETA", [64, NCH, G])
    gt, gtb = AR.sb("gt", [64, NCH, G]); GC, GCb = AR.sb("GC", [64, NCH, G]); NGC, NGCb = AR.sb("NGC", [64, NCH, G])
    BEG, BEGb = AR.sb("BEG", [64, NCH, G]); EKD, EKDb = AR.sb("EKD", [64, NCH, G]); EGL, EGLb = AR.sb("EGL", [128, NCH, G])
    P.op("act", "activation", out=BETA[:], in_=bl[:], func=AF.Sigmoid, reads=[blb], writes=[BETAb])
    P.op("dve", "tensor_scalar", NBETA[:], BETA[:], -1.0, None, ALU.mult, reads=[BETAb], writes=[NBETAb])
    for c in range(G):
        P.op("act", "activation", out=gt[:, :, c], in_=al[:, :, c], func=AF.Exp, bias=dtb[:, c:c + 1], reads=[alb, dtbb], writes=[gtb])
    P.op("act", "activation", out=gt[:], in_=gt[:], func=AF.Ln, bias=one1[0:64, 0:1], reads=[gtb, one1b], writes=[gtb])
    P.op("act", "activation", out=alog[:], in_=alog[:], func=AF.Exp, reads=[alogb], writes=[alogb])
    P.op("dve", "tensor_scalar", alog[:], alog[:], -1.0, None, ALU.mult, reads=[alogb], writes=[alogb])
    for c in range(G):
        P.op("dve", "tensor_scalar", gt[:, :, c], gt[:, :, c], alog[:, c:c + 1], None, ALU.mult, reads=[gtb, alogb], writes=[gtb])
    gflat = gt[:].rearrange("p n g -> p (n g)")
    NF = NCH * G; H2 = NF // 2
    for d in range(2):
        pt, pb = psp.get(); pt2, pb2 = psp.get()
        for (pp, ppb, c0) in ((pt, pb, 0), (pt2, pb2, H2)):
            P.op("pe", "matmul", pp[0:64, 0:H2], TRI2[:, d, :], gflat[:, c0:c0 + H2], start=True, stop=True, reads=[TRI2b, gtb], writes=[ppb])
        for (pp, ppb, c0) in ((pt, pb, 0), (pt2, pb2, H2)):
            nn = H2 // G
            src = pp[0:64, 0:H2].rearrange("p (n g) -> p n g", g=G)
            P.op("dve", "tensor_copy", GC[:, c0 // G:c0 // G + nn, d * NH:(d + 1) * NH], src[:, :, d * NH:(d + 1) * NH], reads=[ppb], writes=[GCb])
    P.op("dve", "tensor_scalar", NGC[:], GC[:], -1.0, None, ALU.mult, reads=[GCb], writes=[NGCb])
    P.op("act", "activation", out=BEG[:], in_=GC[:], func=AF.Exp, reads=[GCb], writes=[BEGb])
    P.op("dve", "tensor_tensor", BEG[:], BEG[:], BETA[:], ALU.mult, reads=[BEGb, BETAb], writes=[BEGb])
    EGLf = EGL[:].rearrange("p n g -> p (n g)"); EKDf = EKD[:].rearrange("p n g -> p (n g)"); GCf = GC[:].rearrange("p n g -> p (n g)")
    for c0 in (0, H2):
        pt, pb = psp.get()
        P.op("pe", "matmul", pt[:, 0:H2], ones[0:64, :], gflat[:, c0:c0 + H2], start=True, stop=True, reads=[onesb, gtb], writes=[pb])
        P.op("dve", "tensor_tensor", EKDf[:, c0:c0 + H2], pt[0:64, 0:H2], GCf[:, c0:c0 + H2], ALU.subtract, reads=[pb, GCb], writes=[EKDb])
        P.op("act", "activation", out=EGLf[:, c0:c0 + H2], in_=pt[:, 0:H2], func=AF.Exp, reads=[pb], writes=[EGLb])
    P.op("act", "activation", out=EKD[:], in_=EKD[:], func=AF.Exp, reads=[EKDb], writes=[EKDb])
    QT, QTb = AR.sb("QT", [128, NTOT]); KT, KTb = AR.sb("KT", [128, NTOT]); VT, VTb = AR.sb("VT", [128, NTOT])
    Xr, Xrb = AR.sb("Xr", [128, NTOT]); O, Ob = AR.sb("O", [64, NCH, 128])
    rs, rsb = E.rstd, E.rstdb
    w128 = AR.pool("w128", [64, 2, 64], 12); xxp = AR.pool("xx", [64, 2, 128], 6); zp = AR.pool("zz", [64, 2, 64], 26); qkp = AR.pool("qk", [64, 2, 64], 6)
    egp = AR.pool("egr", [128, 2, 64], 4); qgp = AR.pool("qg", [128, 2, 64], 6); nwp = AR.pool("nw", [128, 2, 64], 6)
    tmp = AR.pool("tm", [64, 2, 128], 16)
    Sp = [AR.pool(f"S{d}", [128, 128], 2) for d in range(2)]
    GRP = 4
    zt, ztb = AR.sb("zt", [64, GRP, 128]); yt, ytb = AR.sb("yt", [64, GRP, 128])
    st17, st17b = AR.sb("st17", [64, GRP]); yT_p = AR.pool("yT", [128, 512], 2)
    segs = [(0, NCTX), (NCTX, NTOT)]
    for h in range(NH):
        for qi, (dst, dstb) in enumerate(((QT, QTb), (KT, KTb), (VT, VTb))):
            ci = qi * NH + h
            P.dma("sp", Xr[:], PF[qi * 512 + h * 128: qi * 512 + (h + 1) * 128, :], writes=[Xrb])
            for (a, b) in segs:
                P.op("act", "activation", out=dst[:, a:b], in_=Xr[:, a:b], func=AF.Copy, scale=cw[:, ci, 2:3], reads=[Xrb, cwb], writes=[dstb])
                for tap in (0, 1, 3, 4):
                    off = tap - 2
                    if off < 0: o0, o1, i0, i1 = a - off, b, a, b + off
                    else: o0, o1, i0, i1 = a, b - off, a + off, b
                    P.op("dve", "scalar_tensor_tensor", dst[:, o0:o1], Xr[:, i0:i1], cw[:, ci, tap:tap + 1], dst[:, o0:o1], ALU.mult, ALU.add, reads=[Xrb, cwb, dstb], writes=[dstb])
            P.op("act", "activation", out=dst[:], in_=dst[:], func=AF.Silu, reads=[dstb], writes=[dstb])
            if qi < 2:
                for t0 in range(0, NTOT, 512):
                    tb = min(512, NTOT - t0)
                    sq, sqb = E.sqp.get()
                    P.op("act", "activation", out=sq[:, 0:tb], in_=dst[:, t0:t0 + tb], func=AF.Square, reads=[dstb], writes=[sqb])
                    pt, pb = psp.get()
                    P.op("pe", "matmul", pt[:, 0:tb], ones[:], sq[:, 0:tb], start=True, stop=True, reads=[onesb, sqb], writes=[pb])
                    P.op("act", "activation", out=rs[:, 0:tb], in_=pt[:, 0:tb], func=AF.Sqrt, bias=eps6[:, 0:1], reads=[pb, eps6b], writes=[rsb])
                    P.op("dve", "reciprocal", rs[:, 0:tb], rs[:, 0:tb], reads=[rsb], writes=[rsb])
                    if qi == 0:
                        P.op("dve", "scalar_tensor_tensor", dst[:, t0:t0 + tb], dst[:, t0:t0 + tb], 128 ** -0.5, rs[:, 0:tb], ALU.mult, ALU.mult, reads=[dstb, rsb], writes=[dstb])
                    else:
                        P.op("dve", "tensor_tensor", dst[:, t0:t0 + tb], dst[:, t0:t0 + tb], rs[:, 0:tb], ALU.mult, reads=[dstb, rsb], writes=[dstb])
        S = []
        for d in range(2):
            st, stb = Sp[d].get()
            P.op("pool", "memset", st[:], 0.0, writes=[stb])
            S.append((st, stb))
        visited = set()
        def chunk_of(s, d):
            if d == 0: return s
            return 3 - s if s < 4 else 71 - s
        def pre(s):
            ns = [chunk_of(s, d) for d in range(2)]; cols = [d * NH + h for d in range(2)]
            toks = [slice(n * 64, (n + 1) * 64) for n in ns]
            Gd, Gdb = w128.get()
            for d in range(2):
                P.op("dve", "tensor_scalar", Gd[:, d, :], TRI2[:, d, :], gt[:, ns[d], cols[d]:cols[d] + 1], None, ALU.mult, reads=[TRI2b, gtb], writes=[Gdb])
            yield
            pa, pab = psp.get()
            P.op("pe", "matmul", pa[:, 0:128], ones[0:64, :], Gd[:].rearrange("p d j -> p (d j)"), start=True, stop=True, reads=[onesb, Gdb], writes=[pab])
            E1, E1b = w128.get(); E2, E2b = w128.get(); EGr, EGrb = egp.get()
            for d in range(2):
                P.op("act", "activation", out=E1[:, d, :], in_=pa[0:64, d * 64:(d + 1) * 64], func=AF.Exp, scale=-1.0, bias=GC[:, ns[d], cols[d]:cols[d] + 1], reads=[pab, GCb], writes=[E1b])
                P.op("act", "activation", out=E2[:, d, :], in_=pa[0:64, d * 64:(d + 1) * 64], func=AF.Exp, bias=NGC[:, ns[d], cols[d]:cols[d] + 1], reads=[pab, NGCb], writes=[E2b])
            P.op("act", "activation", out=EGr[:].rearrange("p d j -> p (d j)"), in_=pa[:, 0:128], func=AF.Exp, reads=[pab], writes=[EGrb])
            yield
            D1, D1b = w128.get(); D2, D2b = w128.get()
            P.op("dve", "scalar_tensor_tensor", D1[:], E1[:], 1.0, MS2[:], ALU.min, ALU.mult, reads=[E1b, MS2b], writes=[D1b])
            P.op("dve", "scalar_tensor_tensor", D2[:], E2[:], 1.0, TRI2[:], ALU.min, ALU.mult, reads=[E2b, TRI2b], writes=[D2b])
            for d in range(2):
                P.op("pool", "tensor_scalar", D1[:, d, :], D1[:, d, :], NBETA[:, ns[d], cols[d]:cols[d] + 1], None, ALU.mult, reads=[D1b, NBETAb], writes=[D1b])
            yield
            pk, pkb = psp.get()
            for d in range(2):
                P.op("pe", "matmul", pk[0:64, d * 64:(d + 1) * 64], KT[:, toks[d]], KT[:, toks[d]], start=True, stop=True, reads=[KTb], writes=[pkb])
                P.op("pe", "matmul", pk[0:64, 128 + d * 64:128 + (d + 1) * 64], KT[:, toks[d]], QT[:, toks[d]], start=True, stop=True, reads=[KTb, QTb], writes=[pkb])
            XX, XXb = xxp.get(); QK, QKb = qkp.get()
            P.op("dve", "tensor_tensor", XX[:, :, 0:64], pk[0:64, 0:128].rearrange("p (d j) -> p d j", d=2), D1[:], ALU.mult, reads=[pkb, D1b], writes=[XXb])
            P.op("dve", "tensor_tensor", QK[:], pk[0:64, 128:256].rearrange("p (d j) -> p d j", d=2), D2[:], ALU.mult, reads=[pkb, D2b], writes=[QKb])
            yield
            pc, pcb = psp.get()
            for d in range(2):
                P.op("pe", "transpose", pc[0:64, d * 64:(d + 1) * 64], XX[:, d, 0:64], ident[0:64, 0:64], reads=[XXb, identb], writes=[pcb])
            pcv = pc[0:64, 0:128].rearrange("p (d j) -> p d j", d=2)
            P.op("act", "activation", out=XX[:, :, 64:128], in_=pcv, func=AF.Copy, reads=[pcb], writes=[XXb])
            Z, Zb = zp.get()
            P.op("dve", "tensor_tensor", Z[:], pcv, I2[:], ALU.add, reads=[pcb, I2b], writes=[Zb])
            for k in range(1, 6):
                yield
                pd, pdb = psp.get()
                for d in range(2):
                    P.op("pe", "matmul", pd[0:64, d * 128:d * 128 + 64], XX[:, d, 64:128], XX[:, d, 0:64], start=True, stop=True, reads=[XXb], writes=[pdb])
                    P.op("pe", "matmul", pd[0:64, d * 128 + 64:d * 128 + 128], XX[:, d, 0:64], XX[:, d, 64:128], start=True, stop=True, reads=[XXb], writes=[pdb])
                if k > 1:
                    pe_, peb = psp.get()
                    for d in range(2):
                        P.op("pe", "matmul", pe_[0:64, d * 64:(d + 1) * 64], XX[:, d, 0:64], Z[:, d, :], start=True, stop=True, reads=[XXb, Zb], writes=[peb])
                XXn, XXnb = xxp.get()
                P.op("act", "activation", out=XXn[:].rearrange("p d j -> p (d j)"), in_=pd[0:64, 0:256], func=AF.Copy, reads=[pdb], writes=[XXnb])
                if k > 1:
                    Zn, Znb = zp.get()
                    P.op("dve", "tensor_tensor", Zn[:], Z[:], pe_[0:64, 0:128].rearrange("p (d j) -> p d j", d=2), ALU.add, reads=[Zb, peb], writes=[Znb])
                    Z, Zb = Zn, Znb
                XX, XXb = XXn, XXnb
            yield
            pe_, peb = psp.get()
            for d in range(2):
                P.op("pe", "matmul", pe_[0:64, d * 64:(d + 1) * 64], XX[:, d, 0:64], Z[:, d, :], start=True, stop=True, reads=[XXb, Zb], writes=[peb])
            Zn, Znb = zp.get()
            P.op("dve", "tensor_tensor", Zn[:], Z[:], pe_[0:64, 0:128].rearrange("p (d j) -> p d j", d=2), ALU.add, reads=[Zb, peb], writes=[Znb])
            Z, Zb = Zn, Znb
            yield
            ptk, ptkb = psp.get()
            for d in range(2):
                P.op("pe", "transpose", ptk[0:64, d * 128:(d + 1) * 128], KT[:, toks[d]], ident[:], reads=[KTb, identb], writes=[ptkb])
                P.op("pe", "transpose", ptk[0:64, 256 + d * 128:256 + (d + 1) * 128], VT[:, toks[d]], ident[:], reads=[VTb, identb], writes=[ptkb])
            VB, VBb = tmp.get(); KBG, KBGb = tmp.get(); KD, KDb = tmp.get()
            for d in range(2):
                n, c = ns[d], cols[d]
                P.op("act", "activation", out=KBG[:, d, :], in_=ptk[0:64, d * 128:(d + 1) * 128], func=AF.Copy, scale=BEG[:, n, c:c + 1], reads=[ptkb, BEGb], writes=[KBGb])
                P.op("dve", "tensor_scalar", KD[:, d, :], ptk[0:64, d * 128:(d + 1) * 128], EKD[:, n, c:c + 1], None, ALU.mult, reads=[ptkb, EKDb], writes=[KDb])
                P.op("dve", "tensor_scalar", VB[:, d, :], ptk[0:64, 256 + d * 128:256 + (d + 1) * 128], BETA[:, n, c:c + 1], None, ALU.mult, reads=[ptkb, BETAb], writes=[VBb])
            yield
            pw, pwb = psp.get()
            for d in range(2):
                P.op("pe", "matmul", pw[:, d * 64:(d + 1) * 64], KBG[:, d, :], Z[:, d, :], start=True, stop=True, reads=[KBGb, Zb], writes=[pwb])
            NW, NWb = nwp.get()
            P.op("act", "activation", out=NW[:].rearrange("p d j -> p (d j)"), in_=pw[:, 0:128], func=AF.Copy, scale=-1.0, reads=[pwb], writes=[NWb])
            QG, QGb = qgp.get()
            for d in range(2):
                P.op("pool", "tensor_tensor", QG[:, d, :], QT[:, toks[d]], EGr[:, d, :], ALU.mult, reads=[QTb, EGrb], writes=[QGb])
            return dict(ns=ns, cols=cols, Z=(Z, Zb), VB=(VB, VBb), KD=(KD, KDb), NW=(NW, NWb), QG=(QG, QGb), QK=(QK, QKb))
        def seq(R):
            ns, cols = R["ns"], R["cols"]
            Z, Zb = R["Z"]; VB, VBb = R["VB"]; KD, KDb = R["KD"]; NW, NWb = R["NW"]; QG, QGb = R["QG"]; QK, QKb = R["QK"]
            pv, pvb = psp.get()
            for d in range(2):
                P.op("pe", "matmul", pv[0:64, d * 128:(d + 1) * 128], Z[:, d, :], VB[:, d, :], start=True, stop=False, reads=[Zb, VBb], writes=[pvb])
                P.op("pe", "matmul", pv[0:64, d * 128:(d + 1) * 128], NW[:, d, :], S[d][0][:], start=False, stop=True, reads=[NWb, S[d][1]], writes=[pvb])
            VN, VNb = tmp.get()
            P.op("act", "activation", out=VN[:].rearrange("p d e -> p (d e)"), in_=pv[0:64, 0:256], func=AF.Copy, reads=[pvb], writes=[VNb])
            po, pob = psp.get()
            for d in range(2):
                P.op("pe", "matmul", po[0:64, d * 128:(d + 1) * 128], QG[:, d, :], S[d][0][:], start=True, stop=False, reads=[QGb, S[d][1]], writes=[pob])
                P.op("pe", "matmul", po[0:64, d * 128:(d + 1) * 128], QK[:, d, :], VN[:, d, :], start=False, stop=True, reads=[QKb, VNb], writes=[pob])
            for d in range(2):
                n = ns[d]
                if n in visited:
                    P.op("dve", "tensor_tensor", O[:, n, :], O[:, n, :], po[0:64, d * 128:(d + 1) * 128], ALU.add, reads=[Ob, pob], writes=[Ob])
                else:
                    visited.add(n)
                    P.op("dve", "tensor_copy", O[:, n, :], po[0:64, d * 128:(d + 1) * 128], reads=[pob], writes=[Ob])
            pS, pSb = psp.get()
            for d in range(2):
                P.op("pe", "matmul", pS[:, d * 128:(d + 1) * 128], KD[:, d, :], VN[:, d, :], start=True, stop=True, reads=[KDb, VNb], writes=[pSb])
            for d in range(2):
                sn, snb = Sp[d].get()
                P.op("dve", "scalar_tensor_tensor", sn[:], S[d][0][:], EGL[:, ns[d], cols[d]:cols[d] + 1], pS[:, d * 128:(d + 1) * 128], ALU.mult, ALU.add, reads=[S[d][1], EGLb, pSb], writes=[snb])
                S[d] = (sn, snb)
        def drive(gens, res, nstages=None):
            k = 0
            while any(g is not None for g in gens):
                for i, g in enumerate(gens):
                    if g is None: continue
                    try: next(g)
                    except StopIteration as e:
                        res[i] = e.value; gens[i] = None
                k += 1
                if nstages is not None and k >= nstages: break
            return all(g is None for g in gens)
        cur = [None, None]
        drive([pre(0), pre(1)], cur)
        for p in range(0, NCH, 2):
            nxt = [None, None]; gens = [pre(p + 2), pre(p + 3)] if p + 2 < NCH else [None, None]
            drive(gens, nxt, 3)
            seq(cur[0])
            drive(gens, nxt, 6)
            seq(cur[1])
            drive(gens, nxt)
            cur = nxt
        zr = PT[:, h * 128:(h + 1) * 128].rearrange("(n c) e -> c n e", c=64)
        for n0 in range(0, NCH, GRP):
            P.dma("sp", zt[:], zr[:, n0:n0 + GRP, :], writes=[ztb])
            P.op("dve", "tensor_tensor", yt[:], O[:, n0:n0 + GRP, :], O[:, n0:n0 + GRP, :], ALU.mult, reads=[Ob], writes=[ytb])
            P.op("dve", "tensor_reduce", st17[:], yt[:], AX.X, ALU.add, reads=[ytb], writes=[st17b])
            P.op("act", "activation", out=st17[:], in_=st17[:], func=AF.Sqrt, scale=1.0 / 128.0, bias=eps6[0:64, 0:1], reads=[st17b, eps6b], writes=[st17b])
            P.op("dve", "reciprocal", st17[:], st17[:], reads=[st17b], writes=[st17b])
            for i in range(GRP):
                P.op("dve", "scalar_tensor_tensor", yt[:, i, :], O[:, n0 + i, :], st17[:, i:i + 1], gn[:], ALU.mult, ALU.mult, reads=[Ob, st17b, gnb], writes=[ytb])
            P.op("act", "activation", out=zt[:], in_=zt[:], func=AF.Silu, reads=[ztb], writes=[ztb])
            P.op("dve", "tensor_tensor", yt[:], yt[:], zt[:], ALU.mult, reads=[ytb, ztb], writes=[ytb])
            pq, pqb = psp.get()
            for i in range(GRP):
                P.op("pe", "transpose", pq[:, i * 64:(i + 1) * 64], yt[:, i, :], ident[0:64, 0:64], reads=[ytb, identb], writes=[pqb])
            yT, yTb = yT_p.get()
            evac(P, E, yT[:, 0:GRP * 64], pq[:, 0:GRP * 64], [pqb], [yTb])
            yl_write(P, YL, h * 128, n0 * 64, GRP * 64, yT, [yTb])

class LazyDr(dict):
    def __init__(self, nc):
        super().__init__(); self.nc = nc; self.specs = {}; self.used_ext = []
    def __missing__(self, name):
        kind, shape = self.specs[name]
        ap = self.nc.dram_tensor(name, list(shape), F32, kind=kind).ap()
        if kind == "ExternalInput": self.used_ext.append(name)
        self[name] = ap
        return ap

def build_fused(nc, stop=None):
    dr = LazyDr(nc)
    def ext(name, shape): dr.specs[name] = ("ExternalInput", shape)
    def internal(name, shape): dr.specs[name] = ("Internal", shape)
    ext("xT_in", [KC, 256, NTH]); ext("xT_own", [D, NTH]); ext("cT", [128, KC, 2]); ext("msk", [128, 2])
    ext("cw2", [256, 512]); ext("cl", [NLAT, NLAT]); ext("sln", [NLAT, NLAT])
    ext("cosT", [64, NLAT]); ext("sinT", [64, NLAT]); ext("Rm", [64, 64]); ext("ident", [128, 128])
    ext("tri2", [64, 2, 64]); ext("ms2", [64, 2, 64]); ext("i2", [64, 2, 64])
    for l in range(2):
        ext(f"w_ada{l}", [D, 12288]); ext(f"b_ada{l}", [128, 96]); ext(f"g{l}", [128, 4, KC]); ext(f"w_inr{l}", [D, WINR])
        ext(f"w_gate{l}", [D, 6144]); ext(f"w_branch{l}", [3, 1024, D]); ext(f"w_out{l}", [D, D]); ext(f"w_ffn_in{l}", [D, 2 * FF]); ext(f"w_ffn_out{l}", [FF, D])
        ext(f"gq{l}", [128, 4]); ext(f"gkv{l}", [128, 2]); ext(f"wqn{l}", [512, 512]); ext(f"wqr{l}", [512, 256]); ext(f"wk{l}", [256, 512]); ext(f"wv{l}", [256, 512])
        ext(f"convw{l}", [128, 12, 5]); ext(f"alog{l}", [64, 8]); ext(f"dtb{l}", [64, 8]); ext(f"gnorm{l}", [64, 128])
    dr.specs["xo"] = ("ExternalOutput", [D, NLH])
    internal("PF", [FM_ROWS, NTOT]); internal("PT", [NTOT, TM_W]); internal("YL", [17, 1536, 256]); internal("YG", [17, 2 * 1536, 256])
    internal("XL2", [D, NTH]); internal("XG", [KC, 256, NTH])
    dr["xo"]
    with ExitStack() as es:
        P = Prog(nc, es)
        AR = Arena(P)
        def steps():
            for l in range(2):
                last = (l == 1)
                yield f"A{l}", lambda: emit_A(P, AR, dr, l)
                yield f"DN{l}", lambda: emit_DN(P, AR, dr, l, last)
                yield f"FN{l}", lambda: emit_FN(P, AR, dr, last)
                yield f"MLA{l}", lambda: emit_MLA(P, AR, dr, l, last)
                def g1():
                    AR.end()
                    for blk in range(17): P.coll("AllGather", dr["YG"][blk], dr["YL"][blk], PAIRS)
                yield f"G1{l}", g1
                yield f"C{l}", lambda: emit_C(P, AR, dr, l, last)
                if not last:
                    def g2():
                        AR.end()
                        for c in range(KC): P.coll("AllGather", dr["XG"][c], dr["XL2"][c * 128:(c + 1) * 128, :], PAIRS)
                    yield f"G2{l}", g2
        for name, fn in steps():
            if stop is not None and name not in stop: continue
            fn()
        AR.end()
        P.finish()
        print("fused ops", P.n_ops, dict(P.ep))
    nc._used_ext = list(dr.used_ext)
    return nc

from concourse.bass_utils import run_bass_kernel_spmd

def dft_tables():
    n = np.arange(256, dtype=np.float64)
    ang = 2 * np.pi * np.outer(n, n) / 256.0
    cw2 = np.concatenate([np.cos(ang), np.sin(ang)], 1).astype(np.float32)
    n = np.arange(NLAT, dtype=np.int64)
    ang = 2 * np.pi * (np.outer(n, n) % NLAT).astype(np.float64) / NLAT
    return cw2, np.cos(ang).astype(np.float32), (-np.sin(ang)).astype(np.float32)

def rope_tables():
    rows = NLAT // 64
    row = np.repeat(np.arange(rows, dtype=np.float32), 64)
    col = np.tile(np.arange(64, dtype=np.float32), rows)
    inv = (10000.0 ** (-np.arange(0, 32, 2, dtype=np.float32) / 32)).astype(np.float32)
    ar = row[:, None] * inv; ac = col[:, None] * inv
    ang = np.concatenate([ar, ar, ac, ac], -1)
    cosT = np.ascontiguousarray(np.cos(ang).T.astype(np.float32)); sinT = np.ascontiguousarray(np.sin(ang).T.astype(np.float32))
    R = np.zeros((64, 64), np.float32)
    for i in range(16):
        R[16 + i, i] = -1; R[i, 16 + i] = 1; R[48 + i, 32 + i] = -1; R[32 + i, 48 + i] = 1
    return cosT, sinT, R

def dn_consts():
    p = np.arange(64)[:, None]; f = np.arange(64)[None, :]
    ple = (p <= f).astype(np.float32); pge = (p >= f).astype(np.float32)
    pgt = (p > f).astype(np.float32); plt = (p < f).astype(np.float32)
    return {"tri2": np.ascontiguousarray(np.stack([ple, pge], 1)), "ms2": np.ascontiguousarray(np.stack([pgt, plt], 1)),
            "i2": np.ascontiguousarray(np.stack([np.eye(64, dtype=np.float32)] * 2, 1)), "ident": np.eye(128, dtype=np.float32)}

_NC = []
STOP = None
def kernel(**inputs):
    inp = {k: np.asarray(v) for k, v in inputs.items()}
    B = inp["x"].shape[0]
    if not _NC:
        nc = bass.Bass("TRN2", target_bir_lowering=False, num_devices=8)
        build_fused(nc, STOP); _NC.append(nc)
    nc = _NC[0]
    cw2, cl, sln = dft_tables(); cosT, sinT, R = rope_tables(); dnc = dn_consts()
    shared = {"cw2": cw2, "cl": cl, "sln": sln, "cosT": cosT, "sinT": sinT, "Rm": R}
    shared.update(dnc)
    for l in range(2):
        shared[f"w_ada{l}"] = np.ascontiguousarray(inp["w_ada"][l])
        shared[f"b_ada{l}"] = np.ascontiguousarray(inp["b_ada"][l].reshape(96, 128).T)
        shared[f"g{l}"] = np.ascontiguousarray(inp["norm_g"][l].reshape(4, KC, 128).transpose(2, 0, 1))
        shared[f"w_gate{l}"] = np.ascontiguousarray(inp["w_in"][l][:, 5984:])
        shared[f"w_branch{l}"] = np.ascontiguousarray(inp["w_branch"][l]); shared[f"w_out{l}"] = np.ascontiguousarray(inp["w_out"][l])
        shared[f"w_ffn_in{l}"] = np.ascontiguousarray(inp["w_ffn_in"][l]); shared[f"w_ffn_out{l}"] = np.ascontiguousarray(inp["w_ffn_out"][l])
        shared[f"gq{l}"] = np.ascontiguousarray(inp["mla_q_norm_g"][l].reshape(4, 128).T)
        shared[f"gkv{l}"] = np.ascontiguousarray(inp["mla_kv_norm_g"][l].reshape(2, 128).T)
        shared[f"gnorm{l}"] = np.ascontiguousarray(np.tile(inp["dn_norm_g"][l][None], (64, 1)).astype(np.float32))
    percore = {}
    for r in range(2):
        heads = np.arange(r * 4, (r + 1) * 4)
        colsel = np.concatenate([heads, 8 + heads])
        pc = {}
        for l in range(2):
            w = inp["w_in"][l]
            cols = np.concatenate([np.arange(r * 512, (r + 1) * 512), 1024 + np.arange(r * 512, (r + 1) * 512), 2048 + np.arange(r * 512, (r + 1) * 512),
                                   np.arange(4128, 4960), 4960 + np.arange(r * 512, (r + 1) * 512), 3072 + np.arange(r * 512, (r + 1) * 512), 4096 + colsel, 4112 + colsel])
            assert cols.size == WINR
            pc[f"w_inr{l}"] = np.ascontiguousarray(w[:, cols])
            wuq = inp["w_uq"][l]; wukv = inp["w_ukv"][l]
            pc[f"wqn{l}"] = np.ascontiguousarray(np.concatenate([wuq[:, h * 192: h * 192 + 128] for h in heads], 1))
            pc[f"wqr{l}"] = np.ascontiguousarray(np.concatenate([wuq[:, h * 192 + 128: h * 192 + 192] for h in heads], 1))
            pc[f"wk{l}"] = np.ascontiguousarray(np.concatenate([wukv[:, h * 256: h * 256 + 128] for h in heads], 1))
            pc[f"wv{l}"] = np.ascontiguousarray(np.concatenate([wukv[:, h * 256 + 128: h * 256 + 256] for h in heads], 1))
            conv = inp["dn_conv"][l]
            cwl = [conv[:, qi * 1024 + h * 128: qi * 1024 + (h + 1) * 128].T for qi in range(3) for h in heads]
            pc[f"convw{l}"] = np.ascontiguousarray(np.stack(cwl, 1).astype(np.float32))
            pc[f"alog{l}"] = np.ascontiguousarray(np.tile(inp["dn_a_log"][l].reshape(16)[colsel][None], (64, 1)).astype(np.float32))
            pc[f"dtb{l}"] = np.ascontiguousarray(np.tile(inp["dn_dt_bias"][l].reshape(16)[colsel][None], (64, 1)).astype(np.float32))
        m = np.zeros((128, 2), np.float32); m[:, r] = 1.0
        pc["msk"] = m
        percore[r] = pc
    in_maps = []
    for i in range(8):
        b, r = i // 2, i % 2
        halves = [np.concatenate([inp["x"][b, q * NLH:(q + 1) * NLH], inp["ctx"][b, q * NCH2:(q + 1) * NCH2]], 0).T for q in range(2)]
        d = dict(shared); d.update(percore[r])
        d["xT_in"] = np.ascontiguousarray(np.stack([h_.reshape(KC, 128, NTH) for h_ in halves], 1).reshape(KC, 256, NTH))
        d["xT_own"] = np.ascontiguousarray(halves[r])
        cvec = np.stack([inp["c"][b], inp["c_ctx"]], -1)
        d["cT"] = np.ascontiguousarray(cvec.reshape(KC, 128, 2).transpose(1, 0, 2))
        in_maps.append(d)
    in_maps = [{k: d[k] for k in nc._used_ext} for d in in_maps]
    res = run_bass_kernel_spmd(nc, in_maps, core_ids=list(range(8))).results
    out = np.empty((B, NLAT, D), np.float32)
    for i in range(8):
        b, r = i // 2, i % 2
        out[b, r * NLH:(r + 1) * NLH] = res[i]["xo"].T
    return out
```

```python
import numpy as np
from contextlib import ExitStack
import concourse.bass as bass
import concourse.mybir as mybir
F32 = mybir.dt.float32; BF16 = mybir.dt.bfloat16; I32 = mybir.dt.int32
AF = mybir.ActivationFunctionType
ALU = mybir.AluOpType
AX = mybir.AxisListType

class Buf:
    __slots__ = ("name", "w", "r", "excl")
    def __init__(self, name="", excl=False):
        self.name = name
        self.excl = excl
        self.w = None
        self.r = {}

EPOCH = 12000
class Prog:
    ENG = ("pe", "dve", "act", "pool", "sp")
    def __init__(self, nc, es, n_dma_sems=12):
        self.nc = nc; self.es = es; self.es_global = es
        self.engobj = {"pe": nc.tensor, "dve": nc.vector, "act": nc.scalar, "pool": nc.gpsimd, "sp": nc.sync}
        self.streams = {e: [] for e in self.ENG}
        self.sems = {}
        self.cnt = {}
        self.cur = {}
        self.ep = {e: 0 for e in self.ENG}
        for e in self.ENG:
            self._new_epoch(e)
        self.seen = {e: {} for e in self.ENG}
        self.dma_keys = []
        for i in range(n_dma_sems):
            k = ("dma", i)
            self.sems[k] = es.enter_context(nc.semaphore(f"dma{i}"))
            self.cnt[k] = 0
            self.dma_keys.append(k)
        self.dma_rr = 0
        self.n_ops = 0
    def _new_epoch(self, e):
        k = (e, self.ep[e]); self.ep[e] += 1
        self.sems[k] = self.es_global.enter_context(self.nc.semaphore(f"s_{e}_{k[1]}"))
        self.cnt[k] = 0; self.cur[e] = k
    def _deps(self, reads, writes):
        deps = {}
        def need(k, c):
            if deps.get(k, 0) < c: deps[k] = c
        for b in reads:
            if b.w is not None: need(*b.w)
        for b in writes:
            if b.w is not None: need(*b.w)
            for k, c in b.r.items(): need(k, c)
        return deps
    def _emit_waits(self, e, deps, skip_key=None):
        seen = self.seen[e]
        for k, c in deps.items():
            if k == skip_key: continue
            if seen.get(k, 0) >= c: continue
            seen[k] = c
            sem = self.sems[k]
            self.streams[e].append(lambda eng, sem=sem, c=c: eng.wait_ge(sem, c))
    def _mark(self, key, c, reads, writes):
        for b in reads:
            if b.r.get(key, 0) < c: b.r[key] = c
        for b in writes:
            b.w = (key, c); b.r = {}
    def op(self, e, meth, *args, reads=(), writes=(), same_engine_sync=True, **kw):
        writes = list(writes) + [b for b in reads if b.excl]
        reads = [b for b in reads if not b.excl]
        deps = self._deps(reads, writes)
        key = self.cur[e]
        if self.cnt[key] >= EPOCH:
            self._new_epoch(e); key = self.cur[e]
        skip = None
        if e == "pe" or not same_engine_sync:
            deps = {k: c for k, c in deps.items() if k[0] != e}
        self._emit_waits(e, deps, skip)
        self.cnt[key] += 1
        c = self.cnt[key]; sem = self.sems[key]
        self.streams[e].append(lambda eng, meth=meth, args=args, kw=kw, sem=sem: getattr(eng, meth)(*args, **kw).then_inc(sem, 1))
        self._mark(key, c, reads, writes)
        self.n_ops += 1
    def dma(self, q, out, in_, reads=(), writes=(), **kw):
        deps = self._deps(reads, writes)
        k = self.dma_keys[self.dma_rr]; self.dma_rr = (self.dma_rr + 1) % len(self.dma_keys)
        if self.cnt[k] > 0: deps[k] = max(deps.get(k, 0), self.cnt[k])
        self._emit_waits(q, deps)
        self.cnt[k] += 16
        c = self.cnt[k]; sem = self.sems[k]
        self.streams[q].append(lambda eng, out=out, in_=in_, sem=sem, kw=kw: eng.dma_start(out=out, in_=in_, **kw).then_inc(sem, 16))
        self._mark(k, c, reads, writes)
        self.n_ops += 1
    def coll(self, kind, out, in_, groups, reads=(), writes=()):
        deps = self._deps(reads, writes)
        k = ("cc", 0)
        if k not in self.sems:
            self.sems[k] = self.es_global.enter_context(self.nc.semaphore("cc0"))
            self.cnt[k] = 0
            self.dma_keys.append(k)
        self.cnt[k] += 1
        self._emit_waits("pool", deps)
        sem = self.sems[k]
        self.streams["pool"].append(lambda eng, out=out, in_=in_, sem=sem: eng.collective_compute(kind, mybir.AluOpType.bypass, replica_groups=groups, ins=[in_.opt()], outs=[out.opt()]).then_inc(sem, 1))
        self._mark(k, self.cnt[k], reads, writes)
        self.n_ops += 1
    def barrier(self):
        deps = {k: c for k, c in self.cnt.items() if c > 0}
        for e in self.ENG:
            self._emit_waits(e, {k: c for k, c in deps.items() if k != self.cur[e]})
    def flush(self):
        nc = self.nc
        streams = self.streams
        self.streams = {e: [] for e in self.ENG}
        with nc.Block() as block:
            @block.sync
            def _(eng):
                for f in streams["sp"]: f(eng)
            @block.tensor
            def _(eng):
                for f in streams["pe"]: f(eng)
            @block.vector
            def _(eng):
                for f in streams["dve"]: f(eng)
            @block.scalar
            def _(eng):
                for f in streams["act"]: f(eng)
            @block.gpsimd
            def _(eng):
                for f in streams["pool"]: f(eng)
    def finish(self):
        deps = {k: self.cnt[k] for k in self.dma_keys if self.cnt[k] > 0}
        self._emit_waits("sp", deps)
        self.flush()

class Pool:
    def __init__(self, P, name, shape, dtype, n, psum=False):
        self.tiles = []
        for i in range(n):
            if psum:
                t = P.es.enter_context(P.nc.psum_tensor(f"pp_{name}{i}", shape, dtype))
            else:
                t = P.es.enter_context(P.nc.sbuf_tensor(f"sp_{name}{i}", shape, dtype))
            self.tiles.append((t, Buf(f"{name}{i}", excl=psum)))
        self.i = 0
    def get(self):
        t = self.tiles[self.i]; self.i = (self.i + 1) % len(self.tiles)
        return t

def sb(P, name, shape, dtype=F32):
    return P.es.enter_context(P.nc.sbuf_tensor("sb_" + name, shape, dtype)), Buf(name)
def ps(P, name, shape, dtype=F32):
    return P.es.enter_context(P.nc.psum_tensor("ps_" + name, shape, dtype)), Buf(name, excl=True)

import math
D = 2048; KC = 16; FF = 5632; FKC = 44
NCTX = 256; NLAT = 4096; NTOT = NCTX + NLAT; NCH = NTOT // 64
NLH = 2048; NCH2 = 128; NTH = NLH + NCH2
MLA_SCALE = 192 ** -0.5
NAR = 52000
PAIRS = [[0, 1], [2, 3], [4, 5], [6, 7]]
F32R = mybir.dt.float32r
def RR_(ap): return ap.bitcast(F32R)
FM_CHUNKS = [(c0, 128) for c0 in range(0, 2304, 128)] + [(2304, 64)] + [(2368 + i * 128, 128) for i in range(4)]
FM_ROWS = 2880; TM_COL0 = 2880; TM_W = 528; WINR = 3408

class Arena:
    def __init__(self, P):
        self.P = P
        self.banks = [(P.es_global.enter_context(P.nc.psum_tensor(f"bank{i}", [128, 512], F32)), Buf(f"bank{i}", excl=True)) for i in range(8)]
        self.ph = None; self.nph = 0
    def begin(self, nN, nR):
        assert nN + nR <= NAR, (nN, nR)
        self.end()
        self.ph = ExitStack(); self.nph += 1
        self.tN = self.ph.enter_context(self.P.nc.sbuf_tensor(f"arN{self.nph}", [128, max(nN, 8)], F32))
        self.tR = self.ph.enter_context(self.P.nc.sbuf_tensor(f"arR{self.nph}", [128, max(nR, 8)], F32))
        self.cap = {False: nN, True: nR}; self.off = {False: 0, True: 0}; self.bi = 0
    def end(self):
        self.P.barrier()
        if self.ph is not None:
            self.P.flush(); self.ph.close(); self.ph = None
    def sb(self, name, shape, R=False):
        n = int(np.prod(shape[1:]))
        assert self.off[R] + n <= self.cap[R], (name, R, self.off[R], n, self.cap[R])
        t = self.tR if R else self.tN
        v = t[0:shape[0], self.off[R]:self.off[R] + n]
        self.off[R] += n
        if len(shape) == 3: v = v.rearrange("p (a b) -> p a b", a=shape[1])
        elif len(shape) == 4: v = v.rearrange("p (a b c) -> p a b c", a=shape[1], b=shape[2])
        return v, Buf(name)
    def pool(self, name, shape, n, R=False):
        return RR([self.sb(f"{name}{i}", shape, R) for i in range(n)])
    def pspool(self, n):
        b = self.banks[self.bi:self.bi + n]; assert len(b) == n; self.bi += n
        return RR(b)

class RR:
    def __init__(self, tiles): self.tiles = tiles; self.i = 0
    def get(self):
        t = self.tiles[self.i]; self.i = (self.i + 1) % len(self.tiles); return t

def common(P, AR):
    class E: pass
    E = E()
    E.ones, E.onesb = AR.sb("ones", [128, 128]); P.op("pool", "memset", E.ones[:], 1.0, writes=[E.onesb])
    E.eps = {}
    for dim, val in ((2048, 2048e-6), (512, 512e-6), (256, 256e-6), (1, 1e-6)):
        t, b = AR.sb(f"eps{dim}", [128, 1]); P.op("pool", "memset", t[:], val, writes=[b]); E.eps[dim] = (t, b)
    E.sqp = AR.pool("sq", [128, 512], 2)
    E.ssp, E.sspb = AR.pspool(1).get()
    E.rstd, E.rstdb = AR.sb("rstd", [128, 512])
    E.ev = 0
    return E

def rms_rstd(P, E, src, srcb, nch, tb, dim):
    for c in range(nch):
        sq, sqb = E.sqp.get()
        P.op("act", "activation", out=sq[:, 0:tb], in_=src[:, c, 0:tb], func=AF.Square, reads=[srcb], writes=[sqb])
        P.op("pe", "matmul", E.ssp[:, 0:tb], E.ones[:], sq[:, 0:tb], start=(c == 0), stop=(c == nch - 1), reads=[sqb, E.onesb], writes=[E.sspb])
    eb = E.eps[dim]
    P.op("act", "activation", out=E.rstd[:, 0:tb], in_=E.ssp[:, 0:tb], func=AF.Sqrt, bias=eb[0][:, 0:1], reads=[E.sspb, eb[1]], writes=[E.rstdb])
    P.op("dve", "reciprocal", E.rstd[:, 0:tb], E.rstd[:, 0:tb], reads=[E.rstdb], writes=[E.rstdb])

def evac(P, E, dst, src, reads, writes):
    if E.ev % 2 == 0: P.op("dve", "tensor_copy", dst, src, reads=reads, writes=writes)
    else: P.op("act", "activation", out=dst, in_=src, func=AF.Copy, reads=reads, writes=writes)
    E.ev += 1

def ld(P, AR, name, src, shape, q="sp"):
    t, b = AR.sb(name, shape); P.dma(q, t[:], src, writes=[b]); return t, b

def emit_mod(P, AR, E, wget, pmod, cT_d, wada_d, bada_d, nchunks, cpt=2):
    ct, cb = ld(P, AR, "cT", cT_d, [128, KC, 2]); bt, bb = ld(P, AR, "bada", bada_d, [128, 96])
    sc, scb = AR.sb("sc", [128, KC, 2]); mod_t, mod_b = AR.sb("mod", [128, 96, 2])
    P.op("act", "activation", out=sc[:], in_=ct[:], func=AF.Silu, reads=[cb], writes=[scb])
    for n0 in range(0, nchunks, cpt):
        wt, wb = wget()
        P.dma("sp", wt[:, :, 0:cpt * 128], wada_d[:, n0 * 128:(n0 + cpt) * 128].rearrange("(c p) n -> p c n", p=128), writes=[wb])
        for gi in range(cpt):
            n = n0 + gi
            pt, pb = pmod.get()
            for k in range(KC):
                P.op("pe", "matmul", pt[:, 0:2], wt[:, k, gi * 128:(gi + 1) * 128], sc[:, k, :], start=(k == 0), stop=(k == KC - 1), reads=[wb, scb], writes=[pb])
            P.op("dve", "tensor_scalar", mod_t[:, n, :], pt[:, 0:2], bt[:, n:n + 1], None, ALU.add, reads=[pb, bb], writes=[mod_b])
    return mod_t, mod_b

def yl_write(P, YL, row0, col0, width, src, reads):
    c = col0
    while c < col0 + width:
        blk, off = c // 256, c % 256
        w = min(256 - off, col0 + width - c)
        P.dma("act", YL[blk, row0:row0 + 128, off:off + w], src[:, c - col0:c - col0 + w], reads=reads)
        c += w

def emit_A(P, AR, dr, l):
    AR.begin(21500, 24576); E = common(P, AR)
    xsrc = dr["xT_in"] if l == 0 else dr["XG"]
    wpool = AR.pool("w", [128, KC, 512], 2, R=True)
    pmm = AR.pspool(5)
    wmod = AR.pool("wm", [128, KC, 256], 2)
    mod_t, mod_b = emit_mod(P, AR, E, lambda: wmod.get(), AR.pspool(2), dr["cT"], dr[f"w_ada{l}"], dr[f"b_ada{l}"], 32)
    g0, g0b = ld(P, AR, "g0", dr[f"g{l}"][:, 0, :], [128, KC])
    At, Ab = AR.sb("A", [128, KC, 2])
    for j in range(2):
        P.op("dve", "tensor_scalar", At[:, :, j], mod_t[:, KC:2 * KC, j], 1.0, math.sqrt(D), ALU.add, ALU.mult, reads=[mod_b], writes=[Ab])
        P.op("dve", "tensor_tensor", At[:, :, j], At[:, :, j], g0[:], ALU.mult, reads=[Ab, g0b], writes=[Ab])
    xs, xsb = AR.sb("xs", [128, KC, 512]); hs, hsb = AR.sb("hs", [128, KC, 512], R=True)
    opool = AR.pool("o", [128, 512], 3)
    win = dr[f"w_inr{l}"]; PF = dr["PF"]; PT = dr["PT"]
    blocks = []
    for r in range(2):
        for t0 in range(0, NLH, 512): blocks.append((r, t0, 512, 0, NCTX + r * NLH + t0))
        blocks.append((r, NLH, NCH2, 1, r * NCH2))
    for (r, t0, tb, j, dst0) in blocks:
        P.dma("sp", xs[:, :, 0:tb], xsrc.rearrange("c (r p) t -> r p c t", r=2)[r][:, :, t0:t0 + tb], writes=[xsb])
        rms_rstd(P, E, xs, xsb, KC, tb, 2048)
        for c in range(KC):
            P.op("dve", "scalar_tensor_tensor", RR_(hs[:, c, 0:tb]), xs[:, c, 0:tb], At[:, c, j:j + 1], E.rstd[:, 0:tb], ALU.mult, ALU.mult, reads=[xsb, Ab, E.rstdb], writes=[hsb])
            P.op("act", "activation", out=RR_(hs[:, c, 0:tb]), in_=hs[:, c, 0:tb], func=AF.Identity, bias=mod_t[:, c, j:j + 1], reads=[hsb, mod_b], writes=[hsb])
        row = 0; wt = None; wcol0 = None
        for (c0, wd) in FM_CHUNKS:
            if wt is None or not (wcol0 <= c0 and c0 + wd <= wcol0 + 512):
                wt, wb = wpool.get(); wcol0 = c0
                ncols = min(512, FM_ROWS - c0)
                P.dma("pool", RR_(wt[:, :, 0:ncols]), win[:, c0:c0 + ncols].rearrange("(c p) n -> p c n", p=128), writes=[wb])
            pt, pb = pmm.get(); off = c0 - wcol0
            for k in range(KC):
                P.op("pe", "matmul", pt[0:wd, 0:tb], RR_(wt[:, k, off:off + wd]), RR_(hs[:, k, 0:tb]), start=(k == 0), stop=(k == KC - 1), reads=[wb, hsb], writes=[pb])
            ot, ob = opool.get()
            evac(P, E, ot[0:wd, 0:tb], pt[0:wd, 0:tb], [pb], [ob])
            P.dma("act", PF[row:row + wd, dst0:dst0 + tb], ot[0:wd, 0:tb], reads=[ob])
            row += wd
        for n0 in range(0, TM_W, 512):
            nw = min(512, TM_W - n0)
            wt, wb = wpool.get()
            P.dma("pool", RR_(wt[:, :, 0:nw]), win[:, TM_COL0 + n0:TM_COL0 + n0 + nw].rearrange("(c p) n -> p c n", p=128), writes=[wb])
            for ts in range(tb // 128):
                pt, pb = pmm.get()
                for k in range(KC):
                    P.op("pe", "matmul", pt[:, 0:nw], RR_(hs[:, k, ts * 128:(ts + 1) * 128]), RR_(wt[:, k, 0:nw]), start=(k == 0), stop=(k == KC - 1), reads=[wb, hsb], writes=[pb])
                ot, ob = opool.get()
                evac(P, E, ot[:, 0:nw], pt[:, 0:nw], [pb], [ob])
                P.dma("act", PT[dst0 + ts * 128:dst0 + (ts + 1) * 128, n0:n0 + nw], ot[:, 0:nw], reads=[ob])

def emit_C(P, AR, dr, l, last):
    AR.begin(15100, 36864); E = common(P, AR)
    TB = 256
    xown = dr["xT_own"] if l == 0 else dr["XL2"]
    YG = dr["YG"]
    wpool = AR.pool("w", [128, 5632], 2, R=True)
    wmodc = AR.pool("wm", [128, KC, 128], 1)
    def wget():
        t, b = wpool.get(); return t[:, 0:KC * 256].rearrange("p (c n) -> p c n", c=KC), b
    pmm = AR.pspool(5)
    mod_t, mod_b = emit_mod(P, AR, E, lambda: wmodc.get(), AR.pspool(2), dr["cT"], dr[f"w_ada{l}"], dr[f"b_ada{l}"], 96, cpt=1)
    g, gb = ld(P, AR, "g", dr[f"g{l}"], [128, 4, KC])
    msk, mskb = ld(P, AR, "msk", dr["msk"], [128, 2])
    S, Sb = AR.sb("S", [128, 4, KC, 2])
    for j in range(2):
        for i, (mi, plus1) in enumerate([(1, True), (2, False), (4, True), (5, False)]):
            P.op("dve", "tensor_scalar", S[:, i, :, j], mod_t[:, mi * KC:(mi + 1) * KC, j], 1.0 if plus1 else 0.0, math.sqrt(D), ALU.add, ALU.mult, reads=[mod_b], writes=[Sb])
            P.op("dve", "tensor_tensor", S[:, i, :, j], S[:, i, :, j], g[:, i, :], ALU.mult, reads=[Sb, gb], writes=[Sb])
    xs, xsb = AR.sb("xs", [128, KC, TB]); hs, hsb = AR.sb("hs", [128, KC, TB], R=True)
    ys, ysb = AR.sb("ys", [128, 24, TB], R=True); mg, mgb = AR.sb("mg", [128, KC, TB], R=True); yl, ylb = AR.sb("yl", [128, KC, TB])
    act, actb = AR.sb("act", [128, FKC, TB], R=True)
    tmpp = AR.pool("tmp", [128, TB], 6); selp = AR.pool("sel", [128, TB], 4)
    wg = dr[f"w_gate{l}"]; wbr = dr[f"w_branch{l}"]; wout = dr[f"w_out{l}"]; wfi = dr[f"w_ffn_in{l}"]; wfo = dr[f"w_ffn_out{l}"]
    blocks = [(t0, min(TB, NLH - t0), 0) for t0 in range(0, NLH, TB)]
    if not last: blocks += [(NLH, NCH2, 1)]
    xTr = xown.rearrange("(c p) t -> p c t", p=128)
    outd = dr["xo"] if last else dr["XL2"]
    outr = outd.rearrange("(c p) t -> p c t", p=128)
    def normmod(src, srcb, dst, dstb, si, bi, j, tb):
        rms_rstd(P, E, src, srcb, KC, tb, 2048)
        for c in range(KC):
            P.op("dve", "scalar_tensor_tensor", RR_(dst[:, c, 0:tb]), src[:, c, 0:tb], S[:, si, c, j:j + 1], E.rstd[:, 0:tb], ALU.mult, ALU.mult, reads=[srcb, Sb, E.rstdb], writes=[dstb])
            P.op("act", "activation", out=RR_(dst[:, c, 0:tb]), in_=dst[:, c, 0:tb], func=AF.Identity, bias=mod_t[:, bi * KC + c, j:j + 1], reads=[dstb, mod_b], writes=[dstb])
    def resid(src, srcb, si, j, tb):
        rms_rstd(P, E, src, srcb, KC, tb, 2048)
        for c in range(KC):
            tt, ttb = tmpp.get()
            P.op("dve", "scalar_tensor_tensor", tt[:, 0:tb], src[:, c, 0:tb], S[:, si, c, j:j + 1], E.rstd[:, 0:tb], ALU.mult, ALU.mult, reads=[srcb, Sb, E.rstdb], writes=[ttb])
            P.op("dve", "tensor_tensor", xs[:, c, 0:tb], xs[:, c, 0:tb], tt[:, 0:tb], ALU.add, reads=[xsb, ttb], writes=[xsb])
    for (t0, tb, j) in blocks:
        P.dma("sp", xs[:, :, 0:tb], xTr[:, :, t0:t0 + tb], writes=[xsb])
        for i in range(3):
            for gg in range(2):
                for c in range(4):
                    ch = i * 8 + gg * 4 + c
                    r0 = gg * 1536 + i * 512 + c * 128
                    cols = [(NCTX + r * NLH + t0) if j == 0 else (r * NCH2) for r in range(2)]
                    s0, s0b = selp.get(); st, stb = selp.get()
                    P.dma("act", s0[:, 0:tb], YG[cols[0] // 256, r0:r0 + 128, cols[0] % 256:cols[0] % 256 + tb], writes=[s0b])
                    P.dma("act", st[:, 0:tb], YG[cols[1] // 256, r0:r0 + 128, cols[1] % 256:cols[1] % 256 + tb], writes=[stb])
                    P.op("dve", "tensor_scalar", s0[:, 0:tb], s0[:, 0:tb], msk[:, 0:1], None, ALU.mult, reads=[s0b, mskb], writes=[s0b])
                    P.op("dve", "scalar_tensor_tensor", RR_(ys[:, ch, 0:tb]), st[:, 0:tb], msk[:, 1:2], s0[:, 0:tb], ALU.mult, ALU.add, reads=[stb, mskb, s0b], writes=[ysb])
        normmod(xs, xsb, hs, hsb, 0, 0, j, tb)
        for i in range(3):
            for ng in range(0, KC, 4):
                pbs = [pmm.get() for _ in range(4)]
                for kh in range(2):
                    wt, wb = wpool.get()
                    wv = wt[:, 0:8 * 512].rearrange("p (c n) -> p c n", c=8)
                    P.dma("pool", RR_(wv), wg[kh * 1024:(kh + 1) * 1024, i * D + ng * 128: i * D + (ng + 4) * 128].rearrange("(c p) n -> p c n", p=128), writes=[wb])
                    for gi in range(4):
                        for k in range(8):
                            P.op("pe", "matmul", pbs[gi][0][:, 0:tb], RR_(wv[:, k, gi * 128:(gi + 1) * 128]), RR_(hs[:, kh * 8 + k, 0:tb]), start=(kh == 0 and k == 0), stop=(kh == 1 and k == 7), reads=[wb, hsb], writes=[pbs[gi][1]])
                tts = []
                for gi in range(4):
                    tt, ttb = tmpp.get()
                    P.op("act", "activation", out=tt[:, 0:tb], in_=pbs[gi][0][:, 0:tb], func=AF.Sigmoid, reads=[pbs[gi][1]], writes=[ttb])
                    tts.append((tt, ttb))
                pbs = [pmm.get() for _ in range(4)]
                wt, wb = wpool.get()
                wv = wt[:, 0:8 * 512].rearrange("p (c n) -> p c n", c=8)
                P.dma("pool", RR_(wv), wbr[i, :, ng * 128:(ng + 4) * 128].rearrange("(c p) n -> p c n", p=128), writes=[wb])
                for gi in range(4):
                    for k in range(8):
                        P.op("pe", "matmul", pbs[gi][0][:, 0:tb], RR_(wv[:, k, gi * 128:(gi + 1) * 128]), RR_(ys[:, i * 8 + k, 0:tb]), start=(k == 0), stop=(k == 7), reads=[wb, ysb], writes=[pbs[gi][1]])
                for gi in range(4):
                    n = ng + gi
                    tt, ttb = tts[gi]
                    if i == 0:
                        P.op("dve", "tensor_tensor", RR_(mg[:, n, 0:tb]), tt[:, 0:tb], pbs[gi][0][:, 0:tb], ALU.mult, reads=[ttb, pbs[gi][1]], writes=[mgb])
                    else:
                        P.op("dve", "tensor_tensor", tt[:, 0:tb], tt[:, 0:tb], pbs[gi][0][:, 0:tb], ALU.mult, reads=[ttb, pbs[gi][1]], writes=[ttb])
                        P.op("dve", "tensor_tensor", RR_(mg[:, n, 0:tb]), mg[:, n, 0:tb], tt[:, 0:tb], ALU.add, reads=[ttb, mgb], writes=[mgb])
        for ng in range(0, KC, 4):
            pbs = [pmm.get() for _ in range(4)]
            for kh in range(2):
                wt, wb = wpool.get()
                wv = wt[:, 0:8 * 512].rearrange("p (c n) -> p c n", c=8)
                P.dma("pool", RR_(wv), wout[kh * 1024:(kh + 1) * 1024, ng * 128:(ng + 4) * 128].rearrange("(c p) n -> p c n", p=128), writes=[wb])
                for gi in range(4):
                    for k in range(8):
                        P.op("pe", "matmul", pbs[gi][0][:, 0:tb], RR_(wv[:, k, gi * 128:(gi + 1) * 128]), RR_(mg[:, kh * 8 + k, 0:tb]), start=(kh == 0 and k == 0), stop=(kh == 1 and k == 7), reads=[wb, mgb], writes=[pbs[gi][1]])
            for gi in range(4):
                P.op("act", "activation", out=yl[:, ng + gi, 0:tb], in_=pbs[gi][0][:, 0:tb], func=AF.Copy, reads=[pbs[gi][1]], writes=[ylb])
        resid(yl, ylb, 1, j, tb)
        normmod(xs, xsb, hs, hsb, 2, 3, j, tb)
        for h0 in range(0, FKC, 4):
            tts = []
            for part in range(2):
                pbs = [pmm.get() for _ in range(4)]
                for kh in range(2):
                    wt, wb = wpool.get()
                    wv = wt[:, 0:8 * 512].rearrange("p (c n) -> p c n", c=8)
                    P.dma("pool", RR_(wv), wfi[kh * 1024:(kh + 1) * 1024, part * FF + h0 * 128:part * FF + (h0 + 4) * 128].rearrange("(c p) n -> p c n", p=128), writes=[wb])
                    for gi in range(4):
                        for k in range(8):
                            P.op("pe", "matmul", pbs[gi][0][:, 0:tb], RR_(wv[:, k, gi * 128:(gi + 1) * 128]), RR_(hs[:, kh * 8 + k, 0:tb]), start=(kh == 0 and k == 0), stop=(kh == 1 and k == 7), reads=[wb, hsb], writes=[pbs[gi][1]])
                for gi in range(4):
                    if part == 0:
                        tt, ttb = tmpp.get()
                        P.op("act", "activation", out=tt[:, 0:tb], in_=pbs[gi][0][:, 0:tb], func=AF.Silu, reads=[pbs[gi][1]], writes=[ttb])
                        tts.append((tt, ttb))
                    else:
                        tt, ttb = tts[gi]
                        P.op("dve", "tensor_tensor", RR_(act[:, h0 + gi, 0:tb]), tt[:, 0:tb], pbs[gi][0][:, 0:tb], ALU.mult, reads=[ttb, pbs[gi][1]], writes=[actb])
        for ng in range(0, KC, 4):
            pbs = [pmm.get() for _ in range(4)]
            for kq in range(4):
                wt, wb = wpool.get()
                wv = wt[:, 0:11 * 512].rearrange("p (c n) -> p c n", c=11)
                P.dma("pool", RR_(wv), wfo[kq * 1408:(kq + 1) * 1408, ng * 128:(ng + 4) * 128].rearrange("(c p) n -> p c n", p=128), writes=[wb])
                for gi in range(4):
                    for k in range(11):
                        P.op("pe", "matmul", pbs[gi][0][:, 0:tb], RR_(wv[:, k, gi * 128:(gi + 1) * 128]), RR_(act[:, kq * 11 + k, 0:tb]), start=(kq == 0 and k == 0), stop=(kq == 3 and k == 10), reads=[wb, actb], writes=[pbs[gi][1]])
            for gi in range(4):
                P.op("act", "activation", out=yl[:, ng + gi, 0:tb], in_=pbs[gi][0][:, 0:tb], func=AF.Copy, reads=[pbs[gi][1]], writes=[ylb])
        resid(yl, ylb, 3, j, tb)
        P.dma("sp", outr[:, :, t0:t0 + tb], xs[:, :, 0:tb], reads=[xsb])

def emit_FN(P, AR, dr, last):
    AR.begin(4000, 39424); E = common(P, AR)
    PF = dr["PF"]; YL = dr["YL"]; cld = dr["cl"]; sld = dr["sln"]
    cw2, cw2b = AR.sb("cw2s", [128, 2, 512], R=True); P.dma("pool", RR_(cw2[:]), dr["cw2"].rearrange("(c p) n -> p c n", p=128), writes=[cw2b])
    us, usb = AR.sb("us", [128, 2, NTOT], R=True); Zs, Zsb = AR.sb("Zs", [128, 32, 512], R=True); Zc, Zcb = AR.sb("Zc", [128, 2, 512], R=True)
    cp = AR.pool("ct", [128, 4, 512], 3, R=True); spn = AR.pool("st", [128, 4, 512], 3, R=True)
    pz = AR.pspool(2); py = AR.pspool(4); op = AR.pool("o", [128, 512], 3)
    uTr = PF[2368:2880, :].rearrange("(c p) t -> p c t", p=128)
    for g in range(2):
        P.dma("pool", RR_(us[:]), uTr[:, 2 * g:2 * g + 2, :], writes=[usb])
        for t in range(32):
            pt, pb = pz.get()
            for kc in range(2):
                P.op("pe", "matmul", pt[:], RR_(us[:, kc, NCTX + t * 128:NCTX + (t + 1) * 128]), RR_(cw2[:, kc, :]), start=(kc == 0), stop=(kc == 1), reads=[usb, cw2b], writes=[pb])
            evac(P, E, RR_(Zs[:, t, :]), pt[:], [pb], [Zsb])
        if not last:
            for t in range(2):
                pt, pb = pz.get()
                for kc in range(2):
                    P.op("pe", "matmul", pt[:], RR_(us[:, kc, t * 128:(t + 1) * 128]), RR_(cw2[:, kc, :]), start=(kc == 0), stop=(kc == 1), reads=[usb, cw2b], writes=[pb])
                P.op("dve", "tensor_copy", RR_(Zc[:, t, 0:256]), pt[:, 0:256], reads=[pb], writes=[Zcb])
                P.op("dve", "tensor_scalar", RR_(Zc[:, t, 256:512]), pt[:, 256:512], -1.0, None, ALU.mult, reads=[pb], writes=[Zcb])
            for ch in range(2):
                pt, pb = py.get(); i = 0
                for t in range(2):
                    for part in range(2):
                        P.op("pe", "matmul", pt[:, 0:256], RR_(Zc[:, t, part * 256 + ch * 128: part * 256 + (ch + 1) * 128]), RR_(cw2[:, t, part * 256:(part + 1) * 256]), start=(i == 0), stop=(i == 3), reads=[Zcb, cw2b], writes=[pb])
                        i += 1
                ot, ob = op.get()
                P.op("act", "activation", out=ot[:, 0:256], in_=pt[:, 0:256], func=AF.Copy, scale=1.0 / 256.0, reads=[pb], writes=[ob])
                yl_write(P, YL, 512 + g * 256 + ch * 128, 0, 256, ot, [ob])
        for o in range(8):
            pts = [py.get(), py.get()]
            for t0 in range(0, 32, 4):
                ct, cb = cp.get(); st, stb = spn.get()
                P.dma("pool", RR_(ct[:]), cld[t0 * 128:(t0 + 4) * 128, o * 512:(o + 1) * 512].rearrange("(t p) n -> p t n", p=128), writes=[cb])
                P.dma("pool", RR_(st[:]), sld[t0 * 128:(t0 + 4) * 128, o * 512:(o + 1) * 512].rearrange("(t p) n -> p t n", p=128), writes=[stb])
                for tt in range(4):
                    t = t0 + tt
                    for ch in range(2):
                        pt, pb = pts[ch]
                        P.op("pe", "matmul", pt[:], RR_(Zs[:, t, ch * 128:(ch + 1) * 128]), RR_(ct[:, tt, :]), start=(t == 0), stop=False, reads=[Zsb, cb], writes=[pb])
                        P.op("pe", "matmul", pt[:], RR_(Zs[:, t, 256 + ch * 128:256 + (ch + 1) * 128]), RR_(st[:, tt, :]), start=False, stop=(t == 31), reads=[Zsb, stb], writes=[pb])
            for ch in range(2):
                pt, pb = pts[ch]; ot, ob = op.get()
                if ch == 0: P.op("act", "activation", out=ot[:], in_=pt[:], func=AF.Copy, scale=1.0 / 1024.0, reads=[pb], writes=[ob])
                else: P.op("dve", "tensor_scalar", ot[:], pt[:], 1.0 / 1024.0, None, ALU.mult, reads=[pb], writes=[ob])
                yl_write(P, YL, 512 + g * 256 + ch * 128, NCTX + o * 512, 512, ot, [ob])

def emit_MLA(P, AR, dr, l, last, NH=4):
    AR.begin(28000, 17408); E = common(P, AR)
    PF = dr["PF"]; YL = dr["YL"]
    cin = PF[1536:2368, :]
    gq, gqb = ld(P, AR, "gq_s", dr[f"gq{l}"], [128, 4]); gkv, gkvb = ld(P, AR, "gkv_s", dr[f"gkv{l}"], [128, 2])
    P.op("dve", "tensor_scalar", gq[:], gq[:], math.sqrt(512.0), None, ALU.mult, reads=[gqb], writes=[gqb])
    P.op("dve", "tensor_scalar", gkv[:], gkv[:], math.sqrt(256.0), None, ALU.mult, reads=[gkvb], writes=[gkvb])
    wqn, wqnb = ld(P, AR, "wqn_s", dr[f"wqn{l}"].rearrange("(c p) n -> p c n", p=128), [128, 4, NH * 128])
    wqr, wqrb = ld(P, AR, "wqr_s", dr[f"wqr{l}"].rearrange("(c p) n -> p c n", p=128), [128, 4, NH * 64])
    wk, wkb = ld(P, AR, "wk_s", dr[f"wk{l}"].rearrange("(c p) n -> p c n", p=128), [128, 2, NH * 128])
    wv, wvb = ld(P, AR, "wv_s", dr[f"wv{l}"].rearrange("(c p) n -> p c n", p=128), [128, 2, NH * 128])
    Rm, Rmb = ld(P, AR, "R_s", dr["Rm"], [64, 64]); ident, identb = ld(P, AR, "id_s", dr["ident"], [128, 128])
    Qn, Qnb = AR.sb("Qn", [128, NTOT], R=True); Qr, Qrb = AR.sb("Qr", [64, NTOT], R=True)
    Kn, Knb = AR.sb("Kn", [128, NTOT], R=True); Kr, Krb = AR.sb("Kr", [64, NTOT], R=True)
    V, Vb = AR.sb("V", [128, NTOT // 128, 128]); Ss, Ssb = AR.sb("Ss", [128, NTOT])
    cb_t, cb_b = AR.sb("cblk", [128, 6, 512]); krb_t, krb_b = AR.sb("krblk", [64, 512])
    cqn, cqnb = AR.sb("cqn", [128, 4, 512]); ckvn, ckvnb = AR.sb("ckvn", [128, 2, 512])
    cs_t, cs_b = AR.sb("cosb", [64, 512]); sn_t, sn_b = AR.sb("sinb", [64, 512])
    tq, tqb = AR.sb("tq", [64, 512]); t2, t2b = AR.sb("t2", [64, 512])
    pmm = AR.pspool(5); po_p = AR.pspool(2)
    PTp = AR.pool("PT", [128, 512], 2); osb_p = AR.pool("osb", [128, 128], 2); oT_p = AR.pool("oT", [128, 128], 2); st_p = AR.pool("stat", [128, 4], 2)
    cinr = cin[0:768, :].rearrange("(c p) t -> p c t", p=128)
    cos_d = dr["cosT"]; sin_d = dr["sinT"]
    blocks = [(0, 256, False)] + [(NCTX + i * 512, 512, True) for i in range(8)]
    for h in range(NH):
        for (t0, tb, lat) in blocks:
            P.dma("sp", cb_t[:, :, 0:tb], cinr[:, :, t0:t0 + tb], writes=[cb_b])
            P.dma("sp", krb_t[:, 0:tb], cin[768:832, t0:t0 + tb], writes=[krb_b])
            if lat:
                P.dma("act", cs_t[:, 0:tb], cos_d[:, t0 - NCTX:t0 - NCTX + tb], writes=[cs_b])
                P.dma("act", sn_t[:, 0:tb], sin_d[:, t0 - NCTX:t0 - NCTX + tb], writes=[sn_b])
            rms_rstd(P, E, cb_t[:, 0:4, :], cb_b, 4, tb, 512)
            for c in range(4):
                P.op("dve", "scalar_tensor_tensor", cqn[:, c, 0:tb], cb_t[:, c, 0:tb], gq[:, c:c + 1], E.rstd[:, 0:tb], ALU.mult, ALU.mult, reads=[cb_b, gqb, E.rstdb], writes=[cqnb])
            rms_rstd(P, E, cb_t[:, 4:6, :], cb_b, 2, tb, 256)
            for c in range(2):
                P.op("dve", "scalar_tensor_tensor", ckvn[:, c, 0:tb], cb_t[:, 4 + c, 0:tb], gkv[:, c:c + 1], E.rstd[:, 0:tb], ALU.mult, ALU.mult, reads=[cb_b, gkvb, E.rstdb], writes=[ckvnb])
            pt, pb = pmm.get()
            for kc in range(4):
                P.op("pe", "matmul", pt[:, 0:tb], wqn[:, kc, h * 128:(h + 1) * 128], cqn[:, kc, 0:tb], start=(kc == 0), stop=(kc == 3), reads=[wqnb, cqnb], writes=[pb])
            evac(P, E, RR_(Qn[:, t0:t0 + tb]), pt[:, 0:tb], [pb], [Qnb])
            pt, pb = pmm.get()
            for kc in range(2):
                P.op("pe", "matmul", pt[:, 0:tb], wk[:, kc, h * 128:(h + 1) * 128], ckvn[:, kc, 0:tb], start=(kc == 0), stop=(kc == 1), reads=[wkb, ckvnb], writes=[pb])
            evac(P, E, RR_(Kn[:, t0:t0 + tb]), pt[:, 0:tb], [pb], [Knb])
            for ts in range(tb // 128):
                pt, pb = pmm.get()
                for kc in range(2):
                    P.op("pe", "matmul", pt[:, 0:128], ckvn[:, kc, ts * 128:(ts + 1) * 128], wv[:, kc, h * 128:(h + 1) * 128], start=(kc == 0), stop=(kc == 1), reads=[wvb, ckvnb], writes=[pb])
                evac(P, E, V[:, t0 // 128 + ts, :], pt[:, 0:128], [pb], [Vb])
            pt, pb = pmm.get()
            for kc in range(4):
                P.op("pe", "matmul", pt[0:64, 0:tb], wqr[:, kc, h * 64:(h + 1) * 64], cqn[:, kc, 0:tb], start=(kc == 0), stop=(kc == 3), reads=[wqrb, cqnb], writes=[pb])
            def rope(dst, dstb, src, srcb):
                p2, p2b = pmm.get()
                P.op("pe", "matmul", p2[0:64, 0:tb], Rm[:, :], src, start=True, stop=True, reads=[Rmb, srcb], writes=[p2b])
                P.op("dve", "tensor_tensor", t2[:, 0:tb], p2[0:64, 0:tb], sn_t[:, 0:tb], ALU.mult, reads=[p2b, sn_b], writes=[t2b])
                P.op("pool", "tensor_tensor", RR_(dst[:, t0:t0 + tb]), src, cs_t[:, 0:tb], ALU.mult, reads=[srcb, cs_b], writes=[dstb])
                P.op("dve", "tensor_tensor", RR_(dst[:, t0:t0 + tb]), dst[:, t0:t0 + tb], t2[:, 0:tb], ALU.add, reads=[dstb, t2b], writes=[dstb])
            if lat:
                evac(P, E, tq[:, 0:tb], pt[0:64, 0:tb], [pb], [tqb])
                rope(Qr, Qrb, tq[:, 0:tb], tqb)
                rope(Kr, Krb, krb_t[:, 0:tb], krb_b)
            else:
                evac(P, E, RR_(Qr[:, t0:t0 + tb]), pt[0:64, 0:tb], [pb], [Qrb])
                P.op("pool", "tensor_copy", RR_(Kr[:, t0:t0 + tb]), krb_t[:, 0:tb], reads=[krb_b], writes=[Krb])
        qtiles = [(NCTX + qt * 128, 0, NTOT) for qt in range(32)]
        if not last: qtiles = [(qt * 128, 0, NCTX) for qt in range(2)] + qtiles
        for (q0, k0, k1) in qtiles:
            nk = k1 - k0
            for kb0 in range(k0, k1, 512):
                kw = min(512, k1 - kb0)
                pt, pb = pmm.get()
                P.op("pe", "matmul", pt[:, 0:kw], RR_(Qn[:, q0:q0 + 128]), RR_(Kn[:, kb0:kb0 + kw]), start=True, stop=False, reads=[Qnb, Knb], writes=[pb])
                P.op("pe", "matmul", pt[:, 0:kw], RR_(Qr[:, q0:q0 + 128]), RR_(Kr[:, kb0:kb0 + kw]), start=False, stop=True, reads=[Qrb, Krb], writes=[pb])
                evac(P, E, Ss[:, kb0:kb0 + kw], pt[:, 0:kw], [pb], [Ssb])
            stt, stb = st_p.get()
            P.op("dve", "tensor_reduce", stt[:, 0:1], Ss[:, k0:k1], AX.X, ALU.max, reads=[Ssb], writes=[stb])
            P.op("dve", "tensor_scalar", stt[:, 1:2], stt[:, 0:1], -MLA_SCALE, None, ALU.mult, reads=[stb], writes=[stb])
            P.op("pool", "memset", stt[:, 2:3], 0.0, writes=[stb])
            P.op("act", "activation", out=Ss[:, k0:k1], in_=Ss[:, k0:k1], func=AF.Exp, scale=MLA_SCALE, bias=stt[:, 1:2], accum_out=stt[:, 2:3], reads=[Ssb, stb], writes=[Ssb, stb])
            P.op("dve", "reciprocal", stt[:, 3:4], stt[:, 2:3], reads=[stb], writes=[stb])
            po, pob = po_p.get()
            ntile = nk // 128
            for g0 in range(0, ntile, 4):
                gn = min(4, ntile - g0)
                ptp, ptpb = pmm.get()
                for i in range(gn):
                    kt = k0 // 128 + g0 + i
                    P.op("pe", "transpose", ptp[:, i * 128:(i + 1) * 128], Ss[:, kt * 128:(kt + 1) * 128], ident[:], reads=[Ssb, identb], writes=[ptpb])
                PT, PTb = PTp.get()
                evac(P, E, PT[:, 0:gn * 128], ptp[:, 0:gn * 128], [ptpb], [PTb])
                for i in range(gn):
                    kt = k0 // 128 + g0 + i
                    P.op("pe", "matmul", po[:, 0:128], PT[:, i * 128:(i + 1) * 128], V[:, kt, :], start=(g0 + i == 0), stop=(g0 + i == ntile - 1), reads=[PTb, Vb], writes=[pob])
            ot, ob = osb_p.get()
            P.op("dve", "tensor_scalar", ot[:], po[:, 0:128], stt[:, 3:4], None, ALU.mult, reads=[pob, stb], writes=[ob])
            pq, pqb = pmm.get()
            P.op("pe", "transpose", pq[:, 0:128], ot[:], ident[:], reads=[ob, identb], writes=[pqb])
            oT, oTb = oT_p.get()
            evac(P, E, oT[:], pq[:, 0:128], [pqb], [oTb])
            yl_write(P, YL, 1024 + h * 128, q0, 128, oT, [oTb])

def emit_DN(P, AR, dr, l, last, NH=4):
    AR.begin(51900, 8); E = common(P, AR)
    G = 2 * NH
    PF = dr["PF"]; PT = dr["PT"]; YL = dr["YL"]
    TRI2, TRI2b = ld(P, AR, "tri2s", dr["tri2"], [64, 2, 64]); MS2, MS2b = ld(P, AR, "ms2s", dr["ms2"], [64, 2, 64])
    I2, I2b = ld(P, AR, "i2s", dr["i2"], [64, 2, 64]); ident, identb = ld(P, AR, "ids", dr["ident"], [128, 128])
    cw, cwb = ld(P, AR, "cws", dr[f"convw{l}"], [128, 3 * NH, 5]); gn, gnb = ld(P, AR, "gns", dr[f"gnorm{l}"], [64, 128])
    alog, alogb = ld(P, AR, "alogs", dr[f"alog{l}"], [64, G]); dtb, dtbb = ld(P, AR, "dtbs", dr[f"dtb{l}"], [64, G])
    ones, onesb = E.ones, E.onesb
    one1, one1b = AR.sb("one1", [128, 1]); P.op("pool", "memset", one1[:], 1.0, writes=[one1b])
    eps6, eps6b = E.eps[1]
    psp = AR.pspool(7)
    bl, blb = ld(P, AR, "bls", PT[:, 512:512 + G].rearrange("(n c) x -> c n x", c=64), [64, NCH, G])
    al, alb = ld(P, AR, "als", PT[:, 512 + G:512 + 2 * G].rearrange("(n c) x -> c n x", c=64), [64, NCH, G], q="act")
    BETA, BETAb = AR.sb("BETA", [64, NCH, G]); NBETA, NBETAb = AR.sb("NBETA", [64, NCH, G])
    gt, gtb = AR.sb("gt", [64, NCH, G]); GC, GCb = AR.sb("GC", [64, NCH, G]); NGC, NGCb = AR.sb("NGC", [64, NCH, G])
    BEG, BEGb = AR.sb("BEG", [64, NCH, G]); EKD, EKDb = AR.sb("EKD", [64, NCH, G]); EGL, EGLb = AR.sb("EGL", [128, NCH, G])
    P.op("act", "activation", out=BETA[:], in_=bl[:], func=AF.Sigmoid, reads=[blb], writes=[BETAb])
    P.op("dve", "tensor_scalar", NBETA[:], BETA[:], -1.0, None, ALU.mult, reads=[BETAb], writes=[NBETAb])
    for c in range(G):
        P.op("act", "activation", out=gt[:, :, c], in_=al[:, :, c], func=AF.Exp, bias=dtb[:, c:c + 1], reads=[alb, dtbb], writes=[gtb])
    P.op("act", "activation", out=gt[:], in_=gt[:], func=AF.Ln, bias=one1[0:64, 0:1], reads=[gtb, one1b], writes=[gtb])
    P.op("act", "activation", out=alog[:], in_=alog[:], func=AF.Exp, reads=[alogb], writes=[alogb])
    P.op("dve", "tensor_scalar", alog[:], alog[:], -1.0, None, ALU.mult, reads=[alogb], writes=[alogb])
    for c in range(G):
        P.op("dve", "tensor_scalar", gt[:, :, c], gt[:, :, c], alog[:, c:c + 1], None, ALU.mult, reads=[gtb, alogb], writes=[gtb])
    gflat = gt[:].rearrange("p n g -> p (n g)")
    NF = NCH * G; H2 = NF // 2
    for d in range(2):
        pt, pb = psp.get(); pt2, pb2 = psp.get()
        for (pp, ppb, c0) in ((pt, pb, 0), (pt2, pb2, H2)):
            P.op("pe", "matmul", pp[0:64, 0:H2], TRI2[:, d, :], gflat[:, c0:c0 + H2], start=True, stop=True, reads=[TRI2b, gtb], writes=[ppb])
        for (pp, ppb, c0) in ((pt, pb, 0), (pt2, pb2, H2)):
            nn = H2 // G
            src = pp[0:64, 0:H2].rearrange("p (n g) -> p n g", g=G)
            P.op("dve", "tensor_copy", GC[:, c0 // G:c0 // G + nn, d * NH:(d + 1) * NH], src[:, :, d * NH:(d + 1) * NH], reads=[ppb], writes=[GCb])
    P.op("dve", "tensor_scalar", NGC[:], GC[:], -1.0, None, ALU.mult, reads=[GCb], writes=[NGCb])
    P.op("act", "activation", out=BEG[:], in_=GC[:], func=AF.Exp, reads=[GCb], writes=[BEGb])
    P.op("dve", "tensor_tensor", BEG[:], BEG[:], BETA[:], ALU.mult, reads=[BEGb, BETAb], writes=[BEGb])
    EGLf = EGL[:].rearrange("p n g -> p (n g)"); EKDf = EKD[:].rearrange("p n g -> p (n g)"); GCf = GC[:].rearrange("p n g -> p (n g)")
    for c0 in (0, H2):
        pt, pb = psp.get()
        P.op("pe", "matmul", pt[:, 0:H2], ones[0:64, :], gflat[:, c0:c0 + H2], start=True, stop=True, reads=[onesb, gtb], writes=[pb])
        P.op("dve", "tensor_tensor", EKDf[:, c0:c0 + H2], pt[0:64, 0:H2], GCf[:, c0:c0 + H2], ALU.subtract, reads=[pb, GCb], writes=[EKDb])
        P.op("act", "activation", out=EGLf[:, c0:c0 + H2], in_=pt[:, 0:H2], func=AF.Exp, reads=[pb], writes=[EGLb])
    P.op("act", "activation", out=EKD[:], in_=EKD[:], func=AF.Exp, reads=[EKDb], writes=[EKDb])
    QT, QTb = AR.sb("QT", [128, NTOT]); KT, KTb = AR.sb("KT", [128, NTOT]); VT, VTb = AR.sb("VT", [128, NTOT])
    Xr, Xrb = AR.sb("Xr", [128, NTOT]); O, Ob = AR.sb("O", [64, NCH, 128])
    rs, rsb = E.rstd, E.rstdb
    w128 = AR.pool("w128", [64, 2, 64], 12); xxp = AR.pool("xx", [64, 2, 128], 6); zp = AR.pool("zz", [64, 2, 64], 26); qkp = AR.pool("qk", [64, 2, 64], 6)
    egp = AR.pool("egr", [128, 2, 64], 4); qgp = AR.pool("qg", [128, 2, 64], 6); nwp = AR.pool("nw", [128, 2, 64], 6)
    tmp = AR.pool("tm", [64, 2, 128], 16)
    Sp = [AR.pool(f"S{d}", [128, 128], 2) for d in range(2)]
    GRP = 4
    zt, ztb = AR.sb("zt", [64, GRP, 128]); yt, ytb = AR.sb("yt", [64, GRP, 128])
    st17, st17b = AR.sb("st17", [64, GRP]); yT_p = AR.pool("yT", [128, 512], 2)
    segs = [(0, NCTX), (NCTX, NTOT)]
    for h in range(NH):
        for qi, (dst, dstb) in enumerate(((QT, QTb), (KT, KTb), (VT, VTb))):
            ci = qi * NH + h
            P.dma("sp", Xr[:], PF[qi * 512 + h * 128: qi * 512 + (h + 1) * 128, :], writes=[Xrb])
            for (a, b) in segs:
                P.op("act", "activation", out=dst[:, a:b], in_=Xr[:, a:b], func=AF.Copy, scale=cw[:, ci, 2:3], reads=[Xrb, cwb], writes=[dstb])
                for tap in (0, 1, 3, 4):
                    off = tap - 2
                    if off < 0: o0, o1, i0, i1 = a - off, b, a, b + off
                    else: o0, o1, i0, i1 = a, b - off, a + off, b
                    P.op("dve", "scalar_tensor_tensor", dst[:, o0:o1], Xr[:, i0:i1], cw[:, ci, tap:tap + 1], dst[:, o0:o1], ALU.mult, ALU.add, reads=[Xrb, cwb, dstb], writes=[dstb])
            P.op("act", "activation", out=dst[:], in_=dst[:], func=AF.Silu, reads=[dstb], writes=[dstb])
            if qi < 2:
                for t0 in range(0, NTOT, 512):
                    tb = min(512, NTOT - t0)
                    sq, sqb = E.sqp.get()
                    P.op("act", "activation", out=sq[:, 0:tb], in_=dst[:, t0:t0 + tb], func=AF.Square, reads=[dstb], writes=[sqb])
                    pt, pb = psp.get()
                    P.op("pe", "matmul", pt[:, 0:tb], ones[:], sq[:, 0:tb], start=True, stop=True, reads=[onesb, sqb], writes=[pb])
                    P.op("act", "activation", out=rs[:, 0:tb], in_=pt[:, 0:tb], func=AF.Sqrt, bias=eps6[:, 0:1], reads=[pb, eps6b], writes=[rsb])
                    P.op("dve", "reciprocal", rs[:, 0:tb], rs[:, 0:tb], reads=[rsb], writes=[rsb])
                    if qi == 0:
                        P.op("dve", "scalar_tensor_tensor", dst[:, t0:t0 + tb], dst[:, t0:t0 + tb], 128 ** -0.5, rs[:, 0:tb], ALU.mult, ALU.mult, reads=[dstb, rsb], writes=[dstb])
                    else:
                        P.op("dve", "tensor_tensor", dst[:, t0:t0 + tb], dst[:, t0:t0 + tb], rs[:, 0:tb], ALU.mult, reads=[dstb, rsb], writes=[dstb])
        S = []
        for d in range(2):
            st, stb = Sp[d].get()
            P.op("pool", "memset", st[:], 0.0, writes=[stb])
            S.append((st, stb))
        visited = set()
        def chunk_of(s, d):
            if d == 0: return s
            return 3 - s if s < 4 else 71 - s
        def pre(s):
            ns = [chunk_of(s, d) for d in range(2)]; cols = [d * NH + h for d in range(2)]
            toks = [slice(n * 64, (n + 1) * 64) for n in ns]
            Gd, Gdb = w128.get()
            for d in range(2):
                P.op("dve", "tensor_scalar", Gd[:, d, :], TRI2[:, d, :], gt[:, ns[d], cols[d]:cols[d] + 1], None, ALU.mult, reads=[TRI2b, gtb], writes=[Gdb])
            yield
            pa, pab = psp.get()
            P.op("pe", "matmul", pa[:, 0:128], ones[0:64, :], Gd[:].rearrange("p d j -> p (d j)"), start=True, stop=True, reads=[onesb, Gdb], writes=[pab])
            E1, E1b = w128.get(); E2, E2b = w128.get(); EGr, EGrb = egp.get()
            for d in range(2):
                P.op("act", "activation", out=E1[:, d, :], in_=pa[0:64, d * 64:(d + 1) * 64], func=AF.Exp, scale=-1.0, bias=GC[:, ns[d], cols[d]:cols[d] + 1], reads=[pab, GCb], writes=[E1b])
                P.op("act", "activation", out=E2[:, d, :], in_=pa[0:64, d * 64:(d + 1) * 64], func=AF.Exp, bias=NGC[:, ns[d], cols[d]:cols[d] + 1], reads=[pab, NGCb], writes=[E2b])
            P.op("act", "activation", out=EGr[:].rearrange("p d j -> p (d j)"), in_=pa[:, 0:128], func=AF.Exp, reads=[pab], writes=[EGrb])
            yield
            D1, D1b = w128.get(); D2, D2b = w128.get()
            P.op("dve", "scalar_tensor_tensor", D1[:], E1[:], 1.0, MS2[:], ALU.min, ALU.mult, reads=[E1b, MS2b], writes=[D1b])
            P.op("dve", "scalar_tensor_tensor", D2[:], E2[:], 1.0, TRI2[:], ALU.min, ALU.mult, reads=[E2b, TRI2b], writes=[D2b])
            for d in range(2):
                P.op("pool", "tensor_scalar", D1[:, d, :], D1[:, d, :], NBETA[:, ns[d], cols[d]:cols[d] + 1], None, ALU.mult, reads=[D1b, NBETAb], writes=[D1b])
            yield
            pk, pkb = psp.get()
            for d in range(2):
                P.op("pe", "matmul", pk[0:64, d * 64:(d + 1) * 64], KT[:, toks[d]], KT[:, toks[d]], start=True, stop=True, reads=[KTb], writes=[pkb])
                P.op("pe", "matmul", pk[0:64, 128 + d * 64:128 + (d + 1) * 64], KT[:, toks[d]], QT[:, toks[d]], start=True, stop=True, reads=[KTb, QTb], writes=[pkb])
            XX, XXb = xxp.get(); QK, QKb = qkp.get()
            P.op("dve", "tensor_tensor", XX[:, :, 0:64], pk[0:64, 0:128].rearrange("p (d j) -> p d j", d=2), D1[:], ALU.mult, reads=[pkb, D1b], writes=[XXb])
            P.op("dve", "tensor_tensor", QK[:], pk[0:64, 128:256].rearrange("p (d j) -> p d j", d=2), D2[:], ALU.mult, reads=[pkb, D2b], writes=[QKb])
            yield
            pc, pcb = psp.get()
            for d in range(2):
                P.op("pe", "transpose", pc[0:64, d * 64:(d + 1) * 64], XX[:, d, 0:64], ident[0:64, 0:64], reads=[XXb, identb], writes=[pcb])
            pcv = pc[0:64, 0:128].rearrange("p (d j) -> p d j", d=2)
            P.op("act", "activation", out=XX[:, :, 64:128], in_=pcv, func=AF.Copy, reads=[pcb], writes=[XXb])
            Z, Zb = zp.get()
            P.op("dve", "tensor_tensor", Z[:], pcv, I2[:], ALU.add, reads=[pcb, I2b], writes=[Zb])
            for k in range(1, 6):
                yield
                pd, pdb = psp.get()
                for d in range(2):
                    P.op("pe", "matmul", pd[0:64, d * 128:d * 128 + 64], XX[:, d, 64:128], XX[:, d, 0:64], start=True, stop=True, reads=[XXb], writes=[pdb])
                    P.op("pe", "matmul", pd[0:64, d * 128 + 64:d * 128 + 128], XX[:, d, 0:64], XX[:, d, 64:128], start=True, stop=True, reads=[XXb], writes=[pdb])
                if k > 1:
                    pe_, peb = psp.get()
                    for d in range(2):
                        P.op("pe", "matmul", pe_[0:64, d * 64:(d + 1) * 64], XX[:, d, 0:64], Z[:, d, :], start=True, stop=True, reads=[XXb, Zb], writes=[peb])
                XXn, XXnb = xxp.get()
                P.op("act", "activation", out=XXn[:].rearrange("p d j -> p (d j)"), in_=pd[0:64, 0:256], func=AF.Copy, reads=[pdb], writes=[XXnb])
                if k > 1:
                    Zn, Znb = zp.get()
                    P.op("dve", "tensor_tensor", Zn[:], Z[:], pe_[0:64, 0:128].rearrange("p (d j) -> p d j", d=2), ALU.add, reads=[Zb, peb], writes=[Znb])
                    Z, Zb = Zn, Znb
                XX, XXb = XXn, XXnb
            yield
            pe_, peb = psp.get()
            for d in range(2):
                P.op("pe", "matmul", pe_[0:64, d * 64:(d + 1) * 64], XX[:, d, 0:64], Z[:, d, :], start=True, stop=True, reads=[XXb, Zb], writes=[peb])
            Zn, Znb = zp.get()
            P.op("dve", "tensor_tensor", Zn[:], Z[:], pe_[0:64, 0:128].rearrange("p (d j) -> p d j", d=2), ALU.add, reads=[Zb, peb], writes=[Znb])
            Z, Zb = Zn, Znb
            yield
            ptk, ptkb = psp.get()
            for d in range(2):
                P.op("pe", "transpose", ptk[0:64, d * 128:(d + 1) * 128], KT[:, toks[d]], ident[:], reads=[KTb, identb], writes=[ptkb])
                P.op("pe", "transpose", ptk[0:64, 256 + d * 128:256 + (d + 1) * 128], VT[:, toks[d]], ident[:], reads=[VTb, identb], writes=[ptkb])
            VB, VBb = tmp.get(); KBG, KBGb = tmp.get(); KD, KDb = tmp.get()
            for d in range(2):
                n, c = ns[d], cols[d]
                P.op("act", "activation", out=KBG[:, d, :], in_=ptk[0:64, d * 128:(d + 1) * 128], func=AF.Copy, scale=BEG[:, n, c:c + 1], reads=[ptkb, BEGb], writes=[KBGb])
                P.op("dve", "tensor_scalar", KD[:, d, :], ptk[0:64, d * 128:(d + 1) * 128], EKD[:, n, c:c + 1], None, ALU.mult, reads=[ptkb, EKDb], writes=[KDb])
                P.op("dve", "tensor_scalar", VB[:, d, :], ptk[0:64, 256 + d * 128:256 + (d + 1) * 128], BETA[:, n, c:c + 1], None, ALU.mult, reads=[ptkb, BETAb], writes=[VBb])
            yield
            pw, pwb = psp.get()
            for d in range(2):
                P.op("pe", "matmul", pw[:, d * 64:(d + 1) * 64], KBG[:, d, :], Z[:, d, :], start=True, stop=True, reads=[KBGb, Zb], writes=[pwb])
            NW, NWb = nwp.get()
            P.op("act", "activation", out=NW[:].rearrange("p d j -> p (d j)"), in_=pw[:, 0:128], func=AF.Copy, scale=-1.0, reads=[pwb], writes=[NWb])
            QG, QGb = qgp.get()
            for d in range(2):
                P.op("pool", "tensor_tensor", QG[:, d, :], QT[:, toks[d]], EGr[:, d, :], ALU.mult, reads=[QTb, EGrb], writes=[QGb])
            return dict(ns=ns, cols=cols, Z=(Z, Zb), VB=(VB, VBb), KD=(KD, KDb), NW=(NW, NWb), QG=(QG, QGb), QK=(QK, QKb))
        def seq(R):
            ns, cols = R["ns"], R["cols"]
            Z, Zb = R["Z"]; VB, VBb = R["VB"]; KD, KDb = R["KD"]; NW, NWb = R["NW"]; QG, QGb = R["QG"]; QK, QKb = R["QK"]
            pv, pvb = psp.get()
            for d in range(2):
                P.op("pe", "matmul", pv[0:64, d * 128:(d + 1) * 128], Z[:, d, :], VB[:, d, :], start=True, stop=False, reads=[Zb, VBb], writes=[pvb])
                P.op("pe", "matmul", pv[0:64, d * 128:(d + 1) * 128], NW[:, d, :], S[d][0][:], start=False, stop=True, reads=[NWb, S[d][1]], writes=[pvb])
            VN, VNb = tmp.get()
            P.op("act", "activation", out=VN[:].rearrange("p d e -> p (d e)"), in_=pv[0:64, 0:256], func=AF.Copy, reads=[pvb], writes=[VNb])
            po, pob = psp.get()
            for d in range(2):
                P.op("pe", "matmul", po[0:64, d * 128:(d + 1) * 128], QG[:, d, :], S[d][0][:], start=True, stop=False, reads=[QGb, S[d][1]], writes=[pob])
                P.op("pe", "matmul", po[0:64, d * 128:(d + 1) * 128], QK[:, d, :], VN[:, d, :], start=False, stop=True, reads=[QKb, VNb], writes=[pob])
            for d in range(2):
                n = ns[d]
                if n in visited:
                    P.op("dve", "tensor_tensor", O[:, n, :], O[:, n, :], po[0:64, d * 128:(d + 1) * 128], ALU.add, reads=[Ob, pob], writes=[Ob])
                else:
                    visited.add(n)
                    P.op("dve", "tensor_copy", O[:, n, :], po[0:64, d * 128:(d + 1) * 128], reads=[pob], writes=[Ob])
            pS, pSb = psp.get()
            for d in range(2):
                P.op("pe", "matmul", pS[:, d * 128:(d + 1) * 128], KD[:, d, :], VN[:, d, :], start=True, stop=True, reads=[KDb, VNb], writes=[pSb])
            for d in range(2):
                sn, snb = Sp[d].get()
                P.op("dve", "scalar_tensor_tensor", sn[:], S[d][0][:], EGL[:, ns[d], cols[d]:cols[d] + 1], pS[:, d * 128:(d + 1) * 128], ALU.mult, ALU.add, reads=[S[d][1], EGLb, pSb], writes=[snb])
                S[d] = (sn, snb)
        def drive(gens, res, nstages=None):
            k = 0
            while any(g is not None for g in gens):
                for i, g in enumerate(gens):
                    if g is None: continue
                    try: next(g)
                    except StopIteration as e:
                        res[i] = e.value; gens[i] = None
                k += 1
                if nstages is not None and k >= nstages: break
            return all(g is None for g in gens)
        cur = [None, None]
        drive([pre(0), pre(1)], cur)
        for p in range(0, NCH, 2):
            nxt = [None, None]; gens = [pre(p + 2), pre(p + 3)] if p + 2 < NCH else [None, None]
            drive(gens, nxt, 3)
            seq(cur[0])
            drive(gens, nxt, 6)
            seq(cur[1])
            drive(gens, nxt)
            cur = nxt
        zr = PT[:, h * 128:(h + 1) * 128].rearrange("(n c) e -> c n e", c=64)
        for n0 in range(0, NCH, GRP):
            P.dma("sp", zt[:], zr[:, n0:n0 + GRP, :], writes=[ztb])
            P.op("dve", "tensor_tensor", yt[:], O[:, n0:n0 + GRP, :], O[:, n0:n0 + GRP, :], ALU.mult, reads=[Ob], writes=[ytb])
            P.op("dve", "tensor_reduce", st17[:], yt[:], AX.X, ALU.add, reads=[ytb], writes=[st17b])
            P.op("act", "activation", out=st17[:], in_=st17[:], func=AF.Sqrt, scale=1.0 / 128.0, bias=eps6[0:64, 0:1], reads=[st17b, eps6b], writes=[st17b])
            P.op("dve", "reciprocal", st17[:], st17[:], reads=[st17b], writes=[st17b])
            for i in range(GRP):
                P.op("dve", "scalar_tensor_tensor", yt[:, i, :], O[:, n0 + i, :], st17[:, i:i + 1], gn[:], ALU.mult, ALU.mult, reads=[Ob, st17b, gnb], writes=[ytb])
            P.op("act", "activation", out=zt[:], in_=zt[:], func=AF.Silu, reads=[ztb], writes=[ztb])
            P.op("dve", "tensor_tensor", yt[:], yt[:], zt[:], ALU.mult, reads=[ytb, ztb], writes=[ytb])
            pq, pqb = psp.get()
            for i in range(GRP):
                P.op("pe", "transpose", pq[:, i * 64:(i + 1) * 64], yt[:, i, :], ident[0:64, 0:64], reads=[ytb, identb], writes=[pqb])
            yT, yTb = yT_p.get()
            evac(P, E, yT[:, 0:GRP * 64], pq[:, 0:GRP * 64], [pqb], [yTb])
            yl_write(P, YL, h * 128, n0 * 64, GRP * 64, yT, [yTb])

class LazyDr(dict):
    def __init__(self, nc):
        super().__init__(); self.nc = nc; self.specs = {}; self.used_ext = []
    def __missing__(self, name):
        kind, shape = self.specs[name]
        ap = self.nc.dram_tensor(name, list(shape), F32, kind=kind).ap()
        if kind == "ExternalInput": self.used_ext.append(name)
        self[name] = ap
        return ap

def build_fused(nc, stop=None):
    dr = LazyDr(nc)
    def ext(name, shape): dr.specs[name] = ("ExternalInput", shape)
    def internal(name, shape): dr.specs[name] = ("Internal", shape)
    ext("xT_in", [KC, 256, NTH]); ext("xT_own", [D, NTH]); ext("cT", [128, KC, 2]); ext("msk", [128, 2])
    ext("cw2", [256, 512]); ext("cl", [NLAT, NLAT]); ext("sln", [NLAT, NLAT])
    ext("cosT", [64, NLAT]); ext("sinT", [64, NLAT]); ext("Rm", [64, 64]); ext("ident", [128, 128])
    ext("tri2", [64, 2, 64]); ext("ms2", [64, 2, 64]); ext("i2", [64, 2, 64])
    for l in range(2):
        ext(f"w_ada{l}", [D, 12288]); ext(f"b_ada{l}", [128, 96]); ext(f"g{l}", [128, 4, KC]); ext(f"w_inr{l}", [D, WINR])
        ext(f"w_gate{l}", [D, 6144]); ext(f"w_branch{l}", [3, 1024, D]); ext(f"w_out{l}", [D, D]); ext(f"w_ffn_in{l}", [D, 2 * FF]); ext(f"w_ffn_out{l}", [FF, D])
        ext(f"gq{l}", [128, 4]); ext(f"gkv{l}", [128, 2]); ext(f"wqn{l}", [512, 512]); ext(f"wqr{l}", [512, 256]); ext(f"wk{l}", [256, 512]); ext(f"wv{l}", [256, 512])
        ext(f"convw{l}", [128, 12, 5]); ext(f"alog{l}", [64, 8]); ext(f"dtb{l}", [64, 8]); ext(f"gnorm{l}", [64, 128])
    dr.specs["xo"] = ("ExternalOutput", [D, NLH])
    internal("PF", [FM_ROWS, NTOT]); internal("PT", [NTOT, TM_W]); internal("YL", [17, 1536, 256]); internal("YG", [17, 2 * 1536, 256])
    internal("XL2", [D, NTH]); internal("XG", [KC, 256, NTH])
    dr["xo"]
    with ExitStack() as es:
        P = Prog(nc, es)
        AR = Arena(P)
        def steps():
            for l in range(2):
                last = (l == 1)
                yield f"A{l}", lambda: emit_A(P, AR, dr, l)
                yield f"DN{l}", lambda: emit_DN(P, AR, dr, l, last)
                yield f"FN{l}", lambda: emit_FN(P, AR, dr, last)
                yield f"MLA{l}", lambda: emit_MLA(P, AR, dr, l, last)
                def g1():
                    AR.end()
                    for blk in range(17): P.coll("AllGather", dr["YG"][blk], dr["YL"][blk], PAIRS)
                yield f"G1{l}", g1
                yield f"C{l}", lambda: emit_C(P, AR, dr, l, last)
                if not last:
                    def g2():
                        AR.end()
                        for c in range(KC): P.coll("AllGather", dr["XG"][c], dr["XL2"][c * 128:(c + 1) * 128, :], PAIRS)
                    yield f"G2{l}", g2
        for name, fn in steps():
            if stop is not None and name not in stop: continue
            fn()
        AR.end()
        P.finish()
        print("fused ops", P.n_ops, dict(P.ep))
    nc._used_ext = list(dr.used_ext)
    return nc

from concourse.bass_utils import run_bass_kernel_spmd

def dft_tables():
    n = np.arange(256, dtype=np.float64)
    ang = 2 * np.pi * np.outer(n, n) / 256.0
    cw2 = np.concatenate([np.cos(ang), np.sin(ang)], 1).astype(np.float32)
    n = np.arange(NLAT, dtype=np.int64)
    ang = 2 * np.pi * (np.outer(n, n) % NLAT).astype(np.float64) / NLAT
    return cw2, np.cos(ang).astype(np.float32), (-np.sin(ang)).astype(np.float32)

def rope_tables():
    rows = NLAT // 64
    row = np.repeat(np.arange(rows, dtype=np.float32), 64)
    col = np.tile(np.arange(64, dtype=np.float32), rows)
    inv = (10000.0 ** (-np.arange(0, 32, 2, dtype=np.float32) / 32)).astype(np.float32)
    ar = row[:, None] * inv; ac = col[:, None] * inv
    ang = np.concatenate([ar, ar, ac, ac], -1)
    cosT = np.ascontiguousarray(np.cos(ang).T.astype(np.float32)); sinT = np.ascontiguousarray(np.sin(ang).T.astype(np.float32))
    R = np.zeros((64, 64), np.float32)
    for i in range(16):
        R[16 + i, i] = -1; R[i, 16 + i] = 1; R[48 + i, 32 + i] = -1; R[32 + i, 48 + i] = 1
    return cosT, sinT, R

def dn_consts():
    p = np.arange(64)[:, None]; f = np.arange(64)[None, :]
    ple = (p <= f).astype(np.float32); pge = (p >= f).astype(np.float32)
    pgt = (p > f).astype(np.float32); plt = (p < f).astype(np.float32)
    return {"tri2": np.ascontiguousarray(np.stack([ple, pge], 1)), "ms2": np.ascontiguousarray(np.stack([pgt, plt], 1)),
            "i2": np.ascontiguousarray(np.stack([np.eye(64, dtype=np.float32)] * 2, 1)), "ident": np.eye(128, dtype=np.float32)}

_NC = []
STOP = None
def kernel(**inputs):
    inp = {k: np.asarray(v) for k, v in inputs.items()}
    B = inp["x"].shape[0]
    if not _NC:
        nc = bass.Bass("TRN2", target_bir_lowering=False, num_devices=8)
        build_fused(nc, STOP); _NC.append(nc)
    nc = _NC[0]
    cw2, cl, sln = dft_tables(); cosT, sinT, R = rope_tables(); dnc = dn_consts()
    shared = {"cw2": cw2, "cl": cl, "sln": sln, "cosT": cosT, "sinT": sinT, "Rm": R}
    shared.update(dnc)
    for l in range(2):
        shared[f"w_ada{l}"] = np.ascontiguousarray(inp["w_ada"][l])
        shared[f"b_ada{l}"] = np.ascontiguousarray(inp["b_ada"][l].reshape(96, 128).T)
        shared[f"g{l}"] = np.ascontiguousarray(inp["norm_g"][l].reshape(4, KC, 128).transpose(2, 0, 1))
        shared[f"w_gate{l}"] = np.ascontiguousarray(inp["w_in"][l][:, 5984:])
        shared[f"w_branch{l}"] = np.ascontiguousarray(inp["w_branch"][l]); shared[f"w_out{l}"] = np.ascontiguousarray(inp["w_out"][l])
        shared[f"w_ffn_in{l}"] = np.ascontiguousarray(inp["w_ffn_in"][l]); shared[f"w_ffn_out{l}"] = np.ascontiguousarray(inp["w_ffn_out"][l])
        shared[f"gq{l}"] = np.ascontiguousarray(inp["mla_q_norm_g"][l].reshape(4, 128).T)
        shared[f"gkv{l}"] = np.ascontiguousarray(inp["mla_kv_norm_g"][l].reshape(2, 128).T)
        shared[f"gnorm{l}"] = np.ascontiguousarray(np.tile(inp["dn_norm_g"][l][None], (64, 1)).astype(np.float32))
    percore = {}
    for r in range(2):
        heads = np.arange(r * 4, (r + 1) * 4)
        colsel = np.concatenate([heads, 8 + heads])
        pc = {}
        for l in range(2):
            w = inp["w_in"][l]
            cols = np.concatenate([np.arange(r * 512, (r + 1) * 512), 1024 + np.arange(r * 512, (r + 1) * 512), 2048 + np.arange(r * 512, (r + 1) * 512),
                                   np.arange(4128, 4960), 4960 + np.arange(r * 512, (r + 1) * 512), 3072 + np.arange(r * 512, (r + 1) * 512), 4096 + colsel, 4112 + colsel])
            assert cols.size == WINR
            pc[f"w_inr{l}"] = np.ascontiguousarray(w[:, cols])
            wuq = inp["w_uq"][l]; wukv = inp["w_ukv"][l]
            pc[f"wqn{l}"] = np.ascontiguousarray(np.concatenate([wuq[:, h * 192: h * 192 + 128] for h in heads], 1))
            pc[f"wqr{l}"] = np.ascontiguousarray(np.concatenate([wuq[:, h * 192 + 128: h * 192 + 192] for h in heads], 1))
            pc[f"wk{l}"] = np.ascontiguousarray(np.concatenate([wukv[:, h * 256: h * 256 + 128] for h in heads], 1))
            pc[f"wv{l}"] = np.ascontiguousarray(np.concatenate([wukv[:, h * 256 + 128: h * 256 + 256] for h in heads], 1))
            conv = inp["dn_conv"][l]
            cwl = [conv[:, qi * 1024 + h * 128: qi * 1024 + (h + 1) * 128].T for qi in range(3) for h in heads]
            pc[f"convw{l}"] = np.ascontiguousarray(np.stack(cwl, 1).astype(np.float32))
            pc[f"alog{l}"] = np.ascontiguousarray(np.tile(inp["dn_a_log"][l].reshape(16)[colsel][None], (64, 1)).astype(np.float32))
            pc[f"dtb{l}"] = np.ascontiguousarray(np.tile(inp["dn_dt_bias"][l].reshape(16)[colsel][None], (64, 1)).astype(np.float32))
        m = np.zeros((128, 2), np.float32); m[:, r] = 1.0
        pc["msk"] = m
        percore[r] = pc
    in_maps = []
    for i in range(8):
        b, r = i // 2, i % 2
        halves = [np.concatenate([inp["x"][b, q * NLH:(q + 1) * NLH], inp["ctx"][b, q * NCH2:(q + 1) * NCH2]], 0).T for q in range(2)]
        d = dict(shared); d.update(percore[r])
        d["xT_in"] = np.ascontiguousarray(np.stack([h_.reshape(KC, 128, NTH) for h_ in halves], 1).reshape(KC, 256, NTH))
        d["xT_own"] = np.ascontiguousarray(halves[r])
        cvec = np.stack([inp["c"][b], inp["c_ctx"]], -1)
        d["cT"] = np.ascontiguousarray(cvec.reshape(KC, 128, 2).transpose(1, 0, 2))
        in_maps.append(d)
    in_maps = [{k: d[k] for k in nc._used_ext} for d in in_maps]
    res = run_bass_kernel_spmd(nc, in_maps, core_ids=list(range(8))).results
    out = np.empty((B, NLAT, D), np.float32)
    for i in range(8):
        b, r = i // 2, i % 2
        out[b, r * NLH:(r + 1) * NLH] = res[i]["xo"].T
    return out
```

```python
import numpy as np
from contextlib import ExitStack
import concourse.bass as bass
import concourse.mybir as mybir
F32 = mybir.dt.float32; BF16 = mybir.dt.bfloat16; I32 = mybir.dt.int32
AF = mybir.ActivationFunctionType
ALU = mybir.AluOpType
AX = mybir.AxisListType

class Buf:
    __slots__ = ("name", "w", "r", "excl")
    def __init__(self, name="", excl=False):
        self.name = name
        self.excl = excl
        self.w = None
        self.r = {}

EPOCH = 12000
class Prog:
    ENG = ("pe", "dve", "act", "pool", "sp")
    def __init__(self, nc, es, n_dma_sems=12):
        self.nc = nc; self.es = es; self.es_global = es
        self.engobj = {"pe": nc.tensor, "dve": nc.vector, "act": nc.scalar, "pool": nc.gpsimd, "sp": nc.sync}
        self.streams = {e: [] for e in self.ENG}
        self.sems = {}
        self.cnt = {}
        self.cur = {}
        self.ep = {e: 0 for e in self.ENG}
        for e in self.ENG:
            self._new_epoch(e)
        self.seen = {e: {} for e in self.ENG}
        self.dma_keys = []
        for i in range(n_dma_sems):
            k = ("dma", i)
            self.sems[k] = es.enter_context(nc.semaphore(f"dma{i}"))
            self.cnt[k] = 0
            self.dma_keys.append(k)
        self.dma_rr = 0
        self.n_ops = 0
    def _new_epoch(self, e):
        k = (e, self.ep[e]); self.ep[e] += 1
        self.sems[k] = self.es_global.enter_context(self.nc.semaphore(f"s_{e}_{k[1]}"))
        self.cnt[k] = 0; self.cur[e] = k
    def _deps(self, reads, writes):
        deps = {}
        def need(k, c):
            if deps.get(k, 0) < c: deps[k] = c
        for b in reads:
            if b.w is not None: need(*b.w)
        for b in writes:
            if b.w is not None: need(*b.w)
            for k, c in b.r.items(): need(k, c)
        return deps
    def _emit_waits(self, e, deps, skip_key=None):
        seen = self.seen[e]
        for k, c in deps.items():
            if k == skip_key: continue
            if seen.get(k, 0) >= c: continue
            seen[k] = c
            sem = self.sems[k]
            self.streams[e].append(lambda eng, sem=sem, c=c: eng.wait_ge(sem, c))
    def _mark(self, key, c, reads, writes):
        for b in reads:
            if b.r.get(key, 0) < c: b.r[key] = c
        for b in writes:
            b.w = (key, c); b.r = {}
    def op(self, e, meth, *args, reads=(), writes=(), same_engine_sync=True, **kw):
        writes = list(writes) + [b for b in reads if b.excl]
        reads = [b for b in reads if not b.excl]
        deps = self._deps(reads, writes)
        key = self.cur[e]
        if self.cnt[key] >= EPOCH:
            self._new_epoch(e); key = self.cur[e]
        skip = None
        if e == "pe" or not same_engine_sync:
            deps = {k: c for k, c in deps.items() if k[0] != e}
        self._emit_waits(e, deps, skip)
        self.cnt[key] += 1
        c = self.cnt[key]; sem = self.sems[key]
        self.streams[e].append(lambda eng, meth=meth, args=args, kw=kw, sem=sem: getattr(eng, meth)(*args, **kw).then_inc(sem, 1))
        self._mark(key, c, reads, writes)
        self.n_ops += 1
    def dma(self, q, out, in_, reads=(), writes=(), **kw):
        deps = self._deps(reads, writes)
        k = self.dma_keys[self.dma_rr]; self.dma_rr = (self.dma_rr + 1) % len(self.dma_keys)
        if self.cnt[k] > 0: deps[k] = max(deps.get(k, 0), self.cnt[k])
        self._emit_waits(q, deps)
        self.cnt[k] += 16
        c = self.cnt[k]; sem = self.sems[k]
        self.streams[q].append(lambda eng, out=out, in_=in_, sem=sem, kw=kw: eng.dma_start(out=out, in_=in_, **kw).then_inc(sem, 16))
        self._mark(k, c, reads, writes)
        self.n_ops += 1
    def coll(self, kind, out, in_, groups, reads=(), writes=()):
        deps = self._deps(reads, writes)
        k = ("cc", 0)
        if k not in self.sems:
            self.sems[k] = self.es_global.enter_context(self.nc.semaphore("cc0"))
            self.cnt[k] = 0
            self.dma_keys.append(k)
        self.cnt[k] += 1
        self._emit_waits("pool", deps)
        sem = self.sems[k]
        self.streams["pool"].append(lambda eng, out=out, in_=in_, sem=sem: eng.collective_compute(kind, mybir.AluOpType.bypass, replica_groups=groups, ins=[in_.opt()], outs=[out.opt()]).then_inc(sem, 1))
        self._mark(k, self.cnt[k], reads, writes)
        self.n_ops += 1
    def barrier(self):
        deps = {k: c for k, c in self.cnt.items() if c > 0}
        for e in self.ENG:
            self._emit_waits(e, {k: c for k, c in deps.items() if k != self.cur[e]})
    def flush(self):
        nc = self.nc
        streams = self.streams
        self.streams = {e: [] for e in self.ENG}
        with nc.Block() as block:
            @block.sync
            def _(eng):
                for f in streams["sp"]: f(eng)
            @block.tensor
            def _(eng):
                for f in streams["pe"]: f(eng)
            @block.vector
            def _(eng):
                for f in streams["dve"]: f(eng)
            @block.scalar
            def _(eng):
                for f in streams["act"]: f(eng)
            @block.gpsimd
            def _(eng):
                for f in streams["pool"]: f(eng)
    def finish(self):
        deps = {k: self.cnt[k] for k in self.dma_keys if self.cnt[k] > 0}
        self._emit_waits("sp", deps)
        self.flush()

class Pool:
    def __init__(self, P, name, shape, dtype, n, psum=False):
        self.tiles = []
        for i in range(n):
            if psum:
                t = P.es.enter_context(P.nc.psum_tensor(f"pp_{name}{i}", shape, dtype))
            else:
                t = P.es.enter_context(P.nc.sbuf_tensor(f"sp_{name}{i}", shape, dtype))
            self.tiles.append((t, Buf(f"{name}{i}", excl=psum)))
        self.i = 0
    def get(self):
        t = self.tiles[self.i]; self.i = (self.i + 1) % len(self.tiles)
        return t

def sb(P, name, shape, dtype=F32):
    return P.es.enter_context(P.nc.sbuf_tensor("sb_" + name, shape, dtype)), Buf(name)
def ps(P, name, shape, dtype=F32):
    return P.es.enter_context(P.nc.psum_tensor("ps_" + name, shape, dtype)), Buf(name, excl=True)

import math
D = 2048; KC = 16; FF = 5632; FKC = 44
NCTX = 256; NLAT = 4096; NTOT = NCTX + NLAT; NCH = NTOT // 64
NLH = 2048; NCH2 = 128; NTH = NLH + NCH2
MLA_SCALE = 192 ** -0.5
NAR = 52000
PAIRS = [[0, 1], [2, 3], [4, 5], [6, 7]]
F32R = mybir.dt.float32r
def RR_(ap): return ap.bitcast(F32R)
FM_CHUNKS = [(c0, 128) for c0 in range(0, 2304, 128)] + [(2304, 64)] + [(2368 + i * 128, 128) for i in range(4)]
FM_ROWS = 2880; TM_COL0 = 2880; TM_W = 528; WINR = 3408

class Arena:
    def __init__(self, P):
        self.P = P
        self.banks = [(P.es_global.enter_context(P.nc.psum_tensor(f"bank{i}", [128, 512], F32)), Buf(f"bank{i}", excl=True)) for i in range(8)]
        self.ph = None; self.nph = 0
    def begin(self, nN, nR):
        assert nN + nR <= NAR, (nN, nR)
        self.end()
        self.ph = ExitStack(); self.nph += 1
        self.tN = self.ph.enter_context(self.P.nc.sbuf_tensor(f"arN{self.nph}", [128, max(nN, 8)], F32))
        self.tR = self.ph.enter_context(self.P.nc.sbuf_tensor(f"arR{self.nph}", [128, max(nR, 8)], F32))
        self.cap = {False: nN, True: nR}; self.off = {False: 0, True: 0}; self.bi = 0
    def end(self):
        self.P.barrier()
        if self.ph is not None:
            self.P.flush(); self.ph.close(); self.ph = None
    def sb(self, name, shape, R=False):
        n = int(np.prod(shape[1:]))
        assert self.off[R] + n <= self.cap[R], (name, R, self.off[R], n, self.cap[R])
        t = self.tR if R else self.tN
        v = t[0:shape[0], self.off[R]:self.off[R] + n]
        self.off[R] += n
        if len(shape) == 3: v = v.rearrange("p (a b) -> p a b", a=shape[1])
        elif len(shape) == 4: v = v.rearrange("p (a b c) -> p a b c", a=shape[1], b=shape[2])
        return v, Buf(name)
    def pool(self, name, shape, n, R=False):
        return RR([self.sb(f"{name}{i}", shape, R) for i in range(n)])
    def pspool(self, n):
        b = self.banks[self.bi:self.bi + n]; assert len(b) == n; self.bi += n
        return RR(b)

class RR:
    def __init__(self, tiles): self.tiles = tiles; self.i = 0
    def get(self):
        t = self.tiles[self.i]; self.i = (self.i + 1) % len(self.tiles); return t

def common(P, AR):
    class E: pass
    E = E()
    E.ones, E.onesb = AR.sb("ones", [128, 128]); P.op("pool", "memset", E.ones[:], 1.0, writes=[E.onesb])
    E.eps = {}
    for dim, val in ((2048, 2048e-6), (512, 512e-6), (256, 256e-6), (1, 1e-6)):
        t, b = AR.sb(f"eps{dim}", [128, 1]); P.op("pool", "memset", t[:], val, writes=[b]); E.eps[dim] = (t, b)
    E.sqp = AR.pool("sq", [128, 512], 2)
    E.ssp, E.sspb = AR.pspool(1).get()
    E.rstd, E.rstdb = AR.sb("rstd", [128, 512])
    E.ev = 0
    return E

def rms_rstd(P, E, src, srcb, nch, tb, dim):
    for c in range(nch):
        sq, sqb = E.sqp.get()
        P.op("act", "activation", out=sq[:, 0:tb], in_=src[:, c, 0:tb], func=AF.Square, reads=[srcb], writes=[sqb])
        P.op("pe", "matmul", E.ssp[:, 0:tb], E.ones[:], sq[:, 0:tb], start=(c == 0), stop=(c == nch - 1), reads=[sqb, E.onesb], writes=[E.sspb])
    eb = E.eps[dim]
    P.op("act", "activation", out=E.rstd[:, 0:tb], in_=E.ssp[:, 0:tb], func=AF.Sqrt, bias=eb[0][:, 0:1], reads=[E.sspb, eb[1]], writes=[E.rstdb])
    P.op("dve", "reciprocal", E.rstd[:, 0:tb], E.rstd[:, 0:tb], reads=[E.rstdb], writes=[E.rstdb])

def evac(P, E, dst, src, reads, writes):
    if E.ev % 2 == 0: P.op("dve", "tensor_copy", dst, src, reads=reads, writes=writes)
    else: P.op("act", "activation", out=dst, in_=src, func=AF.Copy, reads=reads, writes=writes)
    E.ev += 1

def ld(P, AR, name, src, shape, q="sp"):
    t, b = AR.sb(name, shape); P.dma(q, t[:], src, writes=[b]); return t, b

def emit_mod(P, AR, E, wget, pmod, cT_d, wada_d, bada_d, nchunks, cpt=2):
    ct, cb = ld(P, AR, "cT", cT_d, [128, KC, 2]); bt, bb = ld(P, AR, "bada", bada_d, [128, 96])
    sc, scb = AR.sb("sc", [128, KC, 2]); mod_t, mod_b = AR.sb("mod", [128, 96, 2])
    P.op("act", "activation", out=sc[:], in_=ct[:], func=AF.Silu, reads=[cb], writes=[scb])
    for n0 in range(0, nchunks, cpt):
        wt, wb = wget()
        P.dma("sp", wt[:, :, 0:cpt * 128], wada_d[:, n0 * 128:(n0 + cpt) * 128].rearrange("(c p) n -> p c n", p=128), writes=[wb])
        for gi in range(cpt):
            n = n0 + gi
            pt, pb = pmod.get()
            for k in range(KC):
                P.op("pe", "matmul", pt[:, 0:2], wt[:, k, gi * 128:(gi + 1) * 128], sc[:, k, :], start=(k == 0), stop=(k == KC - 1), reads=[wb, scb], writes=[pb])
            P.op("dve", "tensor_scalar", mod_t[:, n, :], pt[:, 0:2], bt[:, n:n + 1], None, ALU.add, reads=[pb, bb], writes=[mod_b])
    return mod_t, mod_b

def yl_write(P, YL, row0, col0, width, src, reads):
    c = col0
    while c < col0 + width:
        blk, off = c // 256, c % 256
        w = min(256 - off, col0 + width - c)
        P.dma("act", YL[blk, row0:row0 + 128, off:off + w], src[:, c - col0:c - col0 + w], reads=reads)
        c += w

def emit_A(P, AR, dr, l):
    AR.begin(21500, 24576); E = common(P, AR)
    xsrc = dr["xT_in"] if l == 0 else dr["XG"]
    wpool = AR.pool("w", [128, KC, 512], 2, R=True)
    pmm = AR.pspool(5)
    wmod = AR.pool("wm", [128, KC, 256], 2)
    mod_t, mod_b = emit_mod(P, AR, E, lambda: wmod.get(), AR.pspool(2), dr["cT"], dr[f"w_ada{l}"], dr[f"b_ada{l}"], 32)
    g0, g0b = ld(P, AR, "g0", dr[f"g{l}"][:, 0, :], [128, KC])
    At, Ab = AR.sb("A", [128, KC, 2])
    for j in range(2):
        P.op("dve", "tensor_scalar", At[:, :, j], mod_t[:, KC:2 * KC, j], 1.0, math.sqrt(D), ALU.add, ALU.mult, reads=[mod_b], writes=[Ab])
        P.op("dve", "tensor_tensor", At[:, :, j], At[:, :, j], g0[:], ALU.mult, reads=[Ab, g0b], writes=[Ab])
    xs, xsb = AR.sb("xs", [128, KC, 512]); hs, hsb = AR.sb("hs", [128, KC, 512], R=True)
    opool = AR.pool("o", [128, 512], 3)
    win = dr[f"w_inr{l}"]; PF = dr["PF"]; PT = dr["PT"]
    blocks = []
    for r in range(2):
        for t0 in range(0, NLH, 512): blocks.append((r, t0, 512, 0, NCTX + r * NLH + t0))
        blocks.append((r, NLH, NCH2, 1, r * NCH2))
    for (r, t0, tb, j, dst0) in blocks:
        P.dma("sp", xs[:, :, 0:tb], xsrc.rearrange("c (r p) t -> r p c t", r=2)[r][:, :, t0:t0 + tb], writes=[xsb])
        rms_rstd(P, E, xs, xsb, KC, tb, 2048)
        for c in range(KC):
            P.op("dve", "scalar_tensor_tensor", RR_(hs[:, c, 0:tb]), xs[:, c, 0:tb], At[:, c, j:j + 1], E.rstd[:, 0:tb], ALU.mult, ALU.mult, reads=[xsb, Ab, E.rstdb], writes=[hsb])
            P.op("act", "activation", out=RR_(hs[:, c, 0:tb]), in_=hs[:, c, 0:tb], func=AF.Identity, bias=mod_t[:, c, j:j + 1], reads=[hsb, mod_b], writes=[hsb])
        row = 0; wt = None; wcol0 = None
        for (c0, wd) in FM_CHUNKS:
            if wt is None or not (wcol0 <= c0 and c0 + wd <= wcol0 + 512):
                wt, wb = wpool.get(); wcol0 = c0
                ncols = min(512, FM_ROWS - c0)
                P.dma("pool", RR_(wt[:, :, 0:ncols]), win[:, c0:c0 + ncols].rearrange("(c p) n -> p c n", p=128), writes=[wb])
            pt, pb = pmm.get(); off = c0 - wcol0
            for k in range(KC):
                P.op("pe", "matmul", pt[0:wd, 0:tb], RR_(wt[:, k, off:off + wd]), RR_(hs[:, k, 0:tb]), start=(k == 0), stop=(k == KC - 1), reads=[wb, hsb], writes=[pb])
            ot, ob = opool.get()
            evac(P, E, ot[0:wd, 0:tb], pt[0:wd, 0:tb], [pb], [ob])
            P.dma("act", PF[row:row + wd, dst0:dst0 + tb], ot[0:wd, 0:tb], reads=[ob])
            row += wd
        for n0 in range(0, TM_W, 512):
            nw = min(512, TM_W - n0)
            wt, wb = wpool.get()
            P.dma("pool", RR_(wt[:, :, 0:nw]), win[:, TM_COL0 + n0:TM_COL0 + n0 + nw].rearrange("(c p) n -> p c n", p=128), writes=[wb])
            for ts in range(tb // 128):
                pt, pb = pmm.get()
                for k in range(KC):
                    P.op("pe", "matmul", pt[:, 0:nw], RR_(hs[:, k, ts * 128:(ts + 1) * 128]), RR_(wt[:, k, 0:nw]), start=(k == 0), stop=(k == KC - 1), reads=[wb, hsb], writes=[pb])
                ot, ob = opool.get()
                evac(P, E, ot[:, 0:nw], pt[:, 0:nw], [pb], [ob])
                P.dma("act", PT[dst0 + ts * 128:dst0 + (ts + 1) * 128, n0:n0 + nw], ot[:, 0:nw], reads=[ob])

def emit_C(P, AR, dr, l, last):
    AR.begin(15100, 36864); E = common(P, AR)
    TB = 256
    xown = dr["xT_own"] if l == 0 else dr["XL2"]
    YG = dr["YG"]
    wpool = AR.pool("w", [128, 5632], 2, R=True)
    wmodc = AR.pool("wm", [128, KC, 128], 1)
    def wget():
        t, b = wpool.get(); return t[:, 0:KC * 256].rearrange("p (c n) -> p c n", c=KC), b
    pmm = AR.pspool(5)
    mod_t, mod_b = emit_mod(P, AR, E, lambda: wmodc.get(), AR.pspool(2), dr["cT"], dr[f"w_ada{l}"], dr[f"b_ada{l}"], 96, cpt=1)
    g, gb = ld(P, AR, "g", dr[f"g{l}"], [128, 4, KC])
    msk, mskb = ld(P, AR, "msk", dr["msk"], [128, 2])
    S, Sb = AR.sb("S", [128, 4, KC, 2])
    for j in range(2):
        for i, (mi, plus1) in enumerate([(1, True), (2, False), (4, True), (5, False)]):
            P.op("dve", "tensor_scalar", S[:, i, :, j], mod_t[:, mi * KC:(mi + 1) * KC, j], 1.0 if plus1 else 0.0, math.sqrt(D), ALU.add, ALU.mult, reads=[mod_b], writes=[Sb])
            P.op("dve", "tensor_tensor", S[:, i, :, j], S[:, i, :, j], g[:, i, :], ALU.mult, reads=[Sb, gb], writes=[Sb])
    xs, xsb = AR.sb("xs", [128, KC, TB]); hs, hsb = AR.sb("hs", [128, KC, TB], R=True)
    ys, ysb = AR.sb("ys", [128, 24, TB], R=True); mg, mgb = AR.sb("mg", [128, KC, TB], R=True); yl, ylb = AR.sb("yl", [128, KC, TB])
    act, actb = AR.sb("act", [128, FKC, TB], R=True)
    tmpp = AR.pool("tmp", [128, TB], 6); selp = AR.pool("sel", [128, TB], 4)
    wg = dr[f"w_gate{l}"]; wbr = dr[f"w_branch{l}"]; wout = dr[f"w_out{l}"]; wfi = dr[f"w_ffn_in{l}"]; wfo = dr[f"w_ffn_out{l}"]
    blocks = [(t0, min(TB, NLH - t0), 0) for t0 in range(0, NLH, TB)]
    if not last: blocks += [(NLH, NCH2, 1)]
    xTr = xown.rearrange("(c p) t -> p c t", p=128)
    outd = dr["xo"] if last else dr["XL2"]
    outr = outd.rearrange("(c p) t -> p c t", p=128)
    def normmod(src, srcb, dst, dstb, si, bi, j, tb):
        rms_rstd(P, E, src, srcb, KC, tb, 2048)
        for c in range(KC):
            P.op("dve", "scalar_tensor_tensor", RR_(dst[:, c, 0:tb]), src[:, c, 0:tb], S[:, si, c, j:j + 1], E.rstd[:, 0:tb], ALU.mult, ALU.mult, reads=[srcb, Sb, E.rstdb], writes=[dstb])
            P.op("act", "activation", out=RR_(dst[:, c, 0:tb]), in_=dst[:, c, 0:tb], func=AF.Identity, bias=mod_t[:, bi * KC + c, j:j + 1], reads=[dstb, mod_b], writes=[dstb])
    def resid(src, srcb, si, j, tb):
        rms_rstd(P, E, src, srcb, KC, tb, 2048)
        for c in range(KC):
            tt, ttb = tmpp.get()
            P.op("dve", "scalar_tensor_tensor", tt[:, 0:tb], src[:, c, 0:tb], S[:, si, c, j:j + 1], E.rstd[:, 0:tb], ALU.mult, ALU.mult, reads=[srcb, Sb, E.rstdb], writes=[ttb])
            P.op("dve", "tensor_tensor", xs[:, c, 0:tb], xs[:, c, 0:tb], tt[:, 0:tb], ALU.add, reads=[xsb, ttb], writes=[xsb])
    for (t0, tb, j) in blocks:
        P.dma("sp", xs[:, :, 0:tb], xTr[:, :, t0:t0 + tb], writes=[xsb])
        for i in range(3):
            for gg in range(2):
                for c in range(4):
                    ch = i * 8 + gg * 4 + c
                    r0 = gg * 1536 + i * 512 + c * 128
                    cols = [(NCTX + r * NLH + t0) if j == 0 else (r * NCH2) for r in range(2)]
                    s0, s0b = selp.get(); st, stb = selp.get()
                    P.dma("sp", s0[:, 0:tb], YG[cols[0] // 256, r0:r0 + 128, cols[0] % 256:cols[0] % 256 + tb], writes=[s0b])
                    P.dma("sp", st[:, 0:tb], YG[cols[1] // 256, r0:r0 + 128, cols[1] % 256:cols[1] % 256 + tb], writes=[stb])
                    P.op("dve", "tensor_scalar", s0[:, 0:tb], s0[:, 0:tb], msk[:, 0:1], None, ALU.mult, reads=[s0b, mskb], writes=[s0b])
                    P.op("dve", "scalar_tensor_tensor", RR_(ys[:, ch, 0:tb]), st[:, 0:tb], msk[:, 1:2], s0[:, 0:tb], ALU.mult, ALU.add, reads=[stb, mskb, s0b], writes=[ysb])
        normmod(xs, xsb, hs, hsb, 0, 0, j, tb)
        for i in range(3):
            for ng in range(0, KC, 4):
                pbs = [pmm.get() for _ in range(4)]
                for kh in range(2):
                    wt, wb = wpool.get()
                    wv = wt[:, 0:8 * 512].rearrange("p (c n) -> p c n", c=8)
                    P.dma("pool", RR_(wv), wg[kh * 1024:(kh + 1) * 1024, i * D + ng * 128: i * D + (ng + 4) * 128].rearrange("(c p) n -> p c n", p=128), writes=[wb])
                    for gi in range(4):
                        for k in range(8):
                            P.op("pe", "matmul", pbs[gi][0][:, 0:tb], RR_(wv[:, k, gi * 128:(gi + 1) * 128]), RR_(hs[:, kh * 8 + k, 0:tb]), start=(kh == 0 and k == 0), stop=(kh == 1 and k == 7), reads=[wb, hsb], writes=[pbs[gi][1]])
                tts = []
                for gi in range(4):
                    tt, ttb = tmpp.get()
                    P.op("act", "activation", out=tt[:, 0:tb], in_=pbs[gi][0][:, 0:tb], func=AF.Sigmoid, reads=[pbs[gi][1]], writes=[ttb])
                    tts.append((tt, ttb))
                pbs = [pmm.get() for _ in range(4)]
                wt, wb = wpool.get()
                wv = wt[:, 0:8 * 512].rearrange("p (c n) -> p c n", c=8)
                P.dma("pool", RR_(wv), wbr[i, :, ng * 128:(ng + 4) * 128].rearrange("(c p) n -> p c n", p=128), writes=[wb])
                for gi in range(4):
                    for k in range(8):
                        P.op("pe", "matmul", pbs[gi][0][:, 0:tb], RR_(wv[:, k, gi * 128:(gi + 1) * 128]), RR_(ys[:, i * 8 + k, 0:tb]), start=(k == 0), stop=(k == 7), reads=[wb, ysb], writes=[pbs[gi][1]])
                for gi in range(4):
                    n = ng + gi
                    tt, ttb = tts[gi]
                    if i == 0:
                        P.op("dve", "tensor_tensor", RR_(mg[:, n, 0:tb]), tt[:, 0:tb], pbs[gi][0][:, 0:tb], ALU.mult, reads=[ttb, pbs[gi][1]], writes=[mgb])
                    else:
                        P.op("dve", "tensor_tensor", tt[:, 0:tb], tt[:, 0:tb], pbs[gi][0][:, 0:tb], ALU.mult, reads=[ttb, pbs[gi][1]], writes=[ttb])
                        P.op("dve", "tensor_tensor", RR_(mg[:, n, 0:tb]), mg[:, n, 0:tb], tt[:, 0:tb], ALU.add, reads=[ttb, mgb], writes=[mgb])
        for ng in range(0, KC, 4):
            pbs = [pmm.get() for _ in range(4)]
            for kh in range(2):
                wt, wb = wpool.get()
                wv = wt[:, 0:8 * 512].rearrange("p (c n) -> p c n", c=8)
                P.dma("pool", RR_(wv), wout[kh * 1024:(kh + 1) * 1024, ng * 128:(ng + 4) * 128].rearrange("(c p) n -> p c n", p=128), writes=[wb])
                for gi in range(4):
                    for k in range(8):
                        P.op("pe", "matmul", pbs[gi][0][:, 0:tb], RR_(wv[:, k, gi * 128:(gi + 1) * 128]), RR_(mg[:, kh * 8 + k, 0:tb]), start=(kh == 0 and k == 0), stop=(kh == 1 and k == 7), reads=[wb, mgb], writes=[pbs[gi][1]])
            for gi in range(4):
                P.op("act", "activation", out=yl[:, ng + gi, 0:tb], in_=pbs[gi][0][:, 0:tb], func=AF.Copy, reads=[pbs[gi][1]], writes=[ylb])
        resid(yl, ylb, 1, j, tb)
        normmod(xs, xsb, hs, hsb, 2, 3, j, tb)
        for h0 in range(0, FKC, 4):
            tts = []
            for part in range(2):
                pbs = [pmm.get() for _ in range(4)]
                for kh in range(2):
                    wt, wb = wpool.get()
                    wv = wt[:, 0:8 * 512].rearrange("p (c n) -> p c n", c=8)
                    P.dma("pool", RR_(wv), wfi[kh * 1024:(kh + 1) * 1024, part * FF + h0 * 128:part * FF + (h0 + 4) * 128].rearrange("(c p) n -> p c n", p=128), writes=[wb])
                    for gi in range(4):
                        for k in range(8):
                            P.op("pe", "matmul", pbs[gi][0][:, 0:tb], RR_(wv[:, k, gi * 128:(gi + 1) * 128]), RR_(hs[:, kh * 8 + k, 0:tb]), start=(kh == 0 and k == 0), stop=(kh == 1 and k == 7), reads=[wb, hsb], writes=[pbs[gi][1]])
                for gi in range(4):
                    if part == 0:
                        tt, ttb = tmpp.get()
                        P.op("act", "activation", out=tt[:, 0:tb], in_=pbs[gi][0][:, 0:tb], func=AF.Silu, reads=[pbs[gi][1]], writes=[ttb])
                        tts.append((tt, ttb))
                    else:
                        tt, ttb = tts[gi]
                        P.op("dve", "tensor_tensor", RR_(act[:, h0 + gi, 0:tb]), tt[:, 0:tb], pbs[gi][0][:, 0:tb], ALU.mult, reads=[ttb, pbs[gi][1]], writes=[actb])
        for ng in range(0, KC, 4):
            pbs = [pmm.get() for _ in range(4)]
            for kq in range(4):
                wt, wb = wpool.get()
                wv = wt[:, 0:11 * 512].rearrange("p (c n) -> p c n", c=11)
                P.dma("pool", RR_(wv), wfo[kq * 1408:(kq + 1) * 1408, ng * 128:(ng + 4) * 128].rearrange("(c p) n -> p c n", p=128), writes=[wb])
                for gi in range(4):
                    for k in range(11):
                        P.op("pe", "matmul", pbs[gi][0][:, 0:tb], RR_(wv[:, k, gi * 128:(gi + 1) * 128]), RR_(act[:, kq * 11 + k, 0:tb]), start=(kq == 0 and k == 0), stop=(kq == 3 and k == 10), reads=[wb, actb], writes=[pbs[gi][1]])
            for gi in range(4):
                P.op("act", "activation", out=yl[:, ng + gi, 0:tb], in_=pbs[gi][0][:, 0:tb], func=AF.Copy, reads=[pbs[gi][1]], writes=[ylb])
        resid(yl, ylb, 3, j, tb)
        P.dma("sp", outr[:, :, t0:t0 + tb], xs[:, :, 0:tb], reads=[xsb])

def emit_FN(P, AR, dr, last):
    AR.begin(4000, 39424); E = common(P, AR)
    PF = dr["PF"]; YL = dr["YL"]; cld = dr["cl"]; sld = dr["sln"]
    cw2, cw2b = AR.sb("cw2s", [128, 2, 512], R=True); P.dma("pool", RR_(cw2[:]), dr["cw2"].rearrange("(c p) n -> p c n", p=128), writes=[cw2b])
    us, usb = AR.sb("us", [128, 2, NTOT], R=True); Zs, Zsb = AR.sb("Zs", [128, 32, 512], R=True); Zc, Zcb = AR.sb("Zc", [128, 2, 512], R=True)
    cp = AR.pool("ct", [128, 4, 512], 3, R=True); spn = AR.pool("st", [128, 4, 512], 3, R=True)
    pz = AR.pspool(2); py = AR.pspool(4); op = AR.pool("o", [128, 512], 3)
    uTr = PF[2368:2880, :].rearrange("(c p) t -> p c t", p=128)
    for g in range(2):
        P.dma("pool", RR_(us[:]), uTr[:, 2 * g:2 * g + 2, :], writes=[usb])
        for t in range(32):
            pt, pb = pz.get()
            for kc in range(2):
                P.op("pe", "matmul", pt[:], RR_(us[:, kc, NCTX + t * 128:NCTX + (t + 1) * 128]), RR_(cw2[:, kc, :]), start=(kc == 0), stop=(kc == 1), reads=[usb, cw2b], writes=[pb])
            evac(P, E, RR_(Zs[:, t, :]), pt[:], [pb], [Zsb])
        if not last:
            for t in range(2):
                pt, pb = pz.get()
                for kc in range(2):
                    P.op("pe", "matmul", pt[:], RR_(us[:, kc, t * 128:(t + 1) * 128]), RR_(cw2[:, kc, :]), start=(kc == 0), stop=(kc == 1), reads=[usb, cw2b], writes=[pb])
                P.op("dve", "tensor_copy", RR_(Zc[:, t, 0:256]), pt[:, 0:256], reads=[pb], writes=[Zcb])
                P.op("dve", "tensor_scalar", RR_(Zc[:, t, 256:512]), pt[:, 256:512], -1.0, None, ALU.mult, reads=[pb], writes=[Zcb])
            for ch in range(2):
                pt, pb = py.get(); i = 0
                for t in range(2):
                    for part in range(2):
                        P.op("pe", "matmul", pt[:, 0:256], RR_(Zc[:, t, part * 256 + ch * 128: part * 256 + (ch + 1) * 128]), RR_(cw2[:, t, part * 256:(part + 1) * 256]), start=(i == 0), stop=(i == 3), reads=[Zcb, cw2b], writes=[pb])
                        i += 1
                ot, ob = op.get()
                P.op("act", "activation", out=ot[:, 0:256], in_=pt[:, 0:256], func=AF.Copy, scale=1.0 / 256.0, reads=[pb], writes=[ob])
                yl_write(P, YL, 512 + g * 256 + ch * 128, 0, 256, ot, [ob])
        for o in range(8):
            pts = [py.get(), py.get()]
            for t0 in range(0, 32, 4):
                ct, cb = cp.get(); st, stb = spn.get()
                P.dma("pool", RR_(ct[:]), cld[t0 * 128:(t0 + 4) * 128, o * 512:(o + 1) * 512].rearrange("(t p) n -> p t n", p=128), writes=[cb])
                P.dma("pool", RR_(st[:]), sld[t0 * 128:(t0 + 4) * 128, o * 512:(o + 1) * 512].rearrange("(t p) n -> p t n", p=128), writes=[stb])
                for tt in range(4):
                    t = t0 + tt
                    for ch in range(2):
                        pt, pb = pts[ch]
                        P.op("pe", "matmul", pt[:], RR_(Zs[:, t, ch * 128:(ch + 1) * 128]), RR_(ct[:, tt, :]), start=(t == 0), stop=False, reads=[Zsb, cb], writes=[pb])
                        P.op("pe", "matmul", pt[:], RR_(Zs[:, t, 256 + ch * 128:256 + (ch + 1) * 128]), RR_(st[:, tt, :]), start=False, stop=(t == 31), reads=[Zsb, stb], writes=[pb])
            for ch in range(2):
                pt, pb = pts[ch]; ot, ob = op.get()
                if ch == 0: P.op("act", "activation", out=ot[:], in_=pt[:], func=AF.Copy, scale=1.0 / 1024.0, reads=[pb], writes=[ob])
                else: P.op("dve", "tensor_scalar", ot[:], pt[:], 1.0 / 1024.0, None, ALU.mult, reads=[pb], writes=[ob])
                yl_write(P, YL, 512 + g * 256 + ch * 128, NCTX + o * 512, 512, ot, [ob])

def emit_MLA(P, AR, dr, l, last, NH=4):
    AR.begin(28000, 17408); E = common(P, AR)
    PF = dr["PF"]; YL = dr["YL"]
    cin = PF[1536:2368, :]
    gq, gqb = ld(P, AR, "gq_s", dr[f"gq{l}"], [128, 4]); gkv, gkvb = ld(P, AR, "gkv_s", dr[f"gkv{l}"], [128, 2])
    P.op("dve", "tensor_scalar", gq[:], gq[:], math.sqrt(512.0), None, ALU.mult, reads=[gqb], writes=[gqb])
    P.op("dve", "tensor_scalar", gkv[:], gkv[:], math.sqrt(256.0), None, ALU.mult, reads=[gkvb], writes=[gkvb])
    wqn, wqnb = ld(P, AR, "wqn_s", dr[f"wqn{l}"].rearrange("(c p) n -> p c n", p=128), [128, 4, NH * 128])
    wqr, wqrb = ld(P, AR, "wqr_s", dr[f"wqr{l}"].rearrange("(c p) n -> p c n", p=128), [128, 4, NH * 64])
    wk, wkb = ld(P, AR, "wk_s", dr[f"wk{l}"].rearrange("(c p) n -> p c n", p=128), [128, 2, NH * 128])
    wv, wvb = ld(P, AR, "wv_s", dr[f"wv{l}"].rearrange("(c p) n -> p c n", p=128), [128, 2, NH * 128])
    Rm, Rmb = ld(P, AR, "R_s", dr["Rm"], [64, 64]); ident, identb = ld(P, AR, "id_s", dr["ident"], [128, 128])
    Qn, Qnb = AR.sb("Qn", [128, NTOT], R=True); Qr, Qrb = AR.sb("Qr", [64, NTOT], R=True)
    Kn, Knb = AR.sb("Kn", [128, NTOT], R=True); Kr, Krb = AR.sb("Kr", [64, NTOT], R=True)
    V, Vb = AR.sb("V", [128, NTOT // 128, 128]); Ss, Ssb = AR.sb("Ss", [128, NTOT])
    cb_t, cb_b = AR.sb("cblk", [128, 6, 512]); krb_t, krb_b = AR.sb("krblk", [64, 512])
    cqn, cqnb = AR.sb("cqn", [128, 4, 512]); ckvn, ckvnb = AR.sb("ckvn", [128, 2, 512])
    cs_t, cs_b = AR.sb("cosb", [64, 512]); sn_t, sn_b = AR.sb("sinb", [64, 512])
    tq, tqb = AR.sb("tq", [64, 512]); t2, t2b = AR.sb("t2", [64, 512])
    pmm = AR.pspool(5); po_p = AR.pspool(2)
    PTp = AR.pool("PT", [128, 512], 2); osb_p = AR.pool("osb", [128, 128], 2); oT_p = AR.pool("oT", [128, 128], 2); st_p = AR.pool("stat", [128, 4], 2)
    cinr = cin[0:768, :].rearrange("(c p) t -> p c t", p=128)
    cos_d = dr["cosT"]; sin_d = dr["sinT"]
    blocks = [(0, 256, False)] + [(NCTX + i * 512, 512, True) for i in range(8)]
    for h in range(NH):
        for (t0, tb, lat) in blocks:
            P.dma("sp", cb_t[:, :, 0:tb], cinr[:, :, t0:t0 + tb], writes=[cb_b])
            P.dma("sp", krb_t[:, 0:tb], cin[768:832, t0:t0 + tb], writes=[krb_b])
            if lat:
                P.dma("act", cs_t[:, 0:tb], cos_d[:, t0 - NCTX:t0 - NCTX + tb], writes=[cs_b])
                P.dma("act", sn_t[:, 0:tb], sin_d[:, t0 - NCTX:t0 - NCTX + tb], writes=[sn_b])
            rms_rstd(P, E, cb_t[:, 0:4, :], cb_b, 4, tb, 512)
            for c in range(4):
                P.op("dve", "scalar_tensor_tensor", cqn[:, c, 0:tb], cb_t[:, c, 0:tb], gq[:, c:c + 1], E.rstd[:, 0:tb], ALU.mult, ALU.mult, reads=[cb_b, gqb, E.rstdb], writes=[cqnb])
            rms_rstd(P, E, cb_t[:, 4:6, :], cb_b, 2, tb, 256)
            for c in range(2):
                P.op("dve", "scalar_tensor_tensor", ckvn[:, c, 0:tb], cb_t[:, 4 + c, 0:tb], gkv[:, c:c + 1], E.rstd[:, 0:tb], ALU.mult, ALU.mult, reads=[cb_b, gkvb, E.rstdb], writes=[ckvnb])
            pt, pb = pmm.get()
            for kc in range(4):
                P.op("pe", "matmul", pt[:, 0:tb], wqn[:, kc, h * 128:(h + 1) * 128], cqn[:, kc, 0:tb], start=(kc == 0), stop=(kc == 3), reads=[wqnb, cqnb], writes=[pb])
            evac(P, E, RR_(Qn[:, t0:t0 + tb]), pt[:, 0:tb], [pb], [Qnb])
            pt, pb = pmm.get()
            for kc in range(2):
                P.op("pe", "matmul", pt[:, 0:tb], wk[:, kc, h * 128:(h + 1) * 128], ckvn[:, kc, 0:tb], start=(kc == 0), stop=(kc == 1), reads=[wkb, ckvnb], writes=[pb])
            evac(P, E, RR_(Kn[:, t0:t0 + tb]), pt[:, 0:tb], [pb], [Knb])
            for ts in range(tb // 128):
                pt, pb = pmm.get()
                for kc in range(2):
                    P.op("pe", "matmul", pt[:, 0:128], ckvn[:, kc, ts * 128:(ts + 1) * 128], wv[:, kc, h * 128:(h + 1) * 128], start=(kc == 0), stop=(kc == 1), reads=[wvb, ckvnb], writes=[pb])
                evac(P, E, V[:, t0 // 128 + ts, :], pt[:, 0:128], [pb], [Vb])
            pt, pb = pmm.get()
            for kc in range(4):
                P.op("pe", "matmul", pt[0:64, 0:tb], wqr[:, kc, h * 64:(h + 1) * 64], cqn[:, kc, 0:tb], start=(kc == 0), stop=(kc == 3), reads=[wqrb, cqnb], writes=[pb])
            def rope(dst, dstb, src, srcb):
                p2, p2b = pmm.get()
                P.op("pe", "matmul", p2[0:64, 0:tb], Rm[:, :], src, start=True, stop=True, reads=[Rmb, srcb], writes=[p2b])
                P.op("dve", "tensor_tensor", t2[:, 0:tb], p2[0:64, 0:tb], sn_t[:, 0:tb], ALU.mult, reads=[p2b, sn_b], writes=[t2b])
                P.op("pool", "tensor_tensor", RR_(dst[:, t0:t0 + tb]), src, cs_t[:, 0:tb], ALU.mult, reads=[srcb, cs_b], writes=[dstb])
                P.op("dve", "tensor_tensor", RR_(dst[:, t0:t0 + tb]), dst[:, t0:t0 + tb], t2[:, 0:tb], ALU.add, reads=[dstb, t2b], writes=[dstb])
            if lat:
                evac(P, E, tq[:, 0:tb], pt[0:64, 0:tb], [pb], [tqb])
                rope(Qr, Qrb, tq[:, 0:tb], tqb)
                rope(Kr, Krb, krb_t[:, 0:tb], krb_b)
            else:
                evac(P, E, RR_(Qr[:, t0:t0 + tb]), pt[0:64, 0:tb], [pb], [Qrb])
                P.op("pool", "tensor_copy", RR_(Kr[:, t0:t0 + tb]), krb_t[:, 0:tb], reads=[krb_b], writes=[Krb])
        qtiles = [(NCTX + qt * 128, 0, NTOT) for qt in range(32)]
        if not last: qtiles = [(qt * 128, 0, NCTX) for qt in range(2)] + qtiles
        for (q0, k0, k1) in qtiles:
            nk = k1 - k0
            for kb0 in range(k0, k1, 512):
                kw = min(512, k1 - kb0)
                pt, pb = pmm.get()
                P.op("pe", "matmul", pt[:, 0:kw], RR_(Qn[:, q0:q0 + 128]), RR_(Kn[:, kb0:kb0 + kw]), start=True, stop=False, reads=[Qnb, Knb], writes=[pb])
                P.op("pe", "matmul", pt[:, 0:kw], RR_(Qr[:, q0:q0 + 128]), RR_(Kr[:, kb0:kb0 + kw]), start=False, stop=True, reads=[Qrb, Krb], writes=[pb])
                evac(P, E, Ss[:, kb0:kb0 + kw], pt[:, 0:kw], [pb], [Ssb])
            stt, stb = st_p.get()
            P.op("dve", "tensor_reduce", stt[:, 0:1], Ss[:, k0:k1], AX.X, ALU.max, reads=[Ssb], writes=[stb])
            P.op("dve", "tensor_scalar", stt[:, 1:2], stt[:, 0:1], -MLA_SCALE, None, ALU.mult, reads=[stb], writes=[stb])
            P.op("pool", "memset", stt[:, 2:3], 0.0, writes=[stb])
            P.op("act", "activation", out=Ss[:, k0:k1], in_=Ss[:, k0:k1], func=AF.Exp, scale=MLA_SCALE, bias=stt[:, 1:2], accum_out=stt[:, 2:3], reads=[Ssb, stb], writes=[Ssb, stb])
            P.op("dve", "reciprocal", stt[:, 3:4], stt[:, 2:3], reads=[stb], writes=[stb])
            po, pob = po_p.get()
            ntile = nk // 128
            for g0 in range(0, ntile, 4):
                gn = min(4, ntile - g0)
                ptp, ptpb = pmm.get()
                for i in range(gn):
                    kt = k0 // 128 + g0 + i
                    P.op("pe", "transpose", ptp[:, i * 128:(i + 1) * 128], Ss[:, kt * 128:(kt + 1) * 128], ident[:], reads=[Ssb, identb], writes=[ptpb])
                PT, PTb = PTp.get()
                evac(P, E, PT[:, 0:gn * 128], ptp[:, 0:gn * 128], [ptpb], [PTb])
                for i in range(gn):
                    kt = k0 // 128 + g0 + i
                    P.op("pe", "matmul", po[:, 0:128], PT[:, i * 128:(i + 1) * 128], V[:, kt, :], start=(g0 + i == 0), stop=(g0 + i == ntile - 1), reads=[PTb, Vb], writes=[pob])
            ot, ob = osb_p.get()
            P.op("dve", "tensor_scalar", ot[:], po[:, 0:128], stt[:, 3:4], None, ALU.mult, reads=[pob, stb], writes=[ob])
            pq, pqb = pmm.get()
            P.op("pe", "transpose", pq[:, 0:128], ot[:], ident[:], reads=[ob, identb], writes=[pqb])
            oT, oTb = oT_p.get()
            evac(P, E, oT[:], pq[:, 0:128], [pqb], [oTb])
            yl_write(P, YL, 1024 + h * 128, q0, 128, oT, [oTb])

def emit_DN(P, AR, dr, l, last, NH=4):
    AR.begin(51900, 8); E = common(P, AR)
    G = 2 * NH
    PF = dr["PF"]; PT = dr["PT"]; YL = dr["YL"]
    TRI2, TRI2b = ld(P, AR, "tri2s", dr["tri2"], [64, 2, 64]); MS2, MS2b = ld(P, AR, "ms2s", dr["ms2"], [64, 2, 64])
    I2, I2b = ld(P, AR, "i2s", dr["i2"], [64, 2, 64]); ident, identb = ld(P, AR, "ids", dr["ident"], [128, 128])
    cw, cwb = ld(P, AR, "cws", dr[f"convw{l}"], [128, 3 * NH, 5]); gn, gnb = ld(P, AR, "gns", dr[f"gnorm{l}"], [64, 128])
    alog, alogb = ld(P, AR, "alogs", dr[f"alog{l}"], [64, G]); dtb, dtbb = ld(P, AR, "dtbs", dr[f"dtb{l}"], [64, G])
    ones, onesb = E.ones, E.onesb
    one1, one1b = AR.sb("one1", [128, 1]); P.op("pool", "memset", one1[:], 1.0, writes=[one1b])
    eps6, eps6b = E.eps[1]
    psp = AR.pspool(7)
    bl, blb = ld(P, AR, "bls", PT[:, 512:512 + G].rearrange("(n c) x -> c n x", c=64), [64, NCH, G])
    al, alb = ld(P, AR, "als", PT[:, 512 + G:512 + 2 * G].rearrange("(n c) x -> c n x", c=64), [64, NCH, G], q="act")
    BETA, BETAb = AR.sb("BETA", [64, NCH, G]); NBETA, NBETAb = AR.sb("NBETA", [64, NCH, G])
    gt, gtb = AR.sb("gt", [64, NCH, G]); GC, GCb = AR.sb("GC", [64, NCH, G]); NGC, NGCb = AR.sb("NGC", [64, NCH, G])
    BEG, BEGb = AR.sb("BEG", [64, NCH, G]); EKD, EKDb = AR.sb("EKD", [64, NCH, G]); EGL, EGLb = AR.sb("EGL", [128, NCH, G])
    P.op("act", "activation", out=BETA[:], in_=bl[:], func=AF.Sigmoid, reads=[blb], writes=[BETAb])
    P.op("dve", "tensor_scalar", NBETA[:], BETA[:], -1.0, None, ALU.mult, reads=[BETAb], writes=[NBETAb])
    for c in range(G):
        P.op("act", "activation", out=gt[:, :, c], in_=al[:, :, c], func=AF.Exp, bias=dtb[:, c:c + 1], reads=[alb, dtbb], writes=[gtb])
    P.op("act", "activation", out=gt[:], in_=gt[:], func=AF.Ln, bias=one1[0:64, 0:1], reads=[gtb, one1b], writes=[gtb])
    P.op("act", "activation", out=alog[:], in_=alog[:], func=AF.Exp, reads=[alogb], writes=[alogb])
    P.op("dve", "tensor_scalar", alog[:], alog[:], -1.0, None, ALU.mult, reads=[alogb], writes=[alogb])
    for c in range(G):
        P.op("dve", "tensor_scalar", gt[:, :, c], gt[:, :, c], alog[:, c:c + 1], None, ALU.mult, reads=[gtb, alogb], writes=[gtb])
    gflat = gt[:].rearrange("p n g -> p (n g)")
    NF = NCH * G; H2 = NF // 2
    for d in range(2):
        pt, pb = psp.get(); pt2, pb2 = psp.get()
        for (pp, ppb, c0) in ((pt, pb, 0), (pt2, pb2, H2)):
            P.op("pe", "matmul", pp[0:64, 0:H2], TRI2[:, d, :], gflat[:, c0:c0 + H2], start=True, stop=True, reads=[TRI2b, gtb], writes=[ppb])
        for (pp, ppb, c0) in ((pt, pb, 0), (pt2, pb2, H2)):
            nn = H2 // G
            src = pp[0:64, 0:H2].rearrange("p (n g) -> p n g", g=G)
            P.op("dve", "tensor_copy", GC[:, c0 // G:c0 // G + nn, d * NH:(d + 1) * NH], src[:, :, d * NH:(d + 1) * NH], reads=[ppb], writes=[GCb])
    P.op("dve", "tensor_scalar", NGC[:], GC[:], -1.0, None, ALU.mult, reads=[GCb], writes=[NGCb])
    P.op("act", "activation", out=BEG[:], in_=GC[:], func=AF.Exp, reads=[GCb], writes=[BEGb])
    P.op("dve", "tensor_tensor", BEG[:], BEG[:], BETA[:], ALU.mult, reads=[BEGb, BETAb], writes=[BEGb])
    EGLf = EGL[:].rearrange("p n g -> p (n g)"); EKDf = EKD[:].rearrange("p n g -> p (n g)"); GCf = GC[:].rearrange("p n g -> p (n g)")
    for c0 in (0, H2):
        pt, pb = psp.get()
        P.op("pe", "matmul", pt[:, 0:H2], ones[0:64, :], gflat[:, c0:c0 + H2], start=True, stop=True, reads=[onesb, gtb], writes=[pb])
        P.op("dve", "tensor_tensor", EKDf[:, c0:c0 + H2], pt[0:64, 0:H2], GCf[:, c0:c0 + H2], ALU.subtract, reads=[pb, GCb], writes=[EKDb])
        P.op("act", "activation", out=EGLf[:, c0:c0 + H2], in_=pt[:, 0:H2], func=AF.Exp, reads=[pb], writes=[EGLb])
    P.op("act", "activation", out=EKD[:], in_=EKD[:], func=AF.Exp, reads=[EKDb], writes=[EKDb])
    QT, QTb = AR.sb("QT", [128, NTOT]); KT, KTb = AR.sb("KT", [128, NTOT]); VT, VTb = AR.sb("VT", [128, NTOT])
    Xr, Xrb = AR.sb("Xr", [128, NTOT]); O, Ob = AR.sb("O", [64, NCH, 128])
    rs, rsb = E.rstd, E.rstdb
    w128 = AR.pool("w128", [64, 2, 64], 12); xxp = AR.pool("xx", [64, 2, 128], 6); zp = AR.pool("zz", [64, 2, 64], 26); qkp = AR.pool("qk", [64, 2, 64], 6)
    egp = AR.pool("egr", [128, 2, 64], 4); qgp = AR.pool("qg", [128, 2, 64], 6); nwp = AR.pool("nw", [128, 2, 64], 6)
    tmp = AR.pool("tm", [64, 2, 128], 16)
    Sp = [AR.pool(f"S{d}", [128, 128], 2) for d in range(2)]
    GRP = 4
    zt, ztb = AR.sb("zt", [64, GRP, 128]); yt, ytb = AR.sb("yt", [64, GRP, 128])
    st17, st17b = AR.sb("st17", [64, GRP]); yT_p = AR.pool("yT", [128, 512], 2)
    segs = [(0, NCTX), (NCTX, NTOT)]
    for h in range(NH):
        for qi, (dst, dstb) in enumerate(((QT, QTb), (KT, KTb), (VT, VTb))):
            ci = qi * NH + h
            P.dma("sp", Xr[:], PF[qi * 512 + h * 128: qi * 512 + (h + 1) * 128, :], writes=[Xrb])
            for (a, b) in segs:
                P.op("act", "activation", out=dst[:, a:b], in_=Xr[:, a:b], func=AF.Copy, scale=cw[:, ci, 2:3], reads=[Xrb, cwb], writes=[dstb])
                for tap in (0, 1, 3, 4):
                    off = tap - 2
                    if off < 0: o0, o1, i0, i1 = a - off, b, a, b + off
                    else: o0, o1, i0, i1 = a, b - off, a + off, b
                    P.op("dve", "scalar_tensor_tensor", dst[:, o0:o1], Xr[:, i0:i1], cw[:, ci, tap:tap + 1], dst[:, o0:o1], ALU.mult, ALU.add, reads=[Xrb, cwb, dstb], writes=[dstb])
            P.op("act", "activation", out=dst[:], in_=dst[:], func=AF.Silu, reads=[dstb], writes=[dstb])
            if qi < 2:
                for t0 in range(0, NTOT, 512):
                    tb = min(512, NTOT - t0)
                    sq, sqb = E.sqp.get()
                    P.op("act", "activation", out=sq[:, 0:tb], in_=dst[:, t0:t0 + tb], func=AF.Square, reads=[dstb], writes=[sqb])
                    pt, pb = psp.get()
                    P.op("pe", "matmul", pt[:, 0:tb], ones[:], sq[:, 0:tb], start=True, stop=True, reads=[onesb, sqb], writes=[pb])
                    P.op("act", "activation", out=rs[:, 0:tb], in_=pt[:, 0:tb], func=AF.Sqrt, bias=eps6[:, 0:1], reads=[pb, eps6b], writes=[rsb])
                    P.op("dve", "reciprocal", rs[:, 0:tb], rs[:, 0:tb], reads=[rsb], writes=[rsb])
                    if qi == 0:
                        P.op("dve", "scalar_tensor_tensor", dst[:, t0:t0 + tb], dst[:, t0:t0 + tb], 128 ** -0.5, rs[:, 0:tb], ALU.mult, ALU.mult, reads=[dstb, rsb], writes=[dstb])
                    else:
                        P.op("dve", "tensor_tensor", dst[:, t0:t0 + tb], dst[:, t0:t0 + tb], rs[:, 0:tb], ALU.mult, reads=[dstb, rsb], writes=[dstb])
        S = []
        for d in range(2):
            st, stb = Sp[d].get()
            P.op("pool", "memset", st[:], 0.0, writes=[stb])
            S.append((st, stb))
        visited = set()
        def chunk_of(s, d):
            if d == 0: return s
            return 3 - s if s < 4 else 71 - s
        def pre(s):
            ns = [chunk_of(s, d) for d in range(2)]; cols = [d * NH + h for d in range(2)]
            toks = [slice(n * 64, (n + 1) * 64) for n in ns]
            Gd, Gdb = w128.get()
            for d in range(2):
                P.op("dve", "tensor_scalar", Gd[:, d, :], TRI2[:, d, :], gt[:, ns[d], cols[d]:cols[d] + 1], None, ALU.mult, reads=[TRI2b, gtb], writes=[Gdb])
            yield
            pa, pab = psp.get()
            P.op("pe", "matmul", pa[:, 0:128], ones[0:64, :], Gd[:].rearrange("p d j -> p (d j)"), start=True, stop=True, reads=[onesb, Gdb], writes=[pab])
            E1, E1b = w128.get(); E2, E2b = w128.get(); EGr, EGrb = egp.get()
            for d in range(2):
                P.op("act", "activation", out=E1[:, d, :], in_=pa[0:64, d * 64:(d + 1) * 64], func=AF.Exp, scale=-1.0, bias=GC[:, ns[d], cols[d]:cols[d] + 1], reads=[pab, GCb], writes=[E1b])
                P.op("act", "activation", out=E2[:, d, :], in_=pa[0:64, d * 64:(d + 1) * 64], func=AF.Exp, bias=NGC[:, ns[d], cols[d]:cols[d] + 1], reads=[pab, NGCb], writes=[E2b])
            P.op("act", "activation", out=EGr[:].rearrange("p d j -> p (d j)"), in_=pa[:, 0:128], func=AF.Exp, reads=[pab], writes=[EGrb])
            yield
            D1, D1b = w128.get(); D2, D2b = w128.get()
            P.op("dve", "scalar_tensor_tensor", D1[:], E1[:], 1.0, MS2[:], ALU.min, ALU.mult, reads=[E1b, MS2b], writes=[D1b])
            P.op("dve", "scalar_tensor_tensor", D2[:], E2[:], 1.0, TRI2[:], ALU.min, ALU.mult, reads=[E2b, TRI2b], writes=[D2b])
            for d in range(2):
                P.op("pool", "tensor_scalar", D1[:, d, :], D1[:, d, :], NBETA[:, ns[d], cols[d]:cols[d] + 1], None, ALU.mult, reads=[D1b, NBETAb], writes=[D1b])
            yield
            pk, pkb = psp.get()
            for d in range(2):
                P.op("pe", "matmul", pk[0:64, d * 64:(d + 1) * 64], KT[:, toks[d]], KT[:, toks[d]], start=True, stop=True, reads=[KTb], writes=[pkb])
                P.op("pe", "matmul", pk[0:64, 128 + d * 64:128 + (d + 1) * 64], KT[:, toks[d]], QT[:, toks[d]], start=True, stop=True, reads=[KTb, QTb], writes=[pkb])
            XX, XXb = xxp.get(); QK, QKb = qkp.get()
            P.op("dve", "tensor_tensor", XX[:, :, 0:64], pk[0:64, 0:128].rearrange("p (d j) -> p d j", d=2), D1[:], ALU.mult, reads=[pkb, D1b], writes=[XXb])
            P.op("dve", "tensor_tensor", QK[:], pk[0:64, 128:256].rearrange("p (d j) -> p d j", d=2), D2[:], ALU.mult, reads=[pkb, D2b], writes=[QKb])
            yield
            pc, pcb = psp.get()
            for d in range(2):
                P.op("pe", "transpose", pc[0:64, d * 64:(d + 1) * 64], XX[:, d, 0:64], ident[0:64, 0:64], reads=[XXb, identb], writes=[pcb])
            pcv = pc[0:64, 0:128].rearrange("p (d j) -> p d j", d=2)
            P.op("act", "activation", out=XX[:, :, 64:128], in_=pcv, func=AF.Copy, reads=[pcb], writes=[XXb])
            Z, Zb = zp.get()
            P.op("dve", "tensor_tensor", Z[:], pcv, I2[:], ALU.add, reads=[pcb, I2b], writes=[Zb])
            for k in range(1, 6):
                yield
                pd, pdb = psp.get()
                for d in range(2):
                    P.op("pe", "matmul", pd[0:64, d * 128:d * 128 + 64], XX[:, d, 64:128], XX[:, d, 0:64], start=True, stop=True, reads=[XXb], writes=[pdb])
                    P.op("pe", "matmul", pd[0:64, d * 128 + 64:d * 128 + 128], XX[:, d, 0:64], XX[:, d, 64:128], start=True, stop=True, reads=[XXb], writes=[pdb])
                if k > 1:
                    pe_, peb = psp.get()
                    for d in range(2):
                        P.op("pe", "matmul", pe_[0:64, d * 64:(d + 1) * 64], XX[:, d, 0:64], Z[:, d, :], start=True, stop=True, reads=[XXb, Zb], writes=[peb])
                XXn, XXnb = xxp.get()
                P.op("act", "activation", out=XXn[:].rearrange("p d j -> p (d j)"), in_=pd[0:64, 0:256], func=AF.Copy, reads=[pdb], writes=[XXnb])
                if k > 1:
                    Zn, Znb = zp.get()
                    P.op("dve", "tensor_tensor", Zn[:], Z[:], pe_[0:64, 0:128].rearrange("p (d j) -> p d j", d=2), ALU.add, reads=[Zb, peb], writes=[Znb])
                    Z, Zb = Zn, Znb
                XX, XXb = XXn, XXnb
            yield
            pe_, peb = psp.get()
            for d in range(2):
                P.op("pe", "matmul", pe_[0:64, d * 64:(d + 1) * 64], XX[:, d, 0:64], Z[:, d, :], start=True, stop=True, reads=[XXb, Zb], writes=[peb])
            Zn, Znb = zp.get()
            P.op("dve", "tensor_tensor", Zn[:], Z[:], pe_[0:64, 0:128].rearrange("p (d j) -> p d j", d=2), ALU.add, reads=[Zb, peb], writes=[Znb])
            Z, Zb = Zn, Znb
            yield
            ptk, ptkb = psp.get()
            for d in range(2):
                P.op("pe", "transpose", ptk[0:64, d * 128:(d + 1) * 128], KT[:, toks[d]], ident[:], reads=[KTb, identb], writes=[ptkb])
                P.op("pe", "transpose", ptk[0:64, 256 + d * 128:256 + (d + 1) * 128], VT[:, toks[d]], ident[:], reads=[VTb, identb], writes=[ptkb])
            VB, VBb = tmp.get(); KBG, KBGb = tmp.get(); KD, KDb = tmp.get()
            for d in range(2):
                n, c = ns[d], cols[d]
                P.op("act", "activation", out=KBG[:, d, :], in_=ptk[0:64, d * 128:(d + 1) * 128], func=AF.Copy, scale=BEG[:, n, c:c + 1], reads=[ptkb, BEGb], writes=[KBGb])
                P.op("dve", "tensor_scalar", KD[:, d, :], ptk[0:64, d * 128:(d + 1) * 128], EKD[:, n, c:c + 1], None, ALU.mult, reads=[ptkb, EKDb], writes=[KDb])
                P.op("dve", "tensor_scalar", VB[:, d, :], ptk[0:64, 256 + d * 128:256 + (d + 1) * 128], BETA[:, n, c:c + 1], None, ALU.mult, reads=[ptkb, BETAb], writes=[VBb])
            yield
            pw, pwb = psp.get()
            for d in range(2):
                P.op("pe", "matmul", pw[:, d * 64:(d + 1) * 64], KBG[:, d, :], Z[:, d, :], start=True, stop=True, reads=[KBGb, Zb], writes=[pwb])
            NW, NWb = nwp.get()
            P.op("act", "activation", out=NW[:].rearrange("p d j -> p (d j)"), in_=pw[:, 0:128], func=AF.Copy, scale=-1.0, reads=[pwb], writes=[NWb])
            QG, QGb = qgp.get()
            for d in range(2):
                P.op("pool", "tensor_tensor", QG[:, d, :], QT[:, toks[d]], EGr[:, d, :], ALU.mult, reads=[QTb, EGrb], writes=[QGb])
            return dict(ns=ns, cols=cols, Z=(Z, Zb), VB=(VB, VBb), KD=(KD, KDb), NW=(NW, NWb), QG=(QG, QGb), QK=(QK, QKb))
        def seq(R):
            ns, cols = R["ns"], R["cols"]
            Z, Zb = R["Z"]; VB, VBb = R["VB"]; KD, KDb = R["KD"]; NW, NWb = R["NW"]; QG, QGb = R["QG"]; QK, QKb = R["QK"]
            pv, pvb = psp.get()
            for d in range(2):
                P.op("pe", "matmul", pv[0:64, d * 128:(d + 1) * 128], Z[:, d, :], VB[:, d, :], start=True, stop=False, reads=[Zb, VBb], writes=[pvb])
                P.op("pe", "matmul", pv[0:64, d * 128:(d + 1) * 128], NW[:, d, :], S[d][0][:], start=False, stop=True, reads=[NWb, S[d][1]], writes=[pvb])
            VN, VNb = tmp.get()
            P.op("act", "activation", out=VN[:].rearrange("p d e -> p (d e)"), in_=pv[0:64, 0:256], func=AF.Copy, reads=[pvb], writes=[VNb])
            po, pob = psp.get()
            for d in range(2):
                P.op("pe", "matmul", po[0:64, d * 128:(d + 1) * 128], QG[:, d, :], S[d][0][:], start=True, stop=False, reads=[QGb, S[d][1]], writes=[pob])
                P.op("pe", "matmul", po[0:64, d * 128:(d + 1) * 128], QK[:, d, :], VN[:, d, :], start=False, stop=True, reads=[QKb, VNb], writes=[pob])
            for d in range(2):
                n = ns[d]
                if n in visited:
                    P.op("dve", "tensor_tensor", O[:, n, :], O[:, n, :], po[0:64, d * 128:(d + 1) * 128], ALU.add, reads=[Ob, pob], writes=[Ob])
                else:
                    visited.add(n)
                    P.op("dve", "tensor_copy", O[:, n, :], po[0:64, d * 128:(d + 1) * 128], reads=[pob], writes=[Ob])
            pS, pSb = psp.get()
            for d in range(2):
                P.op("pe", "matmul", pS[:, d * 128:(d + 1) * 128], KD[:, d, :], VN[:, d, :], start=True, stop=True, reads=[KDb, VNb], writes=[pSb])
            for d in range(2):
                sn, snb = Sp[d].get()
                P.op("dve", "scalar_tensor_tensor", sn[:], S[d][0][:], EGL[:, ns[d], cols[d]:cols[d] + 1], pS[:, d * 128:(d + 1) * 128], ALU.mult, ALU.add, reads=[S[d][1], EGLb, pSb], writes=[snb])
                S[d] = (sn, snb)
        def drive(gens, res, nstages=None):
            k = 0
            while any(g is not None for g in gens):
                for i, g in enumerate(gens):
                    if g is None: continue
                    try: next(g)
                    except StopIteration as e:
                        res[i] = e.value; gens[i] = None
                k += 1
                if nstages is not None and k >= nstages: break
            return all(g is None for g in gens)
        cur = [None, None]
        drive([pre(0), pre(1)], cur)
        for p in range(0, NCH, 2):
            nxt = [None, None]; gens = [pre(p + 2), pre(p + 3)] if p + 2 < NCH else [None, None]
            drive(gens, nxt, 3)
            seq(cur[0])
            drive(gens, nxt, 6)
            seq(cur[1])
            drive(gens, nxt)
            cur = nxt
        zr = PT[:, h * 128:(h + 1) * 128].rearrange("(n c) e -> c n e", c=64)
        for n0 in range(0, NCH, GRP):
            P.dma("sp", zt[:], zr[:, n0:n0 + GRP, :], writes=[ztb])
            P.op("dve", "tensor_tensor", yt[:], O[:, n0:n0 + GRP, :], O[:, n0:n0 + GRP, :], ALU.mult, reads=[Ob], writes=[ytb])
            P.op("dve", "tensor_reduce", st17[:], yt[:], AX.X, ALU.add, reads=[ytb], writes=[st17b])
            P.op("act", "activation", out=st17[:], in_=st17[:], func=AF.Sqrt, scale=1.0 / 128.0, bias=eps6[0:64, 0:1], reads=[st17b, eps6b], writes=[st17b])
            P.op("dve", "reciprocal", st17[:], st17[:], reads=[st17b], writes=[st17b])
            for i in range(GRP):
                P.op("dve", "scalar_tensor_tensor", yt[:, i, :], O[:, n0 + i, :], st17[:, i:i + 1], gn[:], ALU.mult, ALU.mult, reads=[Ob, st17b, gnb], writes=[ytb])
            P.op("act", "activation", out=zt[:], in_=zt[:], func=AF.Silu, reads=[ztb], writes=[ztb])
            P.op("dve", "tensor_tensor", yt[:], yt[:], zt[:], ALU.mult, reads=[ytb, ztb], writes=[ytb])
            pq, pqb = psp.get()
            for i in range(GRP):
                P.op("pe", "transpose", pq[:, i * 64:(i + 1) * 64], yt[:, i, :], ident[0:64, 0:64], reads=[ytb, identb], writes=[pqb])
            yT, yTb = yT_p.get()
            evac(P, E, yT[:, 0:GRP * 64], pq[:, 0:GRP * 64], [pqb], [yTb])
            yl_write(P, YL, h * 128, n0 * 64, GRP * 64, yT, [yTb])

class LazyDr(dict):
    def __init__(self, nc):
        super().__init__(); self.nc = nc; self.specs = {}; self.used_ext = []
    def __missing__(self, name):
        kind, shape = self.specs[name]
        ap = self.nc.dram_tensor(name, list(shape), F32, kind=kind).ap()
        if kind == "ExternalInput": self.used_ext.append(name)
        self[name] = ap
        return ap

def build_fused(nc, stop=None):
    dr = LazyDr(nc)
    def ext(name, shape): dr.specs[name] = ("ExternalInput", shape)
    def internal(name, shape): dr.specs[name] = ("Internal", shape)
    ext("xT_in", [KC, 256, NTH]); ext("xT_own", [D, NTH]); ext("cT", [128, KC, 2]); ext("msk", [128, 2])
    ext("cw2", [256, 512]); ext("cl", [NLAT, NLAT]); ext("sln", [NLAT, NLAT])
    ext("cosT", [64, NLAT]); ext("sinT", [64, NLAT]); ext("Rm", [64, 64]); ext("ident", [128, 128])
    ext("tri2", [64, 2, 64]); ext("ms2", [64, 2, 64]); ext("i2", [64, 2, 64])
    for l in range(2):
        ext(f"w_ada{l}", [D, 12288]); ext(f"b_ada{l}", [128, 96]); ext(f"g{l}", [128, 4, KC]); ext(f"w_inr{l}", [D, WINR])
        ext(f"w_gate{l}", [D, 6144]); ext(f"w_branch{l}", [3, 1024, D]); ext(f"w_out{l}", [D, D]); ext(f"w_ffn_in{l}", [D, 2 * FF]); ext(f"w_ffn_out{l}", [FF, D])
        ext(f"gq{l}", [128, 4]); ext(f"gkv{l}", [128, 2]); ext(f"wqn{l}", [512, 512]); ext(f"wqr{l}", [512, 256]); ext(f"wk{l}", [256, 512]); ext(f"wv{l}", [256, 512])
        ext(f"convw{l}", [128, 12, 5]); ext(f"alog{l}", [64, 8]); ext(f"dtb{l}", [64, 8]); ext(f"gnorm{l}", [64, 128])
    dr.specs["xo"] = ("ExternalOutput", [D, NLH])
    internal("PF", [FM_ROWS, NTOT]); internal("PT", [NTOT, TM_W]); internal("YL", [17, 1536, 256]); internal("YG", [17, 2 * 1536, 256])
    internal("XL2", [D, NTH]); internal("XG", [KC, 256, NTH])
    dr["xo"]
    with ExitStack() as es:
        P = Prog(nc, es)
        AR = Arena(P)
        def steps():
            for l in range(2):
                last = (l == 1)
                yield f"A{l}", lambda: emit_A(P, AR, dr, l)
                yield f"DN{l}", lambda: emit_DN(P, AR, dr, l, last)
                yield f"FN{l}", lambda: emit_FN(P, AR, dr, last)
                yield f"MLA{l}", lambda: emit_MLA(P, AR, dr, l, last)
                def g1():
                    AR.end()
                    for blk in range(17): P.coll("AllGather", dr["YG"][blk], dr["YL"][blk], PAIRS)
                yield f"G1{l}", g1
                yield f"C{l}", lambda: emit_C(P, AR, dr, l, last)
                if not last:
                    def g2():
                        AR.end()
                        for c in range(KC): P.coll("AllGather", dr["XG"][c], dr["XL2"][c * 128:(c + 1) * 128, :], PAIRS)
                    yield f"G2{l}", g2
        for name, fn in steps():
            if stop is not None and name not in stop: continue
            fn()
        AR.end()
        P.finish()
        print("fused ops", P.n_ops, dict(P.ep))
    nc._used_ext = list(dr.used_ext)
    return nc

from concourse.bass_utils import run_bass_kernel_spmd

def dft_tables():
    n = np.arange(256, dtype=np.float64)
    ang = 2 * np.pi * np.outer(n, n) / 256.0
    cw2 = np.concatenate([np.cos(ang), np.sin(ang)], 1).astype(np.float32)
    n = np.arange(NLAT, dtype=np.int64)
    ang = 2 * np.pi * (np.outer(n, n) % NLAT).astype(np.float64) / NLAT
    return cw2, np.cos(ang).astype(np.float32), (-np.sin(ang)).astype(np.float32)

def rope_tables():
    rows = NLAT // 64
    row = np.repeat(np.arange(rows, dtype=np.float32), 64)
    col = np.tile(np.arange(64, dtype=np.float32), rows)
    inv = (10000.0 ** (-np.arange(0, 32, 2, dtype=np.float32) / 32)).astype(np.float32)
    ar = row[:, None] * inv; ac = col[:, None] * inv
    ang = np.concatenate([ar, ar, ac, ac], -1)
    cosT = np.ascontiguousarray(np.cos(ang).T.astype(np.float32)); sinT = np.ascontiguousarray(np.sin(ang).T.astype(np.float32))
    R = np.zeros((64, 64), np.float32)
    for i in range(16):
        R[16 + i, i] = -1; R[i, 16 + i] = 1; R[48 + i, 32 + i] = -1; R[32 + i, 48 + i] = 1
    return cosT, sinT, R

def dn_consts():
    p = np.arange(64)[:, None]; f = np.arange(64)[None, :]
    ple = (p <= f).astype(np.float32); pge = (p >= f).astype(np.float32)
    pgt = (p > f).astype(np.float32); plt = (p < f).astype(np.float32)
    return {"tri2": np.ascontiguousarray(np.stack([ple, pge], 1)), "ms2": np.ascontiguousarray(np.stack([pgt, plt], 1)),
            "i2": np.ascontiguousarray(np.stack([np.eye(64, dtype=np.float32)] * 2, 1)), "ident": np.eye(128, dtype=np.float32)}

_NC = []
STOP = None
def kernel(**inputs):
    inp = {k: np.asarray(v) for k, v in inputs.items()}
    B = inp["x"].shape[0]
    if not _NC:
        nc = bass.Bass("TRN2", target_bir_lowering=False, num_devices=8)
        build_fused(nc, STOP); _NC.append(nc)
    nc = _NC[0]
    cw2, cl, sln = dft_tables(); cosT, sinT, R = rope_tables(); dnc = dn_consts()
    shared = {"cw2": cw2, "cl": cl, "sln": sln, "cosT": cosT, "sinT": sinT, "Rm": R}
    shared.update(dnc)
    for l in range(2):
        shared[f"w_ada{l}"] = np.ascontiguousarray(inp["w_ada"][l])
        shared[f"b_ada{l}"] = np.ascontiguousarray(inp["b_ada"][l].reshape(96, 128).T)
        shared[f"g{l}"] = np.ascontiguousarray(inp["norm_g"][l].reshape(4, KC, 128).transpose(2, 0, 1))
        shared[f"w_gate{l}"] = np.ascontiguousarray(inp["w_in"][l][:, 5984:])
        shared[f"w_branch{l}"] = np.ascontiguousarray(inp["w_branch"][l]); shared[f"w_out{l}"] = np.ascontiguousarray(inp["w_out"][l])
        shared[f"w_ffn_in{l}"] = np.ascontiguousarray(inp["w_ffn_in"][l]); shared[f"w_ffn_out{l}"] = np.ascontiguousarray(inp["w_ffn_out"][l])
        shared[f"gq{l}"] = np.ascontiguousarray(inp["mla_q_norm_g"][l].reshape(4, 128).T)
        shared[f"gkv{l}"] = np.ascontiguousarray(inp["mla_kv_norm_g"][l].reshape(2, 128).T)
        shared[f"gnorm{l}"] = np.ascontiguousarray(np.tile(inp["dn_norm_g"][l][None], (64, 1)).astype(np.float32))
    percore = {}
    for r in range(2):
        heads = np.arange(r * 4, (r + 1) * 4)
        colsel = np.concatenate([heads, 8 + heads])
        pc = {}
        for l in range(2):
            w = inp["w_in"][l]
            cols = np.concatenate([np.arange(r * 512, (r + 1) * 512), 1024 + np.arange(r * 512, (r + 1) * 512), 2048 + np.arange(r * 512, (r + 1) * 512),
                                   np.arange(4128, 4960), 4960 + np.arange(r * 512, (r + 1) * 512), 3072 + np.arange(r * 512, (r + 1) * 512), 4096 + colsel, 4112 + colsel])
            assert cols.size == WINR
            pc[f"w_inr{l}"] = np.ascontiguousarray(w[:, cols])
            wuq = inp["w_uq"][l]; wukv = inp["w_ukv"][l]
            pc[f"wqn{l}"] = np.ascontiguousarray(np.concatenate([wuq[:, h * 192: h * 192 + 128] for h in heads], 1))
            pc[f"wqr{l}"] = np.ascontiguousarray(np.concatenate([wuq[:, h * 192 + 128: h * 192 + 192] for h in heads], 1))
            pc[f"wk{l}"] = np.ascontiguousarray(np.concatenate([wukv[:, h * 256: h * 256 + 128] for h in heads], 1))
            pc[f"wv{l}"] = np.ascontiguousarray(np.concatenate([wukv[:, h * 256 + 128: h * 256 + 256] for h in heads], 1))
            conv = inp["dn_conv"][l]
            cwl = [conv[:, qi * 1024 + h * 128: qi * 1024 + (h + 1) * 128].T for qi in range(3) for h in heads]
            pc[f"convw{l}"] = np.ascontiguousarray(np.stack(cwl, 1).astype(np.float32))
            pc[f"alog{l}"] = np.ascontiguousarray(np.tile(inp["dn_a_log"][l].reshape(16)[colsel][None], (64, 1)).astype(np.float32))
            pc[f"dtb{l}"] = np.ascontiguousarray(np.tile(inp["dn_dt_bias"][l].reshape(16)[colsel][None], (64, 1)).astype(np.float32))
        m = np.zeros((128, 2), np.float32); m[:, r] = 1.0
        pc["msk"] = m
        percore[r] = pc
    in_maps = []
    for i in range(8):
        b, r = i // 2, i % 2
        halves = [np.concatenate([inp["x"][b, q * NLH:(q + 1) * NLH], inp["ctx"][b, q * NCH2:(q + 1) * NCH2]], 0).T for q in range(2)]
        d = dict(shared); d.update(percore[r])
        d["xT_in"] = np.ascontiguousarray(np.stack([h_.reshape(KC, 128, NTH) for h_ in halves], 1).reshape(KC, 256, NTH))
        d["xT_own"] = np.ascontiguousarray(halves[r])
        cvec = np.stack([inp["c"][b], inp["c_ctx"]], -1)
        d["cT"] = np.ascontiguousarray(cvec.reshape(KC, 128, 2).transpose(1, 0, 2))
        in_maps.append(d)
    in_maps = [{k: d[k] for k in nc._used_ext} for d in in_maps]
    res = run_bass_kernel_spmd(nc, in_maps, core_ids=list(range(8))).results
    out = np.empty((B, NLAT, D), np.float32)
    for i in range(8):
        b, r = i // 2, i % 2
        out[b, r * NLH:(r + 1) * NLH] = res[i]["xo"].T
    return out
```
